# Optimizing a Trainium2 kernel written in Bass

```python
import math
import jax, jax.numpy as jnp
from jax import lax
import numpy as np

D_MODEL = 2048
BATCH = 4
SEQ = 4096
DEPTH = 2

N_EVEN = (DEPTH + 1) // 2
N_ODD = DEPTH // 2

SSD_WIDTH = D_MODEL
SSD_HEAD_DIM = 64
SSD_HEADS = SSD_WIDTH // SSD_HEAD_DIM
SSD_GROUPS = 4
SSD_STATE = 128
SSD_CONV = 4
SSD_CHUNK = 256
SSD_CONV_DIM = SSD_WIDTH + 2 * SSD_GROUPS * SSD_STATE

ATT_HEADS = 16
ATT_HEAD_DIM = 128
ATT_WIDTH = ATT_HEADS * ATT_HEAD_DIM
MOBA_BLOCK = 256
MOBA_TOPK = 3
MOBA_Q_CHUNK = 64
ROPE_THETA = 500000.0
ROPE_DIM = ATT_HEAD_DIM // 4

_Z_END = SSD_WIDTH
_XBC_END = _Z_END + SSD_CONV_DIM
_DT_END = _XBC_END + SSD_HEADS
_Q_END = _DT_END + ATT_WIDTH
_K_END = _Q_END + ATT_WIDTH
_V_END = _K_END + ATT_WIDTH
IN0_WIDTH = _V_END + ATT_WIDTH
SPLITS0 = (_Z_END, _XBC_END, _DT_END, _Q_END, _K_END, _V_END)
MIX0_WIDTH = SSD_WIDTH + ATT_WIDTH

S5_WIDTH = D_MODEL
S5_GROUP = 16
S5_GROUPS = S5_WIDTH // S5_GROUP
S5_STATE = 64
S5_SCAN_CHUNK = 128
S5_C_STD = 0.5

DEEPNORM_ALPHA = (2 * DEPTH) ** 0.25
DEEPNORM_BETA = (8 * DEPTH) ** -0.25
LN_EPS = 1e-5
RMS_EPS = 1e-5
NEG_INF = -1e30

kernel_name = 'hybrid_ssd_moba_s5_deepnorm'


def layer_norm(x, g, b):
    xf = x.astype(jnp.float32)
    mu = jnp.mean(xf, axis=-1, keepdims=True)
    var = jnp.mean(jnp.square(xf - mu), axis=-1, keepdims=True)
    y = (xf - mu) * lax.rsqrt(var + LN_EPS) * g.astype(jnp.float32) + b.astype(jnp.float32)
    return y.astype(x.dtype)


def rms_norm(x, g):
    xf = x.astype(jnp.float32)
    return xf * lax.rsqrt(jnp.mean(jnp.square(xf), axis=-1, keepdims=True) + RMS_EPS) * g.astype(jnp.float32)


def causal_depthwise_conv(x, w, b):
    k = w.shape[0]
    y = lax.conv_general_dilated(x, w[:, None, :].astype(x.dtype), window_strides=(1,),
                                 padding=[(k - 1, 0)], dimension_numbers=('NWC', 'WIO', 'NWC'),
                                 feature_group_count=x.shape[-1])
    return y + b.astype(x.dtype)


def segsum(a):
    t = a.shape[-1]
    cs = jnp.cumsum(a, axis=-1)
    diff = cs[..., :, None] - cs[..., None, :]
    return jnp.where(jnp.tril(jnp.ones((t, t), dtype=bool)), diff, -jnp.inf)


def ssd_chunked_scan(x, a, b, c):
    bsz, s, h, p = x.shape
    g, n = b.shape[-2:]
    r = h // g
    pad = (-s) % SSD_CHUNK
    x = jnp.pad(x, ((0, 0), (0, pad), (0, 0), (0, 0)))
    a = jnp.pad(a, ((0, 0), (0, pad), (0, 0)))
    b = jnp.pad(b, ((0, 0), (0, pad), (0, 0), (0, 0)))
    c = jnp.pad(c, ((0, 0), (0, pad), (0, 0), (0, 0)))
    nc, l = (s + pad) // SSD_CHUNK, SSD_CHUNK
    x = x.reshape(bsz, nc, l, g, r, p)
    a = a.reshape(bsz, nc, l, g, r).transpose(0, 3, 4, 1, 2)
    b = b.reshape(bsz, nc, l, g, n)
    c = c.reshape(bsz, nc, l, g, n)
    a_cum = jnp.cumsum(a, axis=-1)
    decay = jnp.exp(segsum(a))
    cb = jnp.einsum('bclgn,bcsgn->bcgls', c, b)
    y_diag = jnp.einsum('bcgls,bgrcls,bcsgrp->bclgrp', cb, decay, x)
    decay_to_end = jnp.exp(a_cum[..., -1:] - a_cum)
    states = jnp.einsum('bclgn,bgrcl,bclgrp->bcgrpn', b, decay_to_end, x)
    states = jnp.concatenate([jnp.zeros_like(states[:, :1]), states], axis=1)
    chunk_decay = jnp.exp(segsum(jnp.pad(a_cum[..., -1], ((0, 0), (0, 0), (0, 0), (1, 0)))))
    states = jnp.einsum('bgrzc,bcgrpn->bzgrpn', chunk_decay, states)[:, :-1]
    y_off = jnp.einsum('bclgn,bcgrpn,bgrcl->bclgrp', c, states, jnp.exp(a_cum))
    y = (y_diag + y_off).reshape(bsz, nc * l, h, p)
    return y[:, :s]


def apply_partial_rotary(x, pos):
    half = ROPE_DIM // 2
    inv_freq = ROPE_THETA ** (-(jnp.arange(half, dtype=jnp.float32) * 2.0 / ROPE_DIM))
    ang = pos.astype(jnp.float32)[:, None] * inv_freq[None, :]
    cos = jnp.cos(ang)[None, :, None, :]
    sin = jnp.sin(ang)[None, :, None, :]
    xr = x[..., :ROPE_DIM].astype(jnp.float32)
    x1, x2 = xr[..., :half], xr[..., half:]
    rot = jnp.concatenate([x1 * cos - x2 * sin, x1 * sin + x2 * cos], axis=-1)
    return jnp.concatenate([rot.astype(x.dtype), x[..., ROPE_DIM:]], axis=-1)


def moba_attention(q, k, v):
    bsz, s, h, dh = q.shape
    s_pad = -(-s // MOBA_BLOCK) * MOBA_BLOCK
    padw = ((0, 0), (0, s_pad - s), (0, 0), (0, 0))
    q, k, v = jnp.pad(q, padw), jnp.pad(k, padw), jnp.pad(v, padw)
    nb = s_pad // MOBA_BLOCK
    n_sel = min(MOBA_TOPK, nb - 1)
    scale = dh ** -0.5
    kb = k.reshape(bsz, nb, MOBA_BLOCK, h, dh).transpose(0, 3, 1, 2, 4)
    vb = v.reshape(bsz, nb, MOBA_BLOCK, h, dh).transpose(0, 3, 1, 2, 4)
    k_mean = jnp.mean(kb.astype(jnp.float32), axis=3)
    b_ix = jnp.arange(bsz)[:, None, None]
    h_ix = jnp.arange(h)[None, None, :]
    blk_ids = jnp.arange(nb)

    def query_chunk(ci):
        start = ci * MOBA_Q_CHUNK
        blk = start // MOBA_BLOCK
        qc = lax.dynamic_slice_in_dim(q, start, MOBA_Q_CHUNK, axis=1).astype(jnp.float32)
        qpos = start + jnp.arange(MOBA_Q_CHUNK)
        blk_start = blk * MOBA_BLOCK
        k_own = lax.dynamic_slice_in_dim(k, blk_start, MOBA_BLOCK, axis=1).astype(jnp.float32)
        v_own = lax.dynamic_slice_in_dim(v, blk_start, MOBA_BLOCK, axis=1).astype(jnp.float32)
        kpos = blk_start + jnp.arange(MOBA_BLOCK)
        s_own = jnp.einsum('bqhd,bkhd->bqhk', qc, k_own) * scale
        causal = (kpos[None, :] <= qpos[:, None])[None, :, None, :]
        s_own = jnp.where(causal, s_own, NEG_INF)
        if n_sel == 0:
            p = jax.nn.softmax(s_own, axis=-1)
            return jnp.einsum('bqhk,bkhd->bqhd', p, v_own).astype(q.dtype)
        gate = jnp.einsum('bqhd,bhnd->bqhn', qc, k_mean)
        gate = jnp.where((blk_ids < blk)[None, None, None, :], gate, NEG_INF)
        _, top_i = lax.top_k(gate, n_sel)
        valid = jnp.arange(n_sel) < blk
        scores = []
        for r in range(n_sel):
            k_r = kb[b_ix, h_ix, top_i[..., r]].astype(jnp.float32)
            s_r = jnp.einsum('bqhd,bqhkd->bqhk', qc, k_r) * scale
            scores.append(jnp.where(valid[r], s_r, NEG_INF))
        scores.append(s_own)
        p = jax.nn.softmax(jnp.concatenate(scores, axis=-1), axis=-1)
        out = jnp.einsum('bqhk,bkhd->bqhd', p[..., n_sel * MOBA_BLOCK:], v_own)
        for r in range(n_sel):
            v_r = vb[b_ix, h_ix, top_i[..., r]].astype(jnp.float32)
            out = out + jnp.einsum('bqhk,bqhkd->bqhd', p[..., r * MOBA_BLOCK:(r + 1) * MOBA_BLOCK], v_r)
        return out.astype(q.dtype)

    out = lax.map(query_chunk, jnp.arange(s_pad // MOBA_Q_CHUNK))
    out = out.transpose(1, 0, 2, 3, 4).reshape(bsz, s_pad, h, dh)
    return out[:, :s]


def ssd_moba_mixer(x, w_in, conv_w, conv_b, dt_bias, a_log, d_skip, norm_g, w_out):
    bsz, s, _ = x.shape
    f32 = jnp.float32
    z, xbc, dt, q, k, v, gate = jnp.split(x @ w_in, SPLITS0, axis=-1)
    xbc = jax.nn.silu(causal_depthwise_conv(xbc, conv_w, conv_b))
    xs, bm, cm = jnp.split(xbc, (SSD_WIDTH, SSD_WIDTH + SSD_GROUPS * SSD_STATE), axis=-1)
    dt = jax.nn.softplus(dt.astype(f32) + dt_bias.astype(f32))
    a = -jnp.exp(a_log.astype(f32))
    xh = xs.astype(f32).reshape(bsz, s, SSD_HEADS, SSD_HEAD_DIM)
    y = ssd_chunked_scan(xh * dt[..., None], dt * a,
                         bm.astype(f32).reshape(bsz, s, SSD_GROUPS, SSD_STATE),
                         cm.astype(f32).reshape(bsz, s, SSD_GROUPS, SSD_STATE))
    y = (y + d_skip.astype(f32)[:, None] * xh).reshape(bsz, s, SSD_WIDTH)
    y_a = rms_norm(y * jax.nn.silu(z.astype(f32)), norm_g).astype(x.dtype)
    pos = jnp.arange(s)
    q = apply_partial_rotary(q.reshape(bsz, s, ATT_HEADS, ATT_HEAD_DIM), pos)
    k = apply_partial_rotary(k.reshape(bsz, s, ATT_HEADS, ATT_HEAD_DIM), pos)
    v = v.reshape(bsz, s, ATT_HEADS, ATT_HEAD_DIM)
    y_b = moba_attention(q, k, v).reshape(bsz, s, ATT_WIDTH) * jax.nn.silu(gate)
    return jnp.concatenate([y_a, y_b], axis=-1) @ w_out


def _linear_recurrence(e1, e2):
    a1, b1 = e1
    a2, b2 = e2
    return a1 * a2, a2 * b1 + b2


def s5_ssm(u, lam_re, lam_im, log_dt, b_re, b_im, c_re, c_im, d_skip):
    bsz, s, w = u.shape
    f32 = jnp.float32
    lam = lax.complex(lam_re.astype(f32), lam_im.astype(f32))
    dt = jnp.exp(log_dt.astype(f32))[:, None]
    lam_bar = jnp.exp(lam * dt)
    b_bar = ((lam_bar - 1.0) / lam)[..., None] * lax.complex(b_re.astype(f32), b_im.astype(f32))
    c_mat = lax.complex(c_re.astype(f32), c_im.astype(f32))
    nch = s // S5_SCAN_CHUNK
    ug = u.astype(f32).reshape(bsz, nch, S5_SCAN_CHUNK, S5_GROUPS, S5_GROUP).transpose(1, 0, 2, 3, 4)

    def step(h, u_blk):
        bu = jnp.einsum('gnm,blgm->blgn', b_bar, u_blk.astype(jnp.complex64))
        bu = bu.at[:, 0].add(lam_bar * h)
        a = jnp.broadcast_to(lam_bar, bu.shape)
        _, hs = lax.associative_scan(_linear_recurrence, (a, bu), axis=1)
        y = jnp.einsum('gmn,blgn->blgm', c_mat, hs).real
        return hs[:, -1], y

    h0 = jnp.zeros((bsz, S5_GROUPS, S5_STATE), jnp.complex64)
    _, ys = lax.scan(step, h0, ug)
    y = ys.transpose(1, 0, 2, 3, 4).reshape(bsz, s, w)
    return (y + d_skip.astype(f32) * u.astype(f32)).astype(u.dtype)


def s5_mixer(x, w_in, lam_re, lam_im, log_dt, b_re, b_im, c_re, c_im, d_skip, w_glu, w_out):
    u, gate = jnp.split(x @ w_in, 2, axis=-1)
    y = jax.nn.gelu(s5_ssm(u, lam_re, lam_im, log_dt, b_re, b_im, c_re, c_im, d_skip))
    ga, gb = jnp.split(y @ w_glu, 2, axis=-1)
    y = ga * jax.nn.sigmoid(gb)
    return (y * jax.nn.silu(gate)) @ w_out


def setup_inputs(seed: int = 0) -> dict:
    key = jax.random.key(seed)
    ks = jax.random.split(key, 24)
    f32 = jnp.float32

    def nrm(k, shape, std):
        return jax.random.normal(k, shape, f32) * std

    x = nrm(ks[0], (BATCH, SEQ, D_MODEL), 1.0)
    in0_w = nrm(ks[1], (N_EVEN, D_MODEL, IN0_WIDTH), D_MODEL ** -0.5)
    conv_w = nrm(ks[2], (N_EVEN, SSD_CONV, SSD_CONV_DIM), SSD_CONV ** -0.5)
    conv_b = nrm(ks[3], (N_EVEN, SSD_CONV_DIM), 0.01)
    dt0 = jnp.exp(jax.random.uniform(ks[4], (N_EVEN, SSD_HEADS), f32, math.log(1e-3), math.log(1e-1)))
    dt_bias = dt0 + jnp.log(-jnp.expm1(-dt0))
    a_log = jnp.log(jax.random.uniform(ks[5], (N_EVEN, SSD_HEADS), f32, 1.0, 16.0))
    ssd_d = 1.0 + nrm(ks[6], (N_EVEN, SSD_HEADS), 0.1)
    ssd_norm_g = 1.0 + nrm(ks[7], (N_EVEN, SSD_WIDTH), 0.02)
    out0_w = nrm(ks[8], (N_EVEN, MIX0_WIDTH, D_MODEL), DEEPNORM_BETA * MIX0_WIDTH ** -0.5)
    in1_w = nrm(ks[9], (N_ODD, D_MODEL, 2 * S5_WIDTH), D_MODEL ** -0.5)
    s5_lam_re = -0.5 + nrm(ks[10], (N_ODD, S5_GROUPS, S5_STATE), 0.01)
    s5_lam_im = math.pi * jnp.arange(S5_STATE, dtype=f32) + nrm(ks[11], (N_ODD, S5_GROUPS, S5_STATE), 0.01)
    s5_log_dt = jax.random.uniform(ks[12], (N_ODD, S5_GROUPS), f32, math.log(1e-3), math.log(1e-1))
    s5_b_re = nrm(ks[13], (N_ODD, S5_GROUPS, S5_STATE, S5_GROUP), (2 * S5_GROUP) ** -0.5)
    s5_b_im = nrm(ks[14], (N_ODD, S5_GROUPS, S5_STATE, S5_GROUP), (2 * S5_GROUP) ** -0.5)
    s5_c_re = nrm(ks[15], (N_ODD, S5_GROUPS, S5_GROUP, S5_STATE), S5_C_STD)
    s5_c_im = nrm(ks[16], (N_ODD, S5_GROUPS, S5_GROUP, S5_STATE), S5_C_STD)
    s5_d = nrm(ks[17], (N_ODD, S5_WIDTH), 1.0)
    glu_w = nrm(ks[18], (N_ODD, S5_WIDTH, 2 * S5_WIDTH), S5_WIDTH ** -0.5)
    out1_w = nrm(ks[19], (N_ODD, S5_WIDTH, D_MODEL), DEEPNORM_BETA * S5_WIDTH ** -0.5)
    ln_g = 1.0 + nrm(ks[20], (DEPTH, D_MODEL), 0.02)
    ln_b = nrm(ks[21], (DEPTH, D_MODEL), 0.02)
    return {'x': x, 'in0_w': in0_w, 'conv_w': conv_w, 'conv_b': conv_b, 'dt_bias': dt_bias,
            'a_log': a_log, 'ssd_d': ssd_d, 'ssd_norm_g': ssd_norm_g, 'out0_w': out0_w,
            'in1_w': in1_w, 's5_lam_re': s5_lam_re, 's5_lam_im': s5_lam_im, 's5_log_dt': s5_log_dt,
            's5_b_re': s5_b_re, 's5_b_im': s5_b_im, 's5_c_re': s5_c_re, 's5_c_im': s5_c_im,
            's5_d': s5_d, 'glu_w': glu_w, 'out1_w': out1_w, 'ln_g': ln_g, 'ln_b': ln_b}


def reference(x, in0_w, conv_w, conv_b, dt_bias, a_log, ssd_d, ssd_norm_g, out0_w,
              in1_w, s5_lam_re, s5_lam_im, s5_log_dt, s5_b_re, s5_b_im, s5_c_re, s5_c_im,
              s5_d, glu_w, out1_w, ln_g, ln_b):
    for layer in range(DEPTH):
        i = layer // 2
        if layer % 2 == 0:
            h = ssd_moba_mixer(x, in0_w[i], conv_w[i], conv_b[i], dt_bias[i], a_log[i],
                               ssd_d[i], ssd_norm_g[i], out0_w[i])
        else:
            h = s5_mixer(x, in1_w[i], s5_lam_re[i], s5_lam_im[i], s5_log_dt[i], s5_b_re[i],
                         s5_b_im[i], s5_c_re[i], s5_c_im[i], s5_d[i], glu_w[i], out1_w[i])
        x = layer_norm(DEEPNORM_ALPHA * x + h, ln_g[layer], ln_b[layer])
    return x
```

```python
import math
from contextlib import contextmanager, ExitStack
import numpy as np
import concourse.bass as bass
import concourse.mybir as mybir
from concourse.bass_utils import run_bass_kernel_spmd

F32 = mybir.dt.float32
BF16 = mybir.dt.bfloat16
AF = mybir.ActivationFunctionType
ALU = mybir.AluOpType
AX = mybir.AxisListType

T = 4096
NT = T // 128
NG = T // 512
NCH = T // 256
D = 2048
IN0 = 13344
C_Z, C_XBC, C_DT, C_Q, C_K, C_V, C_G = 0, 2048, 5120, 5152, 7200, 9248, 11296
ALPHA = 4.0 ** 0.25
SEM_CAP = 30000


class Ev:
    __slots__ = ("eng", "sem", "val")

    def __init__(self, eng, sem, val):
        self.eng = eng
        self.sem = sem
        self.val = val


class Prog:
    def __init__(self, nc, n_dma_sems=10):
        self.nc = nc
        self.h = {"pe": nc.tensor, "act": nc.scalar, "dve": nc.vector, "pool": nc.gpsimd, "sp": nc.sync}
        self.sem = {}
        self.cnt = {}
        self.gen = {}
        self.waited = {e: {} for e in self.h}
        self._cms = []
        for e in self.h:
            self._new_sem(e)
        self.last_w = {}
        self.readers = {}
        self.bank_last = {}
        self.bankmap = {}
        self.dma_sems = {}
        self.dma_next = {}
        for e in ("sp", "pool", "act"):
            self.dma_sems[e] = [[self._alloc(f"dma_{e}_{i}"), 0] for i in range(n_dma_sems)]
            self.dma_next[e] = 0
        self.n_inst = 0

    def _alloc(self, name):
        cm = self.nc.semaphore(name)
        s = cm.__enter__()
        self._cms.append(cm)
        return s

    def _new_sem(self, e):
        g = self.gen.get(e, -1) + 1
        self.gen[e] = g
        self.sem[e] = self._alloc(f"s_{e}_{g}")
        self.cnt[e] = 0

    def _deps(self, reads, writes, e=None):
        deps = []
        for r in reads:
            ev = self.last_w.get(r)
            if ev is not None:
                deps.append(ev)
        for w in writes:
            ev = self.last_w.get(w)
            if ev is not None:
                deps.append(ev)
        if e in ("act", "dve", "pool"):
            strong = [ev for ev in deps if ev.eng == e]
            self._wait(e, strong, same_engine_ok=False)
        for w in writes:
            deps.extend(self.readers.get(w, ()))
        return deps

    def _wait(self, e, deps, same_engine_ok=True):
        wd = self.waited[e]
        need = {}
        for ev in deps:
            if same_engine_ok and ev.eng == e:
                continue
            k = id(ev.sem)
            if wd.get(k, 0) >= ev.val:
                continue
            if k not in need or need[k].val < ev.val:
                need[k] = ev
        for k, ev in need.items():
            self.h[e].wait_ge(ev.sem, ev.val)
            wd[k] = ev.val
            self.n_inst += 1

    def _commit(self, ev, reads, writes):
        for r in reads:
            lst = self.readers.setdefault(r, [])
            lst[:] = [x for x in lst if x.sem is not ev.sem]
            lst.append(ev)
        for w in writes:
            self.last_w[w] = ev
            self.readers[w] = []

    def op(self, e, fn, reads=(), writes=(), sig=True, banks=(), strict=()):
        if strict:
            sdeps = [self.last_w[r] for r in strict if r in self.last_w]
            self._wait(e, sdeps, same_engine_ok=False)
            reads = list(reads) + list(strict)
        deps = self._deps(reads, writes, e)
        banks = set(banks)
        for r in list(reads) + list(writes):
            nm = r if isinstance(r, str) else r[0]
            if isinstance(nm, str) and (nm.startswith("ps") or nm.startswith("ptr")):
                banks.add(self.bankmap.get(r, r))
        for b in banks:
            for eng, bev in self.bank_last.setdefault(b, {}).items():
                if eng != e:
                    deps.append(bev)
        self._wait(e, deps)
        if sig and self.cnt[e] >= SEM_CAP:
            self._new_sem(e)
        inst = fn(self.h[e])
        self.n_inst += 1
        if sig:
            self.cnt[e] += 1
            inst.then_inc(self.sem[e], 1)
            ev = Ev(e, self.sem[e], self.cnt[e])
        else:
            ev = Ev(e, self.sem[e], self.cnt[e] + 1)
        self._commit(ev, reads, writes)
        for b in banks:
            self.bank_last[b][e] = ev
        return ev

    def dma(self, e, out, in_, reads=(), writes=(), **kw):
        deps = self._deps(reads, writes)
        i = self.dma_next[e]
        self.dma_next[e] = (i + 1) % len(self.dma_sems[e])
        slot = self.dma_sems[e][i]
        if slot[1] > 0:
            deps.append(Ev("dma", slot[0], slot[1]))
        if slot[1] + 16 > SEM_CAP:
            self._wait(e, deps, same_engine_ok=False)
            deps = []
            slot[0] = self._alloc(f"dma_{e}_{i}_{self.n_inst}")
            slot[1] = 0
        self._wait(e, deps, same_engine_ok=False)
        inst = self.h[e].dma_start(out=out, in_=in_, **kw)
        slot[1] += 16
        inst.then_inc(slot[0], 16)
        self.n_inst += 1
        ev = Ev("dma", slot[0], slot[1])
        self._commit(ev, reads, writes)
        return ev

    def barrier(self):
        evs = [Ev(e, self.sem[e], self.cnt[e]) for e in self.h if self.cnt[e] > 0]
        for e in self.dma_sems:
            for s, v in self.dma_sems[e]:
                if v > 0:
                    evs.append(Ev("dma", s, v))
        for e in self.h:
            self._wait(e, evs, same_engine_ok=True)
        self.last_w.clear()
        self.readers.clear()
        self.bank_last.clear()


class MK:
    def __init__(self, nc, dbg=(), feed=()):
        self._lazy = {}
        self.feed = set(feed)
        self.nc = nc
        self.P = Prog(nc)
        self.dbg = set(dbg)
        self._es = None
        self._sid = 0
        self.dr = {}

    @contextmanager
    def stage(self):
        self._sid += 1
        es = ExitStack()
        self._es = es
        try:
            yield
        finally:
            self.P.barrier()
            es.close()

    def sb(self, name, shape, dt):
        return self._es.enter_context(self.nc.sbuf_tensor(f"{name}_s{self._sid}", shape, dt))

    def ps(self, name, shape, dt=F32):
        return self._es.enter_context(self.nc.psum_tensor(f"{name}_s{self._sid}", shape, dt))

    def din(self, name, shape, dt=F32):
        t = self.nc.dram_tensor(name, list(shape), dt, kind="ExternalInput").ap()
        self.dr[name] = t
        return t

    def dout(self, name, shape, dt=F32):
        t = self.nc.dram_tensor(name, list(shape), dt, kind="ExternalOutput").ap()
        self.dr[name] = t
        return t

    def scratch(self, name, shape, dt):
        kind = "ExternalOutput" if name in self.dbg else "Internal"
        t = self.nc.dram_tensor(name, list(shape), dt, kind=kind).ap()
        self.dr[name] = t
        return t

    def __getattr__(self, attr):
        lz = self.__dict__.get("_lazy", {})
        if attr in lz:
            kind, name, shape, dt = lz[attr]
            if kind == "in" or (kind == "scratch" and name in self.feed):
                t = self.din(name, shape, dt)
            elif kind == "out":
                t = self.dout(name, shape, dt)
            else:
                t = self.scratch(name, shape, dt)
            self.__dict__[attr] = t
            return t
        raise AttributeError(attr)

    def declare(self):
        self._lazy["x"] = ("in", "x", [T, D], F32)
        self._lazy["xT"] = ("in", "xT", [D, T], F32)
        self._lazy["w0"] = ("in", "in0_w", [D, IN0], F32)
        self._lazy["cw"] = ("in", "cw", [128, 24, 4], F32)
        self._lazy["cb"] = ("in", "cb", [128, 24], F32)
        self._lazy["dtb"] = ("in", "dtb", [1, 32], F32)
        self._lazy["alog"] = ("in", "alog", [1, 32], F32)
        self._lazy["ssd_dp"] = ("in", "ssd_dp", [128, 16], F32)
        self._lazy["normg"] = ("in", "normg", [128, 16], F32)
        self._lazy["cosT"] = ("in", "cosT", [32, T], F32)
        self._lazy["sinS"] = ("in", "sinS", [32, T], F32)
        self._lazy["pm32"] = ("in", "pm32", [32, 32], F32)
        self._lazy["sel"] = ("in", "sel", [32, 32, 128], F32)
        self._lazy["triu"] = ("in", "triu", [128, 128], F32)
        self._lazy["maskg"] = ("in", "maskg", [128, 384], F32)
        self._lazy["ident"] = ("in", "ident", [128, 128], F32)
        self._lazy["vbias"] = ("in", "vbias", [128, NT * NCH], F32)
        self._lazy["w_out0"] = ("in", "out0_w", [2 * D, D], F32)
        self._lazy["ln_g"] = ("in", "ln_g", [2, D], F32)
        self._lazy["ln_b"] = ("in", "ln_b", [2, D], F32)
        self._lazy["w_in1"] = ("in", "in1_w", [D, 2 * D], F32)
        self._lazy["w_glu"] = ("in", "glu_w", [D, 2 * D], F32)
        self._lazy["w_out1"] = ("in", "out1_w", [D, D], F32)
        self._lazy["out"] = ("out", "out", [T, D], F32)
        self._lazy["s5_pl"] = ("in", "s5_pl", [128, 3, 64], F32)
        self._lazy["s5_wl"] = ("in", "s5_wl", [128, 5, 16, 64], F32)
        self._lazy["s5_cp"] = ("in", "s5_cp", [128, 2, 64, 16], F32)
        self._lazy["s5_d1"] = ("in", "s5_d1", [128, 16], F32)
        self._lazy["rowmask"] = ("in", "rowmask", [128, 8], F32)
        self._lazy["iota513"] = ("in", "iota513", [1, 513], F32)
        self._lazy["xsT"] = ("scratch", "xsT", [D, T], BF16)
        self._lazy["BT"] = ("scratch", "BT", [512, T], BF16)
        self._lazy["CT"] = ("scratch", "CT", [512, T], BF16)
        self._lazy["zs"] = ("scratch", "zs", [D, T], BF16)
        self._lazy["qT16"] = ("scratch", "qT16", [16, 128, T], BF16)
        self._lazy["kT16"] = ("scratch", "kT16", [16, 128, T], BF16)
        self._lazy["qT32"] = ("scratch", "qT32", [16, 128, T], F32)
        self._lazy["kmean"] = ("scratch", "kmean", [128, 16, NCH], F32)
        self._lazy["V1"] = ("scratch", "V1", [T, 16, 129], BF16)
        self._lazy["gs"] = ("scratch", "gs", [T, D], BF16)
        self._lazy["dtk"] = ("scratch", "dtk", [128, NT, 32], F32)
        self._lazy["yaT"] = ("scratch", "yaT", [D, T], BF16)
        self._lazy["rstd_s"] = ("scratch", "rstd_s", [128, NT], F32)
        self._lazy["ybT"] = ("scratch", "ybT", [D, T], BF16)
        self._lazy["x1"] = ("scratch", "x1", [T, D], F32)
        self._lazy["x1T"] = ("scratch", "x1T", [D, T], BF16)
        self._lazy["uT"] = ("scratch", "uT", [D, T], BF16)
        self._lazy["sg1T"] = ("scratch", "sg1T", [D, T], BF16)
        self._lazy["ygT"] = ("scratch", "ygT", [D, T], BF16)
        self._lazy["y2T"] = ("scratch", "y2T", [D, T], BF16)

    def stage_A(self):
        P, nc = self.P, self.nc
        w0v = self.w0.rearrange("(k p) c -> p k c", p=128)
        with self.stage():
            xTb = self.sb("xTb", [128, 16, T], BF16)
            for k in range(16):
                P.dma("pool", xTb[:, k, :], self.xT[k * 128:(k + 1) * 128, :], writes=[("xTb", k)])
            wsl = [self.sb(f"wsl{i}", [128, 16, 128], BF16) for i in range(2)]
            psA = [self.ps(f"psA{i}", [128, 512]) for i in range(4)]
            psw = [self.ps(f"psw{i}", [32, 512]) for i in range(2)]
            raw = [self.sb(f"raw{i}", [128, 515], F32) for i in range(2)]
            acc = [self.sb(f"acc{i}", [128, 512], F32) for i in range(2)]
            ob = [self.sb(f"ob{i}", [128, 512], BF16) for i in range(3)]
            qf = [self.sb(f"qf{i}", [128, 512], F32) for i in range(2)]
            tmp32 = [self.sb(f"tmp32{i}", [32, 512], F32) for i in range(2)]
            cw = self.sb("cw", [128, 24, 4], F32)
            cb = self.sb("cb", [128, 24], F32)
            cosT = self.sb("cosT", [32, T], F32)
            sinS = self.sb("sinS", [32, T], F32)
            pm = self.sb("pm", [32, 32], F32)
            kms = self.sb("kms", [128, 16, NCH], F32)
            P.dma("sp", cw[:], self.cw, writes=["cw"])
            P.dma("sp", cb[:], self.cb, writes=["cb"])
            P.dma("sp", cosT[:], self.cosT, writes=["cosT"])
            P.dma("sp", sinS[:], self.sinS, writes=["sinS"])
            P.dma("sp", pm[:], self.pm32, writes=["pm"])

            ctr = {"blk": 0, "grp": 0, "ob": 0, "qf": 0}

            def load_w(c0, M):
                slot = ctr["blk"] % 2
                ctr["blk"] += 1
                P.dma("pool", wsl[slot][:, :, :M], w0v[:, :, c0:c0 + M], writes=[("w", slot)])
                return slot

            def mm_group(slot, M, g):
                b = ctr["grp"] % 4
                ctr["grp"] += 1
                for k in range(16):
                    P.op("pe", lambda h, k=k: h.matmul(psA[b][:M, :], wsl[slot][:, k, :M],
                                                       xTb[:, k, g * 512:(g + 1) * 512],
                                                       start=(k == 0), stop=(k == 15)),
                         reads=[("w", slot), ("xTb", k)], writes=[("psA", b)], sig=(k == 15))
                return b

            def next_ob():
                i = ctr["ob"] % 3
                ctr["ob"] += 1
                return i

            for i in range(24):
                slot = load_w(C_XBC + 128 * i, 128)
                P.op("pool", lambda h: h.memset(raw[0][:, 0:3], 0.0), writes=[("rawh", 0)])
                for g in range(NG):
                    b = mm_group(slot, 128, g)
                    r = g % 2
                    a = g % 2
                    P.op("act", lambda h: h.activation(raw[r][:, 3:515], psA[b][:, :], AF.Copy),
                         reads=[("psA", b)], writes=[("rawm", r)])
                    P.op("pool", lambda h: h.tensor_copy(raw[1 - r][:, 0:3], raw[r][:, 512:515]),
                         reads=[("rawm", r)], writes=[("rawh", 1 - r)])
                    P.op("dve", lambda h: h.tensor_scalar(acc[a][:, :], raw[r][:, 3:515], cw[:, i, 3:4], cb[:, i:i + 1],
                                                          ALU.mult, ALU.add),
                         reads=[("rawm", r), "cw", "cb"], writes=[("acc", a)])
                    for j in (2, 1, 0):
                        P.op("dve", lambda h, j=j: h.scalar_tensor_tensor(acc[a][:, :], raw[r][:, j:j + 512], cw[:, i, j:j + 1],
                                                                          acc[a][:, :], ALU.mult, ALU.add),
                             reads=[("rawm", r), ("rawh", r), "cw"], writes=[("acc", a)])
                    o = next_ob()
                    P.op("act", lambda h: h.activation(ob[o][:, :], acc[a][:, :], AF.Silu),
                         reads=[("acc", a)], writes=[("ob", o)])
                    if i < 16:
                        dst = self.xsT[i * 128:(i + 1) * 128, g * 512:(g + 1) * 512]
                    elif i < 20:
                        dst = self.BT[(i - 16) * 128:(i - 15) * 128, g * 512:(g + 1) * 512]
                    else:
                        dst = self.CT[(i - 20) * 128:(i - 19) * 128, g * 512:(g + 1) * 512]
                    P.dma("sp", dst, ob[o][:, :], reads=[("ob", o)], writes=[("xbc_out", i, g)])
            for i in range(16):
                slot = load_w(C_Z + 128 * i, 128)
                for g in range(NG):
                    b = mm_group(slot, 128, g)
                    o = next_ob()
                    P.op("act", lambda h: h.activation(ob[o][:, :], psA[b][:, :], AF.Silu),
                         reads=[("psA", b)], writes=[("ob", o)])
                    P.dma("sp", self.zs[i * 128:(i + 1) * 128, g * 512:(g + 1) * 512], ob[o][:, :],
                          reads=[("ob", o)], writes=[("zs", i, g)])
            for hq in range(32):
                is_q = hq < 16
                hd = hq % 16
                slot = load_w((C_Q if is_q else C_K) + 128 * hd, 128)
                for g in range(NG):
                    b = mm_group(slot, 128, g)
                    f = ctr["qf"] % 2
                    ctr["qf"] += 1
                    gs_ = slice(g * 512, (g + 1) * 512)
                    P.op("act", lambda h: h.activation(qf[f][:, :], psA[b][:, :], AF.Copy),
                         reads=[("psA", b)], writes=[("qf", f)])
                    P.op("pe", lambda h: h.matmul(psw[f][:, :], pm[:, :], qf[f][0:32, :], start=True, stop=True),
                         reads=[("qf", f), "pm"], writes=[("psw", f)])
                    P.op("dve", lambda h: h.tensor_tensor(tmp32[f][:, :], psw[f][:, :], sinS[:, gs_], ALU.mult),
                         reads=[("psw", f), "sinS"], writes=[("tmp32", f)])
                    P.op("dve", lambda h: h.tensor_tensor(qf[f][0:32, :], qf[f][0:32, :], cosT[:, gs_], ALU.mult),
                         reads=[("qf", f), "cosT"], writes=[("qf", f)])
                    P.op("dve", lambda h: h.tensor_tensor(qf[f][0:32, :], qf[f][0:32, :], tmp32[f][:, :], ALU.add),
                         reads=[("qf", f), ("tmp32", f)], writes=[("qf", f)])
                    o = next_ob()
                    P.op("act", lambda h: h.activation(ob[o][:, :], qf[f][:, :], AF.Copy),
                         reads=[("qf", f)], writes=[("ob", o)])
                    if is_q:
                        P.dma("sp", self.qT16[hd, :, gs_], ob[o][:, :], reads=[("ob", o)], writes=[("q16", hd, g)])
                        P.dma("sp", self.qT32[hd, :, gs_], qf[f][:, :], reads=[("qf", f)], writes=[("q32", hd, g)])
                    else:
                        P.dma("sp", self.kT16[hd, :, gs_], ob[o][:, :], reads=[("ob", o)], writes=[("k16", hd, g)])
                        P.op("dve", lambda h: h.tensor_reduce(kms[:, hd, 2 * g:2 * g + 2],
                                                              qf[f][:, :].rearrange("p (a b) -> p a b", b=256),
                                                              AX.X, ALU.add),
                             reads=[("qf", f)], writes=["kms"])
            P.op("dve", lambda h: h.tensor_scalar(kms[:, :, :], kms[:, :, :], 1.0 / 256.0, None, ALU.mult),
                 reads=["kms"], writes=["kms"])
            P.dma("sp", self.kmean, kms[:, :, :], reads=["kms"], writes=["kmean"])

    def stage_A2(self):
        P, nc = self.P, self.nc
        w0v = self.w0.rearrange("(k p) c -> p k c", p=128)
        with self.stage():
            xTb = self.sb("xTb", [128, 16, T], BF16)
            for k in range(16):
                P.dma("pool", xTb[:, k, :], self.xT[k * 128:(k + 1) * 128, :], writes=[("xTb", k)])
            wb = [self.sb(f"wb{i}", [128, 16, 512], BF16) for i in range(2)]
            psA = [self.ps(f"psA{i}", [128, 512]) for i in range(4)]
            ob = [self.sb(f"ob{i}", [128, 512], BF16) for i in range(3)]
            vt = [self.sb(f"vt{i}", [128, 4, 129], BF16) for i in range(3)]
            dtb = self.sb("dtb", [128, 32], F32)
            dtt = self.sb("dtt", [128, NT, 32], F32)
            P.dma("sp", dtb[:], self.dtb.partition_broadcast(128), writes=["dtb"])
            for i in range(3):
                P.op("pool", lambda h, i=i: h.memset(vt[i][:, :, :], 1.0), writes=[("vt", i)])
            ctr = {"blk": 0, "grp": 0, "ob": 0}

            def load_w(c0, M):
                slot = ctr["blk"] % 2
                ctr["blk"] += 1
                P.dma("pool", wb[slot][:, :, :M], w0v[:, :, c0:c0 + M], writes=[("w", slot)])
                return slot

            slot = load_w(C_DT, 32)
            for tt in range(NT):
                b = tt // 16
                for k in range(16):
                    P.op("pe", lambda h, k=k: h.matmul(psA[b][:, (tt % 16) * 32:(tt % 16) * 32 + 32],
                                                       xTb[:, k, tt * 128:(tt + 1) * 128], wb[slot][:, k, :32],
                                                       start=(k == 0), stop=(k == 15)),
                         reads=[("w", slot), ("xTb", k)], writes=[("psA", b)], sig=(k == 15))
            for b in range(2):
                dv = dtt[:, b * 16:(b + 1) * 16, :]
                P.op("dve", lambda h: h.tensor_tensor(dv, psA[b][:, :].rearrange("p (a c) -> p a c", c=32),
                                                      dtb[:, None, :].broadcast_to([128, 16, 32]), ALU.add),
                     reads=[("psA", b), "dtb"], writes=[("dtt", b)])
                P.op("act", lambda h: h.activation(dv, dv, AF.Exp), reads=[("dtt", b)], writes=[("dtt", b)])
                P.op("act", lambda h: h.activation(dv, dv, AF.Ln, bias=1.0), reads=[("dtt", b)], writes=[("dtt", b)])
            P.dma("sp", self.dtk, dtt[:, :, :], reads=[("dtt", 0), ("dtt", 1)], writes=["dtk"])
            ctr["grp"] = 2
            for fam in ("v", "g"):
                for cg in range(4):
                    slot = load_w((C_V if fam == "v" else C_G) + 512 * cg, 512)
                    for tt in range(NT):
                        b = ctr["grp"] % 4
                        ctr["grp"] += 1
                        for k in range(16):
                            P.op("pe", lambda h, k=k: h.matmul(psA[b][:, :], xTb[:, k, tt * 128:(tt + 1) * 128],
                                                               wb[slot][:, k, :], start=(k == 0), stop=(k == 15)),
                                 reads=[("w", slot), ("xTb", k)], writes=[("psA", b)], sig=(k == 15))
                        o = ctr["ob"] % 3
                        ctr["ob"] += 1
                        if fam == "v":
                            P.op("act", lambda h: h.activation(vt[o][:, :, 0:128],
                                                               psA[b][:, :].rearrange("p (a c) -> p a c", c=128), AF.Copy),
                                 reads=[("psA", b)], writes=[("vt", o)])
                            P.dma("sp", self.V1[tt * 128:(tt + 1) * 128, 4 * cg:4 * cg + 4, :], vt[o][:, :, :],
                                  reads=[("vt", o)], writes=[("V1", tt, cg)])
                        else:
                            P.op("act", lambda h: h.activation(ob[o][:, :], psA[b][:, :], AF.Silu),
                                 reads=[("psA", b)], writes=[("ob", o)])
                            P.dma("sp", self.gs[tt * 128:(tt + 1) * 128, cg * 512:(cg + 1) * 512], ob[o][:, :],
                                  reads=[("ob", o)], writes=[("gs", tt, cg)])

    def stage_B(self):
        P, nc = self.P, self.nc
        xsTv = self.xsT.rearrange("(k p) t -> p k t", p=128)
        zsv = self.zs.rearrange("(k p) t -> p k t", p=128)
        BTv = self.BT.rearrange("(k p) t -> p k t", p=128)
        CTv = self.CT.rearrange("(k p) t -> p k t", p=128)
        with self.stage():
            sb = self.sb
            pb = [self.ps(f"pb{i}", [128, 512]) for i in range(6)]
            ptrs = [self.ps(f"ptr{i}", [128, 1024], BF16) for i in range(2)]
            ps_small, ps_T = pb[0][:, 0:96], pb[0][0:32, 128:384]
            ps_q = pb[0][:, 100:102]
            ps_g = [pb[1][:, 0:384]]
            ps_bc = [pb[2][:, 0:256], pb[3][:, 0:256]]
            ps_y = [pb[4][:, 0:256]]
            ps_st = [pb[5], pb[5]]
            P.bankmap = {"ps_small": "B0", "ps_q": "B0", ("ps_y", 0, 0): "BY", ("ps_y", 0, 1): "BY",
                         ("ps_st", 0): "BST", ("ps_st", 1): "BST"}
            xsc = [sb(f"xsc{i}", [128, 16, 256], BF16) for i in range(2)]
            zsc = [sb(f"zsc{i}", [128, 16, 256], BF16) for i in range(2)]
            btc = [sb(f"btc{i}", [128, 4, 256], BF16) for i in range(2)]
            ctc = [sb(f"ctc{i}", [128, 4, 256], BF16) for i in range(2)]
            dtt = sb("dtt", [128, NT, 32], F32)
            abc = sb("abc", [128, 32], F32)
            sel = sb("sel", [32, 32, 128], F32)
            triu = sb("triu", [128, 128], F32)
            ones = sb("ones", [128, 128], F32)
            r0 = sb("r0", [128, 256], F32)
            maskg = sb("maskg", [128, 384], F32)
            ident = sb("ident", [128, 128], BF16)
            dp = sb("dp", [128, 16], F32)
            ng = sb("ng", [128, 16], F32)
            atok = sb("atok", [128, 2, 32], F32)
            acum = sb("acum", [128, 3, 32], F32)
            acT = sb("acT", [32, 256], F32)
            dte = sb("dte", [128, 2, 32], F32)
            eAt = sb("eAt", [128, 32], F32)
            X = sb("X", [128, 2, 2048], BF16)
            Xd = sb("Xd", [128, 2, 2048], BF16)
            Btok = sb("Btok", [128, 2, 512], BF16)
            Gm = sb("Gm", [128, 4, 384], F32)
            H32 = sb("H32", [128, 32, 64], F32)
            H16 = sb("H16", [128, 32, 64], BF16)
            Dm = [sb(f"Dm{i}", [128, 384], F32) for i in range(2)]
            MT = [sb(f"MT{i}", [128, 384], BF16) for i in range(2)]
            eA = [sb(f"eA{i}", [128, 256], F32) for i in range(2)]
            Ct = [sb(f"Ct{i}", [128, 256], BF16) for i in range(2)]
            yf = [sb(f"yf{i}", [128, 256], F32) for i in range(2)]
            yg = [sb(f"yg{i}", [128, 256], F32) for i in range(2)]
            sq = [sb(f"sq{i}", [128, 256], F32) for i in range(2)]
            ya16 = [sb(f"ya16{i}", [128, 256], BF16) for i in range(2)]
            rs = sb("rs", [128, NT], F32)

            P.dma("sp", dtt[:], self.dtk, writes=["dtt"])
            P.dma("sp", abc[:], self.alog.partition_broadcast(128), writes=["abc"])
            P.dma("sp", sel[:], self.sel, writes=["sel"])
            P.dma("sp", triu[:], self.triu, writes=["triu"])
            P.dma("sp", maskg[:], self.maskg, writes=["maskg"])
            P.dma("sp", r0[:], self.maskg[:, 0:256], writes=["r0"])
            P.dma("pool", ident[:], self.ident, writes=["ident"])
            P.dma("sp", dp[:], self.ssd_dp, writes=["dp"])
            P.dma("sp", ng[:], self.normg, writes=["ng"])
            P.op("pool", lambda h: h.memset(ones[:], 1.0), writes=["ones"])
            P.op("pool", lambda h: h.memset(H32[:], 0.0), writes=["H32"])
            P.op("pool", lambda h: h.memset(H16[:], 0.0), writes=[("H16", g) for g in range(4)])
            P.op("act", lambda h: h.activation(abc[:], abc[:], AF.Exp), reads=["abc"], writes=["abc"])
            P.op("dve", lambda h: h.tensor_scalar(abc[:], abc[:], -1.0, None, ALU.mult), reads=["abc"], writes=["abc"])

            def load_chunk(c):
                s_ = c % 2
                cs = slice(c * 256, (c + 1) * 256)
                P.dma("sp", xsc[s_][:], xsTv[:, :, cs], writes=[("xsc", s_)])
                P.dma("sp", zsc[s_][:], zsv[:, :, cs], writes=[("zsc", s_)])
                P.dma("pool", btc[s_][:], BTv[:, :, cs], writes=[("btc", s_)])
                P.dma("pool", ctc[s_][:], CTv[:, :, cs], writes=[("ctc", s_)])

            bstop = getattr(self, 'bstop', 99)
            if bstop <= 1:
                return
            load_chunk(0)
            hc = 0
            for c in range(getattr(self, 'b_chunks', NCH)):
                s_ = c % 2
                if c + 1 < NCH:
                    load_chunk(c + 1)
                P.op("dve", lambda h: h.tensor_tensor(atok[:], dtt[:, 2 * c:2 * c + 2, :],
                                                      abc[:, None, :].broadcast_to([128, 2, 32]), ALU.mult),
                     reads=["dtt", "abc"], writes=["atok"])
                mm = lambda out, l, r, st, sp, sg: P.op(
                    "pe", lambda h: h.matmul(out, l, r, start=st, stop=sp), reads=["atok", "triu", "ones", "r0"],
                    writes=["ps_small"], sig=sg)
                mm(ps_small[:, 0:32], triu[:], atok[:, 0, :], True, True, False)
                mm(ps_small[:, 32:64], ones[:], atok[:, 0, :], True, False, False)
                mm(ps_small[:, 32:64], triu[:], atok[:, 1, :], False, True, False)
                mm(ps_small[:, 64:96], ones[:], atok[:, 0, :], True, False, False)
                mm(ps_small[:, 64:96], ones[:], atok[:, 1, :], False, True, False)
                mm(ps_T[:, :], atok[:, 0, :], r0[:], True, False, False)
                mm(ps_T[:, 128:256], atok[:, 1, :], triu[:], False, True, True)
                P.op("act", lambda h: h.activation(acum[:].rearrange("p a b -> p (a b)"), ps_small, AF.Copy),
                     reads=["ps_small"], writes=["acum"])
                P.op("act", lambda h: h.activation(acT[:], ps_T, AF.Copy), reads=["ps_small"], writes=["acT"])
                P.op("dve", lambda h: h.tensor_tensor(dte[:], acum[:, 2:3, :].broadcast_to([128, 2, 32]), acum[:, 0:2, :],
                                                      ALU.subtract), reads=["acum"], writes=["dte"])
                P.op("act", lambda h: h.activation(dte[:], dte[:], AF.Exp), reads=["dte"], writes=["dte"])
                P.op("act", lambda h: h.activation(eAt[:], acum[:, 2, :], AF.Exp), reads=["acum"], writes=["eAt"])
                if bstop <= 2:
                    return
                tb = 0
                bsub = getattr(self, 'bsub', 'xdb')
                for j in range(2):
                    for q4 in range(4):
                        half = tb % 2
                        tb += 1
                        for kk in range(4):
                            k = q4 * 4 + kk
                            P.op("pe", lambda h: h.transpose(ptrs[half][:, kk * 128:(kk + 1) * 128],
                                                             xsc[s_][:, k, j * 128:(j + 1) * 128], ident[:]),
                                 reads=[("xsc", s_), "ident"], writes=[("ptr", half)], sig=(kk == 3))
                        hs = slice(8 * q4, 8 * q4 + 8)
                        cs_ = slice(512 * q4, 512 * q4 + 512)
                        P.op("dve", lambda h: h.tensor_tensor(
                            X[:, j, cs_].rearrange("p (a b) -> p a b", b=64),
                            ptrs[half][:, 0:512].rearrange("p (a b) -> p a b", b=64),
                            dtt[:, 2 * c + j, hs].unsqueeze(2).broadcast_to([128, 8, 64]), ALU.mult),
                            reads=[("ptr", half), "dtt"], writes=[("X", j, q4)])
                        if 'd' in bsub:
                          P.op("pool", lambda h: h.tensor_tensor(
                            Xd[:, j, cs_].rearrange("p (a b) -> p a b", b=64),
                            X[:, j, cs_].rearrange("p (a b) -> p a b", b=64),
                            dte[:, j, hs].unsqueeze(2).broadcast_to([128, 8, 64]), ALU.mult),
                            reads=[("X", j, q4), "dte"], writes=[("Xd", j, q4)])
                    if 'b' not in bsub:
                        continue
                    half = tb % 2
                    tb += 1
                    for g in range(4):
                        P.op("pe", lambda h: h.transpose(ptrs[half][:, g * 128:(g + 1) * 128],
                                                         btc[s_][:, g, j * 128:(j + 1) * 128], ident[:]),
                             reads=[("btc", s_), "ident"], writes=[("ptr", half)], sig=(g == 3))
                    P.op("act", lambda h: h.activation(Btok[:, j, :], ptrs[half][:, 0:512], AF.Copy),
                         reads=[("ptr", half)], writes=[("Btok", j)])
                if bstop <= 3:
                    return
                for g in range(4):
                    gb = 0
                    P.op("pe", lambda h: h.matmul(ps_g[gb][:, 0:256], btc[s_][:, g, 0:128], ctc[s_][:, g, :],
                                                  start=True, stop=True),
                         reads=[("btc", s_), ("ctc", s_)], writes=[("ps_g", gb)], sig=False)
                    P.op("pe", lambda h: h.matmul(ps_g[gb][:, 256:384], btc[s_][:, g, 128:256], ctc[s_][:, g, 128:256],
                                                  start=True, stop=True),
                         reads=[("btc", s_), ("ctc", s_)], writes=[("ps_g", gb)])
                    P.op("dve", lambda h: h.tensor_tensor(Gm[:, g, :], ps_g[gb], maskg[:], ALU.mult),
                         reads=[("ps_g", gb), "maskg"], writes=[("Gm", g)])
                if bstop <= 4:
                    return
                for hd in range(getattr(self, 'b_heads', 32)):
                    g = hd // 8
                    pair = hd // 2
                    hh = hd % 2
                    t_ = hc % 2
                    hc += 1
                    hcols = slice(hd * 64, (hd + 1) * 64)
                    P.op("pe", lambda h: h.matmul(ps_bc[t_], sel[:, hd, :], acT[:], start=True, stop=True),
                         reads=["sel", "acT"], writes=[("ps_bc", t_)])
                    P.op("dve", lambda h: h.tensor_scalar(Dm[t_][:, 0:256], ps_bc[t_], acum[:, 0, hd:hd + 1], 0.0,
                                                          ALU.subtract, ALU.min),
                         reads=[("ps_bc", t_), "acum"], writes=[("Dm", t_)])
                    P.op("dve", lambda h: h.tensor_scalar(Dm[t_][:, 256:384], ps_bc[t_][:, 128:256], acum[:, 1, hd:hd + 1],
                                                          0.0, ALU.subtract, ALU.min),
                         reads=[("ps_bc", t_), "acum"], writes=[("Dm", t_)])
                    P.op("act", lambda h: h.activation(Dm[t_][:], Dm[t_][:], AF.Exp), reads=[("Dm", t_)], writes=[("Dm", t_)])
                    P.op("dve", lambda h: h.tensor_tensor(MT[t_][:], Dm[t_][:], Gm[:, g, :], ALU.mult),
                         reads=[("Dm", t_), ("Gm", g)], writes=[("MT", t_)])
                    P.op("act", lambda h: h.activation(eA[t_][:], ps_bc[t_], AF.Exp), reads=[("ps_bc", t_)], writes=[("eA", t_)])
                    P.op("pool", lambda h: h.tensor_tensor(Ct[t_][:], ctc[s_][:, g, :], eA[t_][:], ALU.mult),
                         reads=[("ctc", s_), ("eA", t_)], writes=[("Ct", t_)])
                    yb = 0
                    yo = ps_y[yb][hh * 64:(hh + 1) * 64, :]
                    yres = ("ps_y", yb, hh)
                    P.op("pe", lambda h: h.matmul(yo[:, 0:256], X[:, 0, hcols], MT[t_][:, 0:256], start=True, stop=False),
                         reads=[("X", 0, hd // 8), ("MT", t_)], writes=[yres], sig=False)
                    P.op("pe", lambda h: h.matmul(yo[:, 128:256], X[:, 1, hcols], MT[t_][:, 256:384], start=False, stop=False),
                         reads=[("X", 1, hd // 8), ("MT", t_)], writes=[yres], sig=False)
                    P.op("pe", lambda h: h.matmul(yo[:, 0:256], H16[:, hd, :], Ct[t_][:], start=False, stop=True),
                         reads=[("H16", g), ("Ct", t_)], writes=[yres])
                    so = ps_st[g % 2][:, (hd % 8) * 64:(hd % 8 + 1) * 64]
                    P.op("pe", lambda h: h.matmul(so, Btok[:, 0, g * 128:(g + 1) * 128], Xd[:, 0, hcols], start=True, stop=False),
                         reads=[("Btok", 0), ("Xd", 0, hd // 8)], writes=[("ps_st", g % 2)], sig=False)
                    P.op("pe", lambda h: h.matmul(so, Btok[:, 1, g * 128:(g + 1) * 128], Xd[:, 1, hcols], start=False, stop=True),
                         reads=[("Btok", 1), ("Xd", 1, hd // 8)], writes=[("ps_st", g % 2)])
                    if hd % 8 == 7:
                        hsl = slice(8 * g, 8 * g + 8)
                        P.op("dve", lambda h: h.tensor_tensor(H32[:, hsl, :], H32[:, hsl, :],
                                                              eAt[:, hsl].unsqueeze(2).broadcast_to([128, 8, 64]), ALU.mult),
                             reads=["H32", "eAt"], writes=["H32"])
                        P.op("dve", lambda h: h.tensor_tensor(H32[:, hsl, :], H32[:, hsl, :],
                                                              ps_st[g % 2][:, :].rearrange("p (a b) -> p a b", b=64), ALU.add),
                             reads=["H32", ("ps_st", g % 2)], writes=["H32"])
                        P.op("act", lambda h: h.activation(H16[:, hsl, :], H32[:, hsl, :], AF.Copy),
                             reads=["H32"], writes=[("H16", g)])
                    if hh == 1:
                        e_ = pair % 2
                        P.op("dve", lambda h: h.scalar_tensor_tensor(yf[e_][:], xsc[s_][:, pair, :], dp[:, pair:pair + 1],
                                                                     ps_y[yb], ALU.mult, ALU.add),
                             reads=[("xsc", s_), "dp", ("ps_y", yb, 0), ("ps_y", yb, 1)], writes=[("yf", e_)])
                        P.op("pool", lambda h: h.tensor_tensor(yg[e_][:], yf[e_][:], zsc[s_][:, pair, :], ALU.mult),
                             reads=[("yf", e_), ("zsc", s_)], writes=[("yg", e_)])
                        P.op("act", lambda h: h.activation(sq[e_][:], yg[e_][:], AF.Square), reads=[("yg", e_)], writes=[("sq", e_)])
                        for j in range(2):
                            P.op("pe", lambda h: h.matmul(ps_q[:, j:j + 1], sq[e_][:, j * 128:(j + 1) * 128], ones[:, 0:1],
                                                          start=(pair == 0 and j == 0), stop=(pair == 15)),
                                 reads=[("sq", e_), "ones"], writes=["ps_q"], sig=(j == 1))
                        P.op("act", lambda h: h.activation(ya16[e_][:], yg[e_][:], AF.Copy, scale=ng[:, pair:pair + 1]),
                             reads=[("yg", e_), "ng"], writes=[("ya16", e_)])
                        P.dma("sp", self.yaT[pair * 128:(pair + 1) * 128, c * 256:(c + 1) * 256], ya16[e_][:],
                              reads=[("ya16", e_)], writes=[("yaT", pair, c)])
                P.op("dve", lambda h: h.tensor_scalar(rs[:, 2 * c:2 * c + 2], ps_q, 1.0 / 2048.0, 1e-5, ALU.mult, ALU.add),
                     reads=["ps_q"], writes=["rs"])
            P.op("act", lambda h: h.activation(rs[:], rs[:], AF.Ln), reads=["rs"], writes=["rs"])
            P.op("act", lambda h: h.activation(rs[:], rs[:], AF.Exp, scale=-0.5), reads=["rs"], writes=["rs"])
            P.dma("sp", self.rstd_s, rs[:], reads=["rs"], writes=["rstd_s"])

    def stage_C(self):
        P, nc = self.P, self.nc
        V1v = self.V1.rearrange("(t p) h c -> p t h c", p=128)
        gsv = self.gs.rearrange("(t p) c -> p t c", p=128)
        SC = 1.0 / math.sqrt(128.0)
        with self.stage():
            sb = self.sb
            psS = [self.ps(f"psS{i}", [128, 512]) for i in range(2)]
            psO = [[self.ps(f"psO{i}{x}", [128, 512]) for x in "XY"] for i in range(2)]
            ps_gate = self.ps("ps_gate", [128, 512])
            ps_tr = self.ps("ps_tr", [128, 1024], BF16)
            q16 = [sb(f"q16{i}", [128, T], BF16) for i in range(2)]
            k16 = [sb(f"k16{i}", [128, T], BF16) for i in range(2)]
            v1 = [sb(f"v1{i}", [128, NT, 129], BF16) for i in range(2)]
            q32 = [sb(f"q32{i}", [128, T], F32) for i in range(2)]
            gsh = [sb(f"gsh{i}", [128, NT, 128], BF16) for i in range(2)]
            km = [sb(f"km{i}", [128, NCH], F32) for i in range(2)]
            vbias = sb("vbias", [128, NT, NCH], F32)
            tri16 = sb("tri16", [128, 128], BF16)
            ident = sb("ident", [128, 128], BF16)
            gm = sb("gm", [128, NT, NCH], F32)
            top8 = sb("top8", [128, NT, 8], F32)
            mask = sb("mask", [128, NT, NCH], F32)
            E = [sb(f"E{i}", [128, 512], BF16) for i in range(3)]
            acc = [sb(f"acc{i}", [128, 4, 129], F32) for i in range(2)]
            rec = sb("rec", [128, 8], F32)
            yb16 = [sb(f"yb16{i}", [128, 128], BF16) for i in range(2)]
            ybo = [sb(f"ybo{i}", [128, 512], BF16) for i in range(2)]
            P.dma("sp", vbias[:].rearrange("p a b -> p (a b)"), self.vbias, writes=["vbias"])
            P.dma("pool", tri16[:], self.triu, writes=["tri16"])
            P.dma("pool", ident[:], self.ident, writes=["ident"])

            def load_head(hd):
                s_ = hd % 2
                P.dma("sp", q16[s_][:], self.qT16[hd], writes=[("q16", s_)])
                P.dma("sp", k16[s_][:], self.kT16[hd], writes=[("k16", s_)])
                P.dma("sp", v1[s_][:], V1v[:, :, hd, :], writes=[("v1", s_)])
                P.dma("sp", q32[s_][:], self.qT32[hd], writes=[("q32", s_)])
                P.dma("sp", gsh[s_][:], gsv[:, :, hd * 128:(hd + 1) * 128], writes=[("gsh", s_)])
                P.dma("sp", km[s_][:], self.kmean[:, hd, :], writes=[("km", s_)])

            load_head(0)
            cS = cE = cY = 0
            nheads = getattr(self, "c_heads", 16)
            for hd in range(nheads):
                s_ = hd % 2
                if hd + 1 < nheads:
                    load_head(hd + 1)
                for qt in range(NT):
                    P.op("pe", lambda h: h.matmul(ps_gate[:, qt * NCH:(qt + 1) * NCH], q32[s_][:, qt * 128:(qt + 1) * 128],
                                                  km[s_][:], start=True, stop=True),
                         reads=[("q32", s_), ("km", s_)], writes=["ps_gate"], sig=(qt == NT - 1))
                P.op("dve", lambda h: h.tensor_tensor(gm[:].rearrange("p a b -> p (a b)"), ps_gate[:, :],
                                                      vbias[:].rearrange("p a b -> p (a b)"), ALU.add),
                     reads=["ps_gate", "vbias"], writes=["gm"])
                for qt in range(NT):
                    P.op("dve", lambda h: h.max(top8[:, qt, :], gm[:, qt, :]), reads=["gm"], writes=["top8"])
                P.op("dve", lambda h: h.tensor_tensor(mask[:], gm[:], top8[:, :, 2:3].broadcast_to([128, NT, NCH]), ALU.is_ge),
                     reads=["gm", "top8"], writes=["mask"])
                for j in range(NG):
                    ab = j % 2
                    first_acc = [True] * 4
                    for n in range(2 * j + 2):
                        nb = n % 2
                        started = {"X": False, "Y": False}
                        vis_any = set()
                        for kt in range(2):
                            K_ = 2 * n + kt
                            if n < 2 * j:
                                first, diag = 0, False
                            elif n == 2 * j:
                                first, diag = kt, True
                            else:
                                first, diag = 2 + kt, True
                            N = (4 - first) * 128
                            q0 = (4 * j + first) * 128
                            sbk = cS % 2
                            cS += 1
                            eb = cE % 3
                            cE += 1
                            P.op("pe", lambda h: h.matmul(psS[sbk][:, 0:N], k16[s_][:, K_ * 128:(K_ + 1) * 128],
                                                          q16[s_][:, q0:q0 + N], start=True, stop=True),
                                 reads=[("k16", s_), ("q16", s_)], writes=[("psS", sbk)])
                            P.op("act", lambda h: h.activation(E[eb][:, 0:N], psS[sbk][:, 0:N], AF.Exp, scale=SC),
                                 reads=[("psS", sbk)], writes=[("E", eb)])
                            if diag:
                                P.op("pool", lambda h: h.tensor_tensor(E[eb][:, 0:128], E[eb][:, 0:128], tri16[:], ALU.mult),
                                     reads=[("E", eb), "tri16"], writes=[("E", eb)])
                            for t in range(first, 4):
                                x = "X" if t < 2 else "Y"
                                ob = psO[nb][0 if t < 2 else 1]
                                last_kt = (kt == 1) or (n == 2 * j and t == 0) or (n == 2 * j + 1 and t == 2)
                                st = not started[x]
                                started[x] = True
                                vis_any.add(t)
                                P.op("pe", lambda h: h.matmul(ob[:, (t % 2) * 256:(t % 2) * 256 + 129],
                                                              E[eb][:, (t - first) * 128:(t - first + 1) * 128],
                                                              v1[s_][:, K_, :], start=st, stop=last_kt),
                                     reads=[("E", eb), ("v1", s_)], writes=[("psO", nb, x)], sig=(t == 3))
                        for t in sorted(vis_any):
                            x = "X" if t < 2 else "Y"
                            ob = psO[nb][0 if t < 2 else 1][:, (t % 2) * 256:(t % 2) * 256 + 129]
                            own = (n == 2 * j + t // 2)
                            mcol = mask[:, 4 * j + t, n:n + 1]
                            a_t = acc[ab][:, t, :]
                            if first_acc[t]:
                                first_acc[t] = False
                                if own:
                                    P.op("dve", lambda h: h.tensor_copy(a_t, ob), reads=[("psO", nb, x)], writes=[("acc", ab, t)])
                                else:
                                    P.op("dve", lambda h: h.tensor_scalar(a_t, ob, mcol, None, ALU.mult),
                                         reads=[("psO", nb, x)], writes=[("acc", ab, t)], strict=["mask"])
                            elif own:
                                P.op("dve", lambda h: h.tensor_tensor(a_t, ob, a_t, ALU.add),
                                     reads=[("psO", nb, x), ("acc", ab, t)], writes=[("acc", ab, t)])
                            else:
                                P.op("dve", lambda h: h.scalar_tensor_tensor(a_t, ob, mcol, a_t, ALU.mult, ALU.add),
                                     reads=[("psO", nb, x), ("acc", ab, t)], writes=[("acc", ab, t)], strict=["mask"])
                    yo = cY % 2
                    cY += 1
                    for t in range(4):
                        yb_ = t % 2
                        P.op("dve", lambda h: h.reciprocal(rec[:, t:t + 1], acc[ab][:, t, 128:129]),
                             reads=[("acc", ab, t)], writes=[("rec", t)])
                        P.op("dve", lambda h: h.scalar_tensor_tensor(yb16[yb_][:], acc[ab][:, t, 0:128], rec[:, t:t + 1],
                                                                     gsh[s_][:, 4 * j + t, :], ALU.mult, ALU.mult),
                             reads=[("acc", ab, t), ("gsh", s_)], writes=[("yb16", yb_)], strict=[("rec", t)])
                        P.op("pe", lambda h: h.transpose(ps_tr[:, t * 128:(t + 1) * 128], yb16[yb_][:], ident[:]),
                             reads=[("yb16", yb_), "ident"], writes=["ps_tr"])
                    P.op("act", lambda h: h.activation(ybo[yo][:], ps_tr[:, 0:512], AF.Copy), reads=["ps_tr"], writes=[("ybo", yo)])
                    P.dma("sp", self.ybT[hd * 128:(hd + 1) * 128, j * 512:(j + 1) * 512], ybo[yo][:],
                          reads=[("ybo", yo)], writes=[("ybT", hd, j)])

    def outproj_ln(self, parts, w_dram, nk_total, resid, layer, out_dram, xT_out=None, rstd_dram=None):
        P, nc = self.P, self.nc
        wv = w_dram.rearrange("(k p) c -> p k c", p=128)
        with self.stage():
            sb = self.sb
            W = sb("W", [128, nk_total, D], BF16)
            for k in range(nk_total):
                P.dma("pool", W[:, k, :], wv[:, k, :], writes=[("W", k)])
            npart = len(parts)
            psP = [[self.ps(f"psP{i}{a}", [128, 512]) for a in range(npart)] for i in range(2)]
            ps_tr = [self.ps(f"ps_tr{i}", [128, 1024], BF16) for i in range(2)] if xT_out is not None else None
            lt = [[sb(f"lt{i}{a}", [128, parts[a][2], 128], BF16) for a in range(npart)] for i in range(2)]
            xt = [sb(f"xt{i}", [128, D], F32) for i in range(2)]
            v = sb("v", [128, D], F32)
            junk = sb("junk", [128, D], BF16)
            gbc = sb("gbc", [128, D], F32)
            bbc = sb("bbc", [128, D], F32)
            st = sb("st", [128, 8], F32)
            P.dma("sp", gbc[:], self.ln_g[layer:layer + 1, :].partition_broadcast(128), writes=["gbc"])
            P.dma("sp", bbc[:], self.ln_b[layer:layer + 1, :].partition_broadcast(128), writes=["bbc"])
            if rstd_dram is not None:
                rs = sb("rs", [128, NT], F32)
                P.dma("sp", rs[:], rstd_dram, writes=["rs"])
            if xT_out is not None:
                ident = sb("ident", [128, 128], BF16)
                P.dma("pool", ident[:], self.ident, writes=["ident"])
                x1b = sb("x1b", [128, D], BF16)
                xTt = sb("xTt", [128, 16, 128], BF16)
                xTv = xT_out.rearrange("(k p) t -> p k t", p=128)
            fv = [pt[0].rearrange("(k p) t -> p k t", p=128) for pt in parts]

            def load_tile(tt):
                s_ = tt % 2
                for a in range(npart):
                    P.dma("sp", lt[s_][a][:], fv[a][:, :, tt * 128:(tt + 1) * 128], writes=[("lt", s_, a)])
                P.dma("sp", xt[s_][:], resid[tt * 128:(tt + 1) * 128, :], writes=[("xt", s_)])

            load_tile(0)
            cP = 0
            for tt in range(NT):
                s_ = tt % 2
                if tt + 1 < NT:
                    load_tile(tt + 1)
                for cg in range(4):
                    pb = cP % 2
                    cP += 1
                    cs = slice(cg * 512, (cg + 1) * 512)
                    for a, (_, k0, nk, use_rstd) in enumerate(parts):
                        for k in range(nk):
                            P.op("pe", lambda h: h.matmul(psP[pb][a][:, :], lt[s_][a][:, k, :], W[:, k0 + k, cs],
                                                          start=(k == 0), stop=(k == nk - 1)),
                                 reads=[("lt", s_, a), ("W", k0 + k)], writes=[("psP", pb, a)], sig=(k == nk - 1))
                    first = True
                    for a, (_, k0, nk, use_rstd) in enumerate(parts):
                        if use_rstd:
                            continue
                        P.op("dve", lambda h: h.scalar_tensor_tensor(v[:, cs], xt[s_][:, cs], ALPHA, psP[pb][a][:, :],
                                                                     ALU.mult, ALU.add),
                             reads=[("xt", s_), ("psP", pb, a)], writes=[("v", cg)])
                        first = False
                    for a, (_, k0, nk, use_rstd) in enumerate(parts):
                        if not use_rstd:
                            continue
                        P.op("dve", lambda h: h.scalar_tensor_tensor(v[:, cs], psP[pb][a][:, :], rs[:, tt:tt + 1], v[:, cs],
                                                                     ALU.mult, ALU.add),
                             reads=[("psP", pb, a), ("v", cg)], writes=[("v", cg)], strict=["rs"])
                vres = [("v", cg) for cg in range(4)]
                P.op("act", lambda h: h.activation(junk[:], v[:], AF.Square), reads=vres, writes=["junk"])
                P.op("dve", lambda h: h.reduce_sum(st[:, 0:1], v[:], AX.X), reads=vres, writes=[("st", 0)])
                P.op("dve", lambda h: h.reduce_sum(st[:, 1:2], junk[:], AX.X), reads=["junk"], writes=[("st", 1)])
                P.op("dve", lambda h: h.tensor_scalar(st[:, 2:3], st[:, 0:1], 1.0 / D, None, ALU.mult),
                     reads=[("st", 0)], writes=[("st", 2)])
                P.op("dve", lambda h: h.tensor_tensor(st[:, 3:4], st[:, 2:3], st[:, 2:3], ALU.mult),
                     reads=[("st", 2)], writes=[("st", 3)])
                P.op("dve", lambda h: h.scalar_tensor_tensor(st[:, 4:5], st[:, 1:2], 1.0 / D, st[:, 3:4], ALU.mult, ALU.subtract),
                     reads=[("st", 1), ("st", 3)], writes=[("st", 4)])
                P.op("dve", lambda h: h.tensor_scalar(st[:, 4:5], st[:, 4:5], 1e-5, None, ALU.add),
                     reads=[("st", 4)], writes=[("st", 4)])
                P.op("act", lambda h: h.activation(st[:, 5:6], st[:, 4:5], AF.Ln), reads=[("st", 4)], writes=[("st", 5)])
                P.op("act", lambda h: h.activation(st[:, 5:6], st[:, 5:6], AF.Exp, scale=-0.5), reads=[("st", 5)], writes=[("st", 5)])
                P.op("dve", lambda h: h.scalar_tensor_tensor(st[:, 6:7], st[:, 2:3], -1.0, st[:, 5:6], ALU.mult, ALU.mult),
                     reads=[("st", 2), ("st", 5)], writes=[("st", 6)])
                P.op("act", lambda h: h.activation(v[:], v[:], AF.Identity, bias=st[:, 6:7], scale=st[:, 5:6]),
                     reads=vres, writes=vres, strict=[("st", 5), ("st", 6)])
                P.op("dve", lambda h: h.tensor_tensor(v[:], v[:], gbc[:], ALU.mult), reads=vres + ["gbc"], writes=vres)
                P.op("dve", lambda h: h.tensor_tensor(v[:], v[:], bbc[:], ALU.add), reads=vres + ["bbc"], writes=vres)
                P.dma("sp", out_dram[tt * 128:(tt + 1) * 128, :], v[:], reads=vres, writes=[("out", tt)])
                if xT_out is not None:
                    P.op("act", lambda h: h.activation(x1b[:], v[:], AF.Copy), reads=vres, writes=["x1b"])
                    for hb in range(2):
                        for kk in range(8):
                            k = hb * 8 + kk
                            P.op("pe", lambda h: h.transpose(ps_tr[hb][:, kk * 128:(kk + 1) * 128], x1b[:, k * 128:(k + 1) * 128],
                                                             ident[:]),
                                 reads=["x1b", "ident"], writes=[("ps_tr", hb)], sig=(kk == 7))
                        P.op("act" if hb == 0 else "dve",
                             (lambda h: h.activation(xTt[:, 0:8, :].rearrange("p a b -> p (a b)"), ps_tr[0][:, :], AF.Copy)) if hb == 0 else
                             (lambda h: h.tensor_copy(xTt[:, 8:16, :].rearrange("p a b -> p (a b)"), ps_tr[1][:, :])),
                             reads=[("ps_tr", hb)], writes=[("xTt", hb)])
                    P.dma("sp", xTv[:, :, tt * 128:(tt + 1) * 128], xTt[:], reads=[("xTt", 0), ("xTt", 1)], writes=[("xT_out", tt)])

    def stage_D(self):
        self.outproj_ln([(self.ybT, 16, 16, False), (self.yaT, 0, 16, True)], self.w_out0, 32, self.x, 0, self.x1,
                        xT_out=self.x1T, rstd_dram=self.rstd_s)

    def stage_E(self):
        P, nc = self.P, self.nc
        wv = self.w_in1.rearrange("(k p) c -> p k c", p=128)
        x1Tv = self.x1T.rearrange("(k p) t -> p k t", p=128)
        with self.stage():
            xTb = self.sb("xTb", [128, 16, T], BF16)
            for k in range(16):
                P.dma("sp", xTb[:, k, :], x1Tv[:, k, :], writes=[("xTb", k)])
            wsl = [self.sb(f"wsl{i}", [128, 16, 128], BF16) for i in range(2)]
            psA = [self.ps(f"psA{i}", [128, 512]) for i in range(4)]
            ob = [self.sb(f"ob{i}", [128, 512], BF16) for i in range(3)]
            co = 0
            cg_ = 0
            for i in range(32):
                slot = i % 2
                P.dma("pool", wsl[slot][:], wv[:, :, i * 128:(i + 1) * 128], writes=[("w", slot)])
                for g in range(NG):
                    b = cg_ % 4
                    cg_ += 1
                    for k in range(16):
                        P.op("pe", lambda h: h.matmul(psA[b][:, :], wsl[slot][:, k, :], xTb[:, k, g * 512:(g + 1) * 512],
                                                      start=(k == 0), stop=(k == 15)),
                             reads=[("w", slot), ("xTb", k)], writes=[("psA", b)], sig=(k == 15))
                    o = co % 3
                    co += 1
                    P.op("act", lambda h: h.activation(ob[o][:], psA[b][:, :], AF.Copy if i < 16 else AF.Silu),
                         reads=[("psA", b)], writes=[("ob", o)])
                    dst = self.uT if i < 16 else self.sg1T
                    P.dma("sp", dst[(i % 16) * 128:(i % 16 + 1) * 128, g * 512:(g + 1) * 512], ob[o][:],
                          reads=[("ob", o)], writes=[("eo", i, g)])

    def stage_F(self):
        P, nc = self.P, self.nc
        TWO_PI = 2.0 * math.pi
        uTv = self.uT.rearrange("(k p) t -> p k t", p=128)
        with self.stage():
            sb = self.sb
            psV = [[self.ps(f"psV{i}{a}", [128, 512]) for a in "ri"] for i in range(2)]
            psY = [self.ps(f"psY{i}", [128, 512]) for i in range(2)]
            pl = sb("pl", [128, 3, 64], F32)
            wl = sb("wl", [128, 5, 16, 64], F32)
            cp = sb("cp", [128, 2, 64, 16], F32)
            d1 = sb("d1", [128, 16], F32)
            rmk = sb("rmk", [128, 8], F32)
            iot = sb("iot", [128, 513], F32)
            pi_c = sb("pi_c", [128, 1], F32)
            P.dma("sp", pl[:], self.s5_pl, writes=["pl"])
            P.dma("sp", wl[:], self.s5_wl, writes=["wl"])
            P.dma("sp", cp[:], self.s5_cp, writes=["cp"])
            P.dma("sp", d1[:], self.s5_d1, writes=["d1"])
            P.dma("sp", rmk[:], self.rowmask, writes=["rmk"])
            P.dma("sp", iot[:], self.iota513.partition_broadcast(128), writes=["iot"])
            P.op("pool", lambda h: h.memset(pi_c[:], math.pi), writes=["pi_c"])
            dtp = sb("dtp", [128, 64], F32)
            rP = sb("rP", [128, 64], F32)
            thP = sb("thP", [128, 64], F32)
            P.op("act", lambda h: h.activation(dtp[:], pl[:, 2, :], AF.Exp), reads=["pl"], writes=["dtp"])
            P.op("dve", lambda h: h.tensor_tensor(rP[:], pl[:, 0, :], dtp[:], ALU.mult), reads=["pl", "dtp"], writes=["rP"])
            P.op("act", lambda h: h.activation(rP[:], rP[:], AF.Exp), reads=["rP"], writes=["rP"])
            P.op("dve", lambda h: h.tensor_tensor(thP[:], pl[:, 1, :], dtp[:], ALU.mult), reads=["pl", "dtp"], writes=["thP"])
            thm = sb("thm", [128, 64], F32)
            P.op("dve", lambda h: h.tensor_scalar(thm[:], thP[:], 0.0, TWO_PI, ALU.is_lt, ALU.mult), reads=["thP"], writes=["thm"])
            P.op("dve", lambda h: h.tensor_tensor(thP[:], thP[:], thm[:], ALU.add), reads=["thP", "thm"], writes=["thP"])

            def sincos(o_sin, o_cos, a_in, shape, tag):
                ki = sb(f"ki_{tag}", shape, mybir.dt.int32)
                kf = sb(f"kf_{tag}", shape, F32)
                rr = sb(f"rr_{tag}", shape, F32)
                mm_ = sb(f"mm_{tag}", shape, F32)
                r_ = [f"sc_{tag}"]
                P.op("dve", lambda h: h.tensor_scalar(ki[:], a_in, 1.0 / TWO_PI, None, ALU.mult), reads=r_, writes=r_)
                P.op("dve", lambda h: h.tensor_copy(kf[:], ki[:]), reads=r_, writes=r_)
                P.op("dve", lambda h: h.scalar_tensor_tensor(rr[:], kf[:], -TWO_PI, a_in, ALU.mult, ALU.add), reads=r_, writes=r_)
                P.op("dve", lambda h: h.tensor_scalar(mm_[:], rr[:], math.pi, -TWO_PI, ALU.is_gt, ALU.mult), reads=r_, writes=r_)
                P.op("dve", lambda h: h.tensor_tensor(mm_[:], mm_[:], rr[:], ALU.add), reads=r_, writes=r_)
                P.op("act", lambda h: h.activation(o_sin, mm_[:], AF.Sin), reads=r_, writes=r_)
                P.op("dve", lambda h: h.tensor_scalar(rr[:], rr[:], 0.5 * math.pi, None, ALU.add), reads=r_, writes=r_)
                P.op("dve", lambda h: h.tensor_scalar(mm_[:], rr[:], math.pi, -TWO_PI, ALU.is_gt, ALU.mult), reads=r_, writes=r_)
                P.op("dve", lambda h: h.tensor_tensor(mm_[:], mm_[:], rr[:], ALU.add), reads=r_, writes=r_)
                P.op("act", lambda h: h.activation(o_cos, mm_[:], AF.Sin), reads=r_, writes=r_)
            W3 = [128, 16, 64]
            dtw = sb("dtw", W3, F32)
            aw = sb("aw", W3, F32)
            tw = sb("tw", W3, F32)
            cw_ = sb("cw_", W3, F32)
            sw_ = sb("sw_", W3, F32)
            fre = sb("fre", W3, F32)
            fim = sb("fim", W3, F32)
            t0 = sb("t0", W3, F32)
            t1 = sb("t1", W3, F32)
            bbr = sb("bbr", W3, F32)
            bbi = sb("bbi", W3, F32)
            lre, lim, bre, bim = wl[:, 0], wl[:, 1], wl[:, 3], wl[:, 4]
            D_ = lambda fn, r, w: P.op("dve", fn, reads=r, writes=w)
            A_ = lambda fn, r, w, **kw: P.op("act", fn, reads=r, writes=w, **kw)
            A_(lambda h: h.activation(dtw[:], wl[:, 2], AF.Exp), ["wl"], ["dtw"])
            D_(lambda h: h.tensor_tensor(aw[:], lre, dtw[:], ALU.mult), ["wl", "dtw"], ["aw"])
            A_(lambda h: h.activation(aw[:], aw[:], AF.Exp), ["aw"], ["aw"])
            D_(lambda h: h.tensor_tensor(tw[:], lim, dtw[:], ALU.mult), ["wl", "dtw"], ["tw"])
            D_(lambda h: h.tensor_scalar(t0[:], tw[:], 0.0, TWO_PI, ALU.is_lt, ALU.mult), ["tw"], ["t0"])
            D_(lambda h: h.tensor_tensor(tw[:], tw[:], t0[:], ALU.add), ["tw", "t0"], ["tw", "sc_w"])
            sincos(sw_[:], cw_[:], tw[:], W3, "w")
            P.op("dve", lambda h: h.tensor_copy(sw_[:], sw_[:]), reads=["sc_w"], writes=["sw_", "cw_"])
            D_(lambda h: h.tensor_tensor(cw_[:], cw_[:], aw[:], ALU.mult), ["cw_", "aw"], ["cw_"])
            D_(lambda h: h.tensor_tensor(sw_[:], sw_[:], aw[:], ALU.mult), ["sw_", "aw"], ["sw_"])
            D_(lambda h: h.tensor_scalar(cw_[:], cw_[:], -1.0, None, ALU.add), ["cw_"], ["cw_"])
            D_(lambda h: h.tensor_tensor(t0[:], lre, lre, ALU.mult), ["wl"], ["t0"])
            D_(lambda h: h.tensor_tensor(t1[:], lim, lim, ALU.mult), ["wl"], ["t1"])
            D_(lambda h: h.tensor_tensor(t0[:], t0[:], t1[:], ALU.add), ["t0", "t1"], ["t0"])
            D_(lambda h: h.reciprocal(t0[:], t0[:]), ["t0"], ["t0"])
            D_(lambda h: h.tensor_tensor(fre[:], cw_[:], lre, ALU.mult), ["cw_", "wl"], ["fre"])
            D_(lambda h: h.tensor_tensor(t1[:], sw_[:], lim, ALU.mult), ["sw_", "wl"], ["t1"])
            D_(lambda h: h.tensor_tensor(fre[:], fre[:], t1[:], ALU.add), ["fre", "t1"], ["fre"])
            D_(lambda h: h.tensor_tensor(fre[:], fre[:], t0[:], ALU.mult), ["fre", "t0"], ["fre"])
            D_(lambda h: h.tensor_tensor(fim[:], sw_[:], lre, ALU.mult), ["sw_", "wl"], ["fim"])
            D_(lambda h: h.tensor_tensor(t1[:], cw_[:], lim, ALU.mult), ["cw_", "wl"], ["t1"])
            D_(lambda h: h.tensor_tensor(fim[:], fim[:], t1[:], ALU.subtract), ["fim", "t1"], ["fim"])
            D_(lambda h: h.tensor_tensor(fim[:], fim[:], t0[:], ALU.mult), ["fim", "t0"], ["fim"])
            D_(lambda h: h.tensor_tensor(bbr[:], fre[:], bre, ALU.mult), ["fre", "wl"], ["bbr"])
            D_(lambda h: h.tensor_tensor(t1[:], fim[:], bim, ALU.mult), ["fim", "wl"], ["t1"])
            D_(lambda h: h.tensor_tensor(bbr[:], bbr[:], t1[:], ALU.subtract), ["bbr", "t1"], ["bbr"])
            D_(lambda h: h.tensor_tensor(bbi[:], fre[:], bim, ALU.mult), ["fre", "wl"], ["bbi"])
            D_(lambda h: h.tensor_tensor(t1[:], fim[:], bre, ALU.mult), ["fim", "wl"], ["t1"])
            D_(lambda h: h.tensor_tensor(bbi[:], bbi[:], t1[:], ALU.add), ["bbi", "t1"], ["bbi"])
            uc = [sb(f"uc{i}", [128, T], BF16) for i in range(2)]
            Lr = [sb(f"Lr{i}", [128, 128], BF16) for i in range(4)]
            Li = [sb(f"Li{i}", [128, 128], BF16) for i in range(4)]
            Cr = [sb(f"Cr{i}", [128, 128], BF16) for i in range(4)]
            nCr = [sb(f"nCr{i}", [128, 128], BF16) for i in range(4)]
            nCi = [sb(f"nCi{i}", [128, 128], BF16) for i in range(4)]
            cosT = [sb(f"cosT{i}", [128, 513], F32) for i in range(4)]
            sinT = [sb(f"sinT{i}", [128, 513], F32) for i in range(4)]
            ang = sb("ang", [128, 513], F32)
            ki_p = sb("ki_p", [128, 513], mybir.dt.int32)
            kf_p = sb("kf_p", [128, 513], F32)
            rr_p = sb("rr_p", [128, 513], F32)
            mm_p = sb("mm_p", [128, 513], F32)

            def sincos_p(o_sin, o_cos):
                r_ = ["sc_p"]
                P.op("dve", lambda h: h.tensor_scalar(ki_p[:], ang[:], 1.0 / TWO_PI, None, ALU.mult), reads=r_, writes=r_)
                P.op("dve", lambda h: h.tensor_copy(kf_p[:], ki_p[:]), reads=r_, writes=r_)
                P.op("dve", lambda h: h.scalar_tensor_tensor(rr_p[:], kf_p[:], -TWO_PI, ang[:], ALU.mult, ALU.add), reads=r_, writes=r_)
                P.op("dve", lambda h: h.tensor_scalar(mm_p[:], rr_p[:], math.pi, -TWO_PI, ALU.is_gt, ALU.mult), reads=r_, writes=r_)
                P.op("dve", lambda h: h.tensor_tensor(mm_p[:], mm_p[:], rr_p[:], ALU.add), reads=r_, writes=r_)
                P.op("act", lambda h: h.activation(o_sin, mm_p[:], AF.Sin), reads=r_, writes=r_)
                P.op("dve", lambda h: h.tensor_scalar(rr_p[:], rr_p[:], 0.5 * math.pi, None, ALU.add), reads=r_, writes=r_)
                P.op("dve", lambda h: h.tensor_scalar(mm_p[:], rr_p[:], math.pi, -TWO_PI, ALU.is_gt, ALU.mult), reads=r_, writes=r_)
                P.op("dve", lambda h: h.tensor_tensor(mm_p[:], mm_p[:], rr_p[:], ALU.add), reads=r_, writes=r_)
                P.op("act", lambda h: h.activation(o_cos, mm_p[:], AF.Sin), reads=r_, writes=r_)
            qst = [sb(f"qst{i}", [128, 2], F32) for i in range(4)]
            qt_ = sb("qt_", [128, 2], F32)
            Vs = [[sb(f"Vs{i}{a}", [128, 512], F32) for a in "ri"] for i in range(2)]
            m1 = [sb(f"m1{i}", [128, 512], F32) for i in range(2)]
            m2 = [sb(f"m2{i}", [128, 512], F32) for i in range(2)]
            Wr = [sb(f"Wr{i}", [128, 512], F32) for i in range(2)]
            Wi = [sb(f"Wi{i}", [128, 512], F32) for i in range(2)]
            Gr = [sb(f"Gr{i}", [128, 512], F32) for i in range(2)]
            Gi = [sb(f"Gi{i}", [128, 512], F32) for i in range(2)]
            Pp = [[sb(f"Pp{i}{a}", [128, 512], BF16) for a in range(4)] for i in range(2)]
            yv = [sb(f"yv{i}", [128, 512], F32) for i in range(2)]
            ge1 = [sb(f"ge1{i}", [128, 512], F32) for i in range(2)]
            ge2 = [sb(f"ge2{i}", [128, 512], F32) for i in range(2)]
            yo = [sb(f"yo{i}", [128, 512], BF16) for i in range(2)]
            for i in range(4):
                for tl, nm in ((Cr, "Cr"), (nCr, "nCr"), (nCi, "nCi")):
                    P.op("pool", lambda h: h.memset(tl[i][:], 0.0), writes=[(nm, i)])
            P.dma("sp", uc[0][:], uTv[:, 0, :], writes=[("uc", 0)])
            cpb = 0
            nchunks = getattr(self, "f_chunks", 16)
            for j in range(nchunks):
                us = j % 2
                if j + 1 < nchunks:
                    P.dma("sp", uc[(j + 1) % 2][:], uTv[:, j + 1, :], writes=[("uc", (j + 1) % 2)])
                for pc in range(4):
                    pr = 4 * j + pc
                    for g2 in range(2):
                        gl = 2 * pc + g2
                        P.op("dve", lambda h: h.tensor_scalar(Lr[pc][:, g2 * 64:(g2 + 1) * 64], bbr[:, j, :], rmk[:, gl:gl + 1], None, ALU.mult),
                             reads=["bbr", "rmk"], writes=[("Lr", pc)])
                        P.op("dve", lambda h: h.tensor_scalar(Li[pc][:, g2 * 64:(g2 + 1) * 64], bbi[:, j, :], rmk[:, gl:gl + 1], None, ALU.mult),
                             reads=["bbi", "rmk"], writes=[("Li", pc)])
                        rs_ = slice(g2 * 64, (g2 + 1) * 64)
                        cs_ = slice(gl * 16, gl * 16 + 16)
                        P.op("act", lambda h: h.activation(Cr[pc][rs_, cs_], cp[rs_, 0, pr, :], AF.Copy), reads=["cp"], writes=[("Cr", pc)])
                        P.op("act", lambda h: h.activation(nCr[pc][rs_, cs_], cp[rs_, 0, pr, :], AF.Copy, scale=-1.0), reads=["cp"], writes=[("nCr", pc)])
                        P.op("act", lambda h: h.activation(nCi[pc][rs_, cs_], cp[rs_, 1, pr, :], AF.Copy, scale=-1.0), reads=["cp"], writes=[("nCi", pc)])
                    P.op("dve", lambda h: h.tensor_scalar(ang[:], iot[:], thP[:, pr:pr + 1], None, ALU.mult),
                         reads=["iot"], writes=["ang", "sc_p", ("sinT", pc), ("cosT", pc)], strict=["thP"])
                    sincos_p(sinT[pc][:], cosT[pc][:])
                    P.op("dve", lambda h: h.tensor_copy(qt_[:, 0:1], qt_[:, 0:1]), reads=["sc_p"], writes=[("sinT", pc), ("cosT", pc)])
                    P.op("pool", lambda h: h.memset(qst[pc][:], 0.0), writes=[("qst", pc)])
                for b in range(NG):
                    bs = slice(b * 512, (b + 1) * 512)
                    yb_ = b % 2
                    for pc in range(4):
                        pr = 4 * j + pc
                        vb = cpb % 2
                        cpb += 1
                        c_, s_t = cosT[pc][:, 0:512], sinT[pc][:, 0:512]
                        P.op("pe", lambda h: h.matmul(psV[vb][0][:, :], Lr[pc][:], uc[us][:, bs], start=True, stop=True),
                             reads=[("Lr", pc), ("uc", us)], writes=[("psV", vb, 0)])
                        P.op("pe", lambda h: h.matmul(psV[vb][1][:, :], Li[pc][:], uc[us][:, bs], start=True, stop=True),
                             reads=[("Li", pc), ("uc", us)], writes=[("psV", vb, 1)])
                        P.op("act", lambda h: h.activation(Vs[vb][0][:], psV[vb][0][:, :], AF.Copy), reads=[("psV", vb, 0)], writes=[("Vs", vb, 0)])
                        P.op("act", lambda h: h.activation(Vs[vb][1][:], psV[vb][1][:, :], AF.Copy), reads=[("psV", vb, 1)], writes=[("Vs", vb, 1)])
                        P.op("pool", lambda h: h.tensor_tensor(m1[vb][:], Vs[vb][0][:], c_, ALU.mult), reads=[("Vs", vb, 0), ("cosT", pc)], writes=[("m1", vb)])
                        P.op("dve", lambda h: h.tensor_tensor(m2[vb][:], Vs[vb][1][:], s_t, ALU.mult), reads=[("Vs", vb, 1), ("sinT", pc)], writes=[("m2", vb)])
                        P.op("dve", lambda h: h.tensor_tensor(Wr[vb][:], m1[vb][:], m2[vb][:], ALU.add), reads=[("m1", vb), ("m2", vb)], writes=[("Wr", vb)])
                        P.op("pool", lambda h: h.tensor_tensor(m1[vb][:], Vs[vb][1][:], c_, ALU.mult), reads=[("Vs", vb, 1), ("cosT", pc)], writes=[("m1", vb)])
                        P.op("pool", lambda h: h.tensor_tensor(m2[vb][:], Vs[vb][0][:], s_t, ALU.mult), reads=[("Vs", vb, 0), ("sinT", pc)], writes=[("m2", vb)])
                        P.op("dve", lambda h: h.tensor_tensor(Wi[vb][:], m1[vb][:], m2[vb][:], ALU.subtract), reads=[("m1", vb), ("m2", vb)], writes=[("Wi", vb)])
                        rbc = rP[:, pr:pr + 1].broadcast_to([128, 512])
                        P.op("dve", lambda h: h.tensor_tensor_scan(Gr[vb][:], rbc, Wr[vb][:], qst[pc][:, 0:1], ALU.mult, ALU.add),
                             reads=[("Wr", vb), "rP"], writes=[("Gr", vb)], strict=[("qst", pc)])
                        P.op("dve", lambda h: h.tensor_tensor_scan(Gi[vb][:], rbc, Wi[vb][:], qst[pc][:, 1:2], ALU.mult, ALU.add),
                             reads=[("Wi", vb), "rP"], writes=[("Gi", vb)], strict=[("qst", pc)])
                        C5, S5 = cosT[pc][:, 512:513], sinT[pc][:, 512:513]
                        P.op("dve", lambda h: h.tensor_tensor(qt_[:, 0:1], Gi[vb][:, 511:512], S5, ALU.mult), reads=[("Gi", vb), ("sinT", pc)], writes=["qt_"])
                        P.op("dve", lambda h: h.tensor_tensor(qt_[:, 1:2], Gr[vb][:, 511:512], S5, ALU.mult), reads=[("Gr", vb), ("sinT", pc)], writes=["qt_"])
                        P.op("dve", lambda h: h.tensor_tensor(qst[pc][:, 0:1], Gr[vb][:, 511:512], C5, ALU.mult), reads=[("Gr", vb), ("cosT", pc)], writes=[("qst", pc)])
                        P.op("dve", lambda h: h.tensor_tensor(qst[pc][:, 1:2], Gi[vb][:, 511:512], C5, ALU.mult), reads=[("Gi", vb), ("cosT", pc)], writes=[("qst", pc)])
                        P.op("dve", lambda h: h.tensor_tensor(qst[pc][:, 0:1], qst[pc][:, 0:1], qt_[:, 0:1], ALU.subtract), reads=[("qst", pc), "qt_"], writes=[("qst", pc)])
                        P.op("dve", lambda h: h.tensor_tensor(qst[pc][:, 1:2], qst[pc][:, 1:2], qt_[:, 1:2], ALU.add), reads=[("qst", pc), "qt_"], writes=[("qst", pc)])
                        P.op("pool", lambda h: h.tensor_tensor(Pp[vb][0][:], Gr[vb][:], c_, ALU.mult), reads=[("Gr", vb), ("cosT", pc)], writes=[("Pp", vb, 0)])
                        P.op("pool", lambda h: h.tensor_tensor(Pp[vb][1][:], Gi[vb][:], c_, ALU.mult), reads=[("Gi", vb), ("cosT", pc)], writes=[("Pp", vb, 1)])
                        P.op("dve", lambda h: h.tensor_tensor(Pp[vb][2][:], Gi[vb][:], s_t, ALU.mult), reads=[("Gi", vb), ("sinT", pc)], writes=[("Pp", vb, 2)])
                        P.op("dve", lambda h: h.tensor_tensor(Pp[vb][3][:], Gr[vb][:], s_t, ALU.mult), reads=[("Gr", vb), ("sinT", pc)], writes=[("Pp", vb, 3)])
                        for a, (wt, wn) in enumerate(((Cr, "Cr"), (nCi, "nCi"), (nCr, "nCr"), (nCi, "nCi"))):
                            P.op("pe", lambda h: h.matmul(psY[yb_][:, :], wt[pc][:], Pp[vb][a][:], start=(pc == 0 and a == 0),
                                                          stop=(pc == 3 and a == 3)),
                                 reads=[(wn, pc), ("Pp", vb, a)], writes=[("psY", yb_)], sig=(a == 3))
                    P.op("dve", lambda h: h.scalar_tensor_tensor(yv[yb_][:], uc[us][:, bs], d1[:, j:j + 1], psY[yb_][:, :], ALU.mult, ALU.add),
                         reads=[("uc", us), "d1", ("psY", yb_)], writes=[("yv", yb_)])
                    P.op("act", lambda h: h.activation(ge1[yb_][:], yv[yb_][:], AF.Square), reads=[("yv", yb_)], writes=[("ge1", yb_)])
                    P.op("pool", lambda h: h.tensor_scalar(ge1[yb_][:], ge1[yb_][:], 0.044715, 1.0, ALU.mult, ALU.add), reads=[("ge1", yb_)], writes=[("ge1", yb_)])
                    P.op("pool", lambda h: h.tensor_tensor(ge2[yb_][:], ge1[yb_][:], yv[yb_][:], ALU.mult), reads=[("ge1", yb_), ("yv", yb_)], writes=[("ge2", yb_)])
                    P.op("act", lambda h: h.activation(ge2[yb_][:], ge2[yb_][:], AF.Sigmoid, scale=1.5957691216057308), reads=[("ge2", yb_)], writes=[("ge2", yb_)])
                    P.op("pool", lambda h: h.tensor_tensor(yo[yb_][:], ge2[yb_][:], yv[yb_][:], ALU.mult), reads=[("ge2", yb_), ("yv", yb_)], writes=[("yo", yb_)])
                    P.dma("sp", self.ygT[j * 128:(j + 1) * 128, bs], yo[yb_][:], reads=[("yo", yb_)], writes=[("ygT", j, b)])

    def stage_G1(self):
        P, nc = self.P, self.nc
        wv = self.w_glu.rearrange("(k p) c -> p k c", p=128)
        ygv = self.ygT.rearrange("(k p) t -> p k t", p=128)
        with self.stage():
            xTb = self.sb("xTb", [128, 16, T], BF16)
            for k in range(16):
                P.dma("sp", xTb[:, k, :], ygv[:, k, :], writes=[("xTb", k)])
            wa = [self.sb(f"wa{i}", [128, 16, 128], BF16) for i in range(2)]
            wb = [self.sb(f"wb{i}", [128, 16, 128], BF16) for i in range(2)]
            psA = [[self.ps(f"psA{i}{a}", [128, 512]) for a in "ab"] for i in range(2)]
            sgt = [self.sb(f"sgt{i}", [128, 512], BF16) for i in range(2)]
            sgb = [self.sb(f"sgb{i}", [128, 512], F32) for i in range(2)]
            tt_ = [self.sb(f"tt{i}", [128, 512], F32) for i in range(2)]
            ob = [self.sb(f"ob{i}", [128, 512], BF16) for i in range(2)]
            c2 = 0
            for i in range(16):
                slot = i % 2
                P.dma("pool", wa[slot][:], wv[:, :, i * 128:(i + 1) * 128], writes=[("wa", slot)])
                P.dma("pool", wb[slot][:], wv[:, :, D + i * 128:D + (i + 1) * 128], writes=[("wb", slot)])
                for g in range(NG):
                    b = c2 % 2
                    c2 += 1
                    gs_ = slice(g * 512, (g + 1) * 512)
                    P.dma("sp", sgt[b][:], self.sg1T[i * 128:(i + 1) * 128, gs_], writes=[("sgt", b)])
                    for a, wt, wn in ((0, wa, "wa"), (1, wb, "wb")):
                        for k in range(16):
                            P.op("pe", lambda h: h.matmul(psA[b][a][:, :], wt[slot][:, k, :], xTb[:, k, gs_], start=(k == 0), stop=(k == 15)),
                                 reads=[(wn, slot), ("xTb", k)], writes=[("psA", b, a)], sig=(k == 15))
                    P.op("act", lambda h: h.activation(sgb[b][:], psA[b][1][:, :], AF.Sigmoid), reads=[("psA", b, 1)], writes=[("sgb", b)])
                    P.op("dve", lambda h: h.tensor_tensor(tt_[b][:], psA[b][0][:, :], sgb[b][:], ALU.mult), reads=[("psA", b, 0), ("sgb", b)], writes=[("tt", b)])
                    P.op("pool", lambda h: h.tensor_tensor(ob[b][:], tt_[b][:], sgt[b][:], ALU.mult), reads=[("tt", b), ("sgt", b)], writes=[("ob", b)])
                    P.dma("sp", self.y2T[i * 128:(i + 1) * 128, gs_], ob[b][:], reads=[("ob", b)], writes=[("y2T", i, g)])

    def stage_G2(self):
        self.outproj_ln([(self.y2T, 0, 16, False)], self.w_out1, 16, self.x1, 1, self.out)

    def build(self, stages="AaBCDEFGH"):
        self.declare()
        for ch, fn in (("A", self.stage_A), ("a", self.stage_A2), ("B", self.stage_B), ("C", self.stage_C),
                       ("D", self.stage_D), ("E", self.stage_E), ("F", self.stage_F), ("G", self.stage_G1),
                       ("H", self.stage_G2)):
            if ch in stages:
                fn()
        return self.nc


def _rope_tables():
    half = 16
    inv_freq = (500000.0 ** (-(np.arange(half, dtype=np.float32) * 2.0 / 32))).astype(np.float32)
    pos = np.arange(T, dtype=np.float32)
    ang = (pos[None, :] * inv_freq[:, None]).astype(np.float32)
    c = np.cos(ang).astype(np.float32)
    s = np.sin(ang).astype(np.float32)
    return np.ascontiguousarray(np.concatenate([c, c], 0)), np.ascontiguousarray(np.concatenate([-s, s], 0))


def _constants():
    cosT, sinS = _rope_tables()
    pm = np.zeros((32, 32), np.float32)
    for m in range(32):
        pm[(m + 16) % 32, m] = 1.0
    sel = np.zeros((32, 32, 128), np.float32)
    for h in range(32):
        sel[h, h, :] = 1.0
    triu = np.triu(np.ones((128, 128), np.float32))
    maskg = np.ascontiguousarray(np.concatenate([triu, np.ones((128, 128), np.float32), triu], 1))
    vb = np.zeros((128, NT, NCH), np.float32)
    for qt in range(NT):
        vb[:, qt, qt // 2:] = -1e30
    rowmask = np.zeros((128, 8), np.float32)
    for p in range(128):
        rowmask[p, p // 16] = 1.0
    return dict(cosT=cosT, sinS=sinS, pm32=pm, sel=sel, triu=triu, maskg=maskg, ident=np.eye(128, dtype=np.float32),
                vbias=np.ascontiguousarray(vb.reshape(128, NT * NCH)), rowmask=rowmask,
                iota513=np.arange(513, dtype=np.float32).reshape(1, 513))


def _shared_inputs(inp):
    f = lambda a: np.ascontiguousarray(np.asarray(a, dtype=np.float32))
    d = {}
    d["in0_w"] = f(inp["in0_w"][0])
    cwv = np.asarray(inp["conv_w"][0])
    d["cw"] = f(cwv.T.reshape(24, 128, 4).transpose(1, 0, 2))
    d["cb"] = f(np.asarray(inp["conv_b"][0]).reshape(24, 128).T)
    d["dtb"] = f(inp["dt_bias"])
    d["alog"] = f(inp["a_log"])
    dsk = np.asarray(inp["ssd_d"][0])
    d["ssd_dp"] = f(np.repeat(dsk.reshape(16, 2), 64, axis=1).T)
    d["normg"] = f(np.asarray(inp["ssd_norm_g"][0]).reshape(16, 128).T)
    d["out0_w"] = f(inp["out0_w"][0])
    d["ln_g"] = f(inp["ln_g"])
    d["ln_b"] = f(inp["ln_b"])
    d["in1_w"] = f(inp["in1_w"][0])
    d["glu_w"] = f(inp["glu_w"][0])
    d["out1_w"] = f(inp["out1_w"][0])
    lre, lim = np.asarray(inp["s5_lam_re"][0]), np.asarray(inp["s5_lam_im"][0])
    ldt = np.asarray(inp["s5_log_dt"][0])
    bre, bim = np.asarray(inp["s5_b_re"][0]), np.asarray(inp["s5_b_im"][0])
    cre, cim = np.asarray(inp["s5_c_re"][0]), np.asarray(inp["s5_c_im"][0])
    ldt_n = np.repeat(ldt[:, None], 64, axis=1)
    pl = np.stack([a.reshape(64, 2, 64).transpose(1, 2, 0).reshape(128, 64) for a in (lre, lim, ldt_n)], 1)
    d["s5_pl"] = f(pl)
    def wl_gn(a):
        return np.repeat(a.reshape(16, 8, 1, 64), 16, axis=2).transpose(1, 2, 0, 3).reshape(128, 16, 64)
    def wl_gnm(a):
        return a.reshape(16, 8, 64, 16).transpose(1, 3, 0, 2).reshape(128, 16, 64)
    d["s5_wl"] = f(np.stack([wl_gn(lre), wl_gn(lim), wl_gn(ldt_n), wl_gnm(bre), wl_gnm(bim)], 1))
    def cp_(a):
        return a.reshape(64, 2, 16, 64).transpose(1, 3, 0, 2).reshape(128, 64, 16)
    d["s5_cp"] = f(np.stack([cp_(cre), cp_(cim)], 1))
    d["s5_d1"] = f(np.asarray(inp["s5_d"][0]).reshape(16, 128).T)
    d.update(_constants())
    return d


N_CORES = 4


def kernel(**inputs):
    x = np.asarray(inputs["x"], dtype=np.float32)
    shared = _shared_inputs(inputs)
    nc = bass.Bass("TRN2", target_bir_lowering=False)
    mk = MK(nc)
    mk.build("AaBCDEFGH")
    in_maps = []
    for b in range(N_CORES):
        m = dict(shared)
        m["x"] = np.ascontiguousarray(x[b])
        m["xT"] = np.ascontiguousarray(x[b].T)
        in_maps.append(m)
    res = run_bass_kernel_spmd(nc, in_maps, core_ids=list(range(N_CORES)))
    return np.stack([np.asarray(res.results[b]["out"], dtype=np.float32) for b in range(N_CORES)], 0)
```

```python
import math
from contextlib import contextmanager, ExitStack
import numpy as np
import concourse.bass as bass
import concourse.mybir as mybir
from concourse.bass_utils import run_bass_kernel_spmd

F32 = mybir.dt.float32
BF16 = mybir.dt.bfloat16
AF = mybir.ActivationFunctionType
ALU = mybir.AluOpType
AX = mybir.AxisListType

T = 4096
NT = T // 128
NG = T // 512
NCH = T // 256
D = 2048
IN0 = 13344
C_Z, C_XBC, C_DT, C_Q, C_K, C_V, C_G = 0, 2048, 5120, 5152, 7200, 9248, 11296
ALPHA = 4.0 ** 0.25
SEM_CAP = 30000


class Ev:
    __slots__ = ("eng", "sem", "val")

    def __init__(self, eng, sem, val):
        self.eng = eng
        self.sem = sem
        self.val = val


class Prog:
    def __init__(self, nc, n_dma_sems=10):
        self.nc = nc
        self.h = {"pe": nc.tensor, "act": nc.scalar, "dve": nc.vector, "pool": nc.gpsimd, "sp": nc.sync}
        self.sem = {}
        self.cnt = {}
        self.gen = {}
        self.waited = {e: {} for e in self.h}
        self._cms = []
        for e in self.h:
            self._new_sem(e)
        self.last_w = {}
        self.readers = {}
        self.bank_last = {}
        self.bankmap = {}
        self.dma_sems = {}
        self.dma_next = {}
        for e in ("sp", "pool", "act"):
            self.dma_sems[e] = [[self._alloc(f"dma_{e}_{i}"), 0] for i in range(n_dma_sems)]
            self.dma_next[e] = 0
        self.n_inst = 0

    def _alloc(self, name):
        cm = self.nc.semaphore(name)
        s = cm.__enter__()
        self._cms.append(cm)
        return s

    def _new_sem(self, e):
        g = self.gen.get(e, -1) + 1
        self.gen[e] = g
        self.sem[e] = self._alloc(f"s_{e}_{g}")
        self.cnt[e] = 0

    def _deps(self, reads, writes, e=None):
        deps = []
        for r in reads:
            ev = self.last_w.get(r)
            if ev is not None:
                deps.append(ev)
        for w in writes:
            ev = self.last_w.get(w)
            if ev is not None:
                deps.append(ev)
        if e in ("act", "dve", "pool"):
            strong = [ev for ev in deps if ev.eng == e]
            self._wait(e, strong, same_engine_ok=False)
        for w in writes:
            deps.extend(self.readers.get(w, ()))
        return deps

    def _wait(self, e, deps, same_engine_ok=True):
        wd = self.waited[e]
        need = {}
        for ev in deps:
            if same_engine_ok and ev.eng == e:
                continue
            k = id(ev.sem)
            if wd.get(k, 0) >= ev.val:
                continue
            if k not in need or need[k].val < ev.val:
                need[k] = ev
        for k, ev in need.items():
            self.h[e].wait_ge(ev.sem, ev.val)
            wd[k] = ev.val
            self.n_inst += 1

    def _commit(self, ev, reads, writes):
        for r in reads:
            lst = self.readers.setdefault(r, [])
            lst[:] = [x for x in lst if x.sem is not ev.sem]
            lst.append(ev)
        for w in writes:
            self.last_w[w] = ev
            self.readers[w] = []

    def op(self, e, fn, reads=(), writes=(), sig=True, banks=(), strict=()):
        if strict:
            sdeps = [self.last_w[r] for r in strict if r in self.last_w]
            self._wait(e, sdeps, same_engine_ok=False)
            reads = list(reads) + list(strict)
        deps = self._deps(reads, writes, e)
        banks = set(banks)
        for r in list(reads) + list(writes):
            nm = r if isinstance(r, str) else r[0]
            if isinstance(nm, str) and (nm.startswith("ps") or nm.startswith("ptr")):
                banks.add(self.bankmap.get(r, r))
        for b in banks:
            for eng, bev in self.bank_last.setdefault(b, {}).items():
                if eng != e:
                    deps.append(bev)
        self._wait(e, deps)
        if sig and self.cnt[e] >= SEM_CAP:
            self._new_sem(e)
        inst = fn(self.h[e])
        self.n_inst += 1
        if sig:
            self.cnt[e] += 1
            inst.then_inc(self.sem[e], 1)
            ev = Ev(e, self.sem[e], self.cnt[e])
        else:
            ev = Ev(e, self.sem[e], self.cnt[e] + 1)
        self._commit(ev, reads, writes)
        for b in banks:
            self.bank_last[b][e] = ev
        return ev

    def dma(self, e, out, in_, reads=(), writes=(), **kw):
        deps = self._deps(reads, writes)
        i = self.dma_next[e]
        self.dma_next[e] = (i + 1) % len(self.dma_sems[e])
        slot = self.dma_sems[e][i]
        if slot[1] > 0:
            deps.append(Ev("dma", slot[0], slot[1]))
        if slot[1] + 16 > SEM_CAP:
            self._wait(e, deps, same_engine_ok=False)
            deps = []
            slot[0] = self._alloc(f"dma_{e}_{i}_{self.n_inst}")
            slot[1] = 0
        self._wait(e, deps, same_engine_ok=False)
        inst = self.h[e].dma_start(out=out, in_=in_, **kw)
        slot[1] += 16
        inst.then_inc(slot[0], 16)
        self.n_inst += 1
        ev = Ev("dma", slot[0], slot[1])
        self._commit(ev, reads, writes)
        return ev

    def barrier(self):
        evs = [Ev(e, self.sem[e], self.cnt[e]) for e in self.h if self.cnt[e] > 0]
        for e in self.dma_sems:
            for s, v in self.dma_sems[e]:
                if v > 0:
                    evs.append(Ev("dma", s, v))
        for e in self.h:
            self._wait(e, evs, same_engine_ok=True)
        self.last_w.clear()
        self.readers.clear()
        self.bank_last.clear()


class MK:
    def __init__(self, nc, dbg=(), feed=()):
        self._lazy = {}
        self.feed = set(feed)
        self.nc = nc
        self.P = Prog(nc)
        self.dbg = set(dbg)
        self._es = None
        self._sid = 0
        self.dr = {}

    @contextmanager
    def stage(self):
        self._sid += 1
        es = ExitStack()
        self._es = es
        try:
            yield
        finally:
            self.P.barrier()
            es.close()

    def sb(self, name, shape, dt):
        return self._es.enter_context(self.nc.sbuf_tensor(f"{name}_s{self._sid}", shape, dt))

    def ps(self, name, shape, dt=F32):
        return self._es.enter_context(self.nc.psum_tensor(f"{name}_s{self._sid}", shape, dt))

    def din(self, name, shape, dt=F32):
        t = self.nc.dram_tensor(name, list(shape), dt, kind="ExternalInput").ap()
        self.dr[name] = t
        return t

    def dout(self, name, shape, dt=F32):
        t = self.nc.dram_tensor(name, list(shape), dt, kind="ExternalOutput").ap()
        self.dr[name] = t
        return t

    def scratch(self, name, shape, dt):
        kind = "ExternalOutput" if name in self.dbg else "Internal"
        t = self.nc.dram_tensor(name, list(shape), dt, kind=kind).ap()
        self.dr[name] = t
        return t

    def __getattr__(self, attr):
        lz = self.__dict__.get("_lazy", {})
        if attr in lz:
            kind, name, shape, dt = lz[attr]
            if kind == "in" or (kind == "scratch" and name in self.feed):
                t = self.din(name, shape, dt)
            elif kind == "out":
                t = self.dout(name, shape, dt)
            else:
                t = self.scratch(name, shape, dt)
            self.__dict__[attr] = t
            return t
        raise AttributeError(attr)

    def declare(self):
        self._lazy["x"] = ("in", "x", [T, D], F32)
        self._lazy["xT"] = ("in", "xT", [D, T], F32)
        self._lazy["w0a"] = ("in", "w0a", [72, 128, 16, 128], F32)
        self._lazy["w0b"] = ("in", "w0b", [8, 128, 16, 512], F32)
        self._lazy["w0dt"] = ("in", "w0dt", [128, 16, 32], F32)
        self._lazy["cw"] = ("in", "cw", [128, 24, 4], F32)
        self._lazy["cb"] = ("in", "cb", [128, 24], F32)
        self._lazy["dtb"] = ("in", "dtb", [1, 32], F32)
        self._lazy["alog"] = ("in", "alog", [1, 32], F32)
        self._lazy["ssd_dp"] = ("in", "ssd_dp", [128, 16], F32)
        self._lazy["normg"] = ("in", "normg", [128, 16], F32)
        self._lazy["cosT"] = ("in", "cosT", [32, T], F32)
        self._lazy["sinS"] = ("in", "sinS", [32, T], F32)
        self._lazy["pm32"] = ("in", "pm32", [32, 32], F32)
        self._lazy["sel"] = ("in", "sel", [32, 32, 128], F32)
        self._lazy["triu"] = ("in", "triu", [128, 128], F32)
        self._lazy["maskg"] = ("in", "maskg", [128, 384], F32)
        self._lazy["ident"] = ("in", "ident", [128, 128], F32)
        self._lazy["vbias"] = ("in", "vbias", [128, NT * NCH], F32)
        self._lazy["w_out0"] = ("in", "out0_w", [2 * D, D], F32)
        self._lazy["ln_g"] = ("in", "ln_g", [2, D], F32)
        self._lazy["ln_b"] = ("in", "ln_b", [2, D], F32)
        self._lazy["w_in1"] = ("in", "w1t", [32, 128, 16, 128], F32)
        self._lazy["w_glu"] = ("in", "wgt", [32, 128, 16, 128], F32)
        self._lazy["w_out1"] = ("in", "out1_w", [D, D], F32)
        self._lazy["out"] = ("out", "out", [T, D], F32)
        self._lazy["s5_pl"] = ("in", "s5_pl", [128, 3, 64], F32)
        self._lazy["s5_wl"] = ("in", "s5_wl", [128, 5, 16, 64], F32)
        self._lazy["s5_cp"] = ("in", "s5_cp", [128, 2, 64, 16], F32)
        self._lazy["s5_d1"] = ("in", "s5_d1", [128, 16], F32)
        self._lazy["rowmask"] = ("in", "rowmask", [128, 8], F32)
        self._lazy["iota513"] = ("in", "iota513", [1, 513], F32)
        self._lazy["xsT"] = ("scratch", "xsT", [D, T], BF16)
        self._lazy["BT"] = ("scratch", "BT", [512, T], BF16)
        self._lazy["CT"] = ("scratch", "CT", [512, T], BF16)
        self._lazy["zs"] = ("scratch", "zs", [D, T], BF16)
        self._lazy["qT16"] = ("scratch", "qT16", [16, 128, T], BF16)
        self._lazy["kT16"] = ("scratch", "kT16", [16, 128, T], BF16)
        self._lazy["qT32"] = ("scratch", "qT32", [16, 128, T], F32)
        self._lazy["kmean"] = ("scratch", "kmean", [128, 16, NCH], F32)
        self._lazy["V1"] = ("scratch", "V1", [T, 16, 129], BF16)
        self._lazy["gs"] = ("scratch", "gs", [T, D], BF16)
        self._lazy["dtk"] = ("scratch", "dtk", [128, NT, 32], F32)
        self._lazy["yaT"] = ("scratch", "yaT", [D, T], BF16)
        self._lazy["rstd_s"] = ("scratch", "rstd_s", [128, NT], F32)
        self._lazy["ybT"] = ("scratch", "ybT", [D, T], BF16)
        self._lazy["x1"] = ("scratch", "x1", [T, D], F32)
        self._lazy["x1T"] = ("scratch", "x1T", [D, T], BF16)
        self._lazy["uT"] = ("scratch", "uT", [D, T], BF16)
        self._lazy["sg1T"] = ("scratch", "sg1T", [D, T], BF16)
        self._lazy["ygT"] = ("scratch", "ygT", [D, T], BF16)
        self._lazy["y2T"] = ("scratch", "y2T", [D, T], BF16)

    def stage_A(self):
        P, nc = self.P, self.nc
        blk_of = {}
        for i in range(24):
            blk_of[C_XBC + 128 * i] = i
        for i in range(16):
            blk_of[C_Z + 128 * i] = 24 + i
            blk_of[C_Q + 128 * i] = 40 + i
            blk_of[C_K + 128 * i] = 56 + i
        with self.stage():
            xTb = self.sb("xTb", [128, 16, T], BF16)
            for k in range(16):
                P.dma("pool", xTb[:, k, :], self.xT[k * 128:(k + 1) * 128, :], writes=[("xTb", k)])
            wsl = [self.sb(f"wsl{i}", [128, 16, 128], BF16) for i in range(2)]
            psA = [self.ps(f"psA{i}", [128, 512]) for i in range(4)]
            psw = [self.ps(f"psw{i}", [32, 512]) for i in range(2)]
            raw = [self.sb(f"raw{i}", [128, 515], F32) for i in range(2)]
            acc = [self.sb(f"acc{i}", [128, 512], F32) for i in range(2)]
            ob = [self.sb(f"ob{i}", [128, 512], BF16) for i in range(3)]
            qf = [self.sb(f"qf{i}", [128, 512], F32) for i in range(2)]
            tmp32 = [self.sb(f"tmp32{i}", [32, 512], F32) for i in range(2)]
            cw = self.sb("cw", [128, 24, 4], F32)
            cb = self.sb("cb", [128, 24], F32)
            cosT = self.sb("cosT", [32, T], F32)
            sinS = self.sb("sinS", [32, T], F32)
            pm = self.sb("pm", [32, 32], F32)
            kms = self.sb("kms", [128, 16, NCH], F32)
            P.dma("sp", cw[:], self.cw, writes=["cw"])
            P.dma("sp", cb[:], self.cb, writes=["cb"])
            P.dma("sp", cosT[:], self.cosT, writes=["cosT"])
            P.dma("sp", sinS[:], self.sinS, writes=["sinS"])
            P.dma("sp", pm[:], self.pm32, writes=["pm"])

            ctr = {"blk": 0, "grp": 0, "ob": 0, "qf": 0}

            stg = [self.sb(f"stg{i}", [128, 16, 128], F32) for i in range(2)]

            def load_w(c0, M):
                slot = ctr["blk"] % 2
                ctr["blk"] += 1
                P.dma("sp", stg[slot][:, :, :], self.w0a[blk_of[c0]], writes=[("stg", slot)])
                P.op("pool", lambda h: h.tensor_copy(wsl[slot][:, :, :M], stg[slot][:, :, :M]),
                     reads=[("stg", slot)], writes=[("w", slot)])
                return slot

            def mm_group(slot, M, g):
                b = ctr["grp"] % 4
                ctr["grp"] += 1
                for k in range(16):
                    P.op("pe", lambda h, k=k: h.matmul(psA[b][:M, :], wsl[slot][:, k, :M],
                                                       xTb[:, k, g * 512:(g + 1) * 512],
                                                       start=(k == 0), stop=(k == 15)),
                         reads=[("w", slot), ("xTb", k)], writes=[("psA", b)], sig=(k == 15))
                return b

            def next_ob():
                i = ctr["ob"] % 3
                ctr["ob"] += 1
                return i

            for i in range(24):
                slot = load_w(C_XBC + 128 * i, 128)
                P.op("pool", lambda h: h.memset(raw[0][:, 0:3], 0.0), writes=[("rawh", 0)])
                for g in range(NG):
                    b = mm_group(slot, 128, g)
                    r = g % 2
                    a = g % 2
                    P.op("act", lambda h: h.activation(raw[r][:, 3:515], psA[b][:, :], AF.Copy),
                         reads=[("psA", b)], writes=[("rawm", r)])
                    P.op("pool", lambda h: h.tensor_copy(raw[1 - r][:, 0:3], raw[r][:, 512:515]),
                         reads=[("rawm", r)], writes=[("rawh", 1 - r)])
                    P.op("dve", lambda h: h.tensor_scalar(acc[a][:, :], raw[r][:, 3:515], cw[:, i, 3:4], cb[:, i:i + 1],
                                                          ALU.mult, ALU.add),
                         reads=[("rawm", r), "cw", "cb"], writes=[("acc", a)])
                    for j in (2, 1, 0):
                        P.op("dve", lambda h, j=j: h.scalar_tensor_tensor(acc[a][:, :], raw[r][:, j:j + 512], cw[:, i, j:j + 1],
                                                                          acc[a][:, :], ALU.mult, ALU.add),
                             reads=[("rawm", r), ("rawh", r), "cw"], writes=[("acc", a)])
                    o = next_ob()
                    P.op("act", lambda h: h.activation(ob[o][:, :], acc[a][:, :], AF.Silu),
                         reads=[("acc", a)], writes=[("ob", o)])
                    if i < 16:
                        dst = self.xsT[i * 128:(i + 1) * 128, g * 512:(g + 1) * 512]
                    elif i < 20:
                        dst = self.BT[(i - 16) * 128:(i - 15) * 128, g * 512:(g + 1) * 512]
                    else:
                        dst = self.CT[(i - 20) * 128:(i - 19) * 128, g * 512:(g + 1) * 512]
                    P.dma("sp", dst, ob[o][:, :], reads=[("ob", o)], writes=[("xbc_out", i, g)])
            for i in range(16):
                slot = load_w(C_Z + 128 * i, 128)
                for g in range(NG):
                    b = mm_group(slot, 128, g)
                    o = next_ob()
                    P.op("act", lambda h: h.activation(ob[o][:, :], psA[b][:, :], AF.Silu),
                         reads=[("psA", b)], writes=[("ob", o)])
                    P.dma("sp", self.zs[i * 128:(i + 1) * 128, g * 512:(g + 1) * 512], ob[o][:, :],
                          reads=[("ob", o)], writes=[("zs", i, g)])
            for hq in range(32):
                is_q = hq < 16
                hd = hq % 16
                slot = load_w((C_Q if is_q else C_K) + 128 * hd, 128)
                for g in range(NG):
                    b = mm_group(slot, 128, g)
                    f = ctr["qf"] % 2
                    ctr["qf"] += 1
                    gs_ = slice(g * 512, (g + 1) * 512)
                    P.op("act", lambda h: h.activation(qf[f][:, :], psA[b][:, :], AF.Copy),
                         reads=[("psA", b)], writes=[("qf", f)])
                    P.op("pe", lambda h: h.matmul(psw[f][:, :], pm[:, :], qf[f][0:32, :], start=True, stop=True),
                         reads=[("qf", f), "pm"], writes=[("psw", f)])
                    P.op("dve", lambda h: h.tensor_tensor(tmp32[f][:, :], psw[f][:, :], sinS[:, gs_], ALU.mult),
                         reads=[("psw", f), "sinS"], writes=[("tmp32", f)])
                    P.op("dve", lambda h: h.tensor_tensor(qf[f][0:32, :], qf[f][0:32, :], cosT[:, gs_], ALU.mult),
                         reads=[("qf", f), "cosT"], writes=[("qf", f)])
                    P.op("dve", lambda h: h.tensor_tensor(qf[f][0:32, :], qf[f][0:32, :], tmp32[f][:, :], ALU.add),
                         reads=[("qf", f), ("tmp32", f)], writes=[("qf", f)])
                    o = next_ob()
                    P.op("act", lambda h: h.activation(ob[o][:, :], qf[f][:, :], AF.Copy),
                         reads=[("qf", f)], writes=[("ob", o)])
                    if is_q:
                        P.dma("sp", self.qT16[hd, :, gs_], ob[o][:, :], reads=[("ob", o)], writes=[("q16", hd, g)])
                        P.dma("sp", self.qT32[hd, :, gs_], qf[f][:, :], reads=[("qf", f)], writes=[("q32", hd, g)])
                    else:
                        P.dma("sp", self.kT16[hd, :, gs_], ob[o][:, :], reads=[("ob", o)], writes=[("k16", hd, g)])
                        P.op("dve", lambda h: h.tensor_reduce(kms[:, hd, 2 * g:2 * g + 2],
                                                              qf[f][:, :].rearrange("p (a b) -> p a b", b=256),
                                                              AX.X, ALU.add),
                             reads=[("qf", f)], writes=["kms"])
            P.op("dve", lambda h: h.tensor_scalar(kms[:, :, :], kms[:, :, :], 1.0 / 256.0, None, ALU.mult),
                 reads=["kms"], writes=["kms"])
            P.dma("sp", self.kmean, kms[:, :, :], reads=["kms"], writes=["kmean"])

    def stage_A2(self):
        P, nc = self.P, self.nc
        with self.stage():
            xTb = self.sb("xTb", [128, 16, T], BF16)
            for k in range(16):
                P.dma("pool", xTb[:, k, :], self.xT[k * 128:(k + 1) * 128, :], writes=[("xTb", k)])
            wb = [self.sb(f"wb{i}", [128, 16, 512], BF16) for i in range(2)]
            psA = [self.ps(f"psA{i}", [128, 512]) for i in range(4)]
            ob = [self.sb(f"ob{i}", [128, 512], BF16) for i in range(3)]
            vt = [self.sb(f"vt{i}", [128, 4, 129], BF16) for i in range(3)]
            dtb = self.sb("dtb", [128, 32], F32)
            dtt = self.sb("dtt", [128, NT, 32], F32)
            P.dma("sp", dtb[:], self.dtb.partition_broadcast(128), writes=["dtb"])
            for i in range(3):
                P.op("pool", lambda h, i=i: h.memset(vt[i][:, :, :], 1.0), writes=[("vt", i)])
            ctr = {"blk": 0, "grp": 0, "ob": 0}

            stg = [self.sb(f"stg{i}", [128, 4, 512], F32) for i in range(2)]
            cst = {"n": 0}

            def load_w(c0, M):
                slot = ctr["blk"] % 2
                ctr["blk"] += 1
                for q in range(4):
                    ss = cst["n"] % 2
                    cst["n"] += 1
                    src = self.w0dt[:, 4 * q:4 * q + 4, :] if c0 == C_DT else \
                        self.w0b[((c0 - C_V) // 512) if c0 < C_G else (4 + (c0 - C_G) // 512), :, 4 * q:4 * q + 4, :]
                    P.dma("sp", stg[ss][:, :, :M], src, writes=[("stg", ss)])
                    P.op("pool", lambda h: h.tensor_copy(wb[slot][:, 4 * q:4 * q + 4, :M], stg[ss][:, :, :M]),
                         reads=[("stg", ss)], writes=[("w", slot)])
                return slot

            slot = load_w(C_DT, 32)
            for tt in range(NT):
                b = tt // 16
                for k in range(16):
                    P.op("pe", lambda h, k=k: h.matmul(psA[b][:, (tt % 16) * 32:(tt % 16) * 32 + 32],
                                                       xTb[:, k, tt * 128:(tt + 1) * 128], wb[slot][:, k, :32],
                                                       start=(k == 0), stop=(k == 15)),
                         reads=[("w", slot), ("xTb", k)], writes=[("psA", b)], sig=(k == 15))
            for b in range(2):
                dv = dtt[:, b * 16:(b + 1) * 16, :]
                P.op("dve", lambda h: h.tensor_tensor(dv, psA[b][:, :].rearrange("p (a c) -> p a c", c=32),
                                                      dtb[:, None, :].broadcast_to([128, 16, 32]), ALU.add),
                     reads=[("psA", b), "dtb"], writes=[("dtt", b)])
                P.op("act", lambda h: h.activation(dv, dv, AF.Exp), reads=[("dtt", b)], writes=[("dtt", b)])
                P.op("act", lambda h: h.activation(dv, dv, AF.Ln, bias=1.0), reads=[("dtt", b)], writes=[("dtt", b)])
            P.dma("sp", self.dtk, dtt[:, :, :], reads=[("dtt", 0), ("dtt", 1)], writes=["dtk"])
            ctr["grp"] = 2
            for fam in ("v", "g"):
                for cg in range(4):
                    slot = load_w((C_V if fam == "v" else C_G) + 512 * cg, 512)
                    for tt in range(NT):
                        b = ctr["grp"] % 4
                        ctr["grp"] += 1
                        for k in range(16):
                            P.op("pe", lambda h, k=k: h.matmul(psA[b][:, :], xTb[:, k, tt * 128:(tt + 1) * 128],
                                                               wb[slot][:, k, :], start=(k == 0), stop=(k == 15)),
                                 reads=[("w", slot), ("xTb", k)], writes=[("psA", b)], sig=(k == 15))
                        o = ctr["ob"] % 3
                        ctr["ob"] += 1
                        if fam == "v":
                            P.op("act", lambda h: h.activation(vt[o][:, :, 0:128],
                                                               psA[b][:, :].rearrange("p (a c) -> p a c", c=128), AF.Copy),
                                 reads=[("psA", b)], writes=[("vt", o)])
                            P.dma("sp", self.V1[tt * 128:(tt + 1) * 128, 4 * cg:4 * cg + 4, :], vt[o][:, :, :],
                                  reads=[("vt", o)], writes=[("V1", tt, cg)])
                        else:
                            P.op("act", lambda h: h.activation(ob[o][:, :], psA[b][:, :], AF.Silu),
                                 reads=[("psA", b)], writes=[("ob", o)])
                            P.dma("sp", self.gs[tt * 128:(tt + 1) * 128, cg * 512:(cg + 1) * 512], ob[o][:, :],
                                  reads=[("ob", o)], writes=[("gs", tt, cg)])

    def stage_B(self):
        P, nc = self.P, self.nc
        xsTv = self.xsT.rearrange("(k p) t -> p k t", p=128)
        zsv = self.zs.rearrange("(k p) t -> p k t", p=128)
        BTv = self.BT.rearrange("(k p) t -> p k t", p=128)
        CTv = self.CT.rearrange("(k p) t -> p k t", p=128)
        with self.stage():
            sb = self.sb
            pb = [self.ps(f"pb{i}", [128, 512]) for i in range(6)]
            ptrs = [self.ps(f"ptr{i}", [128, 1024], BF16) for i in range(2)]
            ps_small, ps_T = pb[0][:, 0:96], pb[0][0:32, 128:384]
            ps_q = pb[0][:, 100:102]
            ps_g = [pb[1][:, 0:384]]
            ps_bc = [pb[2][:, 0:256], pb[3][:, 0:256]]
            ps_y = [pb[4][:, 0:256]]
            ps_st = [pb[5], pb[5]]
            P.bankmap = {"ps_small": "B0", "ps_q": "B0", ("ps_y", 0, 0): "BY", ("ps_y", 0, 1): "BY",
                         ("ps_st", 0): "BST", ("ps_st", 1): "BST"}
            xsc = [sb(f"xsc{i}", [128, 16, 256], BF16) for i in range(2)]
            zsc = [sb(f"zsc{i}", [128, 16, 256], BF16) for i in range(2)]
            btc = [sb(f"btc{i}", [128, 4, 256], BF16) for i in range(2)]
            ctc = [sb(f"ctc{i}", [128, 4, 256], BF16) for i in range(2)]
            dtt = sb("dtt", [128, NT, 32], F32)
            abc = sb("abc", [128, 32], F32)
            sel = sb("sel", [32, 32, 128], F32)
            triu = sb("triu", [128, 128], F32)
            ones = sb("ones", [128, 128], F32)
            r0 = sb("r0", [128, 256], F32)
            maskg = sb("maskg", [128, 384], F32)
            ident = sb("ident", [128, 128], BF16)
            dp = sb("dp", [128, 16], F32)
            ng = sb("ng", [128, 16], F32)
            atok = sb("atok", [128, 2, 32], F32)
            acum = sb("acum", [128, 3, 32], F32)
            acT = sb("acT", [32, 256], F32)
            dte = sb("dte", [128, 2, 32], F32)
            eAt = sb("eAt", [128, 32], F32)
            X = sb("X", [128, 2, 2048], BF16)
            Xd = sb("Xd", [128, 2, 2048], BF16)
            Btok = sb("Btok", [128, 2, 512], BF16)
            Gm = sb("Gm", [128, 4, 384], F32)
            H32 = sb("H32", [128, 32, 64], F32)
            H16 = sb("H16", [128, 32, 64], BF16)
            Dm = [sb(f"Dm{i}", [128, 384], F32) for i in range(2)]
            MT = [sb(f"MT{i}", [128, 384], BF16) for i in range(2)]
            eA = [sb(f"eA{i}", [128, 256], F32) for i in range(2)]
            Ct = [sb(f"Ct{i}", [128, 256], BF16) for i in range(2)]
            yf = [sb(f"yf{i}", [128, 256], F32) for i in range(2)]
            yg = [sb(f"yg{i}", [128, 256], F32) for i in range(2)]
            sq = [sb(f"sq{i}", [128, 256], F32) for i in range(2)]
            ya16 = [sb(f"ya16{i}", [128, 256], BF16) for i in range(2)]
            rs = sb("rs", [128, NT], F32)

            P.dma("sp", dtt[:], self.dtk, writes=["dtt"])
            P.dma("sp", abc[:], self.alog.partition_broadcast(128), writes=["abc"])
            P.dma("sp", sel[:], self.sel, writes=["sel"])
            P.dma("sp", triu[:], self.triu, writes=["triu"])
            P.dma("sp", maskg[:], self.maskg, writes=["maskg"])
            P.dma("sp", r0[:], self.maskg[:, 0:256], writes=["r0"])
            P.dma("pool", ident[:], self.ident, writes=["ident"])
            P.dma("sp", dp[:], self.ssd_dp, writes=["dp"])
            P.dma("sp", ng[:], self.normg, writes=["ng"])
            P.op("pool", lambda h: h.memset(ones[:], 1.0), writes=["ones"])
            P.op("pool", lambda h: h.memset(H32[:], 0.0), writes=["H32"])
            P.op("pool", lambda h: h.memset(H16[:], 0.0), writes=[("H16", g) for g in range(4)])
            P.op("act", lambda h: h.activation(abc[:], abc[:], AF.Exp), reads=["abc"], writes=["abc"])
            P.op("dve", lambda h: h.tensor_scalar(abc[:], abc[:], -1.0, None, ALU.mult), reads=["abc"], writes=["abc"])

            def load_chunk(c):
                s_ = c % 2
                cs = slice(c * 256, (c + 1) * 256)
                P.dma("sp", xsc[s_][:], xsTv[:, :, cs], writes=[("xsc", s_)])
                P.dma("sp", zsc[s_][:], zsv[:, :, cs], writes=[("zsc", s_)])
                P.dma("pool", btc[s_][:], BTv[:, :, cs], writes=[("btc", s_)])
                P.dma("pool", ctc[s_][:], CTv[:, :, cs], writes=[("ctc", s_)])

            bstop = getattr(self, 'bstop', 99)
            if bstop <= 1:
                return
            load_chunk(0)
            hc = 0
            for c in range(getattr(self, 'b_chunks', NCH)):
                s_ = c % 2
                if c + 1 < NCH:
                    load_chunk(c + 1)
                P.op("dve", lambda h: h.tensor_tensor(atok[:], dtt[:, 2 * c:2 * c + 2, :],
                                                      abc[:, None, :].broadcast_to([128, 2, 32]), ALU.mult),
                     reads=["dtt", "abc"], writes=["atok"])
                mm = lambda out, l, r, st, sp, sg: P.op(
                    "pe", lambda h: h.matmul(out, l, r, start=st, stop=sp), reads=["atok", "triu", "ones", "r0"],
                    writes=["ps_small"], sig=sg)
                mm(ps_small[:, 0:32], triu[:], atok[:, 0, :], True, True, False)
                mm(ps_small[:, 32:64], ones[:], atok[:, 0, :], True, False, False)
                mm(ps_small[:, 32:64], triu[:], atok[:, 1, :], False, True, False)
                mm(ps_small[:, 64:96], ones[:], atok[:, 0, :], True, False, False)
                mm(ps_small[:, 64:96], ones[:], atok[:, 1, :], False, True, False)
                mm(ps_T[:, :], atok[:, 0, :], r0[:], True, False, False)
                mm(ps_T[:, 128:256], atok[:, 1, :], triu[:], False, True, True)
                P.op("act", lambda h: h.activation(acum[:].rearrange("p a b -> p (a b)"), ps_small, AF.Copy),
                     reads=["ps_small"], writes=["acum"])
                P.op("act", lambda h: h.activation(acT[:], ps_T, AF.Copy), reads=["ps_small"], writes=["acT"])
                P.op("dve", lambda h: h.tensor_tensor(dte[:], acum[:, 2:3, :].broadcast_to([128, 2, 32]), acum[:, 0:2, :],
                                                      ALU.subtract), reads=["acum"], writes=["dte"])
                P.op("act", lambda h: h.activation(dte[:], dte[:], AF.Exp), reads=["dte"], writes=["dte"])
                P.op("act", lambda h: h.activation(eAt[:], acum[:, 2, :], AF.Exp), reads=["acum"], writes=["eAt"])
                if bstop <= 2:
                    return
                tb = 0
                bsub = getattr(self, 'bsub', 'xdb')
                for j in range(2):
                    for q4 in range(4):
                        half = tb % 2
                        tb += 1
                        for kk in range(4):
                            k = q4 * 4 + kk
                            P.op("pe", lambda h: h.transpose(ptrs[half][:, kk * 128:(kk + 1) * 128],
                                                             xsc[s_][:, k, j * 128:(j + 1) * 128], ident[:]),
                                 reads=[("xsc", s_), "ident"], writes=[("ptr", half)], sig=(kk == 3))
                        hs = slice(8 * q4, 8 * q4 + 8)
                        cs_ = slice(512 * q4, 512 * q4 + 512)
                        P.op("dve", lambda h: h.tensor_tensor(
                            X[:, j, cs_].rearrange("p (a b) -> p a b", b=64),
                            ptrs[half][:, 0:512].rearrange("p (a b) -> p a b", b=64),
                            dtt[:, 2 * c + j, hs].unsqueeze(2).broadcast_to([128, 8, 64]), ALU.mult),
                            reads=[("ptr", half), "dtt"], writes=[("X", j, q4)])
                        if 'd' in bsub:
                          P.op("pool", lambda h: h.tensor_tensor(
                            Xd[:, j, cs_].rearrange("p (a b) -> p a b", b=64),
                            X[:, j, cs_].rearrange("p (a b) -> p a b", b=64),
                            dte[:, j, hs].unsqueeze(2).broadcast_to([128, 8, 64]), ALU.mult),
                            reads=[("X", j, q4), "dte"], writes=[("Xd", j, q4)])
                    if 'b' not in bsub:
                        continue
                    half = tb % 2
                    tb += 1
                    for g in range(4):
                        P.op("pe", lambda h: h.transpose(ptrs[half][:, g * 128:(g + 1) * 128],
                                                         btc[s_][:, g, j * 128:(j + 1) * 128], ident[:]),
                             reads=[("btc", s_), "ident"], writes=[("ptr", half)], sig=(g == 3))
                    P.op("act", lambda h: h.activation(Btok[:, j, :], ptrs[half][:, 0:512], AF.Copy),
                         reads=[("ptr", half)], writes=[("Btok", j)])
                if bstop <= 3:
                    return
                for g in range(4):
                    gb = 0
                    P.op("pe", lambda h: h.matmul(ps_g[gb][:, 0:256], btc[s_][:, g, 0:128], ctc[s_][:, g, :],
                                                  start=True, stop=True),
                         reads=[("btc", s_), ("ctc", s_)], writes=[("ps_g", gb)], sig=False)
                    P.op("pe", lambda h: h.matmul(ps_g[gb][:, 256:384], btc[s_][:, g, 128:256], ctc[s_][:, g, 128:256],
                                                  start=True, stop=True),
                         reads=[("btc", s_), ("ctc", s_)], writes=[("ps_g", gb)])
                    P.op("dve", lambda h: h.tensor_tensor(Gm[:, g, :], ps_g[gb], maskg[:], ALU.mult),
                         reads=[("ps_g", gb), "maskg"], writes=[("Gm", g)])
                if bstop <= 4:
                    return
                def emit_bc(hd_):
                    tt_ = hd_ % 2
                    P.op("pe", lambda h: h.matmul(ps_bc[tt_], sel[:, hd_, :], acT[:], start=True, stop=True),
                         reads=["sel", "acT"], writes=[("ps_bc", tt_)])

                for hd in range(getattr(self, 'b_heads', 32)):
                    g = hd // 8
                    pair = hd // 2
                    hh = hd % 2
                    t_ = hd % 2
                    hcols = slice(hd * 64, (hd + 1) * 64)
                    if hd == 0:
                        emit_bc(0)
                    P.op("dve", lambda h: h.tensor_scalar(Dm[t_][:, 0:256], ps_bc[t_], acum[:, 0, hd:hd + 1], 0.0,
                                                          ALU.subtract, ALU.min),
                         reads=[("ps_bc", t_), "acum"], writes=[("Dm", t_)])
                    P.op("dve", lambda h: h.tensor_scalar(Dm[t_][:, 256:384], ps_bc[t_][:, 128:256], acum[:, 1, hd:hd + 1],
                                                          0.0, ALU.subtract, ALU.min),
                         reads=[("ps_bc", t_), "acum"], writes=[("Dm", t_)])
                    P.op("act", lambda h: h.activation(Dm[t_][:], Dm[t_][:], AF.Exp), reads=[("Dm", t_)], writes=[("Dm", t_)])
                    P.op("dve", lambda h: h.tensor_tensor(MT[t_][:], Dm[t_][:], Gm[:, g, :], ALU.mult),
                         reads=[("Dm", t_), ("Gm", g)], writes=[("MT", t_)])
                    P.op("act", lambda h: h.activation(eA[t_][:], ps_bc[t_], AF.Exp), reads=[("ps_bc", t_)], writes=[("eA", t_)])
                    P.op("pool", lambda h: h.tensor_tensor(Ct[t_][:], ctc[s_][:, g, :], eA[t_][:], ALU.mult),
                         reads=[("ctc", s_), ("eA", t_)], writes=[("Ct", t_)])
                    if hd + 1 < getattr(self, 'b_heads', 32):
                        emit_bc(hd + 1)
                    yb = 0
                    yo = ps_y[yb][hh * 64:(hh + 1) * 64, :]
                    yres = ("ps_y", yb, hh)
                    P.op("pe", lambda h: h.matmul(yo[:, 0:256], X[:, 0, hcols], MT[t_][:, 0:256], start=True, stop=False),
                         reads=[("X", 0, hd // 8), ("MT", t_)], writes=[yres], sig=False)
                    P.op("pe", lambda h: h.matmul(yo[:, 128:256], X[:, 1, hcols], MT[t_][:, 256:384], start=False, stop=False),
                         reads=[("X", 1, hd // 8), ("MT", t_)], writes=[yres], sig=False)
                    P.op("pe", lambda h: h.matmul(yo[:, 0:256], H16[:, hd, :], Ct[t_][:], start=False, stop=True),
                         reads=[("H16", g), ("Ct", t_)], writes=[yres])
                    so = ps_st[g % 2][:, (hd % 8) * 64:(hd % 8 + 1) * 64]
                    P.op("pe", lambda h: h.matmul(so, Btok[:, 0, g * 128:(g + 1) * 128], Xd[:, 0, hcols], start=True, stop=False),
                         reads=[("Btok", 0), ("Xd", 0, hd // 8)], writes=[("ps_st", g % 2)], sig=False)
                    P.op("pe", lambda h: h.matmul(so, Btok[:, 1, g * 128:(g + 1) * 128], Xd[:, 1, hcols], start=False, stop=True),
                         reads=[("Btok", 1), ("Xd", 1, hd // 8)], writes=[("ps_st", g % 2)])
                    if hd % 8 == 7:
                        hsl = slice(8 * g, 8 * g + 8)
                        P.op("dve", lambda h: h.tensor_tensor(H32[:, hsl, :], H32[:, hsl, :],
                                                              eAt[:, hsl].unsqueeze(2).broadcast_to([128, 8, 64]), ALU.mult),
                             reads=["H32", "eAt"], writes=["H32"])
                        P.op("dve", lambda h: h.tensor_tensor(H32[:, hsl, :], H32[:, hsl, :],
                                                              ps_st[g % 2][:, :].rearrange("p (a b) -> p a b", b=64), ALU.add),
                             reads=["H32", ("ps_st", g % 2)], writes=["H32"])
                        P.op("act", lambda h: h.activation(H16[:, hsl, :], H32[:, hsl, :], AF.Copy),
                             reads=["H32"], writes=[("H16", g)])
                    if hh == 1:
                        e_ = pair % 2
                        P.op("dve", lambda h: h.scalar_tensor_tensor(yf[e_][:], xsc[s_][:, pair, :], dp[:, pair:pair + 1],
                                                                     ps_y[yb], ALU.mult, ALU.add),
                             reads=[("xsc", s_), "dp", ("ps_y", yb, 0), ("ps_y", yb, 1)], writes=[("yf", e_)])
                        P.op("pool", lambda h: h.tensor_tensor(yg[e_][:], yf[e_][:], zsc[s_][:, pair, :], ALU.mult),
                             reads=[("yf", e_), ("zsc", s_)], writes=[("yg", e_)])
                        P.op("act", lambda h: h.activation(sq[e_][:], yg[e_][:], AF.Square), reads=[("yg", e_)], writes=[("sq", e_)])
                        for j in range(2):
                            P.op("pe", lambda h: h.matmul(ps_q[:, j:j + 1], sq[e_][:, j * 128:(j + 1) * 128], ones[:, 0:1],
                                                          start=(pair == 0 and j == 0), stop=(pair == 15)),
                                 reads=[("sq", e_), "ones"], writes=["ps_q"], sig=(j == 1))
                        P.op("act", lambda h: h.activation(ya16[e_][:], yg[e_][:], AF.Copy, scale=ng[:, pair:pair + 1]),
                             reads=[("yg", e_), "ng"], writes=[("ya16", e_)])
                        P.dma("sp", self.yaT[pair * 128:(pair + 1) * 128, c * 256:(c + 1) * 256], ya16[e_][:],
                              reads=[("ya16", e_)], writes=[("yaT", pair, c)])
                P.op("dve", lambda h: h.tensor_scalar(rs[:, 2 * c:2 * c + 2], ps_q, 1.0 / 2048.0, 1e-5, ALU.mult, ALU.add),
                     reads=["ps_q"], writes=["rs"])
            P.op("act", lambda h: h.activation(rs[:], rs[:], AF.Ln), reads=["rs"], writes=["rs"])
            P.op("act", lambda h: h.activation(rs[:], rs[:], AF.Exp, scale=-0.5), reads=["rs"], writes=["rs"])
            P.dma("sp", self.rstd_s, rs[:], reads=["rs"], writes=["rstd_s"])

    def stage_C(self):
        P, nc = self.P, self.nc
        V1v = self.V1.rearrange("(t p) h c -> p t h c", p=128)
        gsv = self.gs.rearrange("(t p) c -> p t c", p=128)
        SC = 1.0 / math.sqrt(128.0)
        with self.stage():
            sb = self.sb
            psS = [self.ps(f"psS{i}", [128, 512]) for i in range(2)]
            psO = [[self.ps(f"psO{i}{x}", [128, 512]) for x in "XY"] for i in range(2)]
            ps_gate = self.ps("ps_gate", [128, 512])
            ps_tr = self.ps("ps_tr", [128, 1024], BF16)
            q16 = [sb(f"q16{i}", [128, T], BF16) for i in range(2)]
            k16 = [sb(f"k16{i}", [128, T], BF16) for i in range(2)]
            v1 = [sb(f"v1{i}", [128, NT, 129], BF16) for i in range(2)]
            q32 = [sb(f"q32{i}", [128, T], F32) for i in range(2)]
            gsh = [sb(f"gsh{i}", [128, NT, 128], BF16) for i in range(2)]
            km = [sb(f"km{i}", [128, NCH], F32) for i in range(2)]
            vbias = sb("vbias", [128, NT, NCH], F32)
            tri16 = sb("tri16", [128, 128], BF16)
            ident = sb("ident", [128, 128], BF16)
            gm = sb("gm", [128, NT, NCH], F32)
            top8 = sb("top8", [128, NT, 8], F32)
            mask = sb("mask", [128, NT, NCH], F32)
            E = [sb(f"E{i}", [128, 512], BF16) for i in range(3)]
            acc = [sb(f"acc{i}", [128, 4, 129], F32) for i in range(2)]
            rec = sb("rec", [128, 8], F32)
            yb16 = [sb(f"yb16{i}", [128, 128], BF16) for i in range(2)]
            ybo = [sb(f"ybo{i}", [128, 512], BF16) for i in range(2)]
            P.dma("sp", vbias[:].rearrange("p a b -> p (a b)"), self.vbias, writes=["vbias"])
            P.dma("pool", tri16[:], self.triu, writes=["tri16"])
            P.dma("pool", ident[:], self.ident, writes=["ident"])

            def load_head(hd):
                s_ = hd % 2
                P.dma("sp", q16[s_][:], self.qT16[hd], writes=[("q16", s_)])
                P.dma("sp", k16[s_][:], self.kT16[hd], writes=[("k16", s_)])
                P.dma("sp", v1[s_][:], V1v[:, :, hd, :], writes=[("v1", s_)])
                P.dma("sp", q32[s_][:], self.qT32[hd], writes=[("q32", s_)])
                P.dma("sp", gsh[s_][:], gsv[:, :, hd * 128:(hd + 1) * 128], writes=[("gsh", s_)])
                P.dma("sp", km[s_][:], self.kmean[:, hd, :], writes=[("km", s_)])

            load_head(0)
            cS = cE = cY = 0
            nheads = getattr(self, "c_heads", 16)
            for hd in range(nheads):
                s_ = hd % 2
                if hd + 1 < nheads:
                    load_head(hd + 1)
                for qt in range(NT):
                    P.op("pe", lambda h: h.matmul(ps_gate[:, qt * NCH:(qt + 1) * NCH], q32[s_][:, qt * 128:(qt + 1) * 128],
                                                  km[s_][:], start=True, stop=True),
                         reads=[("q32", s_), ("km", s_)], writes=["ps_gate"], sig=(qt == NT - 1))
                P.op("dve", lambda h: h.tensor_tensor(gm[:].rearrange("p a b -> p (a b)"), ps_gate[:, :],
                                                      vbias[:].rearrange("p a b -> p (a b)"), ALU.add),
                     reads=["ps_gate", "vbias"], writes=["gm"])
                for qt in range(NT):
                    P.op("dve", lambda h: h.max(top8[:, qt, :], gm[:, qt, :]), reads=["gm"], writes=["top8"])
                P.op("dve", lambda h: h.tensor_tensor(mask[:], gm[:], top8[:, :, 2:3].broadcast_to([128, NT, NCH]), ALU.is_ge),
                     reads=["gm", "top8"], writes=["mask"])
                for j in range(NG):
                    ab = j % 2
                    first_acc = [True] * 4
                    steps = []
                    for n in range(2 * j + 2):
                        for kt in range(2):
                            if n < 2 * j:
                                first, diag = 0, False
                            elif n == 2 * j:
                                first, diag = kt, True
                            else:
                                first, diag = 2 + kt, True
                            sbk = cS % 2
                            cS += 1
                            eb = cE % 3
                            cE += 1
                            steps.append(dict(n=n, kt=kt, first=first, diag=diag, sbk=sbk, eb=eb, K=2 * n + kt,
                                              N=(4 - first) * 128, q0=(4 * j + first) * 128))

                    def emit_qk(sp_):
                        sbk, N, q0, K_ = sp_["sbk"], sp_["N"], sp_["q0"], sp_["K"]
                        P.op("pe", lambda h: h.matmul(psS[sbk][:, 0:N], k16[s_][:, K_ * 128:(K_ + 1) * 128],
                                                      q16[s_][:, q0:q0 + N], start=True, stop=True),
                             reads=[("k16", s_), ("q16", s_)], writes=[("psS", sbk)])

                    def emit_exp(sp_):
                        sbk, N, eb = sp_["sbk"], sp_["N"], sp_["eb"]
                        P.op("act", lambda h: h.activation(E[eb][:, 0:N], psS[sbk][:, 0:N], AF.Exp, scale=SC),
                             reads=[("psS", sbk)], writes=[("E", eb)])
                        if sp_["diag"]:
                            P.op("pool", lambda h: h.tensor_tensor(E[eb][:, 0:128], E[eb][:, 0:128], tri16[:], ALU.mult),
                                 reads=[("E", eb), "tri16"], writes=[("E", eb)])

                    blk_state = {}

                    def emit_pv(sp_):
                        n, kt, first, eb, K_ = sp_["n"], sp_["kt"], sp_["first"], sp_["eb"], sp_["K"]
                        nb = n % 2
                        stt = blk_state.setdefault(n, {"X": False, "Y": False, "vis": set()})
                        for t in range(first, 4):
                            x = "X" if t < 2 else "Y"
                            ob = psO[nb][0 if t < 2 else 1]
                            last_kt = (kt == 1) or (n == 2 * j and t == 0) or (n == 2 * j + 1 and t == 2)
                            st = not stt[x]
                            stt[x] = True
                            stt["vis"].add(t)
                            P.op("pe", lambda h: h.matmul(ob[:, (t % 2) * 256:(t % 2) * 256 + 129],
                                                          E[eb][:, (t - first) * 128:(t - first + 1) * 128],
                                                          v1[s_][:, K_, :], start=st, stop=last_kt),
                                 reads=[("E", eb), ("v1", s_)], writes=[("psO", nb, x)], sig=(t == 3))

                    def emit_acc(n):
                        nb = n % 2
                        for t in sorted(blk_state[n]["vis"]):
                            x = "X" if t < 2 else "Y"
                            ob = psO[nb][0 if t < 2 else 1][:, (t % 2) * 256:(t % 2) * 256 + 129]
                            own = (n == 2 * j + t // 2)
                            mcol = mask[:, 4 * j + t, n:n + 1]
                            a_t = acc[ab][:, t, :]
                            if first_acc[t]:
                                first_acc[t] = False
                                if own:
                                    P.op("dve", lambda h: h.tensor_copy(a_t, ob), reads=[("psO", nb, x)], writes=[("acc", ab, t)])
                                else:
                                    P.op("dve", lambda h: h.tensor_scalar(a_t, ob, mcol, None, ALU.mult),
                                         reads=[("psO", nb, x)], writes=[("acc", ab, t)], strict=["mask"])
                            elif own:
                                P.op("dve", lambda h: h.tensor_tensor(a_t, ob, a_t, ALU.add),
                                     reads=[("psO", nb, x), ("acc", ab, t)], writes=[("acc", ab, t)])
                            else:
                                P.op("dve", lambda h: h.scalar_tensor_tensor(a_t, ob, mcol, a_t, ALU.mult, ALU.add),
                                     reads=[("psO", nb, x), ("acc", ab, t)], writes=[("acc", ab, t)], strict=["mask"])

                    emit_qk(steps[0])
                    for i_, sp_ in enumerate(steps):
                        if i_ + 1 < len(steps):
                            emit_qk(steps[i_ + 1])
                        emit_exp(sp_)
                        emit_pv(sp_)
                        if sp_["kt"] == 1:
                            emit_acc(sp_["n"])
                    yo = cY % 2
                    cY += 1
                    for t in range(4):
                        yb_ = t % 2
                        P.op("dve", lambda h: h.reciprocal(rec[:, t:t + 1], acc[ab][:, t, 128:129]),
                             reads=[("acc", ab, t)], writes=[("rec", t)])
                        P.op("dve", lambda h: h.scalar_tensor_tensor(yb16[yb_][:], acc[ab][:, t, 0:128], rec[:, t:t + 1],
                                                                     gsh[s_][:, 4 * j + t, :], ALU.mult, ALU.mult),
                             reads=[("acc", ab, t), ("gsh", s_)], writes=[("yb16", yb_)], strict=[("rec", t)])
                        P.op("pe", lambda h: h.transpose(ps_tr[:, t * 128:(t + 1) * 128], yb16[yb_][:], ident[:]),
                             reads=[("yb16", yb_), "ident"], writes=["ps_tr"])
                    P.op("act", lambda h: h.activation(ybo[yo][:], ps_tr[:, 0:512], AF.Copy), reads=["ps_tr"], writes=[("ybo", yo)])
                    P.dma("sp", self.ybT[hd * 128:(hd + 1) * 128, j * 512:(j + 1) * 512], ybo[yo][:],
                          reads=[("ybo", yo)], writes=[("ybT", hd, j)])

    def outproj_ln(self, parts, w_dram, nk_total, resid, layer, out_dram, xT_out=None, rstd_dram=None):
        P, nc = self.P, self.nc
        wv = w_dram.rearrange("(k p) c -> p k c", p=128)
        with self.stage():
            sb = self.sb
            W = sb("W", [128, nk_total, D], BF16)
            for k in range(nk_total):
                P.dma("pool", W[:, k, :], wv[:, k, :], writes=[("W", k)])
            npart = len(parts)
            psP = [[self.ps(f"psP{i}{a}", [128, 512]) for a in range(npart)] for i in range(2)]
            ps_tr = [self.ps(f"ps_tr{i}", [128, 1024], BF16) for i in range(2)] if xT_out is not None else None
            lt = [[sb(f"lt{i}{a}", [128, parts[a][2], 128], BF16) for a in range(npart)] for i in range(2)]
            xt = [sb(f"xt{i}", [128, D], F32) for i in range(2)]
            v = sb("v", [128, D], F32)
            junk = sb("junk", [128, D], BF16)
            gbc = sb("gbc", [128, D], F32)
            bbc = sb("bbc", [128, D], F32)
            st = sb("st", [128, 8], F32)
            P.dma("sp", gbc[:], self.ln_g[layer:layer + 1, :].partition_broadcast(128), writes=["gbc"])
            P.dma("sp", bbc[:], self.ln_b[layer:layer + 1, :].partition_broadcast(128), writes=["bbc"])
            if rstd_dram is not None:
                rs = sb("rs", [128, NT], F32)
                P.dma("sp", rs[:], rstd_dram, writes=["rs"])
            if xT_out is not None:
                ident = sb("ident", [128, 128], BF16)
                P.dma("pool", ident[:], self.ident, writes=["ident"])
                x1b = sb("x1b", [128, D], BF16)
                xTt = sb("xTt", [128, 16, 128], BF16)
                xTv = xT_out.rearrange("(k p) t -> p k t", p=128)
            fv = [pt[0].rearrange("(k p) t -> p k t", p=128) for pt in parts]

            def load_tile(tt):
                s_ = tt % 2
                for a in range(npart):
                    P.dma("sp", lt[s_][a][:], fv[a][:, :, tt * 128:(tt + 1) * 128], writes=[("lt", s_, a)])
                P.dma("sp", xt[s_][:], resid[tt * 128:(tt + 1) * 128, :], writes=[("xt", s_)])

            load_tile(0)
            cP = 0
            for tt in range(NT):
                s_ = tt % 2
                if tt + 1 < NT:
                    load_tile(tt + 1)
                for cg in range(4):
                    pb = cP % 2
                    cP += 1
                    cs = slice(cg * 512, (cg + 1) * 512)
                    for a, (_, k0, nk, use_rstd) in enumerate(parts):
                        for k in range(nk):
                            P.op("pe", lambda h: h.matmul(psP[pb][a][:, :], lt[s_][a][:, k, :], W[:, k0 + k, cs],
                                                          start=(k == 0), stop=(k == nk - 1)),
                                 reads=[("lt", s_, a), ("W", k0 + k)], writes=[("psP", pb, a)], sig=(k == nk - 1))
                    first = True
                    for a, (_, k0, nk, use_rstd) in enumerate(parts):
                        if use_rstd:
                            continue
                        P.op("dve", lambda h: h.scalar_tensor_tensor(v[:, cs], xt[s_][:, cs], ALPHA, psP[pb][a][:, :],
                                                                     ALU.mult, ALU.add),
                             reads=[("xt", s_), ("psP", pb, a)], writes=[("v", cg)])
                        first = False
                    for a, (_, k0, nk, use_rstd) in enumerate(parts):
                        if not use_rstd:
                            continue
                        P.op("dve", lambda h: h.scalar_tensor_tensor(v[:, cs], psP[pb][a][:, :], rs[:, tt:tt + 1], v[:, cs],
                                                                     ALU.mult, ALU.add),
                             reads=[("psP", pb, a), ("v", cg)], writes=[("v", cg)], strict=["rs"])
                vres = [("v", cg) for cg in range(4)]
                P.op("act", lambda h: h.activation(junk[:], v[:], AF.Square), reads=vres, writes=["junk"])
                P.op("dve", lambda h: h.reduce_sum(st[:, 0:1], v[:], AX.X), reads=vres, writes=[("st", 0)])
                P.op("dve", lambda h: h.reduce_sum(st[:, 1:2], junk[:], AX.X), reads=["junk"], writes=[("st", 1)])
                P.op("dve", lambda h: h.tensor_scalar(st[:, 2:3], st[:, 0:1], 1.0 / D, None, ALU.mult),
                     reads=[("st", 0)], writes=[("st", 2)])
                P.op("dve", lambda h: h.tensor_tensor(st[:, 3:4], st[:, 2:3], st[:, 2:3], ALU.mult),
                     reads=[("st", 2)], writes=[("st", 3)])
                P.op("dve", lambda h: h.scalar_tensor_tensor(st[:, 4:5], st[:, 1:2], 1.0 / D, st[:, 3:4], ALU.mult, ALU.subtract),
                     reads=[("st", 1), ("st", 3)], writes=[("st", 4)])
                P.op("dve", lambda h: h.tensor_scalar(st[:, 4:5], st[:, 4:5], 1e-5, None, ALU.add),
                     reads=[("st", 4)], writes=[("st", 4)])
                P.op("act", lambda h: h.activation(st[:, 5:6], st[:, 4:5], AF.Ln), reads=[("st", 4)], writes=[("st", 5)])
                P.op("act", lambda h: h.activation(st[:, 5:6], st[:, 5:6], AF.Exp, scale=-0.5), reads=[("st", 5)], writes=[("st", 5)])
                P.op("dve", lambda h: h.scalar_tensor_tensor(st[:, 6:7], st[:, 2:3], -1.0, st[:, 5:6], ALU.mult, ALU.mult),
                     reads=[("st", 2), ("st", 5)], writes=[("st", 6)])
                P.op("act", lambda h: h.activation(v[:], v[:], AF.Identity, bias=st[:, 6:7], scale=st[:, 5:6]),
                     reads=vres, writes=vres, strict=[("st", 5), ("st", 6)])
                P.op("dve", lambda h: h.tensor_tensor(v[:], v[:], gbc[:], ALU.mult), reads=vres + ["gbc"], writes=vres)
                P.op("dve", lambda h: h.tensor_tensor(v[:], v[:], bbc[:], ALU.add), reads=vres + ["bbc"], writes=vres)
                P.dma("sp", out_dram[tt * 128:(tt + 1) * 128, :], v[:], reads=vres, writes=[("out", tt)])
                if xT_out is not None:
                    P.op("act", lambda h: h.activation(x1b[:], v[:], AF.Copy), reads=vres, writes=["x1b"])
                    for hb in range(2):
                        for kk in range(8):
                            k = hb * 8 + kk
                            P.op("pe", lambda h: h.transpose(ps_tr[hb][:, kk * 128:(kk + 1) * 128], x1b[:, k * 128:(k + 1) * 128],
                                                             ident[:]),
                                 reads=["x1b", "ident"], writes=[("ps_tr", hb)], sig=(kk == 7))
                        P.op("act" if hb == 0 else "dve",
                             (lambda h: h.activation(xTt[:, 0:8, :].rearrange("p a b -> p (a b)"), ps_tr[0][:, :], AF.Copy)) if hb == 0 else
                             (lambda h: h.tensor_copy(xTt[:, 8:16, :].rearrange("p a b -> p (a b)"), ps_tr[1][:, :])),
                             reads=[("ps_tr", hb)], writes=[("xTt", hb)])
                    P.dma("sp", xTv[:, :, tt * 128:(tt + 1) * 128], xTt[:], reads=[("xTt", 0), ("xTt", 1)], writes=[("xT_out", tt)])

    def stage_D(self):
        self.outproj_ln([(self.ybT, 16, 16, False), (self.yaT, 0, 16, True)], self.w_out0, 32, self.x, 0, self.x1,
                        xT_out=self.x1T, rstd_dram=self.rstd_s)

    def stage_E(self):
        P, nc = self.P, self.nc
        x1Tv = self.x1T.rearrange("(k p) t -> p k t", p=128)
        with self.stage():
            xTb = self.sb("xTb", [128, 16, T], BF16)
            for k in range(16):
                P.dma("sp", xTb[:, k, :], x1Tv[:, k, :], writes=[("xTb", k)])
            wsl = [self.sb(f"wsl{i}", [128, 16, 128], BF16) for i in range(2)]
            stg = [self.sb(f"stg{i}", [128, 16, 128], F32) for i in range(2)]
            psA = [self.ps(f"psA{i}", [128, 512]) for i in range(4)]
            ob = [self.sb(f"ob{i}", [128, 512], BF16) for i in range(3)]
            co = 0
            cg_ = 0
            for i in range(32):
                slot = i % 2
                P.dma("sp", stg[slot][:], self.w_in1[i], writes=[("stg", slot)])
                P.op("pool", lambda h: h.tensor_copy(wsl[slot][:], stg[slot][:]), reads=[("stg", slot)], writes=[("w", slot)])
                for g in range(NG):
                    b = cg_ % 4
                    cg_ += 1
                    for k in range(16):
                        P.op("pe", lambda h: h.matmul(psA[b][:, :], wsl[slot][:, k, :], xTb[:, k, g * 512:(g + 1) * 512],
                                                      start=(k == 0), stop=(k == 15)),
                             reads=[("w", slot), ("xTb", k)], writes=[("psA", b)], sig=(k == 15))
                    o = co % 3
                    co += 1
                    P.op("act", lambda h: h.activation(ob[o][:], psA[b][:, :], AF.Copy if i < 16 else AF.Silu),
                         reads=[("psA", b)], writes=[("ob", o)])
                    dst = self.uT if i < 16 else self.sg1T
                    P.dma("sp", dst[(i % 16) * 128:(i % 16 + 1) * 128, g * 512:(g + 1) * 512], ob[o][:],
                          reads=[("ob", o)], writes=[("eo", i, g)])

    def stage_F(self):
        P, nc = self.P, self.nc
        TWO_PI = 2.0 * math.pi
        uTv = self.uT.rearrange("(k p) t -> p k t", p=128)
        with self.stage():
            sb = self.sb
            psV = [[self.ps(f"psV{i}{a}", [128, 512]) for a in "ri"] for i in range(2)]
            psY = [self.ps(f"psY{i}", [128, 512]) for i in range(2)]
            pl = sb("pl", [128, 3, 64], F32)
            wl = sb("wl", [128, 5, 16, 64], F32)
            cp = sb("cp", [128, 2, 64, 16], F32)
            d1 = sb("d1", [128, 16], F32)
            rmk = sb("rmk", [128, 8], F32)
            iot = sb("iot", [128, 513], F32)
            pi_c = sb("pi_c", [128, 1], F32)
            P.dma("sp", pl[:], self.s5_pl, writes=["pl"])
            P.dma("sp", wl[:], self.s5_wl, writes=["wl"])
            P.dma("sp", cp[:], self.s5_cp, writes=["cp"])
            P.dma("sp", d1[:], self.s5_d1, writes=["d1"])
            P.dma("sp", rmk[:], self.rowmask, writes=["rmk"])
            P.dma("sp", iot[:], self.iota513.partition_broadcast(128), writes=["iot"])
            P.op("pool", lambda h: h.memset(pi_c[:], math.pi), writes=["pi_c"])
            dtp = sb("dtp", [128, 64], F32)
            rP = sb("rP", [128, 64], F32)
            thP = sb("thP", [128, 64], F32)
            P.op("act", lambda h: h.activation(dtp[:], pl[:, 2, :], AF.Exp), reads=["pl"], writes=["dtp"])
            P.op("dve", lambda h: h.tensor_tensor(rP[:], pl[:, 0, :], dtp[:], ALU.mult), reads=["pl", "dtp"], writes=["rP"])
            P.op("act", lambda h: h.activation(rP[:], rP[:], AF.Exp), reads=["rP"], writes=["rP"])
            P.op("dve", lambda h: h.tensor_tensor(thP[:], pl[:, 1, :], dtp[:], ALU.mult), reads=["pl", "dtp"], writes=["thP"])
            thm = sb("thm", [128, 64], F32)
            P.op("dve", lambda h: h.tensor_scalar(thm[:], thP[:], 0.0, TWO_PI, ALU.is_lt, ALU.mult), reads=["thP"], writes=["thm"])
            P.op("dve", lambda h: h.tensor_tensor(thP[:], thP[:], thm[:], ALU.add), reads=["thP", "thm"], writes=["thP"])

            def sincos(o_sin, o_cos, a_in, shape, tag):
                ki = sb(f"ki_{tag}", shape, mybir.dt.int32)
                kf = sb(f"kf_{tag}", shape, F32)
                rr = sb(f"rr_{tag}", shape, F32)
                mm_ = sb(f"mm_{tag}", shape, F32)
                r_ = [f"sc_{tag}"]
                P.op("dve", lambda h: h.tensor_scalar(ki[:], a_in, 1.0 / TWO_PI, None, ALU.mult), reads=r_, writes=r_)
                P.op("dve", lambda h: h.tensor_copy(kf[:], ki[:]), reads=r_, writes=r_)
                P.op("dve", lambda h: h.scalar_tensor_tensor(rr[:], kf[:], -TWO_PI, a_in, ALU.mult, ALU.add), reads=r_, writes=r_)
                P.op("dve", lambda h: h.tensor_scalar(mm_[:], rr[:], math.pi, -TWO_PI, ALU.is_gt, ALU.mult), reads=r_, writes=r_)
                P.op("dve", lambda h: h.tensor_tensor(mm_[:], mm_[:], rr[:], ALU.add), reads=r_, writes=r_)
                P.op("act", lambda h: h.activation(o_sin, mm_[:], AF.Sin), reads=r_, writes=r_)
                P.op("dve", lambda h: h.tensor_scalar(rr[:], rr[:], 0.5 * math.pi, None, ALU.add), reads=r_, writes=r_)
                P.op("dve", lambda h: h.tensor_scalar(mm_[:], rr[:], math.pi, -TWO_PI, ALU.is_gt, ALU.mult), reads=r_, writes=r_)
                P.op("dve", lambda h: h.tensor_tensor(mm_[:], mm_[:], rr[:], ALU.add), reads=r_, writes=r_)
                P.op("act", lambda h: h.activation(o_cos, mm_[:], AF.Sin), reads=r_, writes=r_)
            W3 = [128, 16, 64]
            dtw = sb("dtw", W3, F32)
            aw = sb("aw", W3, F32)
            tw = sb("tw", W3, F32)
            cw_ = sb("cw_", W3, F32)
            sw_ = sb("sw_", W3, F32)
            fre = sb("fre", W3, F32)
            fim = sb("fim", W3, F32)
            t0 = sb("t0", W3, F32)
            t1 = sb("t1", W3, F32)
            bbr = sb("bbr", W3, F32)
            bbi = sb("bbi", W3, F32)
            lre, lim, bre, bim = wl[:, 0], wl[:, 1], wl[:, 3], wl[:, 4]
            D_ = lambda fn, r, w: P.op("dve", fn, reads=r, writes=w)
            A_ = lambda fn, r, w, **kw: P.op("act", fn, reads=r, writes=w, **kw)
            A_(lambda h: h.activation(dtw[:], wl[:, 2], AF.Exp), ["wl"], ["dtw"])
            D_(lambda h: h.tensor_tensor(aw[:], lre, dtw[:], ALU.mult), ["wl", "dtw"], ["aw"])
            A_(lambda h: h.activation(aw[:], aw[:], AF.Exp), ["aw"], ["aw"])
            D_(lambda h: h.tensor_tensor(tw[:], lim, dtw[:], ALU.mult), ["wl", "dtw"], ["tw"])
            D_(lambda h: h.tensor_scalar(t0[:], tw[:], 0.0, TWO_PI, ALU.is_lt, ALU.mult), ["tw"], ["t0"])
            D_(lambda h: h.tensor_tensor(tw[:], tw[:], t0[:], ALU.add), ["tw", "t0"], ["tw", "sc_w"])
            sincos(sw_[:], cw_[:], tw[:], W3, "w")
            P.op("dve", lambda h: h.tensor_copy(sw_[:], sw_[:]), reads=["sc_w"], writes=["sw_", "cw_"])
            D_(lambda h: h.tensor_tensor(cw_[:], cw_[:], aw[:], ALU.mult), ["cw_", "aw"], ["cw_"])
            D_(lambda h: h.tensor_tensor(sw_[:], sw_[:], aw[:], ALU.mult), ["sw_", "aw"], ["sw_"])
            D_(lambda h: h.tensor_scalar(cw_[:], cw_[:], -1.0, None, ALU.add), ["cw_"], ["cw_"])
            D_(lambda h: h.tensor_tensor(t0[:], lre, lre, ALU.mult), ["wl"], ["t0"])
            D_(lambda h: h.tensor_tensor(t1[:], lim, lim, ALU.mult), ["wl"], ["t1"])
            D_(lambda h: h.tensor_tensor(t0[:], t0[:], t1[:], ALU.add), ["t0", "t1"], ["t0"])
            D_(lambda h: h.reciprocal(t0[:], t0[:]), ["t0"], ["t0"])
            D_(lambda h: h.tensor_tensor(fre[:], cw_[:], lre, ALU.mult), ["cw_", "wl"], ["fre"])
            D_(lambda h: h.tensor_tensor(t1[:], sw_[:], lim, ALU.mult), ["sw_", "wl"], ["t1"])
            D_(lambda h: h.tensor_tensor(fre[:], fre[:], t1[:], ALU.add), ["fre", "t1"], ["fre"])
            D_(lambda h: h.tensor_tensor(fre[:], fre[:], t0[:], ALU.mult), ["fre", "t0"], ["fre"])
            D_(lambda h: h.tensor_tensor(fim[:], sw_[:], lre, ALU.mult), ["sw_", "wl"], ["fim"])
            D_(lambda h: h.tensor_tensor(t1[:], cw_[:], lim, ALU.mult), ["cw_", "wl"], ["t1"])
            D_(lambda h: h.tensor_tensor(fim[:], fim[:], t1[:], ALU.subtract), ["fim", "t1"], ["fim"])
            D_(lambda h: h.tensor_tensor(fim[:], fim[:], t0[:], ALU.mult), ["fim", "t0"], ["fim"])
            D_(lambda h: h.tensor_tensor(bbr[:], fre[:], bre, ALU.mult), ["fre", "wl"], ["bbr"])
            D_(lambda h: h.tensor_tensor(t1[:], fim[:], bim, ALU.mult), ["fim", "wl"], ["t1"])
            D_(lambda h: h.tensor_tensor(bbr[:], bbr[:], t1[:], ALU.subtract), ["bbr", "t1"], ["bbr"])
            D_(lambda h: h.tensor_tensor(bbi[:], fre[:], bim, ALU.mult), ["fre", "wl"], ["bbi"])
            D_(lambda h: h.tensor_tensor(t1[:], fim[:], bre, ALU.mult), ["fim", "wl"], ["t1"])
            D_(lambda h: h.tensor_tensor(bbi[:], bbi[:], t1[:], ALU.add), ["bbi", "t1"], ["bbi"])
            uc = [sb(f"uc{i}", [128, T], BF16) for i in range(2)]
            Lr = [sb(f"Lr{i}", [128, 128], BF16) for i in range(4)]
            Li = [sb(f"Li{i}", [128, 128], BF16) for i in range(4)]
            Cr = [sb(f"Cr{i}", [128, 128], BF16) for i in range(4)]
            nCr = [sb(f"nCr{i}", [128, 128], BF16) for i in range(4)]
            nCi = [sb(f"nCi{i}", [128, 128], BF16) for i in range(4)]
            cosT = [sb(f"cosT{i}", [128, 513], F32) for i in range(4)]
            sinT = [sb(f"sinT{i}", [128, 513], F32) for i in range(4)]
            ang = sb("ang", [128, 513], F32)
            ki_p = sb("ki_p", [128, 513], mybir.dt.int32)
            kf_p = sb("kf_p", [128, 513], F32)
            rr_p = sb("rr_p", [128, 513], F32)
            mm_p = sb("mm_p", [128, 513], F32)

            def sincos_p(o_sin, o_cos):
                r_ = ["sc_p"]
                P.op("dve", lambda h: h.tensor_scalar(ki_p[:], ang[:], 1.0 / TWO_PI, None, ALU.mult), reads=r_, writes=r_)
                P.op("dve", lambda h: h.tensor_copy(kf_p[:], ki_p[:]), reads=r_, writes=r_)
                P.op("dve", lambda h: h.scalar_tensor_tensor(rr_p[:], kf_p[:], -TWO_PI, ang[:], ALU.mult, ALU.add), reads=r_, writes=r_)
                P.op("dve", lambda h: h.tensor_scalar(mm_p[:], rr_p[:], math.pi, -TWO_PI, ALU.is_gt, ALU.mult), reads=r_, writes=r_)
                P.op("dve", lambda h: h.tensor_tensor(mm_p[:], mm_p[:], rr_p[:], ALU.add), reads=r_, writes=r_)
                P.op("act", lambda h: h.activation(o_sin, mm_p[:], AF.Sin), reads=r_, writes=r_)
                P.op("dve", lambda h: h.tensor_scalar(rr_p[:], rr_p[:], 0.5 * math.pi, None, ALU.add), reads=r_, writes=r_)
                P.op("dve", lambda h: h.tensor_scalar(mm_p[:], rr_p[:], math.pi, -TWO_PI, ALU.is_gt, ALU.mult), reads=r_, writes=r_)
                P.op("dve", lambda h: h.tensor_tensor(mm_p[:], mm_p[:], rr_p[:], ALU.add), reads=r_, writes=r_)
                P.op("act", lambda h: h.activation(o_cos, mm_p[:], AF.Sin), reads=r_, writes=r_)
            qst = [sb(f"qst{i}", [128, 2], F32) for i in range(4)]
            qt_ = sb("qt_", [128, 2], F32)
            Vs = [[sb(f"Vs{i}{a}", [128, 512], F32) for a in "ri"] for i in range(2)]
            m1 = [sb(f"m1{i}", [128, 512], F32) for i in range(2)]
            m2 = [sb(f"m2{i}", [128, 512], F32) for i in range(2)]
            Wr = [sb(f"Wr{i}", [128, 512], F32) for i in range(2)]
            Wi = [sb(f"Wi{i}", [128, 512], F32) for i in range(2)]
            Gr = [sb(f"Gr{i}", [128, 512], F32) for i in range(2)]
            Gi = [sb(f"Gi{i}", [128, 512], F32) for i in range(2)]
            Pp = [[sb(f"Pp{i}{a}", [128, 512], BF16) for a in range(4)] for i in range(2)]
            yv = [sb(f"yv{i}", [128, 512], F32) for i in range(2)]
            ge1 = [sb(f"ge1{i}", [128, 512], F32) for i in range(2)]
            ge2 = [sb(f"ge2{i}", [128, 512], F32) for i in range(2)]
            yo = [sb(f"yo{i}", [128, 512], BF16) for i in range(2)]
            for i in range(4):
                for tl, nm in ((Cr, "Cr"), (nCr, "nCr"), (nCi, "nCi")):
                    P.op("pool", lambda h: h.memset(tl[i][:], 0.0), writes=[(nm, i)])
            P.dma("sp", uc[0][:], uTv[:, 0, :], writes=[("uc", 0)])
            cpb = 0
            nchunks = getattr(self, "f_chunks", 16)
            for j in range(nchunks):
                us = j % 2
                if j + 1 < nchunks:
                    P.dma("sp", uc[(j + 1) % 2][:], uTv[:, j + 1, :], writes=[("uc", (j + 1) % 2)])
                for pc in range(4):
                    pr = 4 * j + pc
                    for g2 in range(2):
                        gl = 2 * pc + g2
                        P.op("dve", lambda h: h.tensor_scalar(Lr[pc][:, g2 * 64:(g2 + 1) * 64], bbr[:, j, :], rmk[:, gl:gl + 1], None, ALU.mult),
                             reads=["bbr", "rmk"], writes=[("Lr", pc)])
                        P.op("dve", lambda h: h.tensor_scalar(Li[pc][:, g2 * 64:(g2 + 1) * 64], bbi[:, j, :], rmk[:, gl:gl + 1], None, ALU.mult),
                             reads=["bbi", "rmk"], writes=[("Li", pc)])
                        rs_ = slice(g2 * 64, (g2 + 1) * 64)
                        cs_ = slice(gl * 16, gl * 16 + 16)
                        P.op("act", lambda h: h.activation(Cr[pc][rs_, cs_], cp[rs_, 0, pr, :], AF.Copy), reads=["cp"], writes=[("Cr", pc)])
                        P.op("act", lambda h: h.activation(nCr[pc][rs_, cs_], cp[rs_, 0, pr, :], AF.Copy, scale=-1.0), reads=["cp"], writes=[("nCr", pc)])
                        P.op("act", lambda h: h.activation(nCi[pc][rs_, cs_], cp[rs_, 1, pr, :], AF.Copy, scale=-1.0), reads=["cp"], writes=[("nCi", pc)])
                    P.op("dve", lambda h: h.tensor_scalar(ang[:], iot[:], thP[:, pr:pr + 1], None, ALU.mult),
                         reads=["iot"], writes=["ang", "sc_p", ("sinT", pc), ("cosT", pc)], strict=["thP"])
                    sincos_p(sinT[pc][:], cosT[pc][:])
                    P.op("dve", lambda h: h.tensor_copy(qt_[:, 0:1], qt_[:, 0:1]), reads=["sc_p"], writes=[("sinT", pc), ("cosT", pc)])
                    P.op("pool", lambda h: h.memset(qst[pc][:], 0.0), writes=[("qst", pc)])
                for b in range(NG):
                    bs = slice(b * 512, (b + 1) * 512)
                    yb_ = b % 2
                    for pc in range(4):
                        pr = 4 * j + pc
                        vb = cpb % 2
                        cpb += 1
                        c_, s_t = cosT[pc][:, 0:512], sinT[pc][:, 0:512]
                        P.op("pe", lambda h: h.matmul(psV[vb][0][:, :], Lr[pc][:], uc[us][:, bs], start=True, stop=True),
                             reads=[("Lr", pc), ("uc", us)], writes=[("psV", vb, 0)])
                        P.op("pe", lambda h: h.matmul(psV[vb][1][:, :], Li[pc][:], uc[us][:, bs], start=True, stop=True),
                             reads=[("Li", pc), ("uc", us)], writes=[("psV", vb, 1)])
                        P.op("act", lambda h: h.activation(Vs[vb][0][:], psV[vb][0][:, :], AF.Copy), reads=[("psV", vb, 0)], writes=[("Vs", vb, 0)])
                        P.op("act", lambda h: h.activation(Vs[vb][1][:], psV[vb][1][:, :], AF.Copy), reads=[("psV", vb, 1)], writes=[("Vs", vb, 1)])
                        P.op("dve", lambda h: h.tensor_tensor(m1[vb][:], Vs[vb][0][:], c_, ALU.mult), reads=[("Vs", vb, 0), ("cosT", pc)], writes=[("m1", vb)])
                        P.op("dve", lambda h: h.tensor_tensor(m2[vb][:], Vs[vb][1][:], s_t, ALU.mult), reads=[("Vs", vb, 1), ("sinT", pc)], writes=[("m2", vb)])
                        P.op("dve", lambda h: h.tensor_tensor(Wr[vb][:], m1[vb][:], m2[vb][:], ALU.add), reads=[("m1", vb), ("m2", vb)], writes=[("Wr", vb)])
                        P.op("dve", lambda h: h.tensor_tensor(m1[vb][:], Vs[vb][1][:], c_, ALU.mult), reads=[("Vs", vb, 1), ("cosT", pc)], writes=[("m1", vb)])
                        P.op("dve", lambda h: h.tensor_tensor(m2[vb][:], Vs[vb][0][:], s_t, ALU.mult), reads=[("Vs", vb, 0), ("sinT", pc)], writes=[("m2", vb)])
                        P.op("dve", lambda h: h.tensor_tensor(Wi[vb][:], m1[vb][:], m2[vb][:], ALU.subtract), reads=[("m1", vb), ("m2", vb)], writes=[("Wi", vb)])
                        rbc = rP[:, pr:pr + 1].broadcast_to([128, 512])
                        P.op("dve", lambda h: h.tensor_tensor_scan(Gr[vb][:], rbc, Wr[vb][:], qst[pc][:, 0:1], ALU.mult, ALU.add),
                             reads=[("Wr", vb), "rP"], writes=[("Gr", vb)], strict=[("qst", pc)])
                        P.op("dve", lambda h: h.tensor_tensor_scan(Gi[vb][:], rbc, Wi[vb][:], qst[pc][:, 1:2], ALU.mult, ALU.add),
                             reads=[("Wi", vb), "rP"], writes=[("Gi", vb)], strict=[("qst", pc)])
                        C5, S5 = cosT[pc][:, 512:513], sinT[pc][:, 512:513]
                        P.op("dve", lambda h: h.tensor_tensor(qt_[:, 0:1], Gi[vb][:, 511:512], S5, ALU.mult), reads=[("Gi", vb), ("sinT", pc)], writes=["qt_"])
                        P.op("dve", lambda h: h.tensor_tensor(qt_[:, 1:2], Gr[vb][:, 511:512], S5, ALU.mult), reads=[("Gr", vb), ("sinT", pc)], writes=["qt_"])
                        P.op("dve", lambda h: h.tensor_tensor(qst[pc][:, 0:1], Gr[vb][:, 511:512], C5, ALU.mult), reads=[("Gr", vb), ("cosT", pc)], writes=[("qst", pc)])
                        P.op("dve", lambda h: h.tensor_tensor(qst[pc][:, 1:2], Gi[vb][:, 511:512], C5, ALU.mult), reads=[("Gi", vb), ("cosT", pc)], writes=[("qst", pc)])
                        P.op("dve", lambda h: h.tensor_tensor(qst[pc][:, 0:1], qst[pc][:, 0:1], qt_[:, 0:1], ALU.subtract), reads=[("qst", pc), "qt_"], writes=[("qst", pc)])
                        P.op("dve", lambda h: h.tensor_tensor(qst[pc][:, 1:2], qst[pc][:, 1:2], qt_[:, 1:2], ALU.add), reads=[("qst", pc), "qt_"], writes=[("qst", pc)])
                        P.op("dve", lambda h: h.tensor_tensor(Pp[vb][0][:], Gr[vb][:], c_, ALU.mult), reads=[("Gr", vb), ("cosT", pc)], writes=[("Pp", vb, 0)])
                        P.op("dve", lambda h: h.tensor_tensor(Pp[vb][1][:], Gi[vb][:], c_, ALU.mult), reads=[("Gi", vb), ("cosT", pc)], writes=[("Pp", vb, 1)])
                        P.op("dve", lambda h: h.tensor_tensor(Pp[vb][2][:], Gi[vb][:], s_t, ALU.mult), reads=[("Gi", vb), ("sinT", pc)], writes=[("Pp", vb, 2)])
                        P.op("dve", lambda h: h.tensor_tensor(Pp[vb][3][:], Gr[vb][:], s_t, ALU.mult), reads=[("Gr", vb), ("sinT", pc)], writes=[("Pp", vb, 3)])
                        for a, (wt, wn) in enumerate(((Cr, "Cr"), (nCi, "nCi"), (nCr, "nCr"), (nCi, "nCi"))):
                            P.op("pe", lambda h: h.matmul(psY[yb_][:, :], wt[pc][:], Pp[vb][a][:], start=(pc == 0 and a == 0),
                                                          stop=(pc == 3 and a == 3)),
                                 reads=[(wn, pc), ("Pp", vb, a)], writes=[("psY", yb_)], sig=(a == 3))
                    P.op("dve", lambda h: h.scalar_tensor_tensor(yv[yb_][:], uc[us][:, bs], d1[:, j:j + 1], psY[yb_][:, :], ALU.mult, ALU.add),
                         reads=[("uc", us), "d1", ("psY", yb_)], writes=[("yv", yb_)])
                    P.op("act", lambda h: h.activation(ge1[yb_][:], yv[yb_][:], AF.Square), reads=[("yv", yb_)], writes=[("ge1", yb_)])
                    P.op("pool", lambda h: h.tensor_scalar(ge1[yb_][:], ge1[yb_][:], 0.044715, 1.0, ALU.mult, ALU.add), reads=[("ge1", yb_)], writes=[("ge1", yb_)])
                    P.op("pool", lambda h: h.tensor_tensor(ge2[yb_][:], ge1[yb_][:], yv[yb_][:], ALU.mult), reads=[("ge1", yb_), ("yv", yb_)], writes=[("ge2", yb_)])
                    P.op("act", lambda h: h.activation(ge2[yb_][:], ge2[yb_][:], AF.Sigmoid, scale=1.5957691216057308), reads=[("ge2", yb_)], writes=[("ge2", yb_)])
                    P.op("pool", lambda h: h.tensor_tensor(yo[yb_][:], ge2[yb_][:], yv[yb_][:], ALU.mult), reads=[("ge2", yb_), ("yv", yb_)], writes=[("yo", yb_)])
                    P.dma("sp", self.ygT[j * 128:(j + 1) * 128, bs], yo[yb_][:], reads=[("yo", yb_)], writes=[("ygT", j, b)])

    def stage_G1(self):
        P, nc = self.P, self.nc
        ygv = self.ygT.rearrange("(k p) t -> p k t", p=128)
        with self.stage():
            xTb = self.sb("xTb", [128, 16, T], BF16)
            for k in range(16):
                P.dma("sp", xTb[:, k, :], ygv[:, k, :], writes=[("xTb", k)])
            wa = [self.sb(f"wa{i}", [128, 16, 128], BF16) for i in range(2)]
            wb = [self.sb(f"wb{i}", [128, 16, 128], BF16) for i in range(2)]
            stga = [self.sb(f"stga{i}", [128, 16, 128], F32) for i in range(2)]
            stgb = [self.sb(f"stgb{i}", [128, 16, 128], F32) for i in range(2)]
            psA = [[self.ps(f"psA{i}{a}", [128, 512]) for a in "ab"] for i in range(2)]
            sgt = [self.sb(f"sgt{i}", [128, 512], BF16) for i in range(2)]
            sgb = [self.sb(f"sgb{i}", [128, 512], F32) for i in range(2)]
            tt_ = [self.sb(f"tt{i}", [128, 512], F32) for i in range(2)]
            ob = [self.sb(f"ob{i}", [128, 512], BF16) for i in range(2)]
            c2 = 0
            for i in range(16):
                slot = i % 2
                P.dma("sp", stga[slot][:], self.w_glu[i], writes=[("stga", slot)])
                P.op("pool", lambda h: h.tensor_copy(wa[slot][:], stga[slot][:]), reads=[("stga", slot)], writes=[("wa", slot)])
                P.dma("sp", stgb[slot][:], self.w_glu[16 + i], writes=[("stgb", slot)])
                P.op("pool", lambda h: h.tensor_copy(wb[slot][:], stgb[slot][:]), reads=[("stgb", slot)], writes=[("wb", slot)])
                for g in range(NG):
                    b = c2 % 2
                    c2 += 1
                    gs_ = slice(g * 512, (g + 1) * 512)
                    P.dma("sp", sgt[b][:], self.sg1T[i * 128:(i + 1) * 128, gs_], writes=[("sgt", b)])
                    for a, wt, wn in ((0, wa, "wa"), (1, wb, "wb")):
                        for k in range(16):
                            P.op("pe", lambda h: h.matmul(psA[b][a][:, :], wt[slot][:, k, :], xTb[:, k, gs_], start=(k == 0), stop=(k == 15)),
                                 reads=[(wn, slot), ("xTb", k)], writes=[("psA", b, a)], sig=(k == 15))
                    P.op("act", lambda h: h.activation(sgb[b][:], psA[b][1][:, :], AF.Sigmoid), reads=[("psA", b, 1)], writes=[("sgb", b)])
                    P.op("dve", lambda h: h.tensor_tensor(tt_[b][:], psA[b][0][:, :], sgb[b][:], ALU.mult), reads=[("psA", b, 0), ("sgb", b)], writes=[("tt", b)])
                    P.op("pool", lambda h: h.tensor_tensor(ob[b][:], tt_[b][:], sgt[b][:], ALU.mult), reads=[("tt", b), ("sgt", b)], writes=[("ob", b)])
                    P.dma("sp", self.y2T[i * 128:(i + 1) * 128, gs_], ob[b][:], reads=[("ob", b)], writes=[("y2T", i, g)])

    def stage_G2(self):
        self.outproj_ln([(self.y2T, 0, 16, False)], self.w_out1, 16, self.x1, 1, self.out)

    def build(self, stages="AaBCDEFGH"):
        self.declare()
        for ch, fn in (("A", self.stage_A), ("a", self.stage_A2), ("B", self.stage_B), ("C", self.stage_C),
                       ("D", self.stage_D), ("E", self.stage_E), ("F", self.stage_F), ("G", self.stage_G1),
                       ("H", self.stage_G2)):
            if ch in stages:
                fn()
        return self.nc


def _rope_tables():
    half = 16
    inv_freq = (500000.0 ** (-(np.arange(half, dtype=np.float32) * 2.0 / 32))).astype(np.float32)
    pos = np.arange(T, dtype=np.float32)
    ang = (pos[None, :] * inv_freq[:, None]).astype(np.float32)
    c = np.cos(ang).astype(np.float32)
    s = np.sin(ang).astype(np.float32)
    return np.ascontiguousarray(np.concatenate([c, c], 0)), np.ascontiguousarray(np.concatenate([-s, s], 0))


def _constants():
    cosT, sinS = _rope_tables()
    pm = np.zeros((32, 32), np.float32)
    for m in range(32):
        pm[(m + 16) % 32, m] = 1.0
    sel = np.zeros((32, 32, 128), np.float32)
    for h in range(32):
        sel[h, h, :] = 1.0
    triu = np.triu(np.ones((128, 128), np.float32))
    maskg = np.ascontiguousarray(np.concatenate([triu, np.ones((128, 128), np.float32), triu], 1))
    vb = np.zeros((128, NT, NCH), np.float32)
    for qt in range(NT):
        vb[:, qt, qt // 2:] = -1e30
    rowmask = np.zeros((128, 8), np.float32)
    for p in range(128):
        rowmask[p, p // 16] = 1.0
    return dict(cosT=cosT, sinS=sinS, pm32=pm, sel=sel, triu=triu, maskg=maskg, ident=np.eye(128, dtype=np.float32),
                vbias=np.ascontiguousarray(vb.reshape(128, NT * NCH)), rowmask=rowmask,
                iota513=np.arange(513, dtype=np.float32).reshape(1, 513))


def _shared_inputs(inp):
    f = lambda a: np.ascontiguousarray(np.asarray(a, dtype=np.float32))
    d = {}
    w0 = np.asarray(inp["in0_w"][0], dtype=np.float32)
    def tile_cols(w, c0, m):
        return w[:, c0:c0 + m].reshape(16, 128, m).transpose(1, 0, 2)
    cols_a = [C_XBC + 128 * i for i in range(24)] + [C_Z + 128 * i for i in range(16)] + \
             [C_Q + 128 * i for i in range(16)] + [C_K + 128 * i for i in range(16)]
    d["w0a"] = f(np.stack([tile_cols(w0, c, 128) for c in cols_a], 0))
    d["w0b"] = f(np.stack([tile_cols(w0, C_V + 512 * i, 512) for i in range(4)] +
                          [tile_cols(w0, C_G + 512 * i, 512) for i in range(4)], 0))
    d["w0dt"] = f(tile_cols(w0, C_DT, 32))
    cwv = np.asarray(inp["conv_w"][0])
    d["cw"] = f(cwv.T.reshape(24, 128, 4).transpose(1, 0, 2))
    d["cb"] = f(np.asarray(inp["conv_b"][0]).reshape(24, 128).T)
    d["dtb"] = f(inp["dt_bias"])
    d["alog"] = f(inp["a_log"])
    dsk = np.asarray(inp["ssd_d"][0])
    d["ssd_dp"] = f(np.repeat(dsk.reshape(16, 2), 64, axis=1).T)
    d["normg"] = f(np.asarray(inp["ssd_norm_g"][0]).reshape(16, 128).T)
    d["out0_w"] = f(inp["out0_w"][0])
    d["ln_g"] = f(inp["ln_g"])
    d["ln_b"] = f(inp["ln_b"])
    w1 = np.asarray(inp["in1_w"][0], dtype=np.float32)
    d["w1t"] = f(np.stack([tile_cols(w1, 128 * i, 128) for i in range(32)], 0))
    wg = np.asarray(inp["glu_w"][0], dtype=np.float32)
    d["wgt"] = f(np.stack([tile_cols(wg, 128 * i, 128) for i in range(32)], 0))
    d["out1_w"] = f(inp["out1_w"][0])
    lre, lim = np.asarray(inp["s5_lam_re"][0]), np.asarray(inp["s5_lam_im"][0])
    ldt = np.asarray(inp["s5_log_dt"][0])
    bre, bim = np.asarray(inp["s5_b_re"][0]), np.asarray(inp["s5_b_im"][0])
    cre, cim = np.asarray(inp["s5_c_re"][0]), np.asarray(inp["s5_c_im"][0])
    ldt_n = np.repeat(ldt[:, None], 64, axis=1)
    pl = np.stack([a.reshape(64, 2, 64).transpose(1, 2, 0).reshape(128, 64) for a in (lre, lim, ldt_n)], 1)
    d["s5_pl"] = f(pl)
    def wl_gn(a):
        return np.repeat(a.reshape(16, 8, 1, 64), 16, axis=2).transpose(1, 2, 0, 3).reshape(128, 16, 64)
    def wl_gnm(a):
        return a.reshape(16, 8, 64, 16).transpose(1, 3, 0, 2).reshape(128, 16, 64)
    d["s5_wl"] = f(np.stack([wl_gn(lre), wl_gn(lim), wl_gn(ldt_n), wl_gnm(bre), wl_gnm(bim)], 1))
    def cp_(a):
        return a.reshape(64, 2, 16, 64).transpose(1, 3, 0, 2).reshape(128, 64, 16)
    d["s5_cp"] = f(np.stack([cp_(cre), cp_(cim)], 1))
    d["s5_d1"] = f(np.asarray(inp["s5_d"][0]).reshape(16, 128).T)
    d.update(_constants())
    return d


N_CORES = 4


def kernel(**inputs):
    x = np.asarray(inputs["x"], dtype=np.float32)
    shared = _shared_inputs(inputs)
    nc = bass.Bass("TRN2", target_bir_lowering=False)
    mk = MK(nc)
    mk.build("AaBCDEFGH")
    in_maps = []
    for b in range(N_CORES):
        m = dict(shared)
        m["x"] = np.ascontiguousarray(x[b])
        m["xT"] = np.ascontiguousarray(x[b].T)
        in_maps.append(m)
    res = run_bass_kernel_spmd(nc, in_maps, core_ids=list(range(N_CORES)))
    return np.stack([np.asarray(res.results[b]["out"], dtype=np.float32) for b in range(N_CORES)], 0)
```

```python
import math
from contextlib import contextmanager, ExitStack
import numpy as np
import concourse.bass as bass
import concourse.mybir as mybir
from concourse.bass_utils import run_bass_kernel_spmd

F32 = mybir.dt.float32
BF16 = mybir.dt.bfloat16
AF = mybir.ActivationFunctionType
ALU = mybir.AluOpType
AX = mybir.AxisListType

T = 4096
NT = T // 128
NG = T // 512
NCH = T // 256
D = 2048
IN0 = 13344
C_Z, C_XBC, C_DT, C_Q, C_K, C_V, C_G = 0, 2048, 5120, 5152, 7200, 9248, 11296
ALPHA = 4.0 ** 0.25
SEM_CAP = 30000


class Ev:
    __slots__ = ("eng", "sem", "val")

    def __init__(self, eng, sem, val):
        self.eng = eng
        self.sem = sem
        self.val = val


class Prog:
    def __init__(self, nc, n_dma_sems=10):
        self.nc = nc
        self.h = {"pe": nc.tensor, "act": nc.scalar, "dve": nc.vector, "pool": nc.gpsimd, "sp": nc.sync}
        self.sem = {}
        self.cnt = {}
        self.gen = {}
        self.waited = {e: {} for e in self.h}
        self._cms = []
        for e in self.h:
            self._new_sem(e)
        self.last_w = {}
        self.readers = {}
        self.bank_last = {}
        self.bankmap = {}
        self.dma_sems = {}
        self.dma_next = {}
        for e in ("sp", "pool", "act"):
            self.dma_sems[e] = [[self._alloc(f"dma_{e}_{i}"), 0] for i in range(n_dma_sems)]
            self.dma_next[e] = 0
        self.n_inst = 0

    def _alloc(self, name):
        cm = self.nc.semaphore(name)
        s = cm.__enter__()
        self._cms.append(cm)
        return s

    def _new_sem(self, e):
        g = self.gen.get(e, -1) + 1
        self.gen[e] = g
        self.sem[e] = self._alloc(f"s_{e}_{g}")
        self.cnt[e] = 0

    def _deps(self, reads, writes, e=None):
        deps = []
        for r in reads:
            ev = self.last_w.get(r)
            if ev is not None:
                deps.append(ev)
        for w in writes:
            ev = self.last_w.get(w)
            if ev is not None:
                deps.append(ev)
        if e in ("act", "dve", "pool"):
            strong = [ev for ev in deps if ev.eng == e]
            self._wait(e, strong, same_engine_ok=False)
        for w in writes:
            deps.extend(self.readers.get(w, ()))
        return deps

    def _wait(self, e, deps, same_engine_ok=True):
        wd = self.waited[e]
        need = {}
        for ev in deps:
            if same_engine_ok and ev.eng == e:
                continue
            k = id(ev.sem)
            if wd.get(k, 0) >= ev.val:
                continue
            if k not in need or need[k].val < ev.val:
                need[k] = ev
        for k, ev in need.items():
            self.h[e].wait_ge(ev.sem, ev.val)
            wd[k] = ev.val
            self.n_inst += 1

    def _commit(self, ev, reads, writes):
        for r in reads:
            lst = self.readers.setdefault(r, [])
            lst[:] = [x for x in lst if x.sem is not ev.sem]
            lst.append(ev)
        for w in writes:
            self.last_w[w] = ev
            self.readers[w] = []

    def op(self, e, fn, reads=(), writes=(), sig=True, banks=(), strict=()):
        if strict:
            sdeps = [self.last_w[r] for r in strict if r in self.last_w]
            self._wait(e, sdeps, same_engine_ok=False)
            reads = list(reads) + list(strict)
        deps = self._deps(reads, writes, e)
        banks = set(banks)
        for r in list(reads) + list(writes):
            nm = r if isinstance(r, str) else r[0]
            if isinstance(nm, str) and (nm.startswith("ps") or nm.startswith("ptr")):
                banks.add(self.bankmap.get(r, r))
        for b in banks:
            for eng, bev in self.bank_last.setdefault(b, {}).items():
                if eng != e:
                    deps.append(bev)
        self._wait(e, deps)
        if sig and self.cnt[e] >= SEM_CAP:
            self._new_sem(e)
        inst = fn(self.h[e])
        self.n_inst += 1
        if sig:
            self.cnt[e] += 1
            inst.then_inc(self.sem[e], 1)
            ev = Ev(e, self.sem[e], self.cnt[e])
        else:
            ev = Ev(e, self.sem[e], self.cnt[e] + 1)
        self._commit(ev, reads, writes)
        for b in banks:
            self.bank_last[b][e] = ev
        return ev

    def dma(self, e, out, in_, reads=(), writes=(), **kw):
        deps = self._deps(reads, writes)
        i = self.dma_next[e]
        self.dma_next[e] = (i + 1) % len(self.dma_sems[e])
        slot = self.dma_sems[e][i]
        if slot[1] > 0:
            deps.append(Ev("dma", slot[0], slot[1]))
        if slot[1] + 16 > SEM_CAP:
            self._wait(e, deps, same_engine_ok=False)
            deps = []
            slot[0] = self._alloc(f"dma_{e}_{i}_{self.n_inst}")
            slot[1] = 0
        self._wait(e, deps, same_engine_ok=False)
        inst = self.h[e].dma_start(out=out, in_=in_, **kw)
        slot[1] += 16
        inst.then_inc(slot[0], 16)
        self.n_inst += 1
        ev = Ev("dma", slot[0], slot[1])
        self._commit(ev, reads, writes)
        return ev

    def barrier(self):
        evs = [Ev(e, self.sem[e], self.cnt[e]) for e in self.h if self.cnt[e] > 0]
        for e in self.dma_sems:
            for s, v in self.dma_sems[e]:
                if v > 0:
                    evs.append(Ev("dma", s, v))
        for e in self.h:
            self._wait(e, evs, same_engine_ok=True)
        self.last_w.clear()
        self.readers.clear()
        self.bank_last.clear()


class MK:
    def __init__(self, nc, dbg=(), feed=()):
        self._lazy = {}
        self.feed = set(feed)
        self.nc = nc
        self.P = Prog(nc)
        self.dbg = set(dbg)
        self._es = None
        self._sid = 0
        self.dr = {}

    @contextmanager
    def stage(self):
        self._sid += 1
        es = ExitStack()
        self._es = es
        try:
            yield
        finally:
            self.P.barrier()
            es.close()

    def sb(self, name, shape, dt):
        return self._es.enter_context(self.nc.sbuf_tensor(f"{name}_s{self._sid}", shape, dt))

    def ps(self, name, shape, dt=F32):
        return self._es.enter_context(self.nc.psum_tensor(f"{name}_s{self._sid}", shape, dt))

    def din(self, name, shape, dt=F32):
        t = self.nc.dram_tensor(name, list(shape), dt, kind="ExternalInput").ap()
        self.dr[name] = t
        return t

    def dout(self, name, shape, dt=F32):
        t = self.nc.dram_tensor(name, list(shape), dt, kind="ExternalOutput").ap()
        self.dr[name] = t
        return t

    def scratch(self, name, shape, dt):
        kind = "ExternalOutput" if name in self.dbg else "Internal"
        t = self.nc.dram_tensor(name, list(shape), dt, kind=kind).ap()
        self.dr[name] = t
        return t

    def __getattr__(self, attr):
        lz = self.__dict__.get("_lazy", {})
        if attr in lz:
            kind, name, shape, dt = lz[attr]
            if kind == "in" or (kind == "scratch" and name in self.feed):
                t = self.din(name, shape, dt)
            elif kind == "out":
                t = self.dout(name, shape, dt)
            else:
                t = self.scratch(name, shape, dt)
            self.__dict__[attr] = t
            return t
        raise AttributeError(attr)

    def declare(self):
        self._lazy["x"] = ("in", "x", [T, D], F32)
        self._lazy["xT"] = ("in", "xT", [D, T], F32)
        self._lazy["w0a"] = ("in", "w0a", [72, 128, 16, 128], F32)
        self._lazy["w0b"] = ("in", "w0b", [8, 128, 16, 512], F32)
        self._lazy["w0dt"] = ("in", "w0dt", [128, 16, 32], F32)
        self._lazy["cw"] = ("in", "cw", [128, 24, 4], F32)
        self._lazy["cb"] = ("in", "cb", [128, 24], F32)
        self._lazy["dtb"] = ("in", "dtb", [1, 32], F32)
        self._lazy["alog"] = ("in", "alog", [1, 32], F32)
        self._lazy["ssd_dp"] = ("in", "ssd_dp", [128, 16], F32)
        self._lazy["normg"] = ("in", "normg", [128, 16], F32)
        self._lazy["cosT"] = ("in", "cosT", [32, T], F32)
        self._lazy["sinS"] = ("in", "sinS", [32, T], F32)
        self._lazy["pm32"] = ("in", "pm32", [32, 32], F32)
        self._lazy["sel"] = ("in", "sel", [32, 32, 128], F32)
        self._lazy["triu"] = ("in", "triu", [128, 128], F32)
        self._lazy["maskg"] = ("in", "maskg", [128, 384], F32)
        self._lazy["ident"] = ("in", "ident", [128, 128], F32)
        self._lazy["vbias"] = ("in", "vbias", [128, NT * NCH], F32)
        self._lazy["w_out0"] = ("in", "out0_w", [2 * D, D], F32)
        self._lazy["ln_g"] = ("in", "ln_g", [2, D], F32)
        self._lazy["ln_b"] = ("in", "ln_b", [2, D], F32)
        self._lazy["w_in1"] = ("in", "w1t", [32, 128, 16, 128], F32)
        self._lazy["w_glu"] = ("in", "wgt", [32, 128, 16, 128], F32)
        self._lazy["w_out1"] = ("in", "out1_w", [D, D], F32)
        self._lazy["out"] = ("out", "out", [T, D], F32)
        self._lazy["s5_pl"] = ("in", "s5_pl", [128, 3, 64], F32)
        self._lazy["s5_wl"] = ("in", "s5_wl", [128, 5, 16, 64], F32)
        self._lazy["s5_cp"] = ("in", "s5_cp", [128, 2, 64, 16], F32)
        self._lazy["s5_d1"] = ("in", "s5_d1", [128, 16], F32)
        self._lazy["rowmask"] = ("in", "rowmask", [128, 8], F32)
        self._lazy["iota513"] = ("in", "iota513", [1, 513], F32)
        self._lazy["xsT"] = ("scratch", "xsT", [D, T], BF16)
        self._lazy["BT"] = ("scratch", "BT", [512, T], BF16)
        self._lazy["CT"] = ("scratch", "CT", [512, T], BF16)
        self._lazy["zs"] = ("scratch", "zs", [D, T], BF16)
        self._lazy["qT16"] = ("scratch", "qT16", [16, 128, T], BF16)
        self._lazy["kT16"] = ("scratch", "kT16", [16, 128, T], BF16)
        self._lazy["qT32"] = ("scratch", "qT32", [16, 128, T], F32)
        self._lazy["kmean"] = ("scratch", "kmean", [128, 16, NCH], F32)
        self._lazy["V1"] = ("scratch", "V1", [T, 16, 129], BF16)
        self._lazy["gs"] = ("scratch", "gs", [T, D], BF16)
        self._lazy["dtk"] = ("scratch", "dtk", [128, NT, 32], F32)
        self._lazy["yaT"] = ("scratch", "yaT", [D, T], BF16)
        self._lazy["rstd_s"] = ("scratch", "rstd_s", [128, NT], F32)
        self._lazy["ybT"] = ("scratch", "ybT", [D, T], BF16)
        self._lazy["x1"] = ("scratch", "x1", [T, D], F32)
        self._lazy["x1T"] = ("scratch", "x1T", [D, T], BF16)
        self._lazy["uT"] = ("scratch", "uT", [D, T], BF16)
        self._lazy["sg1T"] = ("scratch", "sg1T", [D, T], BF16)
        self._lazy["ygT"] = ("scratch", "ygT", [D, T], BF16)
        self._lazy["y2T"] = ("scratch", "y2T", [D, T], BF16)

    def stage_A(self):
        P, nc = self.P, self.nc
        blk_of = {}
        for i in range(24):
            blk_of[C_XBC + 128 * i] = i
        for i in range(16):
            blk_of[C_Z + 128 * i] = 24 + i
            blk_of[C_Q + 128 * i] = 40 + i
            blk_of[C_K + 128 * i] = 56 + i
        with self.stage():
            xTb = self.sb("xTb", [128, 16, T], BF16)
            for k in range(16):
                P.dma("pool", xTb[:, k, :], self.xT[k * 128:(k + 1) * 128, :], writes=[("xTb", k)])
            wsl = [self.sb(f"wsl{i}", [128, 16, 128], BF16) for i in range(2)]
            psA = [self.ps(f"psA{i}", [128, 512]) for i in range(4)]
            psw = [self.ps(f"psw{i}", [32, 512]) for i in range(2)]
            raw = [self.sb(f"raw{i}", [128, 515], F32) for i in range(2)]
            acc = [self.sb(f"acc{i}", [128, 512], F32) for i in range(2)]
            ob = [self.sb(f"ob{i}", [128, 512], BF16) for i in range(3)]
            qf = [self.sb(f"qf{i}", [128, 512], F32) for i in range(2)]
            tmp32 = [self.sb(f"tmp32{i}", [32, 512], F32) for i in range(2)]
            cw = self.sb("cw", [128, 24, 4], F32)
            cb = self.sb("cb", [128, 24], F32)
            cosT = self.sb("cosT", [32, T], F32)
            sinS = self.sb("sinS", [32, T], F32)
            pm = self.sb("pm", [32, 32], F32)
            kms = self.sb("kms", [128, 16, NCH], F32)
            P.dma("sp", cw[:], self.cw, writes=["cw"])
            P.dma("sp", cb[:], self.cb, writes=["cb"])
            P.dma("sp", cosT[:], self.cosT, writes=["cosT"])
            P.dma("sp", sinS[:], self.sinS, writes=["sinS"])
            P.dma("sp", pm[:], self.pm32, writes=["pm"])

            ctr = {"blk": 0, "grp": 0, "ob": 0, "qf": 0}

            stg = [self.sb(f"stg{i}", [128, 16, 128], F32) for i in range(2)]

            order_a = [C_XBC + 128 * i for i in range(24)] + [C_Z + 128 * i for i in range(16)] + \
                      [C_Q + 128 * i for i in range(16)] + [C_K + 128 * i for i in range(16)]

            def issue_w(jb):
                slot = jb % 2
                P.dma("sp", stg[slot][:, :, :], self.w0a[blk_of[order_a[jb]]], writes=[("stg", slot)])
                P.op("pool", lambda h: h.tensor_copy(wsl[slot][:, :, :], stg[slot][:, :, :]),
                     reads=[("stg", slot)], writes=[("w", slot)])

            def load_w(c0, M):
                jb = ctr["blk"]
                ctr["blk"] += 1
                assert order_a[jb] == c0
                if jb == 0:
                    issue_w(0)
                if jb + 1 < len(order_a):
                    issue_w(jb + 1)
                return jb % 2

            def mm_group(slot, M, g):
                b = ctr["grp"] % 4
                ctr["grp"] += 1
                for k in range(16):
                    P.op("pe", lambda h, k=k: h.matmul(psA[b][:M, :], wsl[slot][:, k, :M],
                                                       xTb[:, k, g * 512:(g + 1) * 512],
                                                       start=(k == 0), stop=(k == 15)),
                         reads=[("w", slot), ("xTb", k)], writes=[("psA", b)], sig=(k == 15))
                return b

            def next_ob():
                i = ctr["ob"] % 3
                ctr["ob"] += 1
                return i

            for i in range(24):
                slot = load_w(C_XBC + 128 * i, 128)
                P.op("pool", lambda h: h.memset(raw[0][:, 0:3], 0.0), writes=[("rawh", 0)])
                for g in range(NG):
                    b = mm_group(slot, 128, g)
                    r = g % 2
                    a = g % 2
                    P.op("act", lambda h: h.activation(raw[r][:, 3:515], psA[b][:, :], AF.Copy),
                         reads=[("psA", b)], writes=[("rawm", r)])
                    P.op("pool", lambda h: h.tensor_copy(raw[1 - r][:, 0:3], raw[r][:, 512:515]),
                         reads=[("rawm", r)], writes=[("rawh", 1 - r)])
                    P.op("dve", lambda h: h.tensor_scalar(acc[a][:, :], raw[r][:, 3:515], cw[:, i, 3:4], cb[:, i:i + 1],
                                                          ALU.mult, ALU.add),
                         reads=[("rawm", r), "cw", "cb"], writes=[("acc", a)])
                    for j in (2, 1, 0):
                        P.op("dve", lambda h, j=j: h.scalar_tensor_tensor(acc[a][:, :], raw[r][:, j:j + 512], cw[:, i, j:j + 1],
                                                                          acc[a][:, :], ALU.mult, ALU.add),
                             reads=[("rawm", r), ("rawh", r), "cw"], writes=[("acc", a)])
                    o = next_ob()
                    P.op("act", lambda h: h.activation(ob[o][:, :], acc[a][:, :], AF.Silu),
                         reads=[("acc", a)], writes=[("ob", o)])
                    if i < 16:
                        dst = self.xsT[i * 128:(i + 1) * 128, g * 512:(g + 1) * 512]
                    elif i < 20:
                        dst = self.BT[(i - 16) * 128:(i - 15) * 128, g * 512:(g + 1) * 512]
                    else:
                        dst = self.CT[(i - 20) * 128:(i - 19) * 128, g * 512:(g + 1) * 512]
                    P.dma("sp", dst, ob[o][:, :], reads=[("ob", o)], writes=[("xbc_out", i, g)])
            for i in range(16):
                slot = load_w(C_Z + 128 * i, 128)
                for g in range(NG):
                    b = mm_group(slot, 128, g)
                    o = next_ob()
                    P.op("act", lambda h: h.activation(ob[o][:, :], psA[b][:, :], AF.Silu),
                         reads=[("psA", b)], writes=[("ob", o)])
                    P.dma("sp", self.zs[i * 128:(i + 1) * 128, g * 512:(g + 1) * 512], ob[o][:, :],
                          reads=[("ob", o)], writes=[("zs", i, g)])
            for hq in range(32):
                is_q = hq < 16
                hd = hq % 16
                slot = load_w((C_Q if is_q else C_K) + 128 * hd, 128)
                for g in range(NG):
                    b = mm_group(slot, 128, g)
                    f = ctr["qf"] % 2
                    ctr["qf"] += 1
                    gs_ = slice(g * 512, (g + 1) * 512)
                    P.op("act", lambda h: h.activation(qf[f][:, :], psA[b][:, :], AF.Copy),
                         reads=[("psA", b)], writes=[("qf", f)])
                    P.op("pe", lambda h: h.matmul(psw[f][:, :], pm[:, :], qf[f][0:32, :], start=True, stop=True),
                         reads=[("qf", f), "pm"], writes=[("psw", f)])
                    P.op("dve", lambda h: h.tensor_tensor(tmp32[f][:, :], psw[f][:, :], sinS[:, gs_], ALU.mult),
                         reads=[("psw", f), "sinS"], writes=[("tmp32", f)])
                    P.op("dve", lambda h: h.tensor_tensor(qf[f][0:32, :], qf[f][0:32, :], cosT[:, gs_], ALU.mult),
                         reads=[("qf", f), "cosT"], writes=[("qf", f)])
                    P.op("dve", lambda h: h.tensor_tensor(qf[f][0:32, :], qf[f][0:32, :], tmp32[f][:, :], ALU.add),
                         reads=[("qf", f), ("tmp32", f)], writes=[("qf", f)])
                    o = next_ob()
                    P.op("act", lambda h: h.activation(ob[o][:, :], qf[f][:, :], AF.Copy),
                         reads=[("qf", f)], writes=[("ob", o)])
                    if is_q:
                        P.dma("sp", self.qT16[hd, :, gs_], ob[o][:, :], reads=[("ob", o)], writes=[("q16", hd, g)])
                        P.dma("sp", self.qT32[hd, :, gs_], qf[f][:, :], reads=[("qf", f)], writes=[("q32", hd, g)])
                    else:
                        P.dma("sp", self.kT16[hd, :, gs_], ob[o][:, :], reads=[("ob", o)], writes=[("k16", hd, g)])
                        P.op("dve", lambda h: h.tensor_reduce(kms[:, hd, 2 * g:2 * g + 2],
                                                              qf[f][:, :].rearrange("p (a b) -> p a b", b=256),
                                                              AX.X, ALU.add),
                             reads=[("qf", f)], writes=["kms"])
            P.op("dve", lambda h: h.tensor_scalar(kms[:, :, :], kms[:, :, :], 1.0 / 256.0, None, ALU.mult),
                 reads=["kms"], writes=["kms"])
            P.dma("sp", self.kmean, kms[:, :, :], reads=["kms"], writes=["kmean"])

    def stage_A2(self):
        P, nc = self.P, self.nc
        with self.stage():
            xTb = self.sb("xTb", [128, 16, T], BF16)
            for k in range(16):
                P.dma("pool", xTb[:, k, :], self.xT[k * 128:(k + 1) * 128, :], writes=[("xTb", k)])
            wb = [self.sb(f"wb{i}", [128, 16, 512], BF16) for i in range(2)]
            psA = [self.ps(f"psA{i}", [128, 512]) for i in range(4)]
            ob = [self.sb(f"ob{i}", [128, 512], BF16) for i in range(3)]
            vt = [self.sb(f"vt{i}", [128, 4, 129], BF16) for i in range(3)]
            dtb = self.sb("dtb", [128, 32], F32)
            dtt = self.sb("dtt", [128, NT, 32], F32)
            P.dma("sp", dtb[:], self.dtb.partition_broadcast(128), writes=["dtb"])
            for i in range(3):
                P.op("pool", lambda h, i=i: h.memset(vt[i][:, :, :], 1.0), writes=[("vt", i)])
            ctr = {"blk": 0, "grp": 0, "ob": 0}

            stg = [self.sb(f"stg{i}", [128, 4, 512], F32) for i in range(2)]
            cst = {"n": 0}

            order_b = [(C_DT, 32)] + [(C_V + 512 * i, 512) for i in range(4)] + [(C_G + 512 * i, 512) for i in range(4)]

            def issue_w(jb):
                c0, M = order_b[jb]
                slot = jb % 2
                for q in range(4):
                    ss = cst["n"] % 2
                    cst["n"] += 1
                    src = self.w0dt[:, 4 * q:4 * q + 4, :] if c0 == C_DT else \
                        self.w0b[((c0 - C_V) // 512) if c0 < C_G else (4 + (c0 - C_G) // 512), :, 4 * q:4 * q + 4, :]
                    P.dma("sp", stg[ss][:, :, :M], src, writes=[("stg", ss)])
                    P.op("pool", lambda h: h.tensor_copy(wb[slot][:, 4 * q:4 * q + 4, :M], stg[ss][:, :, :M]),
                         reads=[("stg", ss)], writes=[("w", slot)])

            def load_w(c0, M):
                jb = ctr["blk"]
                ctr["blk"] += 1
                assert order_b[jb] == (c0, M)
                if jb == 0:
                    issue_w(0)
                if jb + 1 < len(order_b):
                    issue_w(jb + 1)
                return jb % 2

            slot = load_w(C_DT, 32)
            for tt in range(NT):
                b = tt // 16
                for k in range(16):
                    P.op("pe", lambda h, k=k: h.matmul(psA[b][:, (tt % 16) * 32:(tt % 16) * 32 + 32],
                                                       xTb[:, k, tt * 128:(tt + 1) * 128], wb[slot][:, k, :32],
                                                       start=(k == 0), stop=(k == 15)),
                         reads=[("w", slot), ("xTb", k)], writes=[("psA", b)], sig=(k == 15))
            for b in range(2):
                dv = dtt[:, b * 16:(b + 1) * 16, :]
                P.op("dve", lambda h: h.tensor_tensor(dv, psA[b][:, :].rearrange("p (a c) -> p a c", c=32),
                                                      dtb[:, None, :].broadcast_to([128, 16, 32]), ALU.add),
                     reads=[("psA", b), "dtb"], writes=[("dtt", b)])
                P.op("act", lambda h: h.activation(dv, dv, AF.Exp), reads=[("dtt", b)], writes=[("dtt", b)])
                P.op("act", lambda h: h.activation(dv, dv, AF.Ln, bias=1.0), reads=[("dtt", b)], writes=[("dtt", b)])
            P.dma("sp", self.dtk, dtt[:, :, :], reads=[("dtt", 0), ("dtt", 1)], writes=["dtk"])
            ctr["grp"] = 2
            for fam in ("v", "g"):
                for cg in range(4):
                    slot = load_w((C_V if fam == "v" else C_G) + 512 * cg, 512)
                    for tt in range(NT):
                        b = ctr["grp"] % 4
                        ctr["grp"] += 1
                        for k in range(16):
                            P.op("pe", lambda h, k=k: h.matmul(psA[b][:, :], xTb[:, k, tt * 128:(tt + 1) * 128],
                                                               wb[slot][:, k, :], start=(k == 0), stop=(k == 15)),
                                 reads=[("w", slot), ("xTb", k)], writes=[("psA", b)], sig=(k == 15))
                        o = ctr["ob"] % 3
                        ctr["ob"] += 1
                        if fam == "v":
                            P.op("act", lambda h: h.activation(vt[o][:, :, 0:128],
                                                               psA[b][:, :].rearrange("p (a c) -> p a c", c=128), AF.Copy),
                                 reads=[("psA", b)], writes=[("vt", o)])
                            P.dma("sp", self.V1[tt * 128:(tt + 1) * 128, 4 * cg:4 * cg + 4, :], vt[o][:, :, :],
                                  reads=[("vt", o)], writes=[("V1", tt, cg)])
                        else:
                            P.op("act", lambda h: h.activation(ob[o][:, :], psA[b][:, :], AF.Silu),
                                 reads=[("psA", b)], writes=[("ob", o)])
                            P.dma("sp", self.gs[tt * 128:(tt + 1) * 128, cg * 512:(cg + 1) * 512], ob[o][:, :],
                                  reads=[("ob", o)], writes=[("gs", tt, cg)])

    def stage_B(self):
        P, nc = self.P, self.nc
        xsTv = self.xsT.rearrange("(k p) t -> p k t", p=128)
        zsv = self.zs.rearrange("(k p) t -> p k t", p=128)
        BTv = self.BT.rearrange("(k p) t -> p k t", p=128)
        CTv = self.CT.rearrange("(k p) t -> p k t", p=128)
        with self.stage():
            sb = self.sb
            pb = [self.ps(f"pb{i}", [128, 512]) for i in range(6)]
            ptrs = [self.ps(f"ptr{i}", [128, 1024], BF16) for i in range(2)]
            ps_small, ps_T = pb[0][:, 0:96], pb[0][0:32, 128:384]
            ps_q = pb[0][:, 100:102]
            ps_g = [pb[1][:, 0:384]]
            ps_bc = [pb[2][:, 0:256], pb[3][:, 0:256]]
            ps_y = [pb[4][:, 0:256]]
            ps_st = [pb[5], pb[5]]
            P.bankmap = {"ps_small": "B0", "ps_q": "B0", ("ps_y", 0, 0): "BY", ("ps_y", 0, 1): "BY",
                         ("ps_st", 0): "BST", ("ps_st", 1): "BST"}
            xsc = [sb(f"xsc{i}", [128, 16, 256], BF16) for i in range(2)]
            zsc = [sb(f"zsc{i}", [128, 16, 256], BF16) for i in range(2)]
            btc = [sb(f"btc{i}", [128, 4, 256], BF16) for i in range(2)]
            ctc = [sb(f"ctc{i}", [128, 4, 256], BF16) for i in range(2)]
            dtt = sb("dtt", [128, NT, 32], F32)
            abc = sb("abc", [128, 32], F32)
            sel = sb("sel", [32, 32, 128], F32)
            triu = sb("triu", [128, 128], F32)
            ones = sb("ones", [128, 128], F32)
            r0 = sb("r0", [128, 256], F32)
            maskg = sb("maskg", [128, 384], F32)
            ident = sb("ident", [128, 128], BF16)
            dp = sb("dp", [128, 16], F32)
            ng = sb("ng", [128, 16], F32)
            atok = sb("atok", [128, 2, 32], F32)
            acum = sb("acum", [128, 3, 32], F32)
            acT = sb("acT", [32, 256], F32)
            dte = sb("dte", [128, 2, 32], F32)
            eAt = sb("eAt", [128, 32], F32)
            X = sb("X", [128, 2, 2048], BF16)
            Xd = sb("Xd", [128, 2, 2048], BF16)
            Btok = sb("Btok", [128, 2, 512], BF16)
            Gm = sb("Gm", [128, 4, 384], F32)
            H32 = sb("H32", [128, 32, 64], F32)
            H16 = sb("H16", [128, 32, 64], BF16)
            Dm = [sb(f"Dm{i}", [128, 384], F32) for i in range(2)]
            MT = [sb(f"MT{i}", [128, 384], BF16) for i in range(2)]
            eA = [sb(f"eA{i}", [128, 256], F32) for i in range(2)]
            Ct = [sb(f"Ct{i}", [128, 256], BF16) for i in range(2)]
            yf = [sb(f"yf{i}", [128, 256], F32) for i in range(2)]
            yg = [sb(f"yg{i}", [128, 256], F32) for i in range(2)]
            sq = [sb(f"sq{i}", [128, 256], F32) for i in range(2)]
            ya16 = [sb(f"ya16{i}", [128, 256], BF16) for i in range(2)]
            rs = sb("rs", [128, NT], F32)

            P.dma("sp", dtt[:], self.dtk, writes=["dtt"])
            P.dma("sp", abc[:], self.alog.partition_broadcast(128), writes=["abc"])
            P.dma("sp", sel[:], self.sel, writes=["sel"])
            P.dma("sp", triu[:], self.triu, writes=["triu"])
            P.dma("sp", maskg[:], self.maskg, writes=["maskg"])
            P.dma("sp", r0[:], self.maskg[:, 0:256], writes=["r0"])
            P.dma("pool", ident[:], self.ident, writes=["ident"])
            P.dma("sp", dp[:], self.ssd_dp, writes=["dp"])
            P.dma("sp", ng[:], self.normg, writes=["ng"])
            P.op("pool", lambda h: h.memset(ones[:], 1.0), writes=["ones"])
            P.op("pool", lambda h: h.memset(H32[:], 0.0), writes=["H32"])
            P.op("pool", lambda h: h.memset(H16[:], 0.0), writes=[("H16", g) for g in range(4)])
            P.op("act", lambda h: h.activation(abc[:], abc[:], AF.Exp), reads=["abc"], writes=["abc"])
            P.op("dve", lambda h: h.tensor_scalar(abc[:], abc[:], -1.0, None, ALU.mult), reads=["abc"], writes=["abc"])

            def load_chunk(c):
                s_ = c % 2
                cs = slice(c * 256, (c + 1) * 256)
                P.dma("sp", xsc[s_][:], xsTv[:, :, cs], writes=[("xsc", s_)])
                P.dma("sp", zsc[s_][:], zsv[:, :, cs], writes=[("zsc", s_)])
                P.dma("pool", btc[s_][:], BTv[:, :, cs], writes=[("btc", s_)])
                P.dma("pool", ctc[s_][:], CTv[:, :, cs], writes=[("ctc", s_)])

            bstop = getattr(self, 'bstop', 99)
            if bstop <= 1:
                return
            load_chunk(0)
            hc = 0
            for c in range(getattr(self, 'b_chunks', NCH)):
                s_ = c % 2
                if c + 1 < NCH:
                    load_chunk(c + 1)
                P.op("dve", lambda h: h.tensor_tensor(atok[:], dtt[:, 2 * c:2 * c + 2, :],
                                                      abc[:, None, :].broadcast_to([128, 2, 32]), ALU.mult),
                     reads=["dtt", "abc"], writes=["atok"])
                mm = lambda out, l, r, st, sp, sg: P.op(
                    "pe", lambda h: h.matmul(out, l, r, start=st, stop=sp), reads=["atok", "triu", "ones", "r0"],
                    writes=["ps_small"], sig=sg)
                mm(ps_small[:, 0:32], triu[:], atok[:, 0, :], True, True, False)
                mm(ps_small[:, 32:64], ones[:], atok[:, 0, :], True, False, False)
                mm(ps_small[:, 32:64], triu[:], atok[:, 1, :], False, True, False)
                mm(ps_small[:, 64:96], ones[:], atok[:, 0, :], True, False, False)
                mm(ps_small[:, 64:96], ones[:], atok[:, 1, :], False, True, False)
                mm(ps_T[:, :], atok[:, 0, :], r0[:], True, False, False)
                mm(ps_T[:, 128:256], atok[:, 1, :], triu[:], False, True, True)
                P.op("act", lambda h: h.activation(acum[:].rearrange("p a b -> p (a b)"), ps_small, AF.Copy),
                     reads=["ps_small"], writes=["acum"])
                P.op("act", lambda h: h.activation(acT[:], ps_T, AF.Copy), reads=["ps_small"], writes=["acT"])
                P.op("dve", lambda h: h.tensor_tensor(dte[:], acum[:, 2:3, :].broadcast_to([128, 2, 32]), acum[:, 0:2, :],
                                                      ALU.subtract), reads=["acum"], writes=["dte"])
                P.op("act", lambda h: h.activation(dte[:], dte[:], AF.Exp), reads=["dte"], writes=["dte"])
                P.op("act", lambda h: h.activation(eAt[:], acum[:, 2, :], AF.Exp), reads=["acum"], writes=["eAt"])
                if bstop <= 2:
                    return
                tb = 0
                bsub = getattr(self, 'bsub', 'xdb')
                for j in range(2):
                    for q4 in range(4):
                        half = tb % 2
                        tb += 1
                        for kk in range(4):
                            k = q4 * 4 + kk
                            P.op("pe", lambda h: h.transpose(ptrs[half][:, kk * 128:(kk + 1) * 128],
                                                             xsc[s_][:, k, j * 128:(j + 1) * 128], ident[:]),
                                 reads=[("xsc", s_), "ident"], writes=[("ptr", half)], sig=(kk == 3))
                        hs = slice(8 * q4, 8 * q4 + 8)
                        cs_ = slice(512 * q4, 512 * q4 + 512)
                        P.op("dve", lambda h: h.tensor_tensor(
                            X[:, j, cs_].rearrange("p (a b) -> p a b", b=64),
                            ptrs[half][:, 0:512].rearrange("p (a b) -> p a b", b=64),
                            dtt[:, 2 * c + j, hs].unsqueeze(2).broadcast_to([128, 8, 64]), ALU.mult),
                            reads=[("ptr", half), "dtt"], writes=[("X", j, q4)])
                        if 'd' in bsub:
                          P.op("pool", lambda h: h.tensor_tensor(
                            Xd[:, j, cs_].rearrange("p (a b) -> p a b", b=64),
                            X[:, j, cs_].rearrange("p (a b) -> p a b", b=64),
                            dte[:, j, hs].unsqueeze(2).broadcast_to([128, 8, 64]), ALU.mult),
                            reads=[("X", j, q4), "dte"], writes=[("Xd", j, q4)])
                    if 'b' not in bsub:
                        continue
                    half = tb % 2
                    tb += 1
                    for g in range(4):
                        P.op("pe", lambda h: h.transpose(ptrs[half][:, g * 128:(g + 1) * 128],
                                                         btc[s_][:, g, j * 128:(j + 1) * 128], ident[:]),
                             reads=[("btc", s_), "ident"], writes=[("ptr", half)], sig=(g == 3))
                    P.op("act", lambda h: h.activation(Btok[:, j, :], ptrs[half][:, 0:512], AF.Copy),
                         reads=[("ptr", half)], writes=[("Btok", j)])
                if bstop <= 3:
                    return
                for g in range(4):
                    gb = 0
                    P.op("pe", lambda h: h.matmul(ps_g[gb][:, 0:256], btc[s_][:, g, 0:128], ctc[s_][:, g, :],
                                                  start=True, stop=True),
                         reads=[("btc", s_), ("ctc", s_)], writes=[("ps_g", gb)], sig=False)
                    P.op("pe", lambda h: h.matmul(ps_g[gb][:, 256:384], btc[s_][:, g, 128:256], ctc[s_][:, g, 128:256],
                                                  start=True, stop=True),
                         reads=[("btc", s_), ("ctc", s_)], writes=[("ps_g", gb)])
                    P.op("dve", lambda h: h.tensor_tensor(Gm[:, g, :], ps_g[gb], maskg[:], ALU.mult),
                         reads=[("ps_g", gb), "maskg"], writes=[("Gm", g)])
                if bstop <= 4:
                    return
                def emit_bc(hd_):
                    tt_ = hd_ % 2
                    P.op("pe", lambda h: h.matmul(ps_bc[tt_], sel[:, hd_, :], acT[:], start=True, stop=True),
                         reads=["sel", "acT"], writes=[("ps_bc", tt_)])

                for hd in range(getattr(self, 'b_heads', 32)):
                    g = hd // 8
                    pair = hd // 2
                    hh = hd % 2
                    t_ = hd % 2
                    hcols = slice(hd * 64, (hd + 1) * 64)
                    if hd == 0:
                        emit_bc(0)
                    P.op("dve", lambda h: h.tensor_scalar(Dm[t_][:, 0:256], ps_bc[t_], acum[:, 0, hd:hd + 1], 0.0,
                                                          ALU.subtract, ALU.min),
                         reads=[("ps_bc", t_), "acum"], writes=[("Dm", t_)])
                    P.op("dve", lambda h: h.tensor_scalar(Dm[t_][:, 256:384], ps_bc[t_][:, 128:256], acum[:, 1, hd:hd + 1],
                                                          0.0, ALU.subtract, ALU.min),
                         reads=[("ps_bc", t_), "acum"], writes=[("Dm", t_)])
                    P.op("act", lambda h: h.activation(Dm[t_][:], Dm[t_][:], AF.Exp), reads=[("Dm", t_)], writes=[("Dm", t_)])
                    P.op("dve", lambda h: h.tensor_tensor(MT[t_][:], Dm[t_][:], Gm[:, g, :], ALU.mult),
                         reads=[("Dm", t_), ("Gm", g)], writes=[("MT", t_)])
                    P.op("act", lambda h: h.activation(eA[t_][:], ps_bc[t_], AF.Exp), reads=[("ps_bc", t_)], writes=[("eA", t_)])
                    P.op("pool", lambda h: h.tensor_tensor(Ct[t_][:], ctc[s_][:, g, :], eA[t_][:], ALU.mult),
                         reads=[("ctc", s_), ("eA", t_)], writes=[("Ct", t_)])
                    if hd + 1 < getattr(self, 'b_heads', 32):
                        emit_bc(hd + 1)
                    yb = 0
                    yo = ps_y[yb][hh * 64:(hh + 1) * 64, :]
                    yres = ("ps_y", yb, hh)
                    P.op("pe", lambda h: h.matmul(yo[:, 0:256], X[:, 0, hcols], MT[t_][:, 0:256], start=True, stop=False),
                         reads=[("X", 0, hd // 8), ("MT", t_)], writes=[yres], sig=False)
                    P.op("pe", lambda h: h.matmul(yo[:, 128:256], X[:, 1, hcols], MT[t_][:, 256:384], start=False, stop=False),
                         reads=[("X", 1, hd // 8), ("MT", t_)], writes=[yres], sig=False)
                    P.op("pe", lambda h: h.matmul(yo[:, 0:256], H16[:, hd, :], Ct[t_][:], start=False, stop=True),
                         reads=[("H16", g), ("Ct", t_)], writes=[yres])
                    so = ps_st[g % 2][:, (hd % 8) * 64:(hd % 8 + 1) * 64]
                    P.op("pe", lambda h: h.matmul(so, Btok[:, 0, g * 128:(g + 1) * 128], Xd[:, 0, hcols], start=True, stop=False),
                         reads=[("Btok", 0), ("Xd", 0, hd // 8)], writes=[("ps_st", g % 2)], sig=False)
                    P.op("pe", lambda h: h.matmul(so, Btok[:, 1, g * 128:(g + 1) * 128], Xd[:, 1, hcols], start=False, stop=True),
                         reads=[("Btok", 1), ("Xd", 1, hd // 8)], writes=[("ps_st", g % 2)])
                    if hd % 8 == 7:
                        hsl = slice(8 * g, 8 * g + 8)
                        P.op("dve", lambda h: h.tensor_tensor(H32[:, hsl, :], H32[:, hsl, :],
                                                              eAt[:, hsl].unsqueeze(2).broadcast_to([128, 8, 64]), ALU.mult),
                             reads=["H32", "eAt"], writes=["H32"])
                        P.op("dve", lambda h: h.tensor_tensor(H32[:, hsl, :], H32[:, hsl, :],
                                                              ps_st[g % 2][:, :].rearrange("p (a b) -> p a b", b=64), ALU.add),
                             reads=["H32", ("ps_st", g % 2)], writes=["H32"])
                        P.op("act", lambda h: h.activation(H16[:, hsl, :], H32[:, hsl, :], AF.Copy),
                             reads=["H32"], writes=[("H16", g)])
                    if hh == 1:
                        e_ = pair % 2
                        P.op("dve", lambda h: h.scalar_tensor_tensor(yf[e_][:], xsc[s_][:, pair, :], dp[:, pair:pair + 1],
                                                                     ps_y[yb], ALU.mult, ALU.add),
                             reads=[("xsc", s_), "dp", ("ps_y", yb, 0), ("ps_y", yb, 1)], writes=[("yf", e_)])
                        P.op("pool", lambda h: h.tensor_tensor(yg[e_][:], yf[e_][:], zsc[s_][:, pair, :], ALU.mult),
                             reads=[("yf", e_), ("zsc", s_)], writes=[("yg", e_)])
                        P.op("act", lambda h: h.activation(sq[e_][:], yg[e_][:], AF.Square), reads=[("yg", e_)], writes=[("sq", e_)])
                        for j in range(2):
                            P.op("pe", lambda h: h.matmul(ps_q[:, j:j + 1], sq[e_][:, j * 128:(j + 1) * 128], ones[:, 0:1],
                                                          start=(pair == 0 and j == 0), stop=(pair == 15)),
                                 reads=[("sq", e_), "ones"], writes=["ps_q"], sig=(j == 1))
                        P.op("act", lambda h: h.activation(ya16[e_][:], yg[e_][:], AF.Copy, scale=ng[:, pair:pair + 1]),
                             reads=[("yg", e_), "ng"], writes=[("ya16", e_)])
                        P.dma("sp", self.yaT[pair * 128:(pair + 1) * 128, c * 256:(c + 1) * 256], ya16[e_][:],
                              reads=[("ya16", e_)], writes=[("yaT", pair, c)])
                P.op("dve", lambda h: h.tensor_scalar(rs[:, 2 * c:2 * c + 2], ps_q, 1.0 / 2048.0, 1e-5, ALU.mult, ALU.add),
                     reads=["ps_q"], writes=["rs"])
            P.op("act", lambda h: h.activation(rs[:], rs[:], AF.Ln), reads=["rs"], writes=["rs"])
            P.op("act", lambda h: h.activation(rs[:], rs[:], AF.Exp, scale=-0.5), reads=["rs"], writes=["rs"])
            P.dma("sp", self.rstd_s, rs[:], reads=["rs"], writes=["rstd_s"])

    def stage_C(self):
        P, nc = self.P, self.nc
        V1v = self.V1.rearrange("(t p) h c -> p t h c", p=128)
        gsv = self.gs.rearrange("(t p) c -> p t c", p=128)
        SC = 1.0 / math.sqrt(128.0)
        with self.stage():
            sb = self.sb
            psS = [self.ps(f"psS{i}", [128, 512]) for i in range(2)]
            psO = [[self.ps(f"psO{i}{x}", [128, 512]) for x in "XY"] for i in range(2)]
            ps_gate = self.ps("ps_gate", [128, 512])
            ps_tr = self.ps("ps_tr", [128, 1024], BF16)
            q16 = [sb(f"q16{i}", [128, T], BF16) for i in range(2)]
            k16 = [sb(f"k16{i}", [128, T], BF16) for i in range(2)]
            v1 = [sb(f"v1{i}", [128, NT, 129], BF16) for i in range(2)]
            q32 = [sb(f"q32{i}", [128, T], F32) for i in range(2)]
            gsh = [sb(f"gsh{i}", [128, NT, 128], BF16) for i in range(2)]
            km = [sb(f"km{i}", [128, NCH], F32) for i in range(2)]
            vbias = sb("vbias", [128, NT, NCH], F32)
            tri16 = sb("tri16", [128, 128], BF16)
            ident = sb("ident", [128, 128], BF16)
            gm = sb("gm", [128, NT, NCH], F32)
            top8 = sb("top8", [128, NT, 8], F32)
            mask = sb("mask", [128, NT, NCH], F32)
            E = [sb(f"E{i}", [128, 512], BF16) for i in range(3)]
            acc = [sb(f"acc{i}", [128, 4, 129], F32) for i in range(2)]
            rec = sb("rec", [128, 8], F32)
            yb16 = [sb(f"yb16{i}", [128, 128], BF16) for i in range(2)]
            ybo = [sb(f"ybo{i}", [128, 512], BF16) for i in range(2)]
            P.dma("sp", vbias[:].rearrange("p a b -> p (a b)"), self.vbias, writes=["vbias"])
            P.dma("pool", tri16[:], self.triu, writes=["tri16"])
            P.dma("pool", ident[:], self.ident, writes=["ident"])

            def load_head(hd):
                s_ = hd % 2
                P.dma("sp", q16[s_][:], self.qT16[hd], writes=[("q16", s_)])
                P.dma("sp", k16[s_][:], self.kT16[hd], writes=[("k16", s_)])
                P.dma("sp", v1[s_][:], V1v[:, :, hd, :], writes=[("v1", s_)])
                P.dma("sp", q32[s_][:], self.qT32[hd], writes=[("q32", s_)])
                P.dma("sp", gsh[s_][:], gsv[:, :, hd * 128:(hd + 1) * 128], writes=[("gsh", s_)])
                P.dma("sp", km[s_][:], self.kmean[:, hd, :], writes=[("km", s_)])

            load_head(0)
            cS = cE = cY = 0
            nheads = getattr(self, "c_heads", 16)
            for hd in range(nheads):
                s_ = hd % 2
                if hd + 1 < nheads:
                    load_head(hd + 1)
                for qt in range(NT):
                    P.op("pe", lambda h: h.matmul(ps_gate[:, qt * NCH:(qt + 1) * NCH], q32[s_][:, qt * 128:(qt + 1) * 128],
                                                  km[s_][:], start=True, stop=True),
                         reads=[("q32", s_), ("km", s_)], writes=["ps_gate"], sig=(qt == NT - 1))
                P.op("dve", lambda h: h.tensor_tensor(gm[:].rearrange("p a b -> p (a b)"), ps_gate[:, :],
                                                      vbias[:].rearrange("p a b -> p (a b)"), ALU.add),
                     reads=["ps_gate", "vbias"], writes=["gm"])
                for qt in range(NT):
                    P.op("dve", lambda h: h.max(top8[:, qt, :], gm[:, qt, :]), reads=["gm"], writes=["top8"])
                P.op("dve", lambda h: h.tensor_tensor(mask[:], gm[:], top8[:, :, 2:3].broadcast_to([128, NT, NCH]), ALU.is_ge),
                     reads=["gm", "top8"], writes=["mask"])
                for j in range(NG):
                    ab = j % 2
                    first_acc = [True] * 4
                    steps = []
                    for n in range(2 * j + 2):
                        for kt in range(2):
                            if n < 2 * j:
                                first, diag = 0, False
                            elif n == 2 * j:
                                first, diag = kt, True
                            else:
                                first, diag = 2 + kt, True
                            sbk = cS % 2
                            cS += 1
                            eb = cE % 3
                            cE += 1
                            steps.append(dict(n=n, kt=kt, first=first, diag=diag, sbk=sbk, eb=eb, K=2 * n + kt,
                                              N=(4 - first) * 128, q0=(4 * j + first) * 128))

                    def emit_qk(sp_):
                        sbk, N, q0, K_ = sp_["sbk"], sp_["N"], sp_["q0"], sp_["K"]
                        P.op("pe", lambda h: h.matmul(psS[sbk][:, 0:N], k16[s_][:, K_ * 128:(K_ + 1) * 128],
                                                      q16[s_][:, q0:q0 + N], start=True, stop=True),
                             reads=[("k16", s_), ("q16", s_)], writes=[("psS", sbk)])

                    def emit_exp(sp_):
                        sbk, N, eb = sp_["sbk"], sp_["N"], sp_["eb"]
                        P.op("act", lambda h: h.activation(E[eb][:, 0:N], psS[sbk][:, 0:N], AF.Exp, scale=SC),
                             reads=[("psS", sbk)], writes=[("E", eb)])
                        if sp_["diag"]:
                            P.op("pool", lambda h: h.tensor_tensor(E[eb][:, 0:128], E[eb][:, 0:128], tri16[:], ALU.mult),
                                 reads=[("E", eb), "tri16"], writes=[("E", eb)])

                    blk_state = {}

                    def emit_pv(sp_):
                        n, kt, first, eb, K_ = sp_["n"], sp_["kt"], sp_["first"], sp_["eb"], sp_["K"]
                        nb = n % 2
                        stt = blk_state.setdefault(n, {"X": False, "Y": False, "vis": set()})
                        for t in range(first, 4):
                            x = "X" if t < 2 else "Y"
                            ob = psO[nb][0 if t < 2 else 1]
                            last_kt = (kt == 1) or (n == 2 * j and t == 0) or (n == 2 * j + 1 and t == 2)
                            st = not stt[x]
                            stt[x] = True
                            stt["vis"].add(t)
                            P.op("pe", lambda h: h.matmul(ob[:, (t % 2) * 256:(t % 2) * 256 + 129],
                                                          E[eb][:, (t - first) * 128:(t - first + 1) * 128],
                                                          v1[s_][:, K_, :], start=st, stop=last_kt),
                                 reads=[("E", eb), ("v1", s_)], writes=[("psO", nb, x)], sig=(t == 3))

                    def emit_acc(n):
                        nb = n % 2
                        for t in sorted(blk_state[n]["vis"]):
                            x = "X" if t < 2 else "Y"
                            ob = psO[nb][0 if t < 2 else 1][:, (t % 2) * 256:(t % 2) * 256 + 129]
                            own = (n == 2 * j + t // 2)
                            mcol = mask[:, 4 * j + t, n:n + 1]
                            a_t = acc[ab][:, t, :]
                            if first_acc[t]:
                                first_acc[t] = False
                                if own:
                                    P.op("dve", lambda h: h.tensor_copy(a_t, ob), reads=[("psO", nb, x)], writes=[("acc", ab, t)])
                                else:
                                    P.op("dve", lambda h: h.tensor_scalar(a_t, ob, mcol, None, ALU.mult),
                                         reads=[("psO", nb, x)], writes=[("acc", ab, t)], strict=["mask"])
                            elif own:
                                P.op("dve", lambda h: h.tensor_tensor(a_t, ob, a_t, ALU.add),
                                     reads=[("psO", nb, x), ("acc", ab, t)], writes=[("acc", ab, t)])
                            else:
                                P.op("dve", lambda h: h.scalar_tensor_tensor(a_t, ob, mcol, a_t, ALU.mult, ALU.add),
                                     reads=[("psO", nb, x), ("acc", ab, t)], writes=[("acc", ab, t)], strict=["mask"])

                    emit_qk(steps[0])
                    for i_, sp_ in enumerate(steps):
                        if i_ + 1 < len(steps):
                            emit_qk(steps[i_ + 1])
                        emit_exp(sp_)
                        emit_pv(sp_)
                        if sp_["kt"] == 1:
                            emit_acc(sp_["n"])
                    yo = cY % 2
                    cY += 1
                    for t in range(4):
                        yb_ = t % 2
                        P.op("dve", lambda h: h.reciprocal(rec[:, t:t + 1], acc[ab][:, t, 128:129]),
                             reads=[("acc", ab, t)], writes=[("rec", t)])
                        P.op("dve", lambda h: h.scalar_tensor_tensor(yb16[yb_][:], acc[ab][:, t, 0:128], rec[:, t:t + 1],
                                                                     gsh[s_][:, 4 * j + t, :], ALU.mult, ALU.mult),
                             reads=[("acc", ab, t), ("gsh", s_)], writes=[("yb16", yb_)], strict=[("rec", t)])
                        P.op("pe", lambda h: h.transpose(ps_tr[:, t * 128:(t + 1) * 128], yb16[yb_][:], ident[:]),
                             reads=[("yb16", yb_), "ident"], writes=["ps_tr"])
                    P.op("act", lambda h: h.activation(ybo[yo][:], ps_tr[:, 0:512], AF.Copy), reads=["ps_tr"], writes=[("ybo", yo)])
                    P.dma("sp", self.ybT[hd * 128:(hd + 1) * 128, j * 512:(j + 1) * 512], ybo[yo][:],
                          reads=[("ybo", yo)], writes=[("ybT", hd, j)])

    def outproj_ln(self, parts, w_dram, nk_total, resid, layer, out_dram, xT_out=None, rstd_dram=None):
        P, nc = self.P, self.nc
        wv = w_dram.rearrange("(k p) c -> p k c", p=128)
        with self.stage():
            sb = self.sb
            W = sb("W", [128, nk_total, D], BF16)
            for k in range(nk_total):
                P.dma("pool", W[:, k, :], wv[:, k, :], writes=[("W", k)])
            npart = len(parts)
            psP = [[self.ps(f"psP{i}{a}", [128, 512]) for a in range(npart)] for i in range(2)]
            ps_tr = [self.ps(f"ps_tr{i}", [128, 1024], BF16) for i in range(2)] if xT_out is not None else None
            lt = [[sb(f"lt{i}{a}", [128, parts[a][2], 128], BF16) for a in range(npart)] for i in range(2)]
            xt = [sb(f"xt{i}", [128, D], F32) for i in range(2)]
            v = sb("v", [128, D], F32)
            junk = sb("junk", [128, D], BF16)
            gbc = sb("gbc", [128, D], F32)
            bbc = sb("bbc", [128, D], F32)
            st = sb("st", [128, 8], F32)
            P.dma("sp", gbc[:], self.ln_g[layer:layer + 1, :].partition_broadcast(128), writes=["gbc"])
            P.dma("sp", bbc[:], self.ln_b[layer:layer + 1, :].partition_broadcast(128), writes=["bbc"])
            if rstd_dram is not None:
                rs = sb("rs", [128, NT], F32)
                P.dma("sp", rs[:], rstd_dram, writes=["rs"])
            if xT_out is not None:
                ident = sb("ident", [128, 128], BF16)
                P.dma("pool", ident[:], self.ident, writes=["ident"])
                x1b = sb("x1b", [128, D], BF16)
                xTt = sb("xTt", [128, 16, 128], BF16)
                xTv = xT_out.rearrange("(k p) t -> p k t", p=128)
            fv = [pt[0].rearrange("(k p) t -> p k t", p=128) for pt in parts]

            def load_tile(tt):
                s_ = tt % 2
                for a in range(npart):
                    P.dma("sp", lt[s_][a][:], fv[a][:, :, tt * 128:(tt + 1) * 128], writes=[("lt", s_, a)])
                P.dma("sp", xt[s_][:], resid[tt * 128:(tt + 1) * 128, :], writes=[("xt", s_)])

            load_tile(0)
            cP = 0
            for tt in range(NT):
                s_ = tt % 2
                if tt + 1 < NT:
                    load_tile(tt + 1)
                for cg in range(4):
                    pb = cP % 2
                    cP += 1
                    cs = slice(cg * 512, (cg + 1) * 512)
                    for a, (_, k0, nk, use_rstd) in enumerate(parts):
                        for k in range(nk):
                            P.op("pe", lambda h: h.matmul(psP[pb][a][:, :], lt[s_][a][:, k, :], W[:, k0 + k, cs],
                                                          start=(k == 0), stop=(k == nk - 1)),
                                 reads=[("lt", s_, a), ("W", k0 + k)], writes=[("psP", pb, a)], sig=(k == nk - 1))
                    first = True
                    for a, (_, k0, nk, use_rstd) in enumerate(parts):
                        if use_rstd:
                            continue
                        P.op("dve", lambda h: h.scalar_tensor_tensor(v[:, cs], xt[s_][:, cs], ALPHA, psP[pb][a][:, :],
                                                                     ALU.mult, ALU.add),
                             reads=[("xt", s_), ("psP", pb, a)], writes=[("v", cg)])
                        first = False
                    for a, (_, k0, nk, use_rstd) in enumerate(parts):
                        if not use_rstd:
                            continue
                        P.op("dve", lambda h: h.scalar_tensor_tensor(v[:, cs], psP[pb][a][:, :], rs[:, tt:tt + 1], v[:, cs],
                                                                     ALU.mult, ALU.add),
                             reads=[("psP", pb, a), ("v", cg)], writes=[("v", cg)], strict=["rs"])
                vres = [("v", cg) for cg in range(4)]
                P.op("act", lambda h: h.activation(junk[:], v[:], AF.Square), reads=vres, writes=["junk"])
                P.op("dve", lambda h: h.reduce_sum(st[:, 0:1], v[:], AX.X), reads=vres, writes=[("st", 0)])
                P.op("dve", lambda h: h.reduce_sum(st[:, 1:2], junk[:], AX.X), reads=["junk"], writes=[("st", 1)])
                P.op("dve", lambda h: h.tensor_scalar(st[:, 2:3], st[:, 0:1], 1.0 / D, None, ALU.mult),
                     reads=[("st", 0)], writes=[("st", 2)])
                P.op("dve", lambda h: h.tensor_tensor(st[:, 3:4], st[:, 2:3], st[:, 2:3], ALU.mult),
                     reads=[("st", 2)], writes=[("st", 3)])
                P.op("dve", lambda h: h.scalar_tensor_tensor(st[:, 4:5], st[:, 1:2], 1.0 / D, st[:, 3:4], ALU.mult, ALU.subtract),
                     reads=[("st", 1), ("st", 3)], writes=[("st", 4)])
                P.op("dve", lambda h: h.tensor_scalar(st[:, 4:5], st[:, 4:5], 1e-5, None, ALU.add),
                     reads=[("st", 4)], writes=[("st", 4)])
                P.op("act", lambda h: h.activation(st[:, 5:6], st[:, 4:5], AF.Ln), reads=[("st", 4)], writes=[("st", 5)])
                P.op("act", lambda h: h.activation(st[:, 5:6], st[:, 5:6], AF.Exp, scale=-0.5), reads=[("st", 5)], writes=[("st", 5)])
                P.op("dve", lambda h: h.scalar_tensor_tensor(st[:, 6:7], st[:, 2:3], -1.0, st[:, 5:6], ALU.mult, ALU.mult),
                     reads=[("st", 2), ("st", 5)], writes=[("st", 6)])
                P.op("act", lambda h: h.activation(v[:], v[:], AF.Identity, bias=st[:, 6:7], scale=st[:, 5:6]),
                     reads=vres, writes=vres, strict=[("st", 5), ("st", 6)])
                P.op("dve", lambda h: h.tensor_tensor(v[:], v[:], gbc[:], ALU.mult), reads=vres + ["gbc"], writes=vres)
                P.op("dve", lambda h: h.tensor_tensor(v[:], v[:], bbc[:], ALU.add), reads=vres + ["bbc"], writes=vres)
                P.dma("sp", out_dram[tt * 128:(tt + 1) * 128, :], v[:], reads=vres, writes=[("out", tt)])
                if xT_out is not None:
                    P.op("act", lambda h: h.activation(x1b[:], v[:], AF.Copy), reads=vres, writes=["x1b"])
                    for hb in range(2):
                        for kk in range(8):
                            k = hb * 8 + kk
                            P.op("pe", lambda h: h.transpose(ps_tr[hb][:, kk * 128:(kk + 1) * 128], x1b[:, k * 128:(k + 1) * 128],
                                                             ident[:]),
                                 reads=["x1b", "ident"], writes=[("ps_tr", hb)], sig=(kk == 7))
                        P.op("act" if hb == 0 else "dve",
                             (lambda h: h.activation(xTt[:, 0:8, :].rearrange("p a b -> p (a b)"), ps_tr[0][:, :], AF.Copy)) if hb == 0 else
                             (lambda h: h.tensor_copy(xTt[:, 8:16, :].rearrange("p a b -> p (a b)"), ps_tr[1][:, :])),
                             reads=[("ps_tr", hb)], writes=[("xTt", hb)])
                    P.dma("sp", xTv[:, :, tt * 128:(tt + 1) * 128], xTt[:], reads=[("xTt", 0), ("xTt", 1)], writes=[("xT_out", tt)])

    def stage_D(self):
        self.outproj_ln([(self.ybT, 16, 16, False), (self.yaT, 0, 16, True)], self.w_out0, 32, self.x, 0, self.x1,
                        xT_out=self.x1T, rstd_dram=self.rstd_s)

    def stage_E(self):
        P, nc = self.P, self.nc
        x1Tv = self.x1T.rearrange("(k p) t -> p k t", p=128)
        with self.stage():
            xTb = self.sb("xTb", [128, 16, T], BF16)
            for k in range(16):
                P.dma("sp", xTb[:, k, :], x1Tv[:, k, :], writes=[("xTb", k)])
            wsl = [self.sb(f"wsl{i}", [128, 16, 128], BF16) for i in range(2)]
            stg = [self.sb(f"stg{i}", [128, 16, 128], F32) for i in range(2)]
            psA = [self.ps(f"psA{i}", [128, 512]) for i in range(4)]
            ob = [self.sb(f"ob{i}", [128, 512], BF16) for i in range(3)]
            co = 0
            cg_ = 0
            def issue_w(ib):
                sl = ib % 2
                P.dma("sp", stg[sl][:], self.w_in1[ib], writes=[("stg", sl)])
                P.op("pool", lambda h: h.tensor_copy(wsl[sl][:], stg[sl][:]), reads=[("stg", sl)], writes=[("w", sl)])

            issue_w(0)
            for i in range(32):
                slot = i % 2
                if i + 1 < 32:
                    issue_w(i + 1)
                for g in range(NG):
                    b = cg_ % 4
                    cg_ += 1
                    for k in range(16):
                        P.op("pe", lambda h: h.matmul(psA[b][:, :], wsl[slot][:, k, :], xTb[:, k, g * 512:(g + 1) * 512],
                                                      start=(k == 0), stop=(k == 15)),
                             reads=[("w", slot), ("xTb", k)], writes=[("psA", b)], sig=(k == 15))
                    o = co % 3
                    co += 1
                    P.op("act", lambda h: h.activation(ob[o][:], psA[b][:, :], AF.Copy if i < 16 else AF.Silu),
                         reads=[("psA", b)], writes=[("ob", o)])
                    dst = self.uT if i < 16 else self.sg1T
                    P.dma("sp", dst[(i % 16) * 128:(i % 16 + 1) * 128, g * 512:(g + 1) * 512], ob[o][:],
                          reads=[("ob", o)], writes=[("eo", i, g)])

    def stage_F(self):
        P, nc = self.P, self.nc
        TWO_PI = 2.0 * math.pi
        uTv = self.uT.rearrange("(k p) t -> p k t", p=128)
        with self.stage():
            sb = self.sb
            psV = [[self.ps(f"psV{i}{a}", [128, 512]) for a in "ri"] for i in range(2)]
            psY = [self.ps(f"psY{i}", [128, 512]) for i in range(2)]
            pl = sb("pl", [128, 3, 64], F32)
            wl = sb("wl", [128, 5, 16, 64], F32)
            cp = sb("cp", [128, 2, 64, 16], F32)
            d1 = sb("d1", [128, 16], F32)
            rmk = sb("rmk", [128, 8], F32)
            iot = sb("iot", [128, 513], F32)
            pi_c = sb("pi_c", [128, 1], F32)
            P.dma("sp", pl[:], self.s5_pl, writes=["pl"])
            P.dma("sp", wl[:], self.s5_wl, writes=["wl"])
            P.dma("sp", cp[:], self.s5_cp, writes=["cp"])
            P.dma("sp", d1[:], self.s5_d1, writes=["d1"])
            P.dma("sp", rmk[:], self.rowmask, writes=["rmk"])
            P.dma("sp", iot[:], self.iota513.partition_broadcast(128), writes=["iot"])
            P.op("pool", lambda h: h.memset(pi_c[:], math.pi), writes=["pi_c"])
            dtp = sb("dtp", [128, 64], F32)
            rP = sb("rP", [128, 64], F32)
            thP = sb("thP", [128, 64], F32)
            P.op("act", lambda h: h.activation(dtp[:], pl[:, 2, :], AF.Exp), reads=["pl"], writes=["dtp"])
            P.op("dve", lambda h: h.tensor_tensor(rP[:], pl[:, 0, :], dtp[:], ALU.mult), reads=["pl", "dtp"], writes=["rP"])
            P.op("act", lambda h: h.activation(rP[:], rP[:], AF.Exp), reads=["rP"], writes=["rP"])
            P.op("dve", lambda h: h.tensor_tensor(thP[:], pl[:, 1, :], dtp[:], ALU.mult), reads=["pl", "dtp"], writes=["thP"])
            thm = sb("thm", [128, 64], F32)
            P.op("dve", lambda h: h.tensor_scalar(thm[:], thP[:], 0.0, TWO_PI, ALU.is_lt, ALU.mult), reads=["thP"], writes=["thm"])
            P.op("dve", lambda h: h.tensor_tensor(thP[:], thP[:], thm[:], ALU.add), reads=["thP", "thm"], writes=["thP"])

            def sincos(o_sin, o_cos, a_in, shape, tag):
                ki = sb(f"ki_{tag}", shape, mybir.dt.int32)
                kf = sb(f"kf_{tag}", shape, F32)
                rr = sb(f"rr_{tag}", shape, F32)
                mm_ = sb(f"mm_{tag}", shape, F32)
                r_ = [f"sc_{tag}"]
                P.op("dve", lambda h: h.tensor_scalar(ki[:], a_in, 1.0 / TWO_PI, None, ALU.mult), reads=r_, writes=r_)
                P.op("dve", lambda h: h.tensor_copy(kf[:], ki[:]), reads=r_, writes=r_)
                P.op("dve", lambda h: h.scalar_tensor_tensor(rr[:], kf[:], -TWO_PI, a_in, ALU.mult, ALU.add), reads=r_, writes=r_)
                P.op("dve", lambda h: h.tensor_scalar(mm_[:], rr[:], math.pi, -TWO_PI, ALU.is_gt, ALU.mult), reads=r_, writes=r_)
                P.op("dve", lambda h: h.tensor_tensor(mm_[:], mm_[:], rr[:], ALU.add), reads=r_, writes=r_)
                P.op("act", lambda h: h.activation(o_sin, mm_[:], AF.Sin), reads=r_, writes=r_)
                P.op("dve", lambda h: h.tensor_scalar(rr[:], rr[:], 0.5 * math.pi, None, ALU.add), reads=r_, writes=r_)
                P.op("dve", lambda h: h.tensor_scalar(mm_[:], rr[:], math.pi, -TWO_PI, ALU.is_gt, ALU.mult), reads=r_, writes=r_)
                P.op("dve", lambda h: h.tensor_tensor(mm_[:], mm_[:], rr[:], ALU.add), reads=r_, writes=r_)
                P.op("act", lambda h: h.activation(o_cos, mm_[:], AF.Sin), reads=r_, writes=r_)
            W3 = [128, 16, 64]
            dtw = sb("dtw", W3, F32)
            aw = sb("aw", W3, F32)
            tw = sb("tw", W3, F32)
            cw_ = sb("cw_", W3, F32)
            sw_ = sb("sw_", W3, F32)
            fre = sb("fre", W3, F32)
            fim = sb("fim", W3, F32)
            t0 = sb("t0", W3, F32)
            t1 = sb("t1", W3, F32)
            bbr = sb("bbr", W3, F32)
            bbi = sb("bbi", W3, F32)
            lre, lim, bre, bim = wl[:, 0], wl[:, 1], wl[:, 3], wl[:, 4]
            D_ = lambda fn, r, w: P.op("dve", fn, reads=r, writes=w)
            A_ = lambda fn, r, w, **kw: P.op("act", fn, reads=r, writes=w, **kw)
            A_(lambda h: h.activation(dtw[:], wl[:, 2], AF.Exp), ["wl"], ["dtw"])
            D_(lambda h: h.tensor_tensor(aw[:], lre, dtw[:], ALU.mult), ["wl", "dtw"], ["aw"])
            A_(lambda h: h.activation(aw[:], aw[:], AF.Exp), ["aw"], ["aw"])
            D_(lambda h: h.tensor_tensor(tw[:], lim, dtw[:], ALU.mult), ["wl", "dtw"], ["tw"])
            D_(lambda h: h.tensor_scalar(t0[:], tw[:], 0.0, TWO_PI, ALU.is_lt, ALU.mult), ["tw"], ["t0"])
            D_(lambda h: h.tensor_tensor(tw[:], tw[:], t0[:], ALU.add), ["tw", "t0"], ["tw", "sc_w"])
            sincos(sw_[:], cw_[:], tw[:], W3, "w")
            P.op("dve", lambda h: h.tensor_copy(sw_[:], sw_[:]), reads=["sc_w"], writes=["sw_", "cw_"])
            D_(lambda h: h.tensor_tensor(cw_[:], cw_[:], aw[:], ALU.mult), ["cw_", "aw"], ["cw_"])
            D_(lambda h: h.tensor_tensor(sw_[:], sw_[:], aw[:], ALU.mult), ["sw_", "aw"], ["sw_"])
            D_(lambda h: h.tensor_scalar(cw_[:], cw_[:], -1.0, None, ALU.add), ["cw_"], ["cw_"])
            D_(lambda h: h.tensor_tensor(t0[:], lre, lre, ALU.mult), ["wl"], ["t0"])
            D_(lambda h: h.tensor_tensor(t1[:], lim, lim, ALU.mult), ["wl"], ["t1"])
            D_(lambda h: h.tensor_tensor(t0[:], t0[:], t1[:], ALU.add), ["t0", "t1"], ["t0"])
            D_(lambda h: h.reciprocal(t0[:], t0[:]), ["t0"], ["t0"])
            D_(lambda h: h.tensor_tensor(fre[:], cw_[:], lre, ALU.mult), ["cw_", "wl"], ["fre"])
            D_(lambda h: h.tensor_tensor(t1[:], sw_[:], lim, ALU.mult), ["sw_", "wl"], ["t1"])
            D_(lambda h: h.tensor_tensor(fre[:], fre[:], t1[:], ALU.add), ["fre", "t1"], ["fre"])
            D_(lambda h: h.tensor_tensor(fre[:], fre[:], t0[:], ALU.mult), ["fre", "t0"], ["fre"])
            D_(lambda h: h.tensor_tensor(fim[:], sw_[:], lre, ALU.mult), ["sw_", "wl"], ["fim"])
            D_(lambda h: h.tensor_tensor(t1[:], cw_[:], lim, ALU.mult), ["cw_", "wl"], ["t1"])
            D_(lambda h: h.tensor_tensor(fim[:], fim[:], t1[:], ALU.subtract), ["fim", "t1"], ["fim"])
            D_(lambda h: h.tensor_tensor(fim[:], fim[:], t0[:], ALU.mult), ["fim", "t0"], ["fim"])
            D_(lambda h: h.tensor_tensor(bbr[:], fre[:], bre, ALU.mult), ["fre", "wl"], ["bbr"])
            D_(lambda h: h.tensor_tensor(t1[:], fim[:], bim, ALU.mult), ["fim", "wl"], ["t1"])
            D_(lambda h: h.tensor_tensor(bbr[:], bbr[:], t1[:], ALU.subtract), ["bbr", "t1"], ["bbr"])
            D_(lambda h: h.tensor_tensor(bbi[:], fre[:], bim, ALU.mult), ["fre", "wl"], ["bbi"])
            D_(lambda h: h.tensor_tensor(t1[:], fim[:], bre, ALU.mult), ["fim", "wl"], ["t1"])
            D_(lambda h: h.tensor_tensor(bbi[:], bbi[:], t1[:], ALU.add), ["bbi", "t1"], ["bbi"])
            uc = [sb(f"uc{i}", [128, T], BF16) for i in range(2)]
            Lr = [sb(f"Lr{i}", [128, 128], BF16) for i in range(4)]
            Li = [sb(f"Li{i}", [128, 128], BF16) for i in range(4)]
            Cr = [sb(f"Cr{i}", [128, 128], BF16) for i in range(4)]
            nCr = [sb(f"nCr{i}", [128, 128], BF16) for i in range(4)]
            nCi = [sb(f"nCi{i}", [128, 128], BF16) for i in range(4)]
            cosT = [sb(f"cosT{i}", [128, 513], F32) for i in range(4)]
            sinT = [sb(f"sinT{i}", [128, 513], F32) for i in range(4)]
            ang = sb("ang", [128, 513], F32)
            ki_p = sb("ki_p", [128, 513], mybir.dt.int32)
            kf_p = sb("kf_p", [128, 513], F32)
            rr_p = sb("rr_p", [128, 513], F32)
            mm_p = sb("mm_p", [128, 513], F32)

            def sincos_p(o_sin, o_cos):
                r_ = ["sc_p"]
                P.op("dve", lambda h: h.tensor_scalar(ki_p[:], ang[:], 1.0 / TWO_PI, None, ALU.mult), reads=r_, writes=r_)
                P.op("dve", lambda h: h.tensor_copy(kf_p[:], ki_p[:]), reads=r_, writes=r_)
                P.op("dve", lambda h: h.scalar_tensor_tensor(rr_p[:], kf_p[:], -TWO_PI, ang[:], ALU.mult, ALU.add), reads=r_, writes=r_)
                P.op("dve", lambda h: h.tensor_scalar(mm_p[:], rr_p[:], math.pi, -TWO_PI, ALU.is_gt, ALU.mult), reads=r_, writes=r_)
                P.op("dve", lambda h: h.tensor_tensor(mm_p[:], mm_p[:], rr_p[:], ALU.add), reads=r_, writes=r_)
                P.op("act", lambda h: h.activation(o_sin, mm_p[:], AF.Sin), reads=r_, writes=r_)
                P.op("dve", lambda h: h.tensor_scalar(rr_p[:], rr_p[:], 0.5 * math.pi, None, ALU.add), reads=r_, writes=r_)
                P.op("dve", lambda h: h.tensor_scalar(mm_p[:], rr_p[:], math.pi, -TWO_PI, ALU.is_gt, ALU.mult), reads=r_, writes=r_)
                P.op("dve", lambda h: h.tensor_tensor(mm_p[:], mm_p[:], rr_p[:], ALU.add), reads=r_, writes=r_)
                P.op("act", lambda h: h.activation(o_cos, mm_p[:], AF.Sin), reads=r_, writes=r_)
            qst = [sb(f"qst{i}", [128, 2], F32) for i in range(4)]
            qt_ = sb("qt_", [128, 2], F32)
            Vs = [[sb(f"Vs{i}{a}", [128, 512], F32) for a in "ri"] for i in range(2)]
            m1 = [sb(f"m1{i}", [128, 512], F32) for i in range(2)]
            m2 = [sb(f"m2{i}", [128, 512], F32) for i in range(2)]
            Wr = [sb(f"Wr{i}", [128, 512], F32) for i in range(2)]
            Wi = [sb(f"Wi{i}", [128, 512], F32) for i in range(2)]
            Gr = [sb(f"Gr{i}", [128, 512], F32) for i in range(2)]
            Gi = [sb(f"Gi{i}", [128, 512], F32) for i in range(2)]
            Pp = [[sb(f"Pp{i}{a}", [128, 512], BF16) for a in range(4)] for i in range(2)]
            yv = [sb(f"yv{i}", [128, 512], F32) for i in range(2)]
            ge1 = [sb(f"ge1{i}", [128, 512], F32) for i in range(2)]
            ge2 = [sb(f"ge2{i}", [128, 512], F32) for i in range(2)]
            yo = [sb(f"yo{i}", [128, 512], BF16) for i in range(2)]
            for i in range(4):
                for tl, nm in ((Cr, "Cr"), (nCr, "nCr"), (nCi, "nCi")):
                    P.op("pool", lambda h: h.memset(tl[i][:], 0.0), writes=[(nm, i)])
            P.dma("sp", uc[0][:], uTv[:, 0, :], writes=[("uc", 0)])
            cpb = 0
            nchunks = getattr(self, "f_chunks", 16)
            for j in range(nchunks):
                us = j % 2
                if j + 1 < nchunks:
                    P.dma("sp", uc[(j + 1) % 2][:], uTv[:, j + 1, :], writes=[("uc", (j + 1) % 2)])
                for pc in range(4):
                    pr = 4 * j + pc
                    for g2 in range(2):
                        gl = 2 * pc + g2
                        P.op("dve", lambda h: h.tensor_scalar(Lr[pc][:, g2 * 64:(g2 + 1) * 64], bbr[:, j, :], rmk[:, gl:gl + 1], None, ALU.mult),
                             reads=["bbr", "rmk"], writes=[("Lr", pc)])
                        P.op("dve", lambda h: h.tensor_scalar(Li[pc][:, g2 * 64:(g2 + 1) * 64], bbi[:, j, :], rmk[:, gl:gl + 1], None, ALU.mult),
                             reads=["bbi", "rmk"], writes=[("Li", pc)])
                        rs_ = slice(g2 * 64, (g2 + 1) * 64)
                        cs_ = slice(gl * 16, gl * 16 + 16)
                        P.op("act", lambda h: h.activation(Cr[pc][rs_, cs_], cp[rs_, 0, pr, :], AF.Copy), reads=["cp"], writes=[("Cr", pc)])
                        P.op("act", lambda h: h.activation(nCr[pc][rs_, cs_], cp[rs_, 0, pr, :], AF.Copy, scale=-1.0), reads=["cp"], writes=[("nCr", pc)])
                        P.op("act", lambda h: h.activation(nCi[pc][rs_, cs_], cp[rs_, 1, pr, :], AF.Copy, scale=-1.0), reads=["cp"], writes=[("nCi", pc)])
                    P.op("dve", lambda h: h.tensor_scalar(ang[:], iot[:], thP[:, pr:pr + 1], None, ALU.mult),
                         reads=["iot"], writes=["ang", "sc_p", ("sinT", pc), ("cosT", pc)], strict=["thP"])
                    sincos_p(sinT[pc][:], cosT[pc][:])
                    P.op("dve", lambda h: h.tensor_copy(qt_[:, 0:1], qt_[:, 0:1]), reads=["sc_p"], writes=[("sinT", pc), ("cosT", pc)])
                    P.op("pool", lambda h: h.memset(qst[pc][:], 0.0), writes=[("qst", pc)])
                for b in range(NG):
                    bs = slice(b * 512, (b + 1) * 512)
                    yb_ = b % 2
                    for pc in range(4):
                        pr = 4 * j + pc
                        vb = cpb % 2
                        cpb += 1
                        c_, s_t = cosT[pc][:, 0:512], sinT[pc][:, 0:512]
                        P.op("pe", lambda h: h.matmul(psV[vb][0][:, :], Lr[pc][:], uc[us][:, bs], start=True, stop=True),
                             reads=[("Lr", pc), ("uc", us)], writes=[("psV", vb, 0)])
                        P.op("pe", lambda h: h.matmul(psV[vb][1][:, :], Li[pc][:], uc[us][:, bs], start=True, stop=True),
                             reads=[("Li", pc), ("uc", us)], writes=[("psV", vb, 1)])
                        P.op("act", lambda h: h.activation(Vs[vb][0][:], psV[vb][0][:, :], AF.Copy), reads=[("psV", vb, 0)], writes=[("Vs", vb, 0)])
                        P.op("act", lambda h: h.activation(Vs[vb][1][:], psV[vb][1][:, :], AF.Copy), reads=[("psV", vb, 1)], writes=[("Vs", vb, 1)])
                        P.op("dve", lambda h: h.tensor_tensor(m1[vb][:], Vs[vb][0][:], c_, ALU.mult), reads=[("Vs", vb, 0), ("cosT", pc)], writes=[("m1", vb)])
                        P.op("dve", lambda h: h.tensor_tensor(m2[vb][:], Vs[vb][1][:], s_t, ALU.mult), reads=[("Vs", vb, 1), ("sinT", pc)], writes=[("m2", vb)])
                        P.op("dve", lambda h: h.tensor_tensor(Wr[vb][:], m1[vb][:], m2[vb][:], ALU.add), reads=[("m1", vb), ("m2", vb)], writes=[("Wr", vb)])
                        P.op("dve", lambda h: h.tensor_tensor(m1[vb][:], Vs[vb][1][:], c_, ALU.mult), reads=[("Vs", vb, 1), ("cosT", pc)], writes=[("m1", vb)])
                        P.op("dve", lambda h: h.tensor_tensor(m2[vb][:], Vs[vb][0][:], s_t, ALU.mult), reads=[("Vs", vb, 0), ("sinT", pc)], writes=[("m2", vb)])
                        P.op("dve", lambda h: h.tensor_tensor(Wi[vb][:], m1[vb][:], m2[vb][:], ALU.subtract), reads=[("m1", vb), ("m2", vb)], writes=[("Wi", vb)])
                        rbc = rP[:, pr:pr + 1].broadcast_to([128, 512])
                        P.op("dve", lambda h: h.tensor_tensor_scan(Gr[vb][:], rbc, Wr[vb][:], qst[pc][:, 0:1], ALU.mult, ALU.add),
                             reads=[("Wr", vb), "rP"], writes=[("Gr", vb)], strict=[("qst", pc)])
                        P.op("dve", lambda h: h.tensor_tensor_scan(Gi[vb][:], rbc, Wi[vb][:], qst[pc][:, 1:2], ALU.mult, ALU.add),
                             reads=[("Wi", vb), "rP"], writes=[("Gi", vb)], strict=[("qst", pc)])
                        C5, S5 = cosT[pc][:, 512:513], sinT[pc][:, 512:513]
                        P.op("dve", lambda h: h.tensor_tensor(qt_[:, 0:1], Gi[vb][:, 511:512], S5, ALU.mult), reads=[("Gi", vb), ("sinT", pc)], writes=["qt_"])
                        P.op("dve", lambda h: h.tensor_tensor(qt_[:, 1:2], Gr[vb][:, 511:512], S5, ALU.mult), reads=[("Gr", vb), ("sinT", pc)], writes=["qt_"])
                        P.op("dve", lambda h: h.tensor_tensor(qst[pc][:, 0:1], Gr[vb][:, 511:512], C5, ALU.mult), reads=[("Gr", vb), ("cosT", pc)], writes=[("qst", pc)])
                        P.op("dve", lambda h: h.tensor_tensor(qst[pc][:, 1:2], Gi[vb][:, 511:512], C5, ALU.mult), reads=[("Gi", vb), ("cosT", pc)], writes=[("qst", pc)])
                        P.op("dve", lambda h: h.tensor_tensor(qst[pc][:, 0:1], qst[pc][:, 0:1], qt_[:, 0:1], ALU.subtract), reads=[("qst", pc), "qt_"], writes=[("qst", pc)])
                        P.op("dve", lambda h: h.tensor_tensor(qst[pc][:, 1:2], qst[pc][:, 1:2], qt_[:, 1:2], ALU.add), reads=[("qst", pc), "qt_"], writes=[("qst", pc)])
                        P.op("dve", lambda h: h.tensor_tensor(Pp[vb][0][:], Gr[vb][:], c_, ALU.mult), reads=[("Gr", vb), ("cosT", pc)], writes=[("Pp", vb, 0)])
                        P.op("dve", lambda h: h.tensor_tensor(Pp[vb][1][:], Gi[vb][:], c_, ALU.mult), reads=[("Gi", vb), ("cosT", pc)], writes=[("Pp", vb, 1)])
                        P.op("dve", lambda h: h.tensor_tensor(Pp[vb][2][:], Gi[vb][:], s_t, ALU.mult), reads=[("Gi", vb), ("sinT", pc)], writes=[("Pp", vb, 2)])
                        P.op("dve", lambda h: h.tensor_tensor(Pp[vb][3][:], Gr[vb][:], s_t, ALU.mult), reads=[("Gr", vb), ("sinT", pc)], writes=[("Pp", vb, 3)])
                        for a, (wt, wn) in enumerate(((Cr, "Cr"), (nCi, "nCi"), (nCr, "nCr"), (nCi, "nCi"))):
                            P.op("pe", lambda h: h.matmul(psY[yb_][:, :], wt[pc][:], Pp[vb][a][:], start=(pc == 0 and a == 0),
                                                          stop=(pc == 3 and a == 3)),
                                 reads=[(wn, pc), ("Pp", vb, a)], writes=[("psY", yb_)], sig=(a == 3))
                    P.op("dve", lambda h: h.scalar_tensor_tensor(yv[yb_][:], uc[us][:, bs], d1[:, j:j + 1], psY[yb_][:, :], ALU.mult, ALU.add),
                         reads=[("uc", us), "d1", ("psY", yb_)], writes=[("yv", yb_)])
                    P.op("act", lambda h: h.activation(ge1[yb_][:], yv[yb_][:], AF.Square), reads=[("yv", yb_)], writes=[("ge1", yb_)])
                    P.op("pool", lambda h: h.tensor_scalar(ge1[yb_][:], ge1[yb_][:], 0.044715, 1.0, ALU.mult, ALU.add), reads=[("ge1", yb_)], writes=[("ge1", yb_)])
                    P.op("pool", lambda h: h.tensor_tensor(ge2[yb_][:], ge1[yb_][:], yv[yb_][:], ALU.mult), reads=[("ge1", yb_), ("yv", yb_)], writes=[("ge2", yb_)])
                    P.op("act", lambda h: h.activation(ge2[yb_][:], ge2[yb_][:], AF.Sigmoid, scale=1.5957691216057308), reads=[("ge2", yb_)], writes=[("ge2", yb_)])
                    P.op("pool", lambda h: h.tensor_tensor(yo[yb_][:], ge2[yb_][:], yv[yb_][:], ALU.mult), reads=[("ge2", yb_), ("yv", yb_)], writes=[("yo", yb_)])
                    P.dma("sp", self.ygT[j * 128:(j + 1) * 128, bs], yo[yb_][:], reads=[("yo", yb_)], writes=[("ygT", j, b)])

    def stage_G1(self):
        P, nc = self.P, self.nc
        ygv = self.ygT.rearrange("(k p) t -> p k t", p=128)
        with self.stage():
            xTb = self.sb("xTb", [128, 16, T], BF16)
            for k in range(16):
                P.dma("sp", xTb[:, k, :], ygv[:, k, :], writes=[("xTb", k)])
            wa = [self.sb(f"wa{i}", [128, 16, 128], BF16) for i in range(2)]
            wb = [self.sb(f"wb{i}", [128, 16, 128], BF16) for i in range(2)]
            stga = [self.sb(f"stga{i}", [128, 16, 128], F32) for i in range(2)]
            stgb = [self.sb(f"stgb{i}", [128, 16, 128], F32) for i in range(2)]
            psA = [[self.ps(f"psA{i}{a}", [128, 512]) for a in "ab"] for i in range(2)]
            sgt = [self.sb(f"sgt{i}", [128, 512], BF16) for i in range(2)]
            sgb = [self.sb(f"sgb{i}", [128, 512], F32) for i in range(2)]
            tt_ = [self.sb(f"tt{i}", [128, 512], F32) for i in range(2)]
            ob = [self.sb(f"ob{i}", [128, 512], BF16) for i in range(2)]
            c2 = 0
            def issue_w(ib):
                sl = ib % 2
                P.dma("sp", stga[sl][:], self.w_glu[ib], writes=[("stga", sl)])
                P.op("pool", lambda h: h.tensor_copy(wa[sl][:], stga[sl][:]), reads=[("stga", sl)], writes=[("wa", sl)])
                P.dma("sp", stgb[sl][:], self.w_glu[16 + ib], writes=[("stgb", sl)])
                P.op("pool", lambda h: h.tensor_copy(wb[sl][:], stgb[sl][:]), reads=[("stgb", sl)], writes=[("wb", sl)])

            issue_w(0)
            for i in range(16):
                slot = i % 2
                if i + 1 < 16:
                    issue_w(i + 1)
                for g in range(NG):
                    b = c2 % 2
                    c2 += 1
                    gs_ = slice(g * 512, (g + 1) * 512)
                    P.dma("sp", sgt[b][:], self.sg1T[i * 128:(i + 1) * 128, gs_], writes=[("sgt", b)])
                    for a, wt, wn in ((0, wa, "wa"), (1, wb, "wb")):
                        for k in range(16):
                            P.op("pe", lambda h: h.matmul(psA[b][a][:, :], wt[slot][:, k, :], xTb[:, k, gs_], start=(k == 0), stop=(k == 15)),
                                 reads=[(wn, slot), ("xTb", k)], writes=[("psA", b, a)], sig=(k == 15))
                    P.op("act", lambda h: h.activation(sgb[b][:], psA[b][1][:, :], AF.Sigmoid), reads=[("psA", b, 1)], writes=[("sgb", b)])
                    P.op("dve", lambda h: h.tensor_tensor(tt_[b][:], psA[b][0][:, :], sgb[b][:], ALU.mult), reads=[("psA", b, 0), ("sgb", b)], writes=[("tt", b)])
                    P.op("pool", lambda h: h.tensor_tensor(ob[b][:], tt_[b][:], sgt[b][:], ALU.mult), reads=[("tt", b), ("sgt", b)], writes=[("ob", b)])
                    P.dma("sp", self.y2T[i * 128:(i + 1) * 128, gs_], ob[b][:], reads=[("ob", b)], writes=[("y2T", i, g)])

    def stage_G2(self):
        self.outproj_ln([(self.y2T, 0, 16, False)], self.w_out1, 16, self.x1, 1, self.out)

    def build(self, stages="AaBCDEFGH"):
        self.declare()
        for ch, fn in (("A", self.stage_A), ("a", self.stage_A2), ("B", self.stage_B), ("C", self.stage_C),
                       ("D", self.stage_D), ("E", self.stage_E), ("F", self.stage_F), ("G", self.stage_G1),
                       ("H", self.stage_G2)):
            if ch in stages:
                fn()
        return self.nc


def _rope_tables():
    half = 16
    inv_freq = (500000.0 ** (-(np.arange(half, dtype=np.float32) * 2.0 / 32))).astype(np.float32)
    pos = np.arange(T, dtype=np.float32)
    ang = (pos[None, :] * inv_freq[:, None]).astype(np.float32)
    c = np.cos(ang).astype(np.float32)
    s = np.sin(ang).astype(np.float32)
    return np.ascontiguousarray(np.concatenate([c, c], 0)), np.ascontiguousarray(np.concatenate([-s, s], 0))


def _constants():
    cosT, sinS = _rope_tables()
    pm = np.zeros((32, 32), np.float32)
    for m in range(32):
        pm[(m + 16) % 32, m] = 1.0
    sel = np.zeros((32, 32, 128), np.float32)
    for h in range(32):
        sel[h, h, :] = 1.0
    triu = np.triu(np.ones((128, 128), np.float32))
    maskg = np.ascontiguousarray(np.concatenate([triu, np.ones((128, 128), np.float32), triu], 1))
    vb = np.zeros((128, NT, NCH), np.float32)
    for qt in range(NT):
        vb[:, qt, qt // 2:] = -1e30
    rowmask = np.zeros((128, 8), np.float32)
    for p in range(128):
        rowmask[p, p // 16] = 1.0
    return dict(cosT=cosT, sinS=sinS, pm32=pm, sel=sel, triu=triu, maskg=maskg, ident=np.eye(128, dtype=np.float32),
                vbias=np.ascontiguousarray(vb.reshape(128, NT * NCH)), rowmask=rowmask,
                iota513=np.arange(513, dtype=np.float32).reshape(1, 513))


def _shared_inputs(inp):
    f = lambda a: np.ascontiguousarray(np.asarray(a, dtype=np.float32))
    d = {}
    w0 = np.asarray(inp["in0_w"][0], dtype=np.float32)
    def tile_cols(w, c0, m):
        return w[:, c0:c0 + m].reshape(16, 128, m).transpose(1, 0, 2)
    cols_a = [C_XBC + 128 * i for i in range(24)] + [C_Z + 128 * i for i in range(16)] + \
             [C_Q + 128 * i for i in range(16)] + [C_K + 128 * i for i in range(16)]
    d["w0a"] = f(np.stack([tile_cols(w0, c, 128) for c in cols_a], 0))
    d["w0b"] = f(np.stack([tile_cols(w0, C_V + 512 * i, 512) for i in range(4)] +
                          [tile_cols(w0, C_G + 512 * i, 512) for i in range(4)], 0))
    d["w0dt"] = f(tile_cols(w0, C_DT, 32))
    cwv = np.asarray(inp["conv_w"][0])
    d["cw"] = f(cwv.T.reshape(24, 128, 4).transpose(1, 0, 2))
    d["cb"] = f(np.asarray(inp["conv_b"][0]).reshape(24, 128).T)
    d["dtb"] = f(inp["dt_bias"])
    d["alog"] = f(inp["a_log"])
    dsk = np.asarray(inp["ssd_d"][0])
    d["ssd_dp"] = f(np.repeat(dsk.reshape(16, 2), 64, axis=1).T)
    d["normg"] = f(np.asarray(inp["ssd_norm_g"][0]).reshape(16, 128).T)
    d["out0_w"] = f(inp["out0_w"][0])
    d["ln_g"] = f(inp["ln_g"])
    d["ln_b"] = f(inp["ln_b"])
    w1 = np.asarray(inp["in1_w"][0], dtype=np.float32)
    d["w1t"] = f(np.stack([tile_cols(w1, 128 * i, 128) for i in range(32)], 0))
    wg = np.asarray(inp["glu_w"][0], dtype=np.float32)
    d["wgt"] = f(np.stack([tile_cols(wg, 128 * i, 128) for i in range(32)], 0))
    d["out1_w"] = f(inp["out1_w"][0])
    lre, lim = np.asarray(inp["s5_lam_re"][0]), np.asarray(inp["s5_lam_im"][0])
    ldt = np.asarray(inp["s5_log_dt"][0])
    bre, bim = np.asarray(inp["s5_b_re"][0]), np.asarray(inp["s5_b_im"][0])
    cre, cim = np.asarray(inp["s5_c_re"][0]), np.asarray(inp["s5_c_im"][0])
    ldt_n = np.repeat(ldt[:, None], 64, axis=1)
    pl = np.stack([a.reshape(64, 2, 64).transpose(1, 2, 0).reshape(128, 64) for a in (lre, lim, ldt_n)], 1)
    d["s5_pl"] = f(pl)
    def wl_gn(a):
        return np.repeat(a.reshape(16, 8, 1, 64), 16, axis=2).transpose(1, 2, 0, 3).reshape(128, 16, 64)
    def wl_gnm(a):
        return a.reshape(16, 8, 64, 16).transpose(1, 3, 0, 2).reshape(128, 16, 64)
    d["s5_wl"] = f(np.stack([wl_gn(lre), wl_gn(lim), wl_gn(ldt_n), wl_gnm(bre), wl_gnm(bim)], 1))
    def cp_(a):
        return a.reshape(64, 2, 16, 64).transpose(1, 3, 0, 2).reshape(128, 64, 16)
    d["s5_cp"] = f(np.stack([cp_(cre), cp_(cim)], 1))
    d["s5_d1"] = f(np.asarray(inp["s5_d"][0]).reshape(16, 128).T)
    d.update(_constants())
    return d


N_CORES = 4


def kernel(**inputs):
    x = np.asarray(inputs["x"], dtype=np.float32)
    shared = _shared_inputs(inputs)
    nc = bass.Bass("TRN2", target_bir_lowering=False)
    mk = MK(nc)
    mk.build("AaBCDEFGH")
    in_maps = []
    for b in range(N_CORES):
        m = dict(shared)
        m["x"] = np.ascontiguousarray(x[b])
        m["xT"] = np.ascontiguousarray(x[b].T)
        in_maps.append(m)
    res = run_bass_kernel_spmd(nc, in_maps, core_ids=list(range(N_CORES)))
    return np.stack([np.asarray(res.results[b]["out"], dtype=np.float32) for b in range(N_CORES)], 0)
```

```python
import math
from contextlib import contextmanager, ExitStack
import numpy as np
import concourse.bass as bass
import concourse.mybir as mybir
from concourse.bass_utils import run_bass_kernel_spmd

F32 = mybir.dt.float32
BF16 = mybir.dt.bfloat16
AF = mybir.ActivationFunctionType
ALU = mybir.AluOpType
AX = mybir.AxisListType

T = 4096
NT = T // 128
NG = T // 512
NCH = T // 256
D = 2048
IN0 = 13344
C_Z, C_XBC, C_DT, C_Q, C_K, C_V, C_G = 0, 2048, 5120, 5152, 7200, 9248, 11296
ALPHA = 4.0 ** 0.25
SEM_CAP = 30000


class Ev:
    __slots__ = ("eng", "sem", "val", "big")

    def __init__(self, eng, sem, val, big=False):
        self.eng = eng
        self.sem = sem
        self.val = val
        self.big = big


class Prog:
    def __init__(self, nc, n_dma_sems=10):
        self.nc = nc
        self.h = {"pe": nc.tensor, "act": nc.scalar, "dve": nc.vector, "pool": nc.gpsimd, "sp": nc.sync}
        self.sem = {}
        self.cnt = {}
        self.gen = {}
        self.waited = {e: {} for e in self.h}
        self._cms = []
        for e in self.h:
            self._new_sem(e)
        self.last_w = {}
        self.readers = {}
        self.bank_last = {}
        self.bankmap = {}
        self.dma_sems = {}
        self.dma_next = {}
        for e in ("sp", "pool", "act"):
            self.dma_sems[e] = [[self._alloc(f"dma_{e}_{i}"), 0] for i in range(n_dma_sems)]
            self.dma_next[e] = 0
        self.n_inst = 0

    def _alloc(self, name):
        cm = self.nc.semaphore(name)
        s = cm.__enter__()
        self._cms.append(cm)
        return s

    def _new_sem(self, e):
        g = self.gen.get(e, -1) + 1
        self.gen[e] = g
        self.sem[e] = self._alloc(f"s_{e}_{g}")
        self.cnt[e] = 0

    def _deps(self, reads, writes, e=None, big=False):
        deps = []
        for r in reads:
            ev = self.last_w.get(r)
            if ev is not None:
                deps.append(ev)
        for w in writes:
            ev = self.last_w.get(w)
            if ev is not None:
                deps.append(ev)
        if e in ("act", "dve", "pool"):
            strong = [ev for ev in deps if ev.eng == e and not (big and ev.big)]
            self._wait(e, strong, same_engine_ok=False)
        for w in writes:
            deps.extend(self.readers.get(w, ()))
        return deps

    def _wait(self, e, deps, same_engine_ok=True):
        wd = self.waited[e]
        need = {}
        for ev in deps:
            if same_engine_ok and ev.eng == e:
                continue
            k = id(ev.sem)
            if wd.get(k, 0) >= ev.val:
                continue
            if k not in need or need[k].val < ev.val:
                need[k] = ev
        for k, ev in need.items():
            self.h[e].wait_ge(ev.sem, ev.val)
            wd[k] = ev.val
            self.n_inst += 1

    def _commit(self, ev, reads, writes):
        for r in reads:
            lst = self.readers.setdefault(r, [])
            lst[:] = [x for x in lst if x.sem is not ev.sem]
            lst.append(ev)
        for w in writes:
            self.last_w[w] = ev
            self.readers[w] = []

    def op(self, e, fn, reads=(), writes=(), sig=True, banks=(), strict=(), big=False):
        if strict:
            sdeps = [self.last_w[r] for r in strict if r in self.last_w]
            self._wait(e, sdeps, same_engine_ok=False)
            reads = list(reads) + list(strict)
        deps = self._deps(reads, writes, e, big)
        banks = set(banks)
        for r in list(reads) + list(writes):
            nm = r if isinstance(r, str) else r[0]
            if isinstance(nm, str) and (nm.startswith("ps") or nm.startswith("ptr")):
                banks.add(self.bankmap.get(r, r))
        for b in banks:
            for eng, bev in self.bank_last.setdefault(b, {}).items():
                if eng != e:
                    deps.append(bev)
        self._wait(e, deps)
        if sig and self.cnt[e] >= SEM_CAP:
            self._new_sem(e)
        inst = fn(self.h[e])
        self.n_inst += 1
        if sig:
            self.cnt[e] += 1
            inst.then_inc(self.sem[e], 1)
            ev = Ev(e, self.sem[e], self.cnt[e], big)
        else:
            ev = Ev(e, self.sem[e], self.cnt[e] + 1, big)
        self._commit(ev, reads, writes)
        for b in banks:
            self.bank_last[b][e] = ev
        return ev

    def dma(self, e, out, in_, reads=(), writes=(), **kw):
        deps = self._deps(reads, writes)
        i = self.dma_next[e]
        self.dma_next[e] = (i + 1) % len(self.dma_sems[e])
        slot = self.dma_sems[e][i]
        if slot[1] > 0:
            deps.append(Ev("dma", slot[0], slot[1]))
        if slot[1] + 16 > SEM_CAP:
            self._wait(e, deps, same_engine_ok=False)
            deps = []
            slot[0] = self._alloc(f"dma_{e}_{i}_{self.n_inst}")
            slot[1] = 0
        self._wait(e, deps, same_engine_ok=False)
        inst = self.h[e].dma_start(out=out, in_=in_, **kw)
        slot[1] += 16
        inst.then_inc(slot[0], 16)
        self.n_inst += 1
        ev = Ev("dma", slot[0], slot[1])
        self._commit(ev, reads, writes)
        return ev

    def barrier(self):
        evs = [Ev(e, self.sem[e], self.cnt[e]) for e in self.h if self.cnt[e] > 0]
        for e in self.dma_sems:
            for s, v in self.dma_sems[e]:
                if v > 0:
                    evs.append(Ev("dma", s, v))
        for e in self.h:
            self._wait(e, evs, same_engine_ok=True)
        self.last_w.clear()
        self.readers.clear()
        self.bank_last.clear()


class MK:
    def __init__(self, nc, dbg=(), feed=()):
        self._lazy = {}
        self.feed = set(feed)
        self.nc = nc
        self.P = Prog(nc)
        self.dbg = set(dbg)
        self._es = None
        self._sid = 0
        self.dr = {}

    @contextmanager
    def stage(self):
        self._sid += 1
        es = ExitStack()
        self._es = es
        try:
            yield
        finally:
            self.P.barrier()
            es.close()

    def sb(self, name, shape, dt):
        return self._es.enter_context(self.nc.sbuf_tensor(f"{name}_s{self._sid}", shape, dt))

    def ps(self, name, shape, dt=F32):
        return self._es.enter_context(self.nc.psum_tensor(f"{name}_s{self._sid}", shape, dt))

    def din(self, name, shape, dt=F32):
        t = self.nc.dram_tensor(name, list(shape), dt, kind="ExternalInput").ap()
        self.dr[name] = t
        return t

    def dout(self, name, shape, dt=F32):
        t = self.nc.dram_tensor(name, list(shape), dt, kind="ExternalOutput").ap()
        self.dr[name] = t
        return t

    def scratch(self, name, shape, dt):
        kind = "ExternalOutput" if name in self.dbg else "Internal"
        t = self.nc.dram_tensor(name, list(shape), dt, kind=kind).ap()
        self.dr[name] = t
        return t

    def __getattr__(self, attr):
        lz = self.__dict__.get("_lazy", {})
        if attr in lz:
            kind, name, shape, dt = lz[attr]
            if kind == "in" or (kind == "scratch" and name in self.feed):
                t = self.din(name, shape, dt)
            elif kind == "out":
                t = self.dout(name, shape, dt)
            else:
                t = self.scratch(name, shape, dt)
            self.__dict__[attr] = t
            return t
        raise AttributeError(attr)

    def declare(self):
        self._lazy["x"] = ("in", "x", [T, D], F32)
        self._lazy["xT"] = ("in", "xT", [D, T], F32)
        self._lazy["w0a"] = ("in", "w0a", [72, 128, 16, 128], F32)
        self._lazy["w0b"] = ("in", "w0b", [8, 128, 16, 512], F32)
        self._lazy["w0dt"] = ("in", "w0dt", [128, 16, 32], F32)
        self._lazy["cw"] = ("in", "cw", [128, 24, 4], F32)
        self._lazy["cb"] = ("in", "cb", [128, 24], F32)
        self._lazy["dtb"] = ("in", "dtb", [1, 32], F32)
        self._lazy["alog"] = ("in", "alog", [1, 32], F32)
        self._lazy["ssd_dp"] = ("in", "ssd_dp", [128, 16], F32)
        self._lazy["normg"] = ("in", "normg", [128, 16], F32)
        self._lazy["cosT"] = ("in", "cosT", [32, T], F32)
        self._lazy["sinS"] = ("in", "sinS", [32, T], F32)
        self._lazy["pm32"] = ("in", "pm32", [32, 32], F32)
        self._lazy["sel"] = ("in", "sel", [32, 32, 128], F32)
        self._lazy["triu"] = ("in", "triu", [128, 128], F32)
        self._lazy["maskg"] = ("in", "maskg", [128, 384], F32)
        self._lazy["ident"] = ("in", "ident", [128, 128], F32)
        self._lazy["vbias"] = ("in", "vbias", [128, NT * NCH], F32)
        self._lazy["w_out0"] = ("in", "out0_w", [2 * D, D], F32)
        self._lazy["ln_g"] = ("in", "ln_g", [2, D], F32)
        self._lazy["ln_b"] = ("in", "ln_b", [2, D], F32)
        self._lazy["w_in1"] = ("in", "w1t", [32, 128, 16, 128], F32)
        self._lazy["w_glu"] = ("in", "wgt", [32, 128, 16, 128], F32)
        self._lazy["w_out1"] = ("in", "out1_w", [D, D], F32)
        self._lazy["out"] = ("out", "out", [T, D], F32)
        self._lazy["s5_pl"] = ("in", "s5_pl", [128, 3, 64], F32)
        self._lazy["s5_wl"] = ("in", "s5_wl", [128, 5, 16, 64], F32)
        self._lazy["s5_cp"] = ("in", "s5_cp", [128, 2, 64, 16], F32)
        self._lazy["s5_d1"] = ("in", "s5_d1", [128, 16], F32)
        self._lazy["rowmask"] = ("in", "rowmask", [128, 8], F32)
        self._lazy["iota513"] = ("in", "iota513", [1, 513], F32)
        self._lazy["xsT"] = ("scratch", "xsT", [D, T], BF16)
        self._lazy["BT"] = ("scratch", "BT", [512, T], BF16)
        self._lazy["CT"] = ("scratch", "CT", [512, T], BF16)
        self._lazy["zs"] = ("scratch", "zs", [D, T], BF16)
        self._lazy["qT16"] = ("scratch", "qT16", [16, 128, T], BF16)
        self._lazy["kT16"] = ("scratch", "kT16", [16, 128, T], BF16)
        self._lazy["qT32"] = ("scratch", "qT32", [16, 128, T], F32)
        self._lazy["kmean"] = ("scratch", "kmean", [128, 16, NCH], F32)
        self._lazy["V1"] = ("scratch", "V1", [T, 16, 129], BF16)
        self._lazy["gs"] = ("scratch", "gs", [T, D], BF16)
        self._lazy["dtk"] = ("scratch", "dtk", [128, NT, 32], F32)
        self._lazy["yaT"] = ("scratch", "yaT", [D, T], BF16)
        self._lazy["rstd_s"] = ("scratch", "rstd_s", [128, NT], F32)
        self._lazy["ybT"] = ("scratch", "ybT", [D, T], BF16)
        self._lazy["x1"] = ("scratch", "x1", [T, D], F32)
        self._lazy["x1T"] = ("scratch", "x1T", [D, T], BF16)
        self._lazy["uT"] = ("scratch", "uT", [D, T], BF16)
        self._lazy["sg1T"] = ("scratch", "sg1T", [D, T], BF16)
        self._lazy["ygT"] = ("scratch", "ygT", [D, T], BF16)
        self._lazy["y2T"] = ("scratch", "y2T", [D, T], BF16)

    def stage_A(self):
        P, nc = self.P, self.nc
        blk_of = {}
        for i in range(24):
            blk_of[C_XBC + 128 * i] = i
        for i in range(16):
            blk_of[C_Z + 128 * i] = 24 + i
            blk_of[C_Q + 128 * i] = 40 + i
            blk_of[C_K + 128 * i] = 56 + i
        with self.stage():
            xTb = self.sb("xTb", [128, 16, T], BF16)
            for k in range(16):
                P.dma("pool", xTb[:, k, :], self.xT[k * 128:(k + 1) * 128, :], writes=[("xTb", k)])
            wsl = [self.sb(f"wsl{i}", [128, 16, 128], BF16) for i in range(2)]
            psA = [self.ps(f"psA{i}", [128, 512]) for i in range(4)]
            psw = [self.ps(f"psw{i}", [32, 512]) for i in range(2)]
            raw = [self.sb(f"raw{i}", [128, 515], F32) for i in range(2)]
            acc = [self.sb(f"acc{i}", [128, 512], F32) for i in range(2)]
            ob = [self.sb(f"ob{i}", [128, 512], BF16) for i in range(3)]
            qf = [self.sb(f"qf{i}", [128, 512], F32) for i in range(2)]
            tmp32 = [self.sb(f"tmp32{i}", [32, 512], F32) for i in range(2)]
            cw = self.sb("cw", [128, 24, 4], F32)
            cb = self.sb("cb", [128, 24], F32)
            cosT = self.sb("cosT", [32, T], F32)
            sinS = self.sb("sinS", [32, T], F32)
            pm = self.sb("pm", [32, 32], F32)
            kms = self.sb("kms", [128, 16, NCH], F32)
            P.dma("sp", cw[:], self.cw, writes=["cw"])
            P.dma("sp", cb[:], self.cb, writes=["cb"])
            P.dma("sp", cosT[:], self.cosT, writes=["cosT"])
            P.dma("sp", sinS[:], self.sinS, writes=["sinS"])
            P.dma("sp", pm[:], self.pm32, writes=["pm"])

            ctr = {"blk": 0, "grp": 0, "ob": 0, "qf": 0}

            stg = [self.sb(f"stg{i}", [128, 16, 128], F32) for i in range(2)]

            order_a = [C_XBC + 128 * i for i in range(24)] + [C_Z + 128 * i for i in range(16)] + \
                      [C_Q + 128 * i for i in range(16)] + [C_K + 128 * i for i in range(16)]

            def issue_w(jb):
                slot = jb % 2
                P.dma("sp", stg[slot][:, :, :], self.w0a[blk_of[order_a[jb]]], writes=[("stg", slot)])
                P.op("pool", lambda h: h.tensor_copy(wsl[slot][:, :, :], stg[slot][:, :, :]),
                     reads=[("stg", slot)], writes=[("w", slot)])

            def load_w(c0, M):
                jb = ctr["blk"]
                ctr["blk"] += 1
                assert order_a[jb] == c0
                if jb == 0:
                    issue_w(0)
                if jb + 1 < len(order_a):
                    issue_w(jb + 1)
                return jb % 2

            def mm_group(slot, M, g):
                b = ctr["grp"] % 4
                ctr["grp"] += 1
                for k in range(16):
                    P.op("pe", lambda h, k=k: h.matmul(psA[b][:M, :], wsl[slot][:, k, :M],
                                                       xTb[:, k, g * 512:(g + 1) * 512],
                                                       start=(k == 0), stop=(k == 15)),
                         reads=[("w", slot), ("xTb", k)], writes=[("psA", b)], sig=(k == 15))
                return b

            def next_ob():
                i = ctr["ob"] % 3
                ctr["ob"] += 1
                return i

            for i in range(24):
                slot = load_w(C_XBC + 128 * i, 128)
                P.op("pool", lambda h: h.memset(raw[0][:, 0:3], 0.0), writes=[("rawh", 0)])
                for g in range(NG):
                    b = mm_group(slot, 128, g)
                    r = g % 2
                    a = g % 2
                    P.op("act", lambda h: h.activation(raw[r][:, 3:515], psA[b][:, :], AF.Copy),
                         reads=[("psA", b)], writes=[("rawm", r)])
                    P.op("pool", lambda h: h.tensor_copy(raw[1 - r][:, 0:3], raw[r][:, 512:515]),
                         reads=[("rawm", r)], writes=[("rawh", 1 - r)])
                    P.op("dve", lambda h: h.tensor_scalar(acc[a][:, :], raw[r][:, 3:515], cw[:, i, 3:4], cb[:, i:i + 1],
                                                          ALU.mult, ALU.add),
                         reads=[("rawm", r), "cw", "cb"], writes=[("acc", a)])
                    for j in (2, 1, 0):
                        P.op("dve", lambda h, j=j: h.scalar_tensor_tensor(acc[a][:, :], raw[r][:, j:j + 512], cw[:, i, j:j + 1],
                                                                          acc[a][:, :], ALU.mult, ALU.add),
                             reads=[("rawm", r), ("rawh", r), "cw"], writes=[("acc", a)])
                    o = next_ob()
                    P.op("act", lambda h: h.activation(ob[o][:, :], acc[a][:, :], AF.Silu),
                         reads=[("acc", a)], writes=[("ob", o)])
                    if i < 16:
                        dst = self.xsT[i * 128:(i + 1) * 128, g * 512:(g + 1) * 512]
                    elif i < 20:
                        dst = self.BT[(i - 16) * 128:(i - 15) * 128, g * 512:(g + 1) * 512]
                    else:
                        dst = self.CT[(i - 20) * 128:(i - 19) * 128, g * 512:(g + 1) * 512]
                    P.dma("sp", dst, ob[o][:, :], reads=[("ob", o)], writes=[("xbc_out", i, g)])
            for i in range(16):
                slot = load_w(C_Z + 128 * i, 128)
                for g in range(NG):
                    b = mm_group(slot, 128, g)
                    o = next_ob()
                    P.op("act", lambda h: h.activation(ob[o][:, :], psA[b][:, :], AF.Silu),
                         reads=[("psA", b)], writes=[("ob", o)])
                    P.dma("sp", self.zs[i * 128:(i + 1) * 128, g * 512:(g + 1) * 512], ob[o][:, :],
                          reads=[("ob", o)], writes=[("zs", i, g)])
            for hq in range(32):
                is_q = hq < 16
                hd = hq % 16
                slot = load_w((C_Q if is_q else C_K) + 128 * hd, 128)
                for g in range(NG):
                    b = mm_group(slot, 128, g)
                    f = ctr["qf"] % 2
                    ctr["qf"] += 1
                    gs_ = slice(g * 512, (g + 1) * 512)
                    P.op("act", lambda h: h.activation(qf[f][:, :], psA[b][:, :], AF.Copy),
                         reads=[("psA", b)], writes=[("qf", f)])
                    P.op("pe", lambda h: h.matmul(psw[f][:, :], pm[:, :], qf[f][0:32, :], start=True, stop=True),
                         reads=[("qf", f), "pm"], writes=[("psw", f)])
                    P.op("dve", lambda h: h.tensor_tensor(tmp32[f][:, :], psw[f][:, :], sinS[:, gs_], ALU.mult),
                         reads=[("psw", f), "sinS"], writes=[("tmp32", f)])
                    P.op("dve", lambda h: h.tensor_tensor(qf[f][0:32, :], qf[f][0:32, :], cosT[:, gs_], ALU.mult),
                         reads=[("qf", f), "cosT"], writes=[("qf", f)])
                    P.op("dve", lambda h: h.tensor_tensor(qf[f][0:32, :], qf[f][0:32, :], tmp32[f][:, :], ALU.add),
                         reads=[("qf", f), ("tmp32", f)], writes=[("qf", f)])
                    o = next_ob()
                    P.op("act", lambda h: h.activation(ob[o][:, :], qf[f][:, :], AF.Copy),
                         reads=[("qf", f)], writes=[("ob", o)])
                    if is_q:
                        P.dma("sp", self.qT16[hd, :, gs_], ob[o][:, :], reads=[("ob", o)], writes=[("q16", hd, g)])
                        P.dma("sp", self.qT32[hd, :, gs_], qf[f][:, :], reads=[("qf", f)], writes=[("q32", hd, g)])
                    else:
                        P.dma("sp", self.kT16[hd, :, gs_], ob[o][:, :], reads=[("ob", o)], writes=[("k16", hd, g)])
                        P.op("dve", lambda h: h.tensor_reduce(kms[:, hd, 2 * g:2 * g + 2],
                                                              qf[f][:, :].rearrange("p (a b) -> p a b", b=256),
                                                              AX.X, ALU.add),
                             reads=[("qf", f)], writes=["kms"])
            P.op("dve", lambda h: h.tensor_scalar(kms[:, :, :], kms[:, :, :], 1.0 / 256.0, None, ALU.mult),
                 reads=["kms"], writes=["kms"])
            P.dma("sp", self.kmean, kms[:, :, :], reads=["kms"], writes=["kmean"])

    def stage_A2(self):
        P, nc = self.P, self.nc
        with self.stage():
            xTb = self.sb("xTb", [128, 16, T], BF16)
            for k in range(16):
                P.dma("pool", xTb[:, k, :], self.xT[k * 128:(k + 1) * 128, :], writes=[("xTb", k)])
            wb = [self.sb(f"wb{i}", [128, 16, 512], BF16) for i in range(2)]
            psA = [self.ps(f"psA{i}", [128, 512]) for i in range(4)]
            ob = [self.sb(f"ob{i}", [128, 512], BF16) for i in range(3)]
            vt = [self.sb(f"vt{i}", [128, 4, 129], BF16) for i in range(3)]
            dtb = self.sb("dtb", [128, 32], F32)
            dtt = self.sb("dtt", [128, NT, 32], F32)
            P.dma("sp", dtb[:], self.dtb.partition_broadcast(128), writes=["dtb"])
            for i in range(3):
                P.op("pool", lambda h, i=i: h.memset(vt[i][:, :, :], 1.0), writes=[("vt", i)])
            ctr = {"blk": 0, "grp": 0, "ob": 0}

            stg = [self.sb(f"stg{i}", [128, 4, 512], F32) for i in range(2)]
            cst = {"n": 0}

            order_b = [(C_DT, 32)] + [(C_V + 512 * i, 512) for i in range(4)] + [(C_G + 512 * i, 512) for i in range(4)]

            def issue_w(jb):
                c0, M = order_b[jb]
                slot = jb % 2
                for q in range(4):
                    ss = cst["n"] % 2
                    cst["n"] += 1
                    src = self.w0dt[:, 4 * q:4 * q + 4, :] if c0 == C_DT else \
                        self.w0b[((c0 - C_V) // 512) if c0 < C_G else (4 + (c0 - C_G) // 512), :, 4 * q:4 * q + 4, :]
                    P.dma("sp", stg[ss][:, :, :M], src, writes=[("stg", ss)])
                    P.op("pool", lambda h: h.tensor_copy(wb[slot][:, 4 * q:4 * q + 4, :M], stg[ss][:, :, :M]),
                         reads=[("stg", ss)], writes=[("w", slot)])

            def load_w(c0, M):
                jb = ctr["blk"]
                ctr["blk"] += 1
                assert order_b[jb] == (c0, M)
                if jb == 0:
                    issue_w(0)
                if jb + 1 < len(order_b):
                    issue_w(jb + 1)
                return jb % 2

            slot = load_w(C_DT, 32)
            for tt in range(NT):
                b = tt // 16
                for k in range(16):
                    P.op("pe", lambda h, k=k: h.matmul(psA[b][:, (tt % 16) * 32:(tt % 16) * 32 + 32],
                                                       xTb[:, k, tt * 128:(tt + 1) * 128], wb[slot][:, k, :32],
                                                       start=(k == 0), stop=(k == 15)),
                         reads=[("w", slot), ("xTb", k)], writes=[("psA", b)], sig=(k == 15))
            for b in range(2):
                dv = dtt[:, b * 16:(b + 1) * 16, :]
                P.op("dve", lambda h: h.tensor_tensor(dv, psA[b][:, :].rearrange("p (a c) -> p a c", c=32),
                                                      dtb[:, None, :].broadcast_to([128, 16, 32]), ALU.add),
                     reads=[("psA", b), "dtb"], writes=[("dtt", b)])
                P.op("act", lambda h: h.activation(dv, dv, AF.Exp), reads=[("dtt", b)], writes=[("dtt", b)])
                P.op("act", lambda h: h.activation(dv, dv, AF.Ln, bias=1.0), reads=[("dtt", b)], writes=[("dtt", b)])
            P.dma("sp", self.dtk, dtt[:, :, :], reads=[("dtt", 0), ("dtt", 1)], writes=["dtk"])
            ctr["grp"] = 2
            for fam in ("v", "g"):
                for cg in range(4):
                    slot = load_w((C_V if fam == "v" else C_G) + 512 * cg, 512)
                    for tt in range(NT):
                        b = ctr["grp"] % 4
                        ctr["grp"] += 1
                        for k in range(16):
                            P.op("pe", lambda h, k=k: h.matmul(psA[b][:, :], xTb[:, k, tt * 128:(tt + 1) * 128],
                                                               wb[slot][:, k, :], start=(k == 0), stop=(k == 15)),
                                 reads=[("w", slot), ("xTb", k)], writes=[("psA", b)], sig=(k == 15))
                        o = ctr["ob"] % 3
                        ctr["ob"] += 1
                        if fam == "v":
                            P.op("act", lambda h: h.activation(vt[o][:, :, 0:128],
                                                               psA[b][:, :].rearrange("p (a c) -> p a c", c=128), AF.Copy),
                                 reads=[("psA", b)], writes=[("vt", o)])
                            P.dma("sp", self.V1[tt * 128:(tt + 1) * 128, 4 * cg:4 * cg + 4, :], vt[o][:, :, :],
                                  reads=[("vt", o)], writes=[("V1", tt, cg)])
                        else:
                            P.op("act", lambda h: h.activation(ob[o][:, :], psA[b][:, :], AF.Silu),
                                 reads=[("psA", b)], writes=[("ob", o)])
                            P.dma("sp", self.gs[tt * 128:(tt + 1) * 128, cg * 512:(cg + 1) * 512], ob[o][:, :],
                                  reads=[("ob", o)], writes=[("gs", tt, cg)])

    def stage_B(self):
        P, nc = self.P, self.nc
        xsTv = self.xsT.rearrange("(k p) t -> p k t", p=128)
        zsv = self.zs.rearrange("(k p) t -> p k t", p=128)
        BTv = self.BT.rearrange("(k p) t -> p k t", p=128)
        CTv = self.CT.rearrange("(k p) t -> p k t", p=128)
        with self.stage():
            sb = self.sb
            pb = [self.ps(f"pb{i}", [128, 512]) for i in range(6)]
            ptrs = [self.ps(f"ptr{i}", [128, 1024], BF16) for i in range(2)]
            ps_small, ps_T = pb[0][:, 0:96], pb[0][0:32, 128:384]
            ps_q = pb[0][:, 100:102]
            ps_g = [pb[1][:, 0:384]]
            ps_bc = [pb[2][:, 0:256], pb[3][:, 0:256]]
            ps_y = [pb[4][:, 0:256]]
            ps_st = [pb[5], pb[5]]
            P.bankmap = {"ps_small": "B0", "ps_q": "B0", ("ps_y", 0, 0): "BY", ("ps_y", 0, 1): "BY",
                         ("ps_st", 0): "BST", ("ps_st", 1): "BST"}
            xsc = [sb(f"xsc{i}", [128, 16, 256], BF16) for i in range(2)]
            zsc = [sb(f"zsc{i}", [128, 16, 256], BF16) for i in range(2)]
            btc = [sb(f"btc{i}", [128, 4, 256], BF16) for i in range(2)]
            ctc = [sb(f"ctc{i}", [128, 4, 256], BF16) for i in range(2)]
            dtt = sb("dtt", [128, NT, 32], F32)
            abc = sb("abc", [128, 32], F32)
            sel = sb("sel", [32, 32, 128], F32)
            triu = sb("triu", [128, 128], F32)
            ones = sb("ones", [128, 128], F32)
            r0 = sb("r0", [128, 256], F32)
            maskg = sb("maskg", [128, 384], F32)
            ident = sb("ident", [128, 128], BF16)
            dp = sb("dp", [128, 16], F32)
            ng = sb("ng", [128, 16], F32)
            atok = sb("atok", [128, 2, 32], F32)
            acum = sb("acum", [128, 3, 32], F32)
            acT = sb("acT", [32, 256], F32)
            dte = sb("dte", [128, 2, 32], F32)
            eAt = sb("eAt", [128, 32], F32)
            X = sb("X", [128, 2, 2048], BF16)
            Xd = sb("Xd", [128, 2, 2048], BF16)
            Btok = sb("Btok", [128, 2, 512], BF16)
            Gm = sb("Gm", [128, 4, 384], F32)
            H32 = sb("H32", [128, 32, 64], F32)
            H16 = sb("H16", [128, 32, 64], BF16)
            Dm = [sb(f"Dm{i}", [128, 384], F32) for i in range(2)]
            MT = [sb(f"MT{i}", [128, 384], BF16) for i in range(2)]
            eA = [sb(f"eA{i}", [128, 256], F32) for i in range(2)]
            Ct = [sb(f"Ct{i}", [128, 256], BF16) for i in range(2)]
            yf = [sb(f"yf{i}", [128, 256], F32) for i in range(2)]
            yg = [sb(f"yg{i}", [128, 256], F32) for i in range(2)]
            sq = [sb(f"sq{i}", [128, 256], F32) for i in range(2)]
            ya16 = [sb(f"ya16{i}", [128, 256], BF16) for i in range(2)]
            rs = sb("rs", [128, NT], F32)

            P.dma("sp", dtt[:], self.dtk, writes=["dtt"])
            P.dma("sp", abc[:], self.alog.partition_broadcast(128), writes=["abc"])
            P.dma("sp", sel[:], self.sel, writes=["sel"])
            P.dma("sp", triu[:], self.triu, writes=["triu"])
            P.dma("sp", maskg[:], self.maskg, writes=["maskg"])
            P.dma("sp", r0[:], self.maskg[:, 0:256], writes=["r0"])
            P.dma("pool", ident[:], self.ident, writes=["ident"])
            P.dma("sp", dp[:], self.ssd_dp, writes=["dp"])
            P.dma("sp", ng[:], self.normg, writes=["ng"])
            P.op("pool", lambda h: h.memset(ones[:], 1.0), writes=["ones"])
            P.op("pool", lambda h: h.memset(H32[:], 0.0), writes=["H32"])
            P.op("pool", lambda h: h.memset(H16[:], 0.0), writes=[("H16", g) for g in range(4)])
            P.op("act", lambda h: h.activation(abc[:], abc[:], AF.Exp), reads=["abc"], writes=["abc"])
            P.op("dve", lambda h: h.tensor_scalar(abc[:], abc[:], -1.0, None, ALU.mult), reads=["abc"], writes=["abc"])

            def load_chunk(c):
                s_ = c % 2
                cs = slice(c * 256, (c + 1) * 256)
                P.dma("sp", xsc[s_][:], xsTv[:, :, cs], writes=[("xsc", s_)])
                P.dma("sp", zsc[s_][:], zsv[:, :, cs], writes=[("zsc", s_)])
                P.dma("pool", btc[s_][:], BTv[:, :, cs], writes=[("btc", s_)])
                P.dma("pool", ctc[s_][:], CTv[:, :, cs], writes=[("ctc", s_)])

            bstop = getattr(self, 'bstop', 99)
            if bstop <= 1:
                return
            load_chunk(0)
            hc = 0
            for c in range(getattr(self, 'b_chunks', NCH)):
                s_ = c % 2
                if c + 1 < NCH:
                    load_chunk(c + 1)
                P.op("dve", lambda h: h.tensor_tensor(atok[:], dtt[:, 2 * c:2 * c + 2, :],
                                                      abc[:, None, :].broadcast_to([128, 2, 32]), ALU.mult),
                     reads=["dtt", "abc"], writes=["atok"])
                mm = lambda out, l, r, st, sp, sg: P.op(
                    "pe", lambda h: h.matmul(out, l, r, start=st, stop=sp), reads=["atok", "triu", "ones", "r0"],
                    writes=["ps_small"], sig=sg)
                mm(ps_small[:, 0:32], triu[:], atok[:, 0, :], True, True, False)
                mm(ps_small[:, 32:64], ones[:], atok[:, 0, :], True, False, False)
                mm(ps_small[:, 32:64], triu[:], atok[:, 1, :], False, True, False)
                mm(ps_small[:, 64:96], ones[:], atok[:, 0, :], True, False, False)
                mm(ps_small[:, 64:96], ones[:], atok[:, 1, :], False, True, False)
                mm(ps_T[:, :], atok[:, 0, :], r0[:], True, False, False)
                mm(ps_T[:, 128:256], atok[:, 1, :], triu[:], False, True, True)
                P.op("act", lambda h: h.activation(acum[:].rearrange("p a b -> p (a b)"), ps_small, AF.Copy),
                     reads=["ps_small"], writes=["acum"])
                P.op("act", lambda h: h.activation(acT[:], ps_T, AF.Copy), reads=["ps_small"], writes=["acT"])
                P.op("dve", lambda h: h.tensor_tensor(dte[:], acum[:, 2:3, :].broadcast_to([128, 2, 32]), acum[:, 0:2, :],
                                                      ALU.subtract), reads=["acum"], writes=["dte"])
                P.op("act", lambda h: h.activation(dte[:], dte[:], AF.Exp), reads=["dte"], writes=["dte"])
                P.op("act", lambda h: h.activation(eAt[:], acum[:, 2, :], AF.Exp), reads=["acum"], writes=["eAt"])
                if bstop <= 2:
                    return
                tb = 0
                bsub = getattr(self, 'bsub', 'xdb')
                for j in range(2):
                    for q4 in range(4):
                        half = tb % 2
                        tb += 1
                        for kk in range(4):
                            k = q4 * 4 + kk
                            P.op("pe", lambda h: h.transpose(ptrs[half][:, kk * 128:(kk + 1) * 128],
                                                             xsc[s_][:, k, j * 128:(j + 1) * 128], ident[:]),
                                 reads=[("xsc", s_), "ident"], writes=[("ptr", half)], sig=(kk == 3))
                        hs = slice(8 * q4, 8 * q4 + 8)
                        cs_ = slice(512 * q4, 512 * q4 + 512)
                        P.op("dve", lambda h: h.tensor_tensor(
                            X[:, j, cs_].rearrange("p (a b) -> p a b", b=64),
                            ptrs[half][:, 0:512].rearrange("p (a b) -> p a b", b=64),
                            dtt[:, 2 * c + j, hs].unsqueeze(2).broadcast_to([128, 8, 64]), ALU.mult),
                            reads=[("ptr", half), "dtt"], writes=[("X", j, q4)])
                        if 'd' in bsub:
                          P.op("pool", lambda h: h.tensor_tensor(
                            Xd[:, j, cs_].rearrange("p (a b) -> p a b", b=64),
                            X[:, j, cs_].rearrange("p (a b) -> p a b", b=64),
                            dte[:, j, hs].unsqueeze(2).broadcast_to([128, 8, 64]), ALU.mult),
                            reads=[("X", j, q4), "dte"], writes=[("Xd", j, q4)])
                    if 'b' not in bsub:
                        continue
                    half = tb % 2
                    tb += 1
                    for g in range(4):
                        P.op("pe", lambda h: h.transpose(ptrs[half][:, g * 128:(g + 1) * 128],
                                                         btc[s_][:, g, j * 128:(j + 1) * 128], ident[:]),
                             reads=[("btc", s_), "ident"], writes=[("ptr", half)], sig=(g == 3))
                    P.op("act", lambda h: h.activation(Btok[:, j, :], ptrs[half][:, 0:512], AF.Copy),
                         reads=[("ptr", half)], writes=[("Btok", j)])
                if bstop <= 3:
                    return
                for g in range(4):
                    gb = 0
                    P.op("pe", lambda h: h.matmul(ps_g[gb][:, 0:256], btc[s_][:, g, 0:128], ctc[s_][:, g, :],
                                                  start=True, stop=True),
                         reads=[("btc", s_), ("ctc", s_)], writes=[("ps_g", gb)], sig=False)
                    P.op("pe", lambda h: h.matmul(ps_g[gb][:, 256:384], btc[s_][:, g, 128:256], ctc[s_][:, g, 128:256],
                                                  start=True, stop=True),
                         reads=[("btc", s_), ("ctc", s_)], writes=[("ps_g", gb)])
                    P.op("dve", lambda h: h.tensor_tensor(Gm[:, g, :], ps_g[gb], maskg[:], ALU.mult),
                         reads=[("ps_g", gb), "maskg"], writes=[("Gm", g)])
                if bstop <= 4:
                    return
                def emit_bc(hd_):
                    tt_ = hd_ % 2
                    P.op("pe", lambda h: h.matmul(ps_bc[tt_], sel[:, hd_, :], acT[:], start=True, stop=True),
                         reads=["sel", "acT"], writes=[("ps_bc", tt_)])

                for hd in range(getattr(self, 'b_heads', 32)):
                    g = hd // 8
                    pair = hd // 2
                    hh = hd % 2
                    t_ = hd % 2
                    hcols = slice(hd * 64, (hd + 1) * 64)
                    if hd == 0:
                        emit_bc(0)
                    P.op("dve", lambda h: h.tensor_scalar(Dm[t_][:, 0:256], ps_bc[t_], acum[:, 0, hd:hd + 1], 0.0,
                                                          ALU.subtract, ALU.min),
                         reads=[("ps_bc", t_), "acum"], writes=[("Dm", t_)])
                    P.op("dve", lambda h: h.tensor_scalar(Dm[t_][:, 256:384], ps_bc[t_][:, 128:256], acum[:, 1, hd:hd + 1],
                                                          0.0, ALU.subtract, ALU.min),
                         reads=[("ps_bc", t_), "acum"], writes=[("Dm", t_)])
                    P.op("act", lambda h: h.activation(Dm[t_][:], Dm[t_][:], AF.Exp), reads=[("Dm", t_)], writes=[("Dm", t_)])
                    P.op("dve", lambda h: h.tensor_tensor(MT[t_][:], Dm[t_][:], Gm[:, g, :], ALU.mult),
                         reads=[("Dm", t_), ("Gm", g)], writes=[("MT", t_)])
                    P.op("act", lambda h: h.activation(eA[t_][:], ps_bc[t_], AF.Exp), reads=[("ps_bc", t_)], writes=[("eA", t_)])
                    P.op("pool", lambda h: h.tensor_tensor(Ct[t_][:], ctc[s_][:, g, :], eA[t_][:], ALU.mult),
                         reads=[("ctc", s_), ("eA", t_)], writes=[("Ct", t_)])
                    if hd + 1 < getattr(self, 'b_heads', 32):
                        emit_bc(hd + 1)
                    yb = 0
                    yo = ps_y[yb][hh * 64:(hh + 1) * 64, :]
                    yres = ("ps_y", yb, hh)
                    P.op("pe", lambda h: h.matmul(yo[:, 0:256], X[:, 0, hcols], MT[t_][:, 0:256], start=True, stop=False),
                         reads=[("X", 0, hd // 8), ("MT", t_)], writes=[yres], sig=False)
                    P.op("pe", lambda h: h.matmul(yo[:, 128:256], X[:, 1, hcols], MT[t_][:, 256:384], start=False, stop=False),
                         reads=[("X", 1, hd // 8), ("MT", t_)], writes=[yres], sig=False)
                    P.op("pe", lambda h: h.matmul(yo[:, 0:256], H16[:, hd, :], Ct[t_][:], start=False, stop=True),
                         reads=[("H16", g), ("Ct", t_)], writes=[yres])
                    so = ps_st[g % 2][:, (hd % 8) * 64:(hd % 8 + 1) * 64]
                    P.op("pe", lambda h: h.matmul(so, Btok[:, 0, g * 128:(g + 1) * 128], Xd[:, 0, hcols], start=True, stop=False),
                         reads=[("Btok", 0), ("Xd", 0, hd // 8)], writes=[("ps_st", g % 2)], sig=False)
                    P.op("pe", lambda h: h.matmul(so, Btok[:, 1, g * 128:(g + 1) * 128], Xd[:, 1, hcols], start=False, stop=True),
                         reads=[("Btok", 1), ("Xd", 1, hd // 8)], writes=[("ps_st", g % 2)])
                    if hd % 8 == 7:
                        hsl = slice(8 * g, 8 * g + 8)
                        P.op("dve", lambda h: h.tensor_tensor(H32[:, hsl, :], H32[:, hsl, :],
                                                              eAt[:, hsl].unsqueeze(2).broadcast_to([128, 8, 64]), ALU.mult),
                             reads=["H32", "eAt"], writes=["H32"])
                        P.op("dve", lambda h: h.tensor_tensor(H32[:, hsl, :], H32[:, hsl, :],
                                                              ps_st[g % 2][:, :].rearrange("p (a b) -> p a b", b=64), ALU.add),
                             reads=["H32", ("ps_st", g % 2)], writes=["H32"])
                        P.op("act", lambda h: h.activation(H16[:, hsl, :], H32[:, hsl, :], AF.Copy),
                             reads=["H32"], writes=[("H16", g)])
                    if hh == 1:
                        e_ = pair % 2
                        P.op("dve", lambda h: h.scalar_tensor_tensor(yf[e_][:], xsc[s_][:, pair, :], dp[:, pair:pair + 1],
                                                                     ps_y[yb], ALU.mult, ALU.add),
                             reads=[("xsc", s_), "dp", ("ps_y", yb, 0), ("ps_y", yb, 1)], writes=[("yf", e_)])
                        P.op("pool", lambda h: h.tensor_tensor(yg[e_][:], yf[e_][:], zsc[s_][:, pair, :], ALU.mult),
                             reads=[("yf", e_), ("zsc", s_)], writes=[("yg", e_)])
                        P.op("act", lambda h: h.activation(sq[e_][:], yg[e_][:], AF.Square), reads=[("yg", e_)], writes=[("sq", e_)])
                        for j in range(2):
                            P.op("pe", lambda h: h.matmul(ps_q[:, j:j + 1], sq[e_][:, j * 128:(j + 1) * 128], ones[:, 0:1],
                                                          start=(pair == 0 and j == 0), stop=(pair == 15)),
                                 reads=[("sq", e_), "ones"], writes=["ps_q"], sig=(j == 1))
                        P.op("act", lambda h: h.activation(ya16[e_][:], yg[e_][:], AF.Copy, scale=ng[:, pair:pair + 1]),
                             reads=[("yg", e_), "ng"], writes=[("ya16", e_)])
                        P.dma("sp", self.yaT[pair * 128:(pair + 1) * 128, c * 256:(c + 1) * 256], ya16[e_][:],
                              reads=[("ya16", e_)], writes=[("yaT", pair, c)])
                P.op("dve", lambda h: h.tensor_scalar(rs[:, 2 * c:2 * c + 2], ps_q, 1.0 / 2048.0, 1e-5, ALU.mult, ALU.add),
                     reads=["ps_q"], writes=["rs"])
            P.op("act", lambda h: h.activation(rs[:], rs[:], AF.Ln), reads=["rs"], writes=["rs"])
            P.op("act", lambda h: h.activation(rs[:], rs[:], AF.Exp, scale=-0.5), reads=["rs"], writes=["rs"])
            P.dma("sp", self.rstd_s, rs[:], reads=["rs"], writes=["rstd_s"])

    def stage_C(self):
        P, nc = self.P, self.nc
        V1v = self.V1.rearrange("(t p) h c -> p t h c", p=128)
        gsv = self.gs.rearrange("(t p) c -> p t c", p=128)
        SC = 1.0 / math.sqrt(128.0)
        with self.stage():
            sb = self.sb
            psS = [self.ps(f"psS{i}", [128, 512]) for i in range(2)]
            psO = [[self.ps(f"psO{i}{x}", [128, 512]) for x in "XY"] for i in range(2)]
            ps_gate = self.ps("ps_gate", [128, 512])
            ps_tr = self.ps("ps_tr", [128, 1024], BF16)
            q16 = [sb(f"q16{i}", [128, T], BF16) for i in range(2)]
            k16 = [sb(f"k16{i}", [128, T], BF16) for i in range(2)]
            v1 = [sb(f"v1{i}", [128, NT, 129], BF16) for i in range(2)]
            q32 = [sb(f"q32{i}", [128, T], F32) for i in range(2)]
            gsh = [sb(f"gsh{i}", [128, NT, 128], BF16) for i in range(2)]
            km = [sb(f"km{i}", [128, NCH], F32) for i in range(2)]
            vbias = sb("vbias", [128, NT, NCH], F32)
            tri16 = sb("tri16", [128, 128], BF16)
            ident = sb("ident", [128, 128], BF16)
            gm = sb("gm", [128, NT, NCH], F32)
            top8 = sb("top8", [128, NT, 8], F32)
            mask = sb("mask", [128, NT, NCH], F32)
            E = [sb(f"E{i}", [128, 512], BF16) for i in range(3)]
            acc = [sb(f"acc{i}", [128, 4, 129], F32) for i in range(2)]
            rec = sb("rec", [128, 8], F32)
            yb16 = [sb(f"yb16{i}", [128, 128], BF16) for i in range(2)]
            ybo = [sb(f"ybo{i}", [128, 512], BF16) for i in range(2)]
            P.dma("sp", vbias[:].rearrange("p a b -> p (a b)"), self.vbias, writes=["vbias"])
            P.dma("pool", tri16[:], self.triu, writes=["tri16"])
            P.dma("pool", ident[:], self.ident, writes=["ident"])

            def load_head(hd):
                s_ = hd % 2
                P.dma("sp", q16[s_][:], self.qT16[hd], writes=[("q16", s_)])
                P.dma("sp", k16[s_][:], self.kT16[hd], writes=[("k16", s_)])
                P.dma("sp", v1[s_][:], V1v[:, :, hd, :], writes=[("v1", s_)])
                P.dma("sp", q32[s_][:], self.qT32[hd], writes=[("q32", s_)])
                P.dma("sp", gsh[s_][:], gsv[:, :, hd * 128:(hd + 1) * 128], writes=[("gsh", s_)])
                P.dma("sp", km[s_][:], self.kmean[:, hd, :], writes=[("km", s_)])

            load_head(0)
            cS = cE = cY = 0
            nheads = getattr(self, "c_heads", 16)
            for hd in range(nheads):
                s_ = hd % 2
                if hd + 1 < nheads:
                    load_head(hd + 1)
                for qt in range(NT):
                    P.op("pe", lambda h: h.matmul(ps_gate[:, qt * NCH:(qt + 1) * NCH], q32[s_][:, qt * 128:(qt + 1) * 128],
                                                  km[s_][:], start=True, stop=True),
                         reads=[("q32", s_), ("km", s_)], writes=["ps_gate"], sig=(qt == NT - 1))
                P.op("dve", lambda h: h.tensor_tensor(gm[:].rearrange("p a b -> p (a b)"), ps_gate[:, :],
                                                      vbias[:].rearrange("p a b -> p (a b)"), ALU.add),
                     reads=["ps_gate", "vbias"], writes=["gm"])
                for qt in range(NT):
                    P.op("dve", lambda h: h.max(top8[:, qt, :], gm[:, qt, :]), reads=["gm"], writes=["top8"])
                P.op("dve", lambda h: h.tensor_tensor(mask[:], gm[:], top8[:, :, 2:3].broadcast_to([128, NT, NCH]), ALU.is_ge),
                     reads=["gm", "top8"], writes=["mask"])
                for j in range(NG):
                    ab = j % 2
                    first_acc = [True] * 4
                    steps = []
                    for n in range(2 * j + 2):
                        for kt in range(2):
                            if n < 2 * j:
                                first, diag = 0, False
                            elif n == 2 * j:
                                first, diag = kt, True
                            else:
                                first, diag = 2 + kt, True
                            sbk = cS % 2
                            cS += 1
                            eb = cE % 3
                            cE += 1
                            steps.append(dict(n=n, kt=kt, first=first, diag=diag, sbk=sbk, eb=eb, K=2 * n + kt,
                                              N=(4 - first) * 128, q0=(4 * j + first) * 128))

                    def emit_qk(sp_):
                        sbk, N, q0, K_ = sp_["sbk"], sp_["N"], sp_["q0"], sp_["K"]
                        P.op("pe", lambda h: h.matmul(psS[sbk][:, 0:N], k16[s_][:, K_ * 128:(K_ + 1) * 128],
                                                      q16[s_][:, q0:q0 + N], start=True, stop=True),
                             reads=[("k16", s_), ("q16", s_)], writes=[("psS", sbk)])

                    def emit_exp(sp_):
                        sbk, N, eb = sp_["sbk"], sp_["N"], sp_["eb"]
                        P.op("act", lambda h: h.activation(E[eb][:, 0:N], psS[sbk][:, 0:N], AF.Exp, scale=SC),
                             reads=[("psS", sbk)], writes=[("E", eb)])
                        if sp_["diag"]:
                            P.op("pool", lambda h: h.tensor_tensor(E[eb][:, 0:128], E[eb][:, 0:128], tri16[:], ALU.mult),
                                 reads=[("E", eb), "tri16"], writes=[("E", eb)])

                    blk_state = {}

                    def emit_pv(sp_):
                        n, kt, first, eb, K_ = sp_["n"], sp_["kt"], sp_["first"], sp_["eb"], sp_["K"]
                        nb = n % 2
                        stt = blk_state.setdefault(n, {"X": False, "Y": False, "vis": set()})
                        for t in range(first, 4):
                            x = "X" if t < 2 else "Y"
                            ob = psO[nb][0 if t < 2 else 1]
                            last_kt = (kt == 1) or (n == 2 * j and t == 0) or (n == 2 * j + 1 and t == 2)
                            st = not stt[x]
                            stt[x] = True
                            stt["vis"].add(t)
                            P.op("pe", lambda h: h.matmul(ob[:, (t % 2) * 256:(t % 2) * 256 + 129],
                                                          E[eb][:, (t - first) * 128:(t - first + 1) * 128],
                                                          v1[s_][:, K_, :], start=st, stop=last_kt),
                                 reads=[("E", eb), ("v1", s_)], writes=[("psO", nb, x)], sig=(t == 3))

                    def emit_acc(n):
                        nb = n % 2
                        for t in sorted(blk_state[n]["vis"]):
                            x = "X" if t < 2 else "Y"
                            ob = psO[nb][0 if t < 2 else 1][:, (t % 2) * 256:(t % 2) * 256 + 129]
                            own = (n == 2 * j + t // 2)
                            mcol = mask[:, 4 * j + t, n:n + 1]
                            a_t = acc[ab][:, t, :]
                            if first_acc[t]:
                                first_acc[t] = False
                                if own:
                                    P.op("dve", lambda h: h.tensor_copy(a_t, ob), reads=[("psO", nb, x)], writes=[("acc", ab, t)])
                                else:
                                    P.op("dve", lambda h: h.tensor_scalar(a_t, ob, mcol, None, ALU.mult),
                                         reads=[("psO", nb, x)], writes=[("acc", ab, t)], strict=["mask"])
                            elif own:
                                P.op("dve", lambda h: h.tensor_tensor(a_t, ob, a_t, ALU.add),
                                     reads=[("psO", nb, x), ("acc", ab, t)], writes=[("acc", ab, t)])
                            else:
                                P.op("dve", lambda h: h.scalar_tensor_tensor(a_t, ob, mcol, a_t, ALU.mult, ALU.add),
                                     reads=[("psO", nb, x), ("acc", ab, t)], writes=[("acc", ab, t)], strict=["mask"])

                    emit_qk(steps[0])
                    for i_, sp_ in enumerate(steps):
                        if i_ + 1 < len(steps):
                            emit_qk(steps[i_ + 1])
                        emit_exp(sp_)
                        emit_pv(sp_)
                        if sp_["kt"] == 1:
                            emit_acc(sp_["n"])
                    yo = cY % 2
                    cY += 1
                    for t in range(4):
                        yb_ = t % 2
                        P.op("dve", lambda h: h.reciprocal(rec[:, t:t + 1], acc[ab][:, t, 128:129]),
                             reads=[("acc", ab, t)], writes=[("rec", t)])
                        P.op("dve", lambda h: h.scalar_tensor_tensor(yb16[yb_][:], acc[ab][:, t, 0:128], rec[:, t:t + 1],
                                                                     gsh[s_][:, 4 * j + t, :], ALU.mult, ALU.mult),
                             reads=[("acc", ab, t), ("gsh", s_)], writes=[("yb16", yb_)], strict=[("rec", t)])
                        P.op("pe", lambda h: h.transpose(ps_tr[:, t * 128:(t + 1) * 128], yb16[yb_][:], ident[:]),
                             reads=[("yb16", yb_), "ident"], writes=["ps_tr"])
                    P.op("act", lambda h: h.activation(ybo[yo][:], ps_tr[:, 0:512], AF.Copy), reads=["ps_tr"], writes=[("ybo", yo)])
                    P.dma("sp", self.ybT[hd * 128:(hd + 1) * 128, j * 512:(j + 1) * 512], ybo[yo][:],
                          reads=[("ybo", yo)], writes=[("ybT", hd, j)])

    def outproj_ln(self, parts, w_dram, nk_total, resid, layer, out_dram, xT_out=None, rstd_dram=None):
        P, nc = self.P, self.nc
        wv = w_dram.rearrange("(k p) c -> p k c", p=128)
        with self.stage():
            sb = self.sb
            W = sb("W", [128, nk_total, D], BF16)
            for k in range(nk_total):
                P.dma("pool", W[:, k, :], wv[:, k, :], writes=[("W", k)])
            npart = len(parts)
            psP = [[self.ps(f"psP{i}{a}", [128, 512]) for a in range(npart)] for i in range(2)]
            ps_tr = [self.ps(f"ps_tr{i}", [128, 1024], BF16) for i in range(2)] if xT_out is not None else None
            lt = [[sb(f"lt{i}{a}", [128, parts[a][2], 128], BF16) for a in range(npart)] for i in range(2)]
            xt = [sb(f"xt{i}", [128, D], F32) for i in range(2)]
            v = sb("v", [128, D], F32)
            junk = sb("junk", [128, D], BF16)
            gbc = sb("gbc", [128, D], F32)
            bbc = sb("bbc", [128, D], F32)
            st = sb("st", [128, 8], F32)
            P.dma("sp", gbc[:], self.ln_g[layer:layer + 1, :].partition_broadcast(128), writes=["gbc"])
            P.dma("sp", bbc[:], self.ln_b[layer:layer + 1, :].partition_broadcast(128), writes=["bbc"])
            if rstd_dram is not None:
                rs = sb("rs", [128, NT], F32)
                P.dma("sp", rs[:], rstd_dram, writes=["rs"])
            if xT_out is not None:
                ident = sb("ident", [128, 128], BF16)
                P.dma("pool", ident[:], self.ident, writes=["ident"])
                x1b = sb("x1b", [128, D], BF16)
                xTt = sb("xTt", [128, 16, 128], BF16)
                xTv = xT_out.rearrange("(k p) t -> p k t", p=128)
            fv = [pt[0].rearrange("(k p) t -> p k t", p=128) for pt in parts]

            def load_tile(tt):
                s_ = tt % 2
                for a in range(npart):
                    P.dma("sp", lt[s_][a][:], fv[a][:, :, tt * 128:(tt + 1) * 128], writes=[("lt", s_, a)])
                P.dma("sp", xt[s_][:], resid[tt * 128:(tt + 1) * 128, :], writes=[("xt", s_)])

            load_tile(0)
            cP = 0
            for tt in range(NT):
                s_ = tt % 2
                if tt + 1 < NT:
                    load_tile(tt + 1)
                for cg in range(4):
                    pb = cP % 2
                    cP += 1
                    cs = slice(cg * 512, (cg + 1) * 512)
                    for a, (_, k0, nk, use_rstd) in enumerate(parts):
                        for k in range(nk):
                            P.op("pe", lambda h: h.matmul(psP[pb][a][:, :], lt[s_][a][:, k, :], W[:, k0 + k, cs],
                                                          start=(k == 0), stop=(k == nk - 1)),
                                 reads=[("lt", s_, a), ("W", k0 + k)], writes=[("psP", pb, a)], sig=(k == nk - 1))
                    first = True
                    for a, (_, k0, nk, use_rstd) in enumerate(parts):
                        if use_rstd:
                            continue
                        P.op("dve", lambda h: h.scalar_tensor_tensor(v[:, cs], xt[s_][:, cs], ALPHA, psP[pb][a][:, :],
                                                                     ALU.mult, ALU.add),
                             reads=[("xt", s_), ("psP", pb, a)], writes=[("v", cg)], big=True)
                        first = False
                    for a, (_, k0, nk, use_rstd) in enumerate(parts):
                        if not use_rstd:
                            continue
                        P.op("dve", lambda h: h.scalar_tensor_tensor(v[:, cs], psP[pb][a][:, :], rs[:, tt:tt + 1], v[:, cs],
                                                                     ALU.mult, ALU.add),
                             reads=[("psP", pb, a), ("v", cg)], writes=[("v", cg)], strict=["rs"], big=True)
                vres = [("v", cg) for cg in range(4)]
                P.op("act", lambda h: h.activation(junk[:], v[:], AF.Square), reads=vres, writes=["junk"])
                P.op("dve", lambda h: h.reduce_sum(st[:, 0:1], v[:], AX.X), reads=vres, writes=[("st", 0)])
                P.op("dve", lambda h: h.reduce_sum(st[:, 1:2], junk[:], AX.X), reads=["junk"], writes=[("st", 1)])
                P.op("dve", lambda h: h.tensor_scalar(st[:, 2:3], st[:, 0:1], 1.0 / D, None, ALU.mult),
                     reads=[("st", 0)], writes=[("st", 2)])
                P.op("dve", lambda h: h.tensor_tensor(st[:, 3:4], st[:, 2:3], st[:, 2:3], ALU.mult),
                     reads=[("st", 2)], writes=[("st", 3)])
                P.op("dve", lambda h: h.scalar_tensor_tensor(st[:, 4:5], st[:, 1:2], 1.0 / D, st[:, 3:4], ALU.mult, ALU.subtract),
                     reads=[("st", 1), ("st", 3)], writes=[("st", 4)])
                P.op("dve", lambda h: h.tensor_scalar(st[:, 4:5], st[:, 4:5], 1e-5, None, ALU.add),
                     reads=[("st", 4)], writes=[("st", 4)])
                P.op("act", lambda h: h.activation(st[:, 5:6], st[:, 4:5], AF.Ln), reads=[("st", 4)], writes=[("st", 5)])
                P.op("act", lambda h: h.activation(st[:, 5:6], st[:, 5:6], AF.Exp, scale=-0.5), reads=[("st", 5)], writes=[("st", 5)])
                P.op("dve", lambda h: h.scalar_tensor_tensor(st[:, 6:7], st[:, 2:3], -1.0, st[:, 5:6], ALU.mult, ALU.mult),
                     reads=[("st", 2), ("st", 5)], writes=[("st", 6)])
                P.op("act", lambda h: h.activation(v[:], v[:], AF.Identity, bias=st[:, 6:7], scale=st[:, 5:6]),
                     reads=vres, writes=vres, strict=[("st", 5), ("st", 6)])
                P.op("dve", lambda h: h.tensor_tensor(v[:], v[:], gbc[:], ALU.mult), reads=vres + ["gbc"], writes=vres, big=True)
                P.op("dve", lambda h: h.tensor_tensor(v[:], v[:], bbc[:], ALU.add), reads=vres + ["bbc"], writes=vres, big=True)
                P.dma("sp", out_dram[tt * 128:(tt + 1) * 128, :], v[:], reads=vres, writes=[("out", tt)])
                if xT_out is not None:
                    P.op("act", lambda h: h.activation(x1b[:], v[:], AF.Copy), reads=vres, writes=["x1b"])
                    for hb in range(2):
                        for kk in range(8):
                            k = hb * 8 + kk
                            P.op("pe", lambda h: h.transpose(ps_tr[hb][:, kk * 128:(kk + 1) * 128], x1b[:, k * 128:(k + 1) * 128],
                                                             ident[:]),
                                 reads=["x1b", "ident"], writes=[("ps_tr", hb)], sig=(kk == 7))
                        P.op("act" if hb == 0 else "dve",
                             (lambda h: h.activation(xTt[:, 0:8, :].rearrange("p a b -> p (a b)"), ps_tr[0][:, :], AF.Copy)) if hb == 0 else
                             (lambda h: h.tensor_copy(xTt[:, 8:16, :].rearrange("p a b -> p (a b)"), ps_tr[1][:, :])),
                             reads=[("ps_tr", hb)], writes=[("xTt", hb)])
                    P.dma("sp", xTv[:, :, tt * 128:(tt + 1) * 128], xTt[:], reads=[("xTt", 0), ("xTt", 1)], writes=[("xT_out", tt)])

    def stage_D(self):
        self.outproj_ln([(self.ybT, 16, 16, False), (self.yaT, 0, 16, True)], self.w_out0, 32, self.x, 0, self.x1,
                        xT_out=self.x1T, rstd_dram=self.rstd_s)

    def stage_E(self):
        P, nc = self.P, self.nc
        x1Tv = self.x1T.rearrange("(k p) t -> p k t", p=128)
        with self.stage():
            xTb = self.sb("xTb", [128, 16, T], BF16)
            for k in range(16):
                P.dma("sp", xTb[:, k, :], x1Tv[:, k, :], writes=[("xTb", k)])
            wsl = [self.sb(f"wsl{i}", [128, 16, 128], BF16) for i in range(2)]
            stg = [self.sb(f"stg{i}", [128, 16, 128], F32) for i in range(2)]
            psA = [self.ps(f"psA{i}", [128, 512]) for i in range(4)]
            ob = [self.sb(f"ob{i}", [128, 512], BF16) for i in range(3)]
            co = 0
            cg_ = 0
            def issue_w(ib):
                sl = ib % 2
                P.dma("sp", stg[sl][:], self.w_in1[ib], writes=[("stg", sl)])
                P.op("pool", lambda h: h.tensor_copy(wsl[sl][:], stg[sl][:]), reads=[("stg", sl)], writes=[("w", sl)])

            issue_w(0)
            for i in range(32):
                slot = i % 2
                if i + 1 < 32:
                    issue_w(i + 1)
                for g in range(NG):
                    b = cg_ % 4
                    cg_ += 1
                    for k in range(16):
                        P.op("pe", lambda h: h.matmul(psA[b][:, :], wsl[slot][:, k, :], xTb[:, k, g * 512:(g + 1) * 512],
                                                      start=(k == 0), stop=(k == 15)),
                             reads=[("w", slot), ("xTb", k)], writes=[("psA", b)], sig=(k == 15))
                    o = co % 3
                    co += 1
                    P.op("act", lambda h: h.activation(ob[o][:], psA[b][:, :], AF.Copy if i < 16 else AF.Silu),
                         reads=[("psA", b)], writes=[("ob", o)])
                    dst = self.uT if i < 16 else self.sg1T
                    P.dma("sp", dst[(i % 16) * 128:(i % 16 + 1) * 128, g * 512:(g + 1) * 512], ob[o][:],
                          reads=[("ob", o)], writes=[("eo", i, g)])

    def stage_F(self):
        P, nc = self.P, self.nc
        TWO_PI = 2.0 * math.pi
        uTv = self.uT.rearrange("(k p) t -> p k t", p=128)
        with self.stage():
            sb = self.sb
            psV = [[self.ps(f"psV{i}{a}", [128, 512]) for a in "ri"] for i in range(2)]
            psY = [self.ps(f"psY{i}", [128, 512]) for i in range(2)]
            pl = sb("pl", [128, 3, 64], F32)
            wl = sb("wl", [128, 5, 16, 64], F32)
            cp = sb("cp", [128, 2, 64, 16], F32)
            d1 = sb("d1", [128, 16], F32)
            rmk = sb("rmk", [128, 8], F32)
            iot = sb("iot", [128, 513], F32)
            pi_c = sb("pi_c", [128, 1], F32)
            P.dma("sp", pl[:], self.s5_pl, writes=["pl"])
            P.dma("sp", wl[:], self.s5_wl, writes=["wl"])
            P.dma("sp", cp[:], self.s5_cp, writes=["cp"])
            P.dma("sp", d1[:], self.s5_d1, writes=["d1"])
            P.dma("sp", rmk[:], self.rowmask, writes=["rmk"])
            P.dma("sp", iot[:], self.iota513.partition_broadcast(128), writes=["iot"])
            P.op("pool", lambda h: h.memset(pi_c[:], math.pi), writes=["pi_c"])
            dtp = sb("dtp", [128, 64], F32)
            rP = sb("rP", [128, 64], F32)
            thP = sb("thP", [128, 64], F32)
            P.op("act", lambda h: h.activation(dtp[:], pl[:, 2, :], AF.Exp), reads=["pl"], writes=["dtp"])
            P.op("dve", lambda h: h.tensor_tensor(rP[:], pl[:, 0, :], dtp[:], ALU.mult), reads=["pl", "dtp"], writes=["rP"])
            P.op("act", lambda h: h.activation(rP[:], rP[:], AF.Exp), reads=["rP"], writes=["rP"])
            P.op("dve", lambda h: h.tensor_tensor(thP[:], pl[:, 1, :], dtp[:], ALU.mult), reads=["pl", "dtp"], writes=["thP"])
            thm = sb("thm", [128, 64], F32)
            P.op("dve", lambda h: h.tensor_scalar(thm[:], thP[:], 0.0, TWO_PI, ALU.is_lt, ALU.mult), reads=["thP"], writes=["thm"])
            P.op("dve", lambda h: h.tensor_tensor(thP[:], thP[:], thm[:], ALU.add), reads=["thP", "thm"], writes=["thP"])

            def sincos(o_sin, o_cos, a_in, shape, tag):
                ki = sb(f"ki_{tag}", shape, mybir.dt.int32)
                kf = sb(f"kf_{tag}", shape, F32)
                rr = sb(f"rr_{tag}", shape, F32)
                mm_ = sb(f"mm_{tag}", shape, F32)
                r_ = [f"sc_{tag}"]
                P.op("dve", lambda h: h.tensor_scalar(ki[:], a_in, 1.0 / TWO_PI, None, ALU.mult), reads=r_, writes=r_)
                P.op("dve", lambda h: h.tensor_copy(kf[:], ki[:]), reads=r_, writes=r_)
                P.op("dve", lambda h: h.scalar_tensor_tensor(rr[:], kf[:], -TWO_PI, a_in, ALU.mult, ALU.add), reads=r_, writes=r_)
                P.op("dve", lambda h: h.tensor_scalar(mm_[:], rr[:], math.pi, -TWO_PI, ALU.is_gt, ALU.mult), reads=r_, writes=r_)
                P.op("dve", lambda h: h.tensor_tensor(mm_[:], mm_[:], rr[:], ALU.add), reads=r_, writes=r_)
                P.op("act", lambda h: h.activation(o_sin, mm_[:], AF.Sin), reads=r_, writes=r_)
                P.op("dve", lambda h: h.tensor_scalar(rr[:], rr[:], 0.5 * math.pi, None, ALU.add), reads=r_, writes=r_)
                P.op("dve", lambda h: h.tensor_scalar(mm_[:], rr[:], math.pi, -TWO_PI, ALU.is_gt, ALU.mult), reads=r_, writes=r_)
                P.op("dve", lambda h: h.tensor_tensor(mm_[:], mm_[:], rr[:], ALU.add), reads=r_, writes=r_)
                P.op("act", lambda h: h.activation(o_cos, mm_[:], AF.Sin), reads=r_, writes=r_)
            W3 = [128, 16, 64]
            dtw = sb("dtw", W3, F32)
            aw = sb("aw", W3, F32)
            tw = sb("tw", W3, F32)
            cw_ = sb("cw_", W3, F32)
            sw_ = sb("sw_", W3, F32)
            fre = sb("fre", W3, F32)
            fim = sb("fim", W3, F32)
            t0 = sb("t0", W3, F32)
            t1 = sb("t1", W3, F32)
            bbr = sb("bbr", W3, F32)
            bbi = sb("bbi", W3, F32)
            lre, lim, bre, bim = wl[:, 0], wl[:, 1], wl[:, 3], wl[:, 4]
            D_ = lambda fn, r, w: P.op("dve", fn, reads=r, writes=w)
            A_ = lambda fn, r, w, **kw: P.op("act", fn, reads=r, writes=w, **kw)
            A_(lambda h: h.activation(dtw[:], wl[:, 2], AF.Exp), ["wl"], ["dtw"])
            D_(lambda h: h.tensor_tensor(aw[:], lre, dtw[:], ALU.mult), ["wl", "dtw"], ["aw"])
            A_(lambda h: h.activation(aw[:], aw[:], AF.Exp), ["aw"], ["aw"])
            D_(lambda h: h.tensor_tensor(tw[:], lim, dtw[:], ALU.mult), ["wl", "dtw"], ["tw"])
            D_(lambda h: h.tensor_scalar(t0[:], tw[:], 0.0, TWO_PI, ALU.is_lt, ALU.mult), ["tw"], ["t0"])
            D_(lambda h: h.tensor_tensor(tw[:], tw[:], t0[:], ALU.add), ["tw", "t0"], ["tw", "sc_w"])
            sincos(sw_[:], cw_[:], tw[:], W3, "w")
            P.op("dve", lambda h: h.tensor_copy(sw_[:], sw_[:]), reads=["sc_w"], writes=["sw_", "cw_"])
            D_(lambda h: h.tensor_tensor(cw_[:], cw_[:], aw[:], ALU.mult), ["cw_", "aw"], ["cw_"])
            D_(lambda h: h.tensor_tensor(sw_[:], sw_[:], aw[:], ALU.mult), ["sw_", "aw"], ["sw_"])
            D_(lambda h: h.tensor_scalar(cw_[:], cw_[:], -1.0, None, ALU.add), ["cw_"], ["cw_"])
            D_(lambda h: h.tensor_tensor(t0[:], lre, lre, ALU.mult), ["wl"], ["t0"])
            D_(lambda h: h.tensor_tensor(t1[:], lim, lim, ALU.mult), ["wl"], ["t1"])
            D_(lambda h: h.tensor_tensor(t0[:], t0[:], t1[:], ALU.add), ["t0", "t1"], ["t0"])
            D_(lambda h: h.reciprocal(t0[:], t0[:]), ["t0"], ["t0"])
            D_(lambda h: h.tensor_tensor(fre[:], cw_[:], lre, ALU.mult), ["cw_", "wl"], ["fre"])
            D_(lambda h: h.tensor_tensor(t1[:], sw_[:], lim, ALU.mult), ["sw_", "wl"], ["t1"])
            D_(lambda h: h.tensor_tensor(fre[:], fre[:], t1[:], ALU.add), ["fre", "t1"], ["fre"])
            D_(lambda h: h.tensor_tensor(fre[:], fre[:], t0[:], ALU.mult), ["fre", "t0"], ["fre"])
            D_(lambda h: h.tensor_tensor(fim[:], sw_[:], lre, ALU.mult), ["sw_", "wl"], ["fim"])
            D_(lambda h: h.tensor_tensor(t1[:], cw_[:], lim, ALU.mult), ["cw_", "wl"], ["t1"])
            D_(lambda h: h.tensor_tensor(fim[:], fim[:], t1[:], ALU.subtract), ["fim", "t1"], ["fim"])
            D_(lambda h: h.tensor_tensor(fim[:], fim[:], t0[:], ALU.mult), ["fim", "t0"], ["fim"])
            D_(lambda h: h.tensor_tensor(bbr[:], fre[:], bre, ALU.mult), ["fre", "wl"], ["bbr"])
            D_(lambda h: h.tensor_tensor(t1[:], fim[:], bim, ALU.mult), ["fim", "wl"], ["t1"])
            D_(lambda h: h.tensor_tensor(bbr[:], bbr[:], t1[:], ALU.subtract), ["bbr", "t1"], ["bbr"])
            D_(lambda h: h.tensor_tensor(bbi[:], fre[:], bim, ALU.mult), ["fre", "wl"], ["bbi"])
            D_(lambda h: h.tensor_tensor(t1[:], fim[:], bre, ALU.mult), ["fim", "wl"], ["t1"])
            D_(lambda h: h.tensor_tensor(bbi[:], bbi[:], t1[:], ALU.add), ["bbi", "t1"], ["bbi"])
            uc = [sb(f"uc{i}", [128, T], BF16) for i in range(2)]
            Lr = [sb(f"Lr{i}", [128, 128], BF16) for i in range(4)]
            Li = [sb(f"Li{i}", [128, 128], BF16) for i in range(4)]
            Cr = [sb(f"Cr{i}", [128, 128], BF16) for i in range(4)]
            nCr = [sb(f"nCr{i}", [128, 128], BF16) for i in range(4)]
            nCi = [sb(f"nCi{i}", [128, 128], BF16) for i in range(4)]
            cosT = [sb(f"cosT{i}", [128, 513], F32) for i in range(4)]
            sinT = [sb(f"sinT{i}", [128, 513], F32) for i in range(4)]
            ang = sb("ang", [128, 513], F32)
            ki_p = sb("ki_p", [128, 513], mybir.dt.int32)
            kf_p = sb("kf_p", [128, 513], F32)
            rr_p = sb("rr_p", [128, 513], F32)
            mm_p = sb("mm_p", [128, 513], F32)

            def sincos_p(o_sin, o_cos):
                r_ = ["sc_p"]
                P.op("dve", lambda h: h.tensor_scalar(ki_p[:], ang[:], 1.0 / TWO_PI, None, ALU.mult), reads=r_, writes=r_)
                P.op("dve", lambda h: h.tensor_copy(kf_p[:], ki_p[:]), reads=r_, writes=r_)
                P.op("dve", lambda h: h.scalar_tensor_tensor(rr_p[:], kf_p[:], -TWO_PI, ang[:], ALU.mult, ALU.add), reads=r_, writes=r_)
                P.op("dve", lambda h: h.tensor_scalar(mm_p[:], rr_p[:], math.pi, -TWO_PI, ALU.is_gt, ALU.mult), reads=r_, writes=r_)
                P.op("dve", lambda h: h.tensor_tensor(mm_p[:], mm_p[:], rr_p[:], ALU.add), reads=r_, writes=r_)
                P.op("act", lambda h: h.activation(o_sin, mm_p[:], AF.Sin), reads=r_, writes=r_)
                P.op("dve", lambda h: h.tensor_scalar(rr_p[:], rr_p[:], 0.5 * math.pi, None, ALU.add), reads=r_, writes=r_)
                P.op("dve", lambda h: h.tensor_scalar(mm_p[:], rr_p[:], math.pi, -TWO_PI, ALU.is_gt, ALU.mult), reads=r_, writes=r_)
                P.op("dve", lambda h: h.tensor_tensor(mm_p[:], mm_p[:], rr_p[:], ALU.add), reads=r_, writes=r_)
                P.op("act", lambda h: h.activation(o_cos, mm_p[:], AF.Sin), reads=r_, writes=r_)
            qst = [sb(f"qst{i}", [128, 2], F32) for i in range(4)]
            qt_ = sb("qt_", [128, 2], F32)
            Vs = [[sb(f"Vs{i}{a}", [128, 512], F32) for a in "ri"] for i in range(2)]
            m1 = [sb(f"m1{i}", [128, 512], F32) for i in range(2)]
            m2 = [sb(f"m2{i}", [128, 512], F32) for i in range(2)]
            Wr = [sb(f"Wr{i}", [128, 512], F32) for i in range(2)]
            Wi = [sb(f"Wi{i}", [128, 512], F32) for i in range(2)]
            Gr = [sb(f"Gr{i}", [128, 512], F32) for i in range(2)]
            Gi = [sb(f"Gi{i}", [128, 512], F32) for i in range(2)]
            Pp = [[sb(f"Pp{i}{a}", [128, 512], BF16) for a in range(4)] for i in range(2)]
            yv = [sb(f"yv{i}", [128, 512], F32) for i in range(2)]
            ge1 = [sb(f"ge1{i}", [128, 512], F32) for i in range(2)]
            ge2 = [sb(f"ge2{i}", [128, 512], F32) for i in range(2)]
            yo = [sb(f"yo{i}", [128, 512], BF16) for i in range(2)]
            for i in range(4):
                for tl, nm in ((Cr, "Cr"), (nCr, "nCr"), (nCi, "nCi")):
                    P.op("pool", lambda h: h.memset(tl[i][:], 0.0), writes=[(nm, i)])
            P.dma("sp", uc[0][:], uTv[:, 0, :], writes=[("uc", 0)])
            cpb = 0
            nchunks = getattr(self, "f_chunks", 16)
            for j in range(nchunks):
                us = j % 2
                if j + 1 < nchunks:
                    P.dma("sp", uc[(j + 1) % 2][:], uTv[:, j + 1, :], writes=[("uc", (j + 1) % 2)])
                for pc in range(4):
                    pr = 4 * j + pc
                    for g2 in range(2):
                        gl = 2 * pc + g2
                        P.op("dve", lambda h: h.tensor_scalar(Lr[pc][:, g2 * 64:(g2 + 1) * 64], bbr[:, j, :], rmk[:, gl:gl + 1], None, ALU.mult),
                             reads=["bbr", "rmk"], writes=[("Lr", pc)])
                        P.op("dve", lambda h: h.tensor_scalar(Li[pc][:, g2 * 64:(g2 + 1) * 64], bbi[:, j, :], rmk[:, gl:gl + 1], None, ALU.mult),
                             reads=["bbi", "rmk"], writes=[("Li", pc)])
                        rs_ = slice(g2 * 64, (g2 + 1) * 64)
                        cs_ = slice(gl * 16, gl * 16 + 16)
                        P.op("act", lambda h: h.activation(Cr[pc][rs_, cs_], cp[rs_, 0, pr, :], AF.Copy), reads=["cp"], writes=[("Cr", pc)])
                        P.op("act", lambda h: h.activation(nCr[pc][rs_, cs_], cp[rs_, 0, pr, :], AF.Copy, scale=-1.0), reads=["cp"], writes=[("nCr", pc)])
                        P.op("act", lambda h: h.activation(nCi[pc][rs_, cs_], cp[rs_, 1, pr, :], AF.Copy, scale=-1.0), reads=["cp"], writes=[("nCi", pc)])
                    P.op("dve", lambda h: h.tensor_scalar(ang[:], iot[:], thP[:, pr:pr + 1], None, ALU.mult),
                         reads=["iot"], writes=["ang", "sc_p", ("sinT", pc), ("cosT", pc)], strict=["thP"])
                    sincos_p(sinT[pc][:], cosT[pc][:])
                    P.op("dve", lambda h: h.tensor_copy(qt_[:, 0:1], qt_[:, 0:1]), reads=["sc_p"], writes=[("sinT", pc), ("cosT", pc)])
                    P.op("pool", lambda h: h.memset(qst[pc][:], 0.0), writes=[("qst", pc)])
                for b in range(NG):
                    bs = slice(b * 512, (b + 1) * 512)
                    yb_ = b % 2
                    for pc in range(4):
                        pr = 4 * j + pc
                        vb = cpb % 2
                        cpb += 1
                        c_, s_t = cosT[pc][:, 0:512], sinT[pc][:, 0:512]
                        P.op("pe", lambda h: h.matmul(psV[vb][0][:, :], Lr[pc][:], uc[us][:, bs], start=True, stop=True),
                             reads=[("Lr", pc), ("uc", us)], writes=[("psV", vb, 0)])
                        P.op("pe", lambda h: h.matmul(psV[vb][1][:, :], Li[pc][:], uc[us][:, bs], start=True, stop=True),
                             reads=[("Li", pc), ("uc", us)], writes=[("psV", vb, 1)])
                        P.op("act", lambda h: h.activation(Vs[vb][0][:], psV[vb][0][:, :], AF.Copy), reads=[("psV", vb, 0)], writes=[("Vs", vb, 0)])
                        P.op("act", lambda h: h.activation(Vs[vb][1][:], psV[vb][1][:, :], AF.Copy), reads=[("psV", vb, 1)], writes=[("Vs", vb, 1)])
                        P.op("dve", lambda h: h.tensor_tensor(m1[vb][:], Vs[vb][0][:], c_, ALU.mult), reads=[("Vs", vb, 0), ("cosT", pc)], writes=[("m1", vb)], big=True)
                        P.op("dve", lambda h: h.tensor_tensor(m2[vb][:], Vs[vb][1][:], s_t, ALU.mult), reads=[("Vs", vb, 1), ("sinT", pc)], writes=[("m2", vb)], big=True)
                        P.op("dve", lambda h: h.tensor_tensor(Wr[vb][:], m1[vb][:], m2[vb][:], ALU.add), reads=[("m1", vb), ("m2", vb)], writes=[("Wr", vb)], big=True)
                        P.op("dve", lambda h: h.tensor_tensor(m1[vb][:], Vs[vb][1][:], c_, ALU.mult), reads=[("Vs", vb, 1), ("cosT", pc)], writes=[("m1", vb)], big=True)
                        P.op("dve", lambda h: h.tensor_tensor(m2[vb][:], Vs[vb][0][:], s_t, ALU.mult), reads=[("Vs", vb, 0), ("sinT", pc)], writes=[("m2", vb)], big=True)
                        P.op("dve", lambda h: h.tensor_tensor(Wi[vb][:], m1[vb][:], m2[vb][:], ALU.subtract), reads=[("m1", vb), ("m2", vb)], writes=[("Wi", vb)], big=True)
                        rbc = rP[:, pr:pr + 1].broadcast_to([128, 512])
                        P.op("dve", lambda h: h.tensor_tensor_scan(Gr[vb][:], rbc, Wr[vb][:], qst[pc][:, 0:1], ALU.mult, ALU.add),
                             reads=[("Wr", vb), "rP"], writes=[("Gr", vb)], strict=[("qst", pc)], big=True)
                        P.op("dve", lambda h: h.tensor_tensor_scan(Gi[vb][:], rbc, Wi[vb][:], qst[pc][:, 1:2], ALU.mult, ALU.add),
                             reads=[("Wi", vb), "rP"], writes=[("Gi", vb)], strict=[("qst", pc)], big=True)
                        C5, S5 = cosT[pc][:, 512:513], sinT[pc][:, 512:513]
                        P.op("dve", lambda h: h.tensor_tensor(qt_[:, 0:1], Gi[vb][:, 511:512], S5, ALU.mult), reads=[("Gi", vb), ("sinT", pc)], writes=["qt_"])
                        P.op("dve", lambda h: h.tensor_tensor(qt_[:, 1:2], Gr[vb][:, 511:512], S5, ALU.mult), reads=[("Gr", vb), ("sinT", pc)], writes=["qt_"])
                        P.op("dve", lambda h: h.tensor_tensor(qst[pc][:, 0:1], Gr[vb][:, 511:512], C5, ALU.mult), reads=[("Gr", vb), ("cosT", pc)], writes=[("qst", pc)])
                        P.op("dve", lambda h: h.tensor_tensor(qst[pc][:, 1:2], Gi[vb][:, 511:512], C5, ALU.mult), reads=[("Gi", vb), ("cosT", pc)], writes=[("qst", pc)])
                        P.op("dve", lambda h: h.tensor_tensor(qst[pc][:, 0:1], qst[pc][:, 0:1], qt_[:, 0:1], ALU.subtract), reads=[("qst", pc), "qt_"], writes=[("qst", pc)])
                        P.op("dve", lambda h: h.tensor_tensor(qst[pc][:, 1:2], qst[pc][:, 1:2], qt_[:, 1:2], ALU.add), reads=[("qst", pc), "qt_"], writes=[("qst", pc)])
                        P.op("dve", lambda h: h.tensor_tensor(Pp[vb][0][:], Gr[vb][:], c_, ALU.mult), reads=[("Gr", vb), ("cosT", pc)], writes=[("Pp", vb, 0)], big=True)
                        P.op("dve", lambda h: h.tensor_tensor(Pp[vb][1][:], Gi[vb][:], c_, ALU.mult), reads=[("Gi", vb), ("cosT", pc)], writes=[("Pp", vb, 1)], big=True)
                        P.op("dve", lambda h: h.tensor_tensor(Pp[vb][2][:], Gi[vb][:], s_t, ALU.mult), reads=[("Gi", vb), ("sinT", pc)], writes=[("Pp", vb, 2)], big=True)
                        P.op("dve", lambda h: h.tensor_tensor(Pp[vb][3][:], Gr[vb][:], s_t, ALU.mult), reads=[("Gr", vb), ("sinT", pc)], writes=[("Pp", vb, 3)], big=True)
                        for a, (wt, wn) in enumerate(((Cr, "Cr"), (nCi, "nCi"), (nCr, "nCr"), (nCi, "nCi"))):
                            P.op("pe", lambda h: h.matmul(psY[yb_][:, :], wt[pc][:], Pp[vb][a][:], start=(pc == 0 and a == 0),
                                                          stop=(pc == 3 and a == 3)),
                                 reads=[(wn, pc), ("Pp", vb, a)], writes=[("psY", yb_)], sig=(a == 3))
                    P.op("dve", lambda h: h.scalar_tensor_tensor(yv[yb_][:], uc[us][:, bs], d1[:, j:j + 1], psY[yb_][:, :], ALU.mult, ALU.add),
                         reads=[("uc", us), "d1", ("psY", yb_)], writes=[("yv", yb_)])
                    P.op("act", lambda h: h.activation(ge1[yb_][:], yv[yb_][:], AF.Square), reads=[("yv", yb_)], writes=[("ge1", yb_)])
                    P.op("pool", lambda h: h.tensor_scalar(ge1[yb_][:], ge1[yb_][:], 0.044715, 1.0, ALU.mult, ALU.add), reads=[("ge1", yb_)], writes=[("ge1", yb_)])
                    P.op("pool", lambda h: h.tensor_tensor(ge2[yb_][:], ge1[yb_][:], yv[yb_][:], ALU.mult), reads=[("ge1", yb_), ("yv", yb_)], writes=[("ge2", yb_)])
                    P.op("act", lambda h: h.activation(ge2[yb_][:], ge2[yb_][:], AF.Sigmoid, scale=1.5957691216057308), reads=[("ge2", yb_)], writes=[("ge2", yb_)])
                    P.op("pool", lambda h: h.tensor_tensor(yo[yb_][:], ge2[yb_][:], yv[yb_][:], ALU.mult), reads=[("ge2", yb_), ("yv", yb_)], writes=[("yo", yb_)])
                    P.dma("sp", self.ygT[j * 128:(j + 1) * 128, bs], yo[yb_][:], reads=[("yo", yb_)], writes=[("ygT", j, b)])

    def stage_G1(self):
        P, nc = self.P, self.nc
        ygv = self.ygT.rearrange("(k p) t -> p k t", p=128)
        with self.stage():
            xTb = self.sb("xTb", [128, 16, T], BF16)
            for k in range(16):
                P.dma("sp", xTb[:, k, :], ygv[:, k, :], writes=[("xTb", k)])
            wa = [self.sb(f"wa{i}", [128, 16, 128], BF16) for i in range(2)]
            wb = [self.sb(f"wb{i}", [128, 16, 128], BF16) for i in range(2)]
            stga = [self.sb(f"stga{i}", [128, 16, 128], F32) for i in range(2)]
            stgb = [self.sb(f"stgb{i}", [128, 16, 128], F32) for i in range(2)]
            psA = [[self.ps(f"psA{i}{a}", [128, 512]) for a in "ab"] for i in range(2)]
            sgt = [self.sb(f"sgt{i}", [128, 512], BF16) for i in range(2)]
            sgb = [self.sb(f"sgb{i}", [128, 512], F32) for i in range(2)]
            tt_ = [self.sb(f"tt{i}", [128, 512], F32) for i in range(2)]
            ob = [self.sb(f"ob{i}", [128, 512], BF16) for i in range(2)]
            c2 = 0
            def issue_w(ib):
                sl = ib % 2
                P.dma("sp", stga[sl][:], self.w_glu[ib], writes=[("stga", sl)])
                P.op("pool", lambda h: h.tensor_copy(wa[sl][:], stga[sl][:]), reads=[("stga", sl)], writes=[("wa", sl)])
                P.dma("sp", stgb[sl][:], self.w_glu[16 + ib], writes=[("stgb", sl)])
                P.op("pool", lambda h: h.tensor_copy(wb[sl][:], stgb[sl][:]), reads=[("stgb", sl)], writes=[("wb", sl)])

            issue_w(0)
            for i in range(16):
                slot = i % 2
                if i + 1 < 16:
                    issue_w(i + 1)
                for g in range(NG):
                    b = c2 % 2
                    c2 += 1
                    gs_ = slice(g * 512, (g + 1) * 512)
                    P.dma("sp", sgt[b][:], self.sg1T[i * 128:(i + 1) * 128, gs_], writes=[("sgt", b)])
                    for a, wt, wn in ((0, wa, "wa"), (1, wb, "wb")):
                        for k in range(16):
                            P.op("pe", lambda h: h.matmul(psA[b][a][:, :], wt[slot][:, k, :], xTb[:, k, gs_], start=(k == 0), stop=(k == 15)),
                                 reads=[(wn, slot), ("xTb", k)], writes=[("psA", b, a)], sig=(k == 15))
                    P.op("act", lambda h: h.activation(sgb[b][:], psA[b][1][:, :], AF.Sigmoid), reads=[("psA", b, 1)], writes=[("sgb", b)])
                    P.op("dve", lambda h: h.tensor_tensor(tt_[b][:], psA[b][0][:, :], sgb[b][:], ALU.mult), reads=[("psA", b, 0), ("sgb", b)], writes=[("tt", b)])
                    P.op("pool", lambda h: h.tensor_tensor(ob[b][:], tt_[b][:], sgt[b][:], ALU.mult), reads=[("tt", b), ("sgt", b)], writes=[("ob", b)])
                    P.dma("sp", self.y2T[i * 128:(i + 1) * 128, gs_], ob[b][:], reads=[("ob", b)], writes=[("y2T", i, g)])

    def stage_G2(self):
        self.outproj_ln([(self.y2T, 0, 16, False)], self.w_out1, 16, self.x1, 1, self.out)

    def build(self, stages="AaBCDEFGH"):
        self.declare()
        for ch, fn in (("A", self.stage_A), ("a", self.stage_A2), ("B", self.stage_B), ("C", self.stage_C),
                       ("D", self.stage_D), ("E", self.stage_E), ("F", self.stage_F), ("G", self.stage_G1),
                       ("H", self.stage_G2)):
            if ch in stages:
                fn()
        return self.nc


def _rope_tables():
    half = 16
    inv_freq = (500000.0 ** (-(np.arange(half, dtype=np.float32) * 2.0 / 32))).astype(np.float32)
    pos = np.arange(T, dtype=np.float32)
    ang = (pos[None, :] * inv_freq[:, None]).astype(np.float32)
    c = np.cos(ang).astype(np.float32)
    s = np.sin(ang).astype(np.float32)
    return np.ascontiguousarray(np.concatenate([c, c], 0)), np.ascontiguousarray(np.concatenate([-s, s], 0))


def _constants():
    cosT, sinS = _rope_tables()
    pm = np.zeros((32, 32), np.float32)
    for m in range(32):
        pm[(m + 16) % 32, m] = 1.0
    sel = np.zeros((32, 32, 128), np.float32)
    for h in range(32):
        sel[h, h, :] = 1.0
    triu = np.triu(np.ones((128, 128), np.float32))
    maskg = np.ascontiguousarray(np.concatenate([triu, np.ones((128, 128), np.float32), triu], 1))
    vb = np.zeros((128, NT, NCH), np.float32)
    for qt in range(NT):
        vb[:, qt, qt // 2:] = -1e30
    rowmask = np.zeros((128, 8), np.float32)
    for p in range(128):
        rowmask[p, p // 16] = 1.0
    return dict(cosT=cosT, sinS=sinS, pm32=pm, sel=sel, triu=triu, maskg=maskg, ident=np.eye(128, dtype=np.float32),
                vbias=np.ascontiguousarray(vb.reshape(128, NT * NCH)), rowmask=rowmask,
                iota513=np.arange(513, dtype=np.float32).reshape(1, 513))


def _shared_inputs(inp):
    f = lambda a: np.ascontiguousarray(np.asarray(a, dtype=np.float32))
    d = {}
    w0 = np.asarray(inp["in0_w"][0], dtype=np.float32)
    def tile_cols(w, c0, m):
        return w[:, c0:c0 + m].reshape(16, 128, m).transpose(1, 0, 2)
    cols_a = [C_XBC + 128 * i for i in range(24)] + [C_Z + 128 * i for i in range(16)] + \
             [C_Q + 128 * i for i in range(16)] + [C_K + 128 * i for i in range(16)]
    d["w0a"] = f(np.stack([tile_cols(w0, c, 128) for c in cols_a], 0))
    d["w0b"] = f(np.stack([tile_cols(w0, C_V + 512 * i, 512) for i in range(4)] +
                          [tile_cols(w0, C_G + 512 * i, 512) for i in range(4)], 0))
    d["w0dt"] = f(tile_cols(w0, C_DT, 32))
    cwv = np.asarray(inp["conv_w"][0])
    d["cw"] = f(cwv.T.reshape(24, 128, 4).transpose(1, 0, 2))
    d["cb"] = f(np.asarray(inp["conv_b"][0]).reshape(24, 128).T)
    d["dtb"] = f(inp["dt_bias"])
    d["alog"] = f(inp["a_log"])
    dsk = np.asarray(inp["ssd_d"][0])
    d["ssd_dp"] = f(np.repeat(dsk.reshape(16, 2), 64, axis=1).T)
    d["normg"] = f(np.asarray(inp["ssd_norm_g"][0]).reshape(16, 128).T)
    d["out0_w"] = f(inp["out0_w"][0])
    d["ln_g"] = f(inp["ln_g"])
    d["ln_b"] = f(inp["ln_b"])
    w1 = np.asarray(inp["in1_w"][0], dtype=np.float32)
    d["w1t"] = f(np.stack([tile_cols(w1, 128 * i, 128) for i in range(32)], 0))
    wg = np.asarray(inp["glu_w"][0], dtype=np.float32)
    d["wgt"] = f(np.stack([tile_cols(wg, 128 * i, 128) for i in range(32)], 0))
    d["out1_w"] = f(inp["out1_w"][0])
    lre, lim = np.asarray(inp["s5_lam_re"][0]), np.asarray(inp["s5_lam_im"][0])
    ldt = np.asarray(inp["s5_log_dt"][0])
    bre, bim = np.asarray(inp["s5_b_re"][0]), np.asarray(inp["s5_b_im"][0])
    cre, cim = np.asarray(inp["s5_c_re"][0]), np.asarray(inp["s5_c_im"][0])
    ldt_n = np.repeat(ldt[:, None], 64, axis=1)
    pl = np.stack([a.reshape(64, 2, 64).transpose(1, 2, 0).reshape(128, 64) for a in (lre, lim, ldt_n)], 1)
    d["s5_pl"] = f(pl)
    def wl_gn(a):
        return np.repeat(a.reshape(16, 8, 1, 64), 16, axis=2).transpose(1, 2, 0, 3).reshape(128, 16, 64)
    def wl_gnm(a):
        return a.reshape(16, 8, 64, 16).transpose(1, 3, 0, 2).reshape(128, 16, 64)
    d["s5_wl"] = f(np.stack([wl_gn(lre), wl_gn(lim), wl_gn(ldt_n), wl_gnm(bre), wl_gnm(bim)], 1))
    def cp_(a):
        return a.reshape(64, 2, 16, 64).transpose(1, 3, 0, 2).reshape(128, 64, 16)
    d["s5_cp"] = f(np.stack([cp_(cre), cp_(cim)], 1))
    d["s5_d1"] = f(np.asarray(inp["s5_d"][0]).reshape(16, 128).T)
    d.update(_constants())
    return d


N_CORES = 4


def kernel(**inputs):
    x = np.asarray(inputs["x"], dtype=np.float32)
    shared = _shared_inputs(inputs)
    nc = bass.Bass("TRN2", target_bir_lowering=False)
    mk = MK(nc)
    mk.build("AaBCDEFGH")
    in_maps = []
    for b in range(N_CORES):
        m = dict(shared)
        m["x"] = np.ascontiguousarray(x[b])
        m["xT"] = np.ascontiguousarray(x[b].T)
        in_maps.append(m)
    res = run_bass_kernel_spmd(nc, in_maps, core_ids=list(range(N_CORES)))
    return np.stack([np.asarray(res.results[b]["out"], dtype=np.float32) for b in range(N_CORES)], 0)
```

```python
import math
from contextlib import contextmanager, ExitStack
import numpy as np
import concourse.bass as bass
import concourse.mybir as mybir
from concourse.bass_utils import run_bass_kernel_spmd

F32 = mybir.dt.float32
BF16 = mybir.dt.bfloat16
AF = mybir.ActivationFunctionType
ALU = mybir.AluOpType
AX = mybir.AxisListType

T = 4096
NT = T // 128
NG = T // 512
NCH = T // 256
D = 2048
IN0 = 13344
C_Z, C_XBC, C_DT, C_Q, C_K, C_V, C_G = 0, 2048, 5120, 5152, 7200, 9248, 11296
ALPHA = 4.0 ** 0.25
SEM_CAP = 30000


class Ev:
    __slots__ = ("eng", "sem", "val", "big")

    def __init__(self, eng, sem, val, big=False):
        self.eng = eng
        self.sem = sem
        self.val = val
        self.big = big


class Prog:
    def __init__(self, nc, n_dma_sems=10):
        self.nc = nc
        self.h = {"pe": nc.tensor, "act": nc.scalar, "dve": nc.vector, "pool": nc.gpsimd, "sp": nc.sync}
        self.sem = {}
        self.cnt = {}
        self.gen = {}
        self.waited = {e: {} for e in self.h}
        self._cms = []
        for e in self.h:
            self._new_sem(e)
        self.last_w = {}
        self.readers = {}
        self.bank_last = {}
        self.bankmap = {}
        self.dma_sems = {}
        self.dma_next = {}
        for e in ("sp", "pool", "act"):
            self.dma_sems[e] = [[self._alloc(f"dma_{e}_{i}"), 0] for i in range(n_dma_sems)]
            self.dma_next[e] = 0
        self.n_inst = 0

    def _alloc(self, name):
        cm = self.nc.semaphore(name)
        s = cm.__enter__()
        self._cms.append(cm)
        return s

    def _new_sem(self, e):
        g = self.gen.get(e, -1) + 1
        self.gen[e] = g
        self.sem[e] = self._alloc(f"s_{e}_{g}")
        self.cnt[e] = 0

    def _deps(self, reads, writes, e=None, big=False):
        deps = []
        for r in reads:
            ev = self.last_w.get(r)
            if ev is not None:
                deps.append(ev)
        for w in writes:
            ev = self.last_w.get(w)
            if ev is not None:
                deps.append(ev)
        if e in ("act", "dve", "pool"):
            strong = [ev for ev in deps if ev.eng == e and not (big and ev.big)]
            self._wait(e, strong, same_engine_ok=False)
        for w in writes:
            deps.extend(self.readers.get(w, ()))
        return deps

    def _wait(self, e, deps, same_engine_ok=True):
        wd = self.waited[e]
        need = {}
        for ev in deps:
            if same_engine_ok and ev.eng == e:
                continue
            k = id(ev.sem)
            if wd.get(k, 0) >= ev.val:
                continue
            if k not in need or need[k].val < ev.val:
                need[k] = ev
        for k, ev in need.items():
            self.h[e].wait_ge(ev.sem, ev.val)
            wd[k] = ev.val
            self.n_inst += 1

    def _commit(self, ev, reads, writes):
        for r in reads:
            lst = self.readers.setdefault(r, [])
            lst[:] = [x for x in lst if x.sem is not ev.sem]
            lst.append(ev)
        for w in writes:
            self.last_w[w] = ev
            self.readers[w] = []

    def op(self, e, fn, reads=(), writes=(), sig=True, banks=(), strict=(), big=False):
        if strict:
            sdeps = [self.last_w[r] for r in strict if r in self.last_w]
            self._wait(e, sdeps, same_engine_ok=False)
            reads = list(reads) + list(strict)
        deps = self._deps(reads, writes, e, big)
        banks = set(banks)
        for r in list(reads) + list(writes):
            nm = r if isinstance(r, str) else r[0]
            if isinstance(nm, str) and (nm.startswith("ps") or nm.startswith("ptr")):
                banks.add(self.bankmap.get(r, r))
        for b in banks:
            for eng, bev in self.bank_last.setdefault(b, {}).items():
                if eng != e:
                    deps.append(bev)
        self._wait(e, deps)
        if sig and self.cnt[e] >= SEM_CAP:
            self._new_sem(e)
        inst = fn(self.h[e])
        self.n_inst += 1
        if sig:
            self.cnt[e] += 1
            inst.then_inc(self.sem[e], 1)
            ev = Ev(e, self.sem[e], self.cnt[e], big)
        else:
            ev = Ev(e, self.sem[e], self.cnt[e] + 1, big)
        self._commit(ev, reads, writes)
        for b in banks:
            self.bank_last[b][e] = ev
        return ev

    def dma(self, e, out, in_, reads=(), writes=(), **kw):
        deps = self._deps(reads, writes)
        i = self.dma_next[e]
        self.dma_next[e] = (i + 1) % len(self.dma_sems[e])
        slot = self.dma_sems[e][i]
        if slot[1] > 0:
            deps.append(Ev("dma", slot[0], slot[1]))
        if slot[1] + 16 > SEM_CAP:
            self._wait(e, deps, same_engine_ok=False)
            deps = []
            slot[0] = self._alloc(f"dma_{e}_{i}_{self.n_inst}")
            slot[1] = 0
        self._wait(e, deps, same_engine_ok=False)
        inst = self.h[e].dma_start(out=out, in_=in_, **kw)
        slot[1] += 16
        inst.then_inc(slot[0], 16)
        self.n_inst += 1
        ev = Ev("dma", slot[0], slot[1])
        self._commit(ev, reads, writes)
        return ev

    def barrier(self):
        evs = [Ev(e, self.sem[e], self.cnt[e]) for e in self.h if self.cnt[e] > 0]
        for e in self.dma_sems:
            for s, v in self.dma_sems[e]:
                if v > 0:
                    evs.append(Ev("dma", s, v))
        for e in self.h:
            self._wait(e, evs, same_engine_ok=True)
        self.last_w.clear()
        self.readers.clear()
        self.bank_last.clear()


class MK:
    def __init__(self, nc, dbg=(), feed=()):
        self._lazy = {}
        self.feed = set(feed)
        self.nc = nc
        self.P = Prog(nc)
        self.dbg = set(dbg)
        self._es = None
        self._sid = 0
        self.dr = {}

    @contextmanager
    def stage(self):
        self._sid += 1
        es = ExitStack()
        self._es = es
        try:
            yield
        finally:
            self.P.barrier()
            es.close()

    def sb(self, name, shape, dt):
        return self._es.enter_context(self.nc.sbuf_tensor(f"{name}_s{self._sid}", shape, dt))

    def ps(self, name, shape, dt=F32):
        return self._es.enter_context(self.nc.psum_tensor(f"{name}_s{self._sid}", shape, dt))

    def din(self, name, shape, dt=F32):
        t = self.nc.dram_tensor(name, list(shape), dt, kind="ExternalInput").ap()
        self.dr[name] = t
        return t

    def dout(self, name, shape, dt=F32):
        t = self.nc.dram_tensor(name, list(shape), dt, kind="ExternalOutput").ap()
        self.dr[name] = t
        return t

    def scratch(self, name, shape, dt):
        kind = "ExternalOutput" if name in self.dbg else "Internal"
        t = self.nc.dram_tensor(name, list(shape), dt, kind=kind).ap()
        self.dr[name] = t
        return t

    def __getattr__(self, attr):
        lz = self.__dict__.get("_lazy", {})
        if attr in lz:
            kind, name, shape, dt = lz[attr]
            if kind == "in" or (kind == "scratch" and name in self.feed):
                t = self.din(name, shape, dt)
            elif kind == "out":
                t = self.dout(name, shape, dt)
            else:
                t = self.scratch(name, shape, dt)
            self.__dict__[attr] = t
            return t
        raise AttributeError(attr)

    def declare(self):
        self._lazy["x"] = ("in", "x", [T, D], F32)
        self._lazy["xT"] = ("in", "xT", [D, T], F32)
        self._lazy["w0a"] = ("in", "w0a", [72, 128, 16, 128], F32)
        self._lazy["w0b"] = ("in", "w0b", [8, 128, 16, 512], F32)
        self._lazy["w0dt"] = ("in", "w0dt", [128, 16, 32], F32)
        self._lazy["cw"] = ("in", "cw", [128, 24, 4], F32)
        self._lazy["cb"] = ("in", "cb", [128, 24], F32)
        self._lazy["dtb"] = ("in", "dtb", [1, 32], F32)
        self._lazy["alog"] = ("in", "alog", [1, 32], F32)
        self._lazy["ssd_dp"] = ("in", "ssd_dp", [128, 16], F32)
        self._lazy["normg"] = ("in", "normg", [128, 16], F32)
        self._lazy["cosT"] = ("in", "cosT", [32, T], F32)
        self._lazy["sinS"] = ("in", "sinS", [32, T], F32)
        self._lazy["pm32"] = ("in", "pm32", [32, 32], F32)
        self._lazy["sel"] = ("in", "sel", [32, 32, 128], F32)
        self._lazy["triu"] = ("in", "triu", [128, 128], F32)
        self._lazy["maskg"] = ("in", "maskg", [128, 384], F32)
        self._lazy["ident"] = ("in", "ident", [128, 128], F32)
        self._lazy["vbias"] = ("in", "vbias", [128, NT * NCH], F32)
        self._lazy["w_out0"] = ("in", "out0_w", [2 * D, D], F32)
        self._lazy["ln_g"] = ("in", "ln_g", [2, D], F32)
        self._lazy["ln_b"] = ("in", "ln_b", [2, D], F32)
        self._lazy["w_in1"] = ("in", "w1t", [32, 128, 16, 128], F32)
        self._lazy["w_glu"] = ("in", "wgt", [32, 128, 16, 128], F32)
        self._lazy["w_out1"] = ("in", "out1_w", [D, D], F32)
        self._lazy["out"] = ("out", "out", [T, D], F32)
        self._lazy["s5_pl"] = ("in", "s5_pl", [128, 3, 64], F32)
        self._lazy["s5_wl"] = ("in", "s5_wl", [128, 5, 16, 64], F32)
        self._lazy["s5_cp"] = ("in", "s5_cp", [128, 2, 64, 16], F32)
        self._lazy["s5_d1"] = ("in", "s5_d1", [128, 16], F32)
        self._lazy["rowmask"] = ("in", "rowmask", [128, 8], F32)
        self._lazy["iota513"] = ("in", "iota513", [1, 1025], F32)
        self._lazy["xsT"] = ("scratch", "xsT", [D, T], BF16)
        self._lazy["BT"] = ("scratch", "BT", [512, T], BF16)
        self._lazy["CT"] = ("scratch", "CT", [512, T], BF16)
        self._lazy["zs"] = ("scratch", "zs", [D, T], BF16)
        self._lazy["qT16"] = ("scratch", "qT16", [16, 128, T], BF16)
        self._lazy["kT16"] = ("scratch", "kT16", [16, 128, T], BF16)
        self._lazy["qT32"] = ("scratch", "qT32", [16, 128, T], F32)
        self._lazy["kmean"] = ("scratch", "kmean", [128, 16, NCH], F32)
        self._lazy["V1"] = ("scratch", "V1", [T, 16, 129], BF16)
        self._lazy["gs"] = ("scratch", "gs", [T, D], BF16)
        self._lazy["dtk"] = ("scratch", "dtk", [128, NT, 32], F32)
        self._lazy["yaT"] = ("scratch", "yaT", [D, T], BF16)
        self._lazy["rstd_s"] = ("scratch", "rstd_s", [128, NT], F32)
        self._lazy["ybT"] = ("scratch", "ybT", [D, T], BF16)
        self._lazy["x1"] = ("scratch", "x1", [T, D], F32)
        self._lazy["x1T"] = ("scratch", "x1T", [D, T], BF16)
        self._lazy["uT"] = ("scratch", "uT", [D, T], BF16)
        self._lazy["sg1T"] = ("scratch", "sg1T", [D, T], BF16)
        self._lazy["ygT"] = ("scratch", "ygT", [D, T], BF16)
        self._lazy["y2T"] = ("scratch", "y2T", [D, T], BF16)

    def stage_A(self):
        P, nc = self.P, self.nc
        blk_of = {}
        for i in range(24):
            blk_of[C_XBC + 128 * i] = i
        for i in range(16):
            blk_of[C_Z + 128 * i] = 24 + i
            blk_of[C_Q + 128 * i] = 40 + i
            blk_of[C_K + 128 * i] = 56 + i
        with self.stage():
            xTb = self.sb("xTb", [128, 16, T], BF16)
            for k in range(16):
                P.dma("pool", xTb[:, k, :], self.xT[k * 128:(k + 1) * 128, :], writes=[("xTb", k)])
            wsl = [self.sb(f"wsl{i}", [128, 16, 128], BF16) for i in range(2)]
            psA = [self.ps(f"psA{i}", [128, 512]) for i in range(4)]
            psw = [self.ps(f"psw{i}", [32, 512]) for i in range(2)]
            raw = [self.sb(f"raw{i}", [128, 515], F32) for i in range(2)]
            acc = [self.sb(f"acc{i}", [128, 512], F32) for i in range(2)]
            ob = [self.sb(f"ob{i}", [128, 512], BF16) for i in range(3)]
            qf = [self.sb(f"qf{i}", [128, 512], F32) for i in range(2)]
            tmp32 = [self.sb(f"tmp32{i}", [32, 512], F32) for i in range(2)]
            cw = self.sb("cw", [128, 24, 4], F32)
            cb = self.sb("cb", [128, 24], F32)
            cosT = self.sb("cosT", [32, T], F32)
            sinS = self.sb("sinS", [32, T], F32)
            pm = self.sb("pm", [32, 32], F32)
            kms = self.sb("kms", [128, 16, NCH], F32)
            P.dma("sp", cw[:], self.cw, writes=["cw"])
            P.dma("sp", cb[:], self.cb, writes=["cb"])
            P.dma("sp", cosT[:], self.cosT, writes=["cosT"])
            P.dma("sp", sinS[:], self.sinS, writes=["sinS"])
            P.dma("sp", pm[:], self.pm32, writes=["pm"])

            ctr = {"blk": 0, "grp": 0, "ob": 0, "qf": 0}

            stg = [self.sb(f"stg{i}", [128, 16, 128], F32) for i in range(2)]

            order_a = [C_XBC + 128 * i for i in range(24)] + [C_Z + 128 * i for i in range(16)] + \
                      [C_Q + 128 * i for i in range(16)] + [C_K + 128 * i for i in range(16)]

            def issue_w(jb):
                slot = jb % 2
                P.dma("sp", stg[slot][:, :, :], self.w0a[blk_of[order_a[jb]]], writes=[("stg", slot)])
                P.op("pool", lambda h: h.tensor_copy(wsl[slot][:, :, :], stg[slot][:, :, :]),
                     reads=[("stg", slot)], writes=[("w", slot)])

            def load_w(c0, M):
                jb = ctr["blk"]
                ctr["blk"] += 1
                assert order_a[jb] == c0
                if jb == 0:
                    issue_w(0)
                if jb + 1 < len(order_a):
                    issue_w(jb + 1)
                return jb % 2

            def mm_group(slot, M, g):
                b = ctr["grp"] % 4
                ctr["grp"] += 1
                for k in range(16):
                    P.op("pe", lambda h, k=k: h.matmul(psA[b][:M, :], wsl[slot][:, k, :M],
                                                       xTb[:, k, g * 512:(g + 1) * 512],
                                                       start=(k == 0), stop=(k == 15)),
                         reads=[("w", slot), ("xTb", k)], writes=[("psA", b)], sig=(k == 15))
                return b

            def next_ob():
                i = ctr["ob"] % 3
                ctr["ob"] += 1
                return i

            for i in range(24):
                slot = load_w(C_XBC + 128 * i, 128)
                P.op("pool", lambda h: h.memset(raw[0][:, 0:3], 0.0), writes=[("rawh", 0)])
                for g in range(NG):
                    b = mm_group(slot, 128, g)
                    r = g % 2
                    a = g % 2
                    P.op("act", lambda h: h.activation(raw[r][:, 3:515], psA[b][:, :], AF.Copy),
                         reads=[("psA", b)], writes=[("rawm", r)])
                    P.op("pool", lambda h: h.tensor_copy(raw[1 - r][:, 0:3], raw[r][:, 512:515]),
                         reads=[("rawm", r)], writes=[("rawh", 1 - r)])
                    P.op("dve", lambda h: h.tensor_scalar(acc[a][:, :], raw[r][:, 3:515], cw[:, i, 3:4], cb[:, i:i + 1],
                                                          ALU.mult, ALU.add),
                         reads=[("rawm", r), "cw", "cb"], writes=[("acc", a)])
                    for j in (2, 1, 0):
                        P.op("dve", lambda h, j=j: h.scalar_tensor_tensor(acc[a][:, :], raw[r][:, j:j + 512], cw[:, i, j:j + 1],
                                                                          acc[a][:, :], ALU.mult, ALU.add),
                             reads=[("rawm", r), ("rawh", r), "cw"], writes=[("acc", a)])
                    o = next_ob()
                    P.op("act", lambda h: h.activation(ob[o][:, :], acc[a][:, :], AF.Silu),
                         reads=[("acc", a)], writes=[("ob", o)])
                    if i < 16:
                        dst = self.xsT[i * 128:(i + 1) * 128, g * 512:(g + 1) * 512]
                    elif i < 20:
                        dst = self.BT[(i - 16) * 128:(i - 15) * 128, g * 512:(g + 1) * 512]
                    else:
                        dst = self.CT[(i - 20) * 128:(i - 19) * 128, g * 512:(g + 1) * 512]
                    P.dma("sp", dst, ob[o][:, :], reads=[("ob", o)], writes=[("xbc_out", i, g)])
            for i in range(16):
                slot = load_w(C_Z + 128 * i, 128)
                for g in range(NG):
                    b = mm_group(slot, 128, g)
                    o = next_ob()
                    P.op("act", lambda h: h.activation(ob[o][:, :], psA[b][:, :], AF.Silu),
                         reads=[("psA", b)], writes=[("ob", o)])
                    P.dma("sp", self.zs[i * 128:(i + 1) * 128, g * 512:(g + 1) * 512], ob[o][:, :],
                          reads=[("ob", o)], writes=[("zs", i, g)])
            for hq in range(32):
                is_q = hq < 16
                hd = hq % 16
                slot = load_w((C_Q if is_q else C_K) + 128 * hd, 128)
                for g in range(NG):
                    b = mm_group(slot, 128, g)
                    f = ctr["qf"] % 2
                    ctr["qf"] += 1
                    gs_ = slice(g * 512, (g + 1) * 512)
                    P.op("act", lambda h: h.activation(qf[f][:, :], psA[b][:, :], AF.Copy),
                         reads=[("psA", b)], writes=[("qf", f)])
                    P.op("pe", lambda h: h.matmul(psw[f][:, :], pm[:, :], qf[f][0:32, :], start=True, stop=True),
                         reads=[("qf", f), "pm"], writes=[("psw", f)])
                    P.op("dve", lambda h: h.tensor_tensor(tmp32[f][:, :], psw[f][:, :], sinS[:, gs_], ALU.mult),
                         reads=[("psw", f), "sinS"], writes=[("tmp32", f)])
                    P.op("dve", lambda h: h.tensor_tensor(qf[f][0:32, :], qf[f][0:32, :], cosT[:, gs_], ALU.mult),
                         reads=[("qf", f), "cosT"], writes=[("qf", f)])
                    P.op("dve", lambda h: h.tensor_tensor(qf[f][0:32, :], qf[f][0:32, :], tmp32[f][:, :], ALU.add),
                         reads=[("qf", f), ("tmp32", f)], writes=[("qf", f)])
                    o = next_ob()
                    P.op("act", lambda h: h.activation(ob[o][:, :], qf[f][:, :], AF.Copy),
                         reads=[("qf", f)], writes=[("ob", o)])
                    if is_q:
                        P.dma("sp", self.qT16[hd, :, gs_], ob[o][:, :], reads=[("ob", o)], writes=[("q16", hd, g)])
                        P.dma("sp", self.qT32[hd, :, gs_], qf[f][:, :], reads=[("qf", f)], writes=[("q32", hd, g)])
                    else:
                        P.dma("sp", self.kT16[hd, :, gs_], ob[o][:, :], reads=[("ob", o)], writes=[("k16", hd, g)])
                        P.op("dve", lambda h: h.tensor_reduce(kms[:, hd, 2 * g:2 * g + 2],
                                                              qf[f][:, :].rearrange("p (a b) -> p a b", b=256),
                                                              AX.X, ALU.add),
                             reads=[("qf", f)], writes=["kms"])
            P.op("dve", lambda h: h.tensor_scalar(kms[:, :, :], kms[:, :, :], 1.0 / 256.0, None, ALU.mult),
                 reads=["kms"], writes=["kms"])
            P.dma("sp", self.kmean, kms[:, :, :], reads=["kms"], writes=["kmean"])

    def stage_A2(self):
        P, nc = self.P, self.nc
        with self.stage():
            xTb = self.sb("xTb", [128, 16, T], BF16)
            for k in range(16):
                P.dma("pool", xTb[:, k, :], self.xT[k * 128:(k + 1) * 128, :], writes=[("xTb", k)])
            wb = [self.sb(f"wb{i}", [128, 16, 512], BF16) for i in range(2)]
            psA = [self.ps(f"psA{i}", [128, 512]) for i in range(4)]
            ob = [self.sb(f"ob{i}", [128, 512], BF16) for i in range(3)]
            vt = [self.sb(f"vt{i}", [128, 4, 129], BF16) for i in range(3)]
            dtb = self.sb("dtb", [128, 32], F32)
            dtt = self.sb("dtt", [128, NT, 32], F32)
            P.dma("sp", dtb[:], self.dtb.partition_broadcast(128), writes=["dtb"])
            for i in range(3):
                P.op("pool", lambda h, i=i: h.memset(vt[i][:, :, :], 1.0), writes=[("vt", i)])
            ctr = {"blk": 0, "grp": 0, "ob": 0}

            stg = [self.sb(f"stg{i}", [128, 4, 512], F32) for i in range(2)]
            cst = {"n": 0}

            order_b = [(C_DT, 32)] + [(C_V + 512 * i, 512) for i in range(4)] + [(C_G + 512 * i, 512) for i in range(4)]

            def issue_w(jb):
                c0, M = order_b[jb]
                slot = jb % 2
                for q in range(4):
                    ss = cst["n"] % 2
                    cst["n"] += 1
                    src = self.w0dt[:, 4 * q:4 * q + 4, :] if c0 == C_DT else \
                        self.w0b[((c0 - C_V) // 512) if c0 < C_G else (4 + (c0 - C_G) // 512), :, 4 * q:4 * q + 4, :]
                    P.dma("sp", stg[ss][:, :, :M], src, writes=[("stg", ss)])
                    P.op("pool", lambda h: h.tensor_copy(wb[slot][:, 4 * q:4 * q + 4, :M], stg[ss][:, :, :M]),
                         reads=[("stg", ss)], writes=[("w", slot)])

            def load_w(c0, M):
                jb = ctr["blk"]
                ctr["blk"] += 1
                assert order_b[jb] == (c0, M)
                if jb == 0:
                    issue_w(0)
                if jb + 1 < len(order_b):
                    issue_w(jb + 1)
                return jb % 2

            slot = load_w(C_DT, 32)
            for tt in range(NT):
                b = tt // 16
                for k in range(16):
                    P.op("pe", lambda h, k=k: h.matmul(psA[b][:, (tt % 16) * 32:(tt % 16) * 32 + 32],
                                                       xTb[:, k, tt * 128:(tt + 1) * 128], wb[slot][:, k, :32],
                                                       start=(k == 0), stop=(k == 15)),
                         reads=[("w", slot), ("xTb", k)], writes=[("psA", b)], sig=(k == 15))
            for b in range(2):
                dv = dtt[:, b * 16:(b + 1) * 16, :]
                P.op("dve", lambda h: h.tensor_tensor(dv, psA[b][:, :].rearrange("p (a c) -> p a c", c=32),
                                                      dtb[:, None, :].broadcast_to([128, 16, 32]), ALU.add),
                     reads=[("psA", b), "dtb"], writes=[("dtt", b)])
                P.op("act", lambda h: h.activation(dv, dv, AF.Exp), reads=[("dtt", b)], writes=[("dtt", b)])
                P.op("act", lambda h: h.activation(dv, dv, AF.Ln, bias=1.0), reads=[("dtt", b)], writes=[("dtt", b)])
            P.dma("sp", self.dtk, dtt[:, :, :], reads=[("dtt", 0), ("dtt", 1)], writes=["dtk"])
            ctr["grp"] = 2
            for fam in ("v", "g"):
                for cg in range(4):
                    slot = load_w((C_V if fam == "v" else C_G) + 512 * cg, 512)
                    for tt in range(NT):
                        b = ctr["grp"] % 4
                        ctr["grp"] += 1
                        for k in range(16):
                            P.op("pe", lambda h, k=k: h.matmul(psA[b][:, :], xTb[:, k, tt * 128:(tt + 1) * 128],
                                                               wb[slot][:, k, :], start=(k == 0), stop=(k == 15)),
                                 reads=[("w", slot), ("xTb", k)], writes=[("psA", b)], sig=(k == 15))
                        o = ctr["ob"] % 3
                        ctr["ob"] += 1
                        if fam == "v":
                            P.op("act", lambda h: h.activation(vt[o][:, :, 0:128],
                                                               psA[b][:, :].rearrange("p (a c) -> p a c", c=128), AF.Copy),
                                 reads=[("psA", b)], writes=[("vt", o)])
                            P.dma("sp", self.V1[tt * 128:(tt + 1) * 128, 4 * cg:4 * cg + 4, :], vt[o][:, :, :],
                                  reads=[("vt", o)], writes=[("V1", tt, cg)])
                        else:
                            P.op("act", lambda h: h.activation(ob[o][:, :], psA[b][:, :], AF.Silu),
                                 reads=[("psA", b)], writes=[("ob", o)])
                            P.dma("sp", self.gs[tt * 128:(tt + 1) * 128, cg * 512:(cg + 1) * 512], ob[o][:, :],
                                  reads=[("ob", o)], writes=[("gs", tt, cg)])

    def stage_B(self):
        P, nc = self.P, self.nc
        xsTv = self.xsT.rearrange("(k p) t -> p k t", p=128)
        zsv = self.zs.rearrange("(k p) t -> p k t", p=128)
        BTv = self.BT.rearrange("(k p) t -> p k t", p=128)
        CTv = self.CT.rearrange("(k p) t -> p k t", p=128)
        with self.stage():
            sb = self.sb
            pb = [self.ps(f"pb{i}", [128, 512]) for i in range(6)]
            ptrs = [self.ps(f"ptr{i}", [128, 1024], BF16) for i in range(2)]
            ps_small, ps_T = pb[0][:, 0:96], pb[0][0:32, 128:384]
            ps_q = pb[0][:, 100:102]
            ps_g = [pb[1][:, 0:384]]
            ps_bc = [pb[2][:, 0:256], pb[3][:, 0:256]]
            ps_y = [pb[4][:, 0:256]]
            ps_st = [pb[5], pb[5]]
            P.bankmap = {"ps_small": "B0", "ps_q": "B0", ("ps_y", 0, 0): "BY", ("ps_y", 0, 1): "BY",
                         ("ps_st", 0): "BST", ("ps_st", 1): "BST"}
            xsc = [sb(f"xsc{i}", [128, 16, 256], BF16) for i in range(2)]
            zsc = [sb(f"zsc{i}", [128, 16, 256], BF16) for i in range(2)]
            btc = [sb(f"btc{i}", [128, 4, 256], BF16) for i in range(2)]
            ctc = [sb(f"ctc{i}", [128, 4, 256], BF16) for i in range(2)]
            dtt = sb("dtt", [128, NT, 32], F32)
            abc = sb("abc", [128, 32], F32)
            sel = sb("sel", [32, 32, 128], F32)
            triu = sb("triu", [128, 128], F32)
            ones = sb("ones", [128, 128], F32)
            r0 = sb("r0", [128, 256], F32)
            maskg = sb("maskg", [128, 384], F32)
            ident = sb("ident", [128, 128], BF16)
            dp = sb("dp", [128, 16], F32)
            ng = sb("ng", [128, 16], F32)
            atok = sb("atok", [128, 2, 32], F32)
            acum = sb("acum", [128, 3, 32], F32)
            acT = sb("acT", [32, 256], F32)
            dte = sb("dte", [128, 2, 32], F32)
            eAt = sb("eAt", [128, 32], F32)
            X = sb("X", [128, 2, 2048], BF16)
            Xd = sb("Xd", [128, 2, 2048], BF16)
            Btok = sb("Btok", [128, 2, 512], BF16)
            Gm = sb("Gm", [128, 4, 384], F32)
            H32 = sb("H32", [128, 32, 64], F32)
            H16 = sb("H16", [128, 32, 64], BF16)
            Dm = [sb(f"Dm{i}", [128, 384], F32) for i in range(2)]
            MT = [sb(f"MT{i}", [128, 384], BF16) for i in range(2)]
            eA = [sb(f"eA{i}", [128, 256], F32) for i in range(2)]
            Ct = [sb(f"Ct{i}", [128, 256], BF16) for i in range(2)]
            yf = [sb(f"yf{i}", [128, 256], F32) for i in range(2)]
            yg = [sb(f"yg{i}", [128, 256], F32) for i in range(2)]
            sq = [sb(f"sq{i}", [128, 256], F32) for i in range(2)]
            ya16 = [sb(f"ya16{i}", [128, 256], BF16) for i in range(2)]
            rs = sb("rs", [128, NT], F32)

            P.dma("sp", dtt[:], self.dtk, writes=["dtt"])
            P.dma("sp", abc[:], self.alog.partition_broadcast(128), writes=["abc"])
            P.dma("sp", sel[:], self.sel, writes=["sel"])
            P.dma("sp", triu[:], self.triu, writes=["triu"])
            P.dma("sp", maskg[:], self.maskg, writes=["maskg"])
            P.dma("sp", r0[:], self.maskg[:, 0:256], writes=["r0"])
            P.dma("pool", ident[:], self.ident, writes=["ident"])
            P.dma("sp", dp[:], self.ssd_dp, writes=["dp"])
            P.dma("sp", ng[:], self.normg, writes=["ng"])
            P.op("pool", lambda h: h.memset(ones[:], 1.0), writes=["ones"])
            P.op("pool", lambda h: h.memset(H32[:], 0.0), writes=["H32"])
            P.op("pool", lambda h: h.memset(H16[:], 0.0), writes=[("H16", g) for g in range(4)])
            P.op("act", lambda h: h.activation(abc[:], abc[:], AF.Exp), reads=["abc"], writes=["abc"])
            P.op("dve", lambda h: h.tensor_scalar(abc[:], abc[:], -1.0, None, ALU.mult), reads=["abc"], writes=["abc"])

            def load_chunk(c):
                s_ = c % 2
                cs = slice(c * 256, (c + 1) * 256)
                P.dma("sp", xsc[s_][:], xsTv[:, :, cs], writes=[("xsc", s_)])
                P.dma("sp", zsc[s_][:], zsv[:, :, cs], writes=[("zsc", s_)])
                P.dma("pool", btc[s_][:], BTv[:, :, cs], writes=[("btc", s_)])
                P.dma("pool", ctc[s_][:], CTv[:, :, cs], writes=[("ctc", s_)])

            bstop = getattr(self, 'bstop', 99)
            if bstop <= 1:
                return
            load_chunk(0)
            hc = 0
            for c in range(getattr(self, 'b_chunks', NCH)):
                s_ = c % 2
                if c + 1 < NCH:
                    load_chunk(c + 1)
                P.op("dve", lambda h: h.tensor_tensor(atok[:], dtt[:, 2 * c:2 * c + 2, :],
                                                      abc[:, None, :].broadcast_to([128, 2, 32]), ALU.mult),
                     reads=["dtt", "abc"], writes=["atok"])
                mm = lambda out, l, r, st, sp, sg: P.op(
                    "pe", lambda h: h.matmul(out, l, r, start=st, stop=sp), reads=["atok", "triu", "ones", "r0"],
                    writes=["ps_small"], sig=sg)
                mm(ps_small[:, 0:32], triu[:], atok[:, 0, :], True, True, False)
                mm(ps_small[:, 32:64], ones[:], atok[:, 0, :], True, False, False)
                mm(ps_small[:, 32:64], triu[:], atok[:, 1, :], False, True, False)
                mm(ps_small[:, 64:96], ones[:], atok[:, 0, :], True, False, False)
                mm(ps_small[:, 64:96], ones[:], atok[:, 1, :], False, True, False)
                mm(ps_T[:, :], atok[:, 0, :], r0[:], True, False, False)
                mm(ps_T[:, 128:256], atok[:, 1, :], triu[:], False, True, True)
                P.op("act", lambda h: h.activation(acum[:].rearrange("p a b -> p (a b)"), ps_small, AF.Copy),
                     reads=["ps_small"], writes=["acum"])
                P.op("act", lambda h: h.activation(acT[:], ps_T, AF.Copy), reads=["ps_small"], writes=["acT"])
                P.op("dve", lambda h: h.tensor_tensor(dte[:], acum[:, 2:3, :].broadcast_to([128, 2, 32]), acum[:, 0:2, :],
                                                      ALU.subtract), reads=["acum"], writes=["dte"])
                P.op("act", lambda h: h.activation(dte[:], dte[:], AF.Exp), reads=["dte"], writes=["dte"])
                P.op("act", lambda h: h.activation(eAt[:], acum[:, 2, :], AF.Exp), reads=["acum"], writes=["eAt"])
                if bstop <= 2:
                    return
                tb = 0
                bsub = getattr(self, 'bsub', 'xdb')
                for j in range(2):
                    for q4 in range(4):
                        half = tb % 2
                        tb += 1
                        for kk in range(4):
                            k = q4 * 4 + kk
                            P.op("pe", lambda h: h.transpose(ptrs[half][:, kk * 128:(kk + 1) * 128],
                                                             xsc[s_][:, k, j * 128:(j + 1) * 128], ident[:]),
                                 reads=[("xsc", s_), "ident"], writes=[("ptr", half)], sig=(kk == 3))
                        hs = slice(8 * q4, 8 * q4 + 8)
                        cs_ = slice(512 * q4, 512 * q4 + 512)
                        P.op("dve", lambda h: h.tensor_tensor(
                            X[:, j, cs_].rearrange("p (a b) -> p a b", b=64),
                            ptrs[half][:, 0:512].rearrange("p (a b) -> p a b", b=64),
                            dtt[:, 2 * c + j, hs].unsqueeze(2).broadcast_to([128, 8, 64]), ALU.mult),
                            reads=[("ptr", half), "dtt"], writes=[("X", j, q4)])
                        if 'd' in bsub:
                          P.op("pool", lambda h: h.tensor_tensor(
                            Xd[:, j, cs_].rearrange("p (a b) -> p a b", b=64),
                            X[:, j, cs_].rearrange("p (a b) -> p a b", b=64),
                            dte[:, j, hs].unsqueeze(2).broadcast_to([128, 8, 64]), ALU.mult),
                            reads=[("X", j, q4), "dte"], writes=[("Xd", j, q4)])
                    if 'b' not in bsub:
                        continue
                    half = tb % 2
                    tb += 1
                    for g in range(4):
                        P.op("pe", lambda h: h.transpose(ptrs[half][:, g * 128:(g + 1) * 128],
                                                         btc[s_][:, g, j * 128:(j + 1) * 128], ident[:]),
                             reads=[("btc", s_), "ident"], writes=[("ptr", half)], sig=(g == 3))
                    P.op("act", lambda h: h.activation(Btok[:, j, :], ptrs[half][:, 0:512], AF.Copy),
                         reads=[("ptr", half)], writes=[("Btok", j)])
                if bstop <= 3:
                    return
                for g in range(4):
                    gb = 0
                    P.op("pe", lambda h: h.matmul(ps_g[gb][:, 0:256], btc[s_][:, g, 0:128], ctc[s_][:, g, :],
                                                  start=True, stop=True),
                         reads=[("btc", s_), ("ctc", s_)], writes=[("ps_g", gb)], sig=False)
                    P.op("pe", lambda h: h.matmul(ps_g[gb][:, 256:384], btc[s_][:, g, 128:256], ctc[s_][:, g, 128:256],
                                                  start=True, stop=True),
                         reads=[("btc", s_), ("ctc", s_)], writes=[("ps_g", gb)])
                    P.op("dve", lambda h: h.tensor_tensor(Gm[:, g, :], ps_g[gb], maskg[:], ALU.mult),
                         reads=[("ps_g", gb), "maskg"], writes=[("Gm", g)])
                if bstop <= 4:
                    return
                def emit_bc(hd_):
                    tt_ = hd_ % 2
                    P.op("pe", lambda h: h.matmul(ps_bc[tt_], sel[:, hd_, :], acT[:], start=True, stop=True),
                         reads=["sel", "acT"], writes=[("ps_bc", tt_)])

                for hd in range(getattr(self, 'b_heads', 32)):
                    g = hd // 8
                    pair = hd // 2
                    hh = hd % 2
                    t_ = hd % 2
                    hcols = slice(hd * 64, (hd + 1) * 64)
                    if hd == 0:
                        emit_bc(0)
                    P.op("dve", lambda h: h.tensor_scalar(Dm[t_][:, 0:256], ps_bc[t_], acum[:, 0, hd:hd + 1], 0.0,
                                                          ALU.subtract, ALU.min),
                         reads=[("ps_bc", t_), "acum"], writes=[("Dm", t_)])
                    P.op("dve", lambda h: h.tensor_scalar(Dm[t_][:, 256:384], ps_bc[t_][:, 128:256], acum[:, 1, hd:hd + 1],
                                                          0.0, ALU.subtract, ALU.min),
                         reads=[("ps_bc", t_), "acum"], writes=[("Dm", t_)])
                    P.op("act", lambda h: h.activation(Dm[t_][:], Dm[t_][:], AF.Exp), reads=[("Dm", t_)], writes=[("Dm", t_)])
                    P.op("dve", lambda h: h.tensor_tensor(MT[t_][:], Dm[t_][:], Gm[:, g, :], ALU.mult),
                         reads=[("Dm", t_), ("Gm", g)], writes=[("MT", t_)])
                    P.op("act", lambda h: h.activation(eA[t_][:], ps_bc[t_], AF.Exp), reads=[("ps_bc", t_)], writes=[("eA", t_)])
                    P.op("pool", lambda h: h.tensor_tensor(Ct[t_][:], ctc[s_][:, g, :], eA[t_][:], ALU.mult),
                         reads=[("ctc", s_), ("eA", t_)], writes=[("Ct", t_)])
                    if hd + 1 < getattr(self, 'b_heads', 32):
                        emit_bc(hd + 1)
                    yb = 0
                    yo = ps_y[yb][hh * 64:(hh + 1) * 64, :]
                    yres = ("ps_y", yb, hh)
                    P.op("pe", lambda h: h.matmul(yo[:, 0:256], X[:, 0, hcols], MT[t_][:, 0:256], start=True, stop=False),
                         reads=[("X", 0, hd // 8), ("MT", t_)], writes=[yres], sig=False)
                    P.op("pe", lambda h: h.matmul(yo[:, 128:256], X[:, 1, hcols], MT[t_][:, 256:384], start=False, stop=False),
                         reads=[("X", 1, hd // 8), ("MT", t_)], writes=[yres], sig=False)
                    P.op("pe", lambda h: h.matmul(yo[:, 0:256], H16[:, hd, :], Ct[t_][:], start=False, stop=True),
                         reads=[("H16", g), ("Ct", t_)], writes=[yres])
                    so = ps_st[g % 2][:, (hd % 8) * 64:(hd % 8 + 1) * 64]
                    P.op("pe", lambda h: h.matmul(so, Btok[:, 0, g * 128:(g + 1) * 128], Xd[:, 0, hcols], start=True, stop=False),
                         reads=[("Btok", 0), ("Xd", 0, hd // 8)], writes=[("ps_st", g % 2)], sig=False)
                    P.op("pe", lambda h: h.matmul(so, Btok[:, 1, g * 128:(g + 1) * 128], Xd[:, 1, hcols], start=False, stop=True),
                         reads=[("Btok", 1), ("Xd", 1, hd // 8)], writes=[("ps_st", g % 2)])
                    if hd % 8 == 7:
                        hsl = slice(8 * g, 8 * g + 8)
                        P.op("dve", lambda h: h.tensor_tensor(H32[:, hsl, :], H32[:, hsl, :],
                                                              eAt[:, hsl].unsqueeze(2).broadcast_to([128, 8, 64]), ALU.mult),
                             reads=["H32", "eAt"], writes=["H32"])
                        P.op("dve", lambda h: h.tensor_tensor(H32[:, hsl, :], H32[:, hsl, :],
                                                              ps_st[g % 2][:, :].rearrange("p (a b) -> p a b", b=64), ALU.add),
                             reads=["H32", ("ps_st", g % 2)], writes=["H32"])
                        P.op("act", lambda h: h.activation(H16[:, hsl, :], H32[:, hsl, :], AF.Copy),
                             reads=["H32"], writes=[("H16", g)])
                    if hh == 1:
                        e_ = pair % 2
                        P.op("dve", lambda h: h.scalar_tensor_tensor(yf[e_][:], xsc[s_][:, pair, :], dp[:, pair:pair + 1],
                                                                     ps_y[yb], ALU.mult, ALU.add),
                             reads=[("xsc", s_), "dp", ("ps_y", yb, 0), ("ps_y", yb, 1)], writes=[("yf", e_)])
                        P.op("pool", lambda h: h.tensor_tensor(yg[e_][:], yf[e_][:], zsc[s_][:, pair, :], ALU.mult),
                             reads=[("yf", e_), ("zsc", s_)], writes=[("yg", e_)])
                        P.op("act", lambda h: h.activation(sq[e_][:], yg[e_][:], AF.Square), reads=[("yg", e_)], writes=[("sq", e_)])
                        for j in range(2):
                            P.op("pe", lambda h: h.matmul(ps_q[:, j:j + 1], sq[e_][:, j * 128:(j + 1) * 128], ones[:, 0:1],
                                                          start=(pair == 0 and j == 0), stop=(pair == 15)),
                                 reads=[("sq", e_), "ones"], writes=["ps_q"], sig=(j == 1))
                        P.op("act", lambda h: h.activation(ya16[e_][:], yg[e_][:], AF.Copy, scale=ng[:, pair:pair + 1]),
                             reads=[("yg", e_), "ng"], writes=[("ya16", e_)])
                        P.dma("sp", self.yaT[pair * 128:(pair + 1) * 128, c * 256:(c + 1) * 256], ya16[e_][:],
                              reads=[("ya16", e_)], writes=[("yaT", pair, c)])
                P.op("dve", lambda h: h.tensor_scalar(rs[:, 2 * c:2 * c + 2], ps_q, 1.0 / 2048.0, 1e-5, ALU.mult, ALU.add),
                     reads=["ps_q"], writes=["rs"])
            P.op("act", lambda h: h.activation(rs[:], rs[:], AF.Ln), reads=["rs"], writes=["rs"])
            P.op("act", lambda h: h.activation(rs[:], rs[:], AF.Exp, scale=-0.5), reads=["rs"], writes=["rs"])
            P.dma("sp", self.rstd_s, rs[:], reads=["rs"], writes=["rstd_s"])

    def stage_C(self):
        P, nc = self.P, self.nc
        V1v = self.V1.rearrange("(t p) h c -> p t h c", p=128)
        gsv = self.gs.rearrange("(t p) c -> p t c", p=128)
        SC = 1.0 / math.sqrt(128.0)
        with self.stage():
            sb = self.sb
            psS = [self.ps(f"psS{i}", [128, 512]) for i in range(2)]
            psO = [[self.ps(f"psO{i}{x}", [128, 512]) for x in "XY"] for i in range(2)]
            ps_gate = self.ps("ps_gate", [128, 512])
            ps_tr = self.ps("ps_tr", [128, 1024], BF16)
            q16 = [sb(f"q16{i}", [128, T], BF16) for i in range(2)]
            k16 = [sb(f"k16{i}", [128, T], BF16) for i in range(2)]
            v1 = [sb(f"v1{i}", [128, NT, 129], BF16) for i in range(2)]
            q32 = [sb(f"q32{i}", [128, T], F32) for i in range(2)]
            gsh = [sb(f"gsh{i}", [128, NT, 128], BF16) for i in range(2)]
            km = [sb(f"km{i}", [128, NCH], F32) for i in range(2)]
            vbias = sb("vbias", [128, NT, NCH], F32)
            tri16 = sb("tri16", [128, 128], BF16)
            ident = sb("ident", [128, 128], BF16)
            gm = sb("gm", [128, NT, NCH], F32)
            top8 = sb("top8", [128, NT, 8], F32)
            mask = sb("mask", [128, NT, NCH], F32)
            E = [sb(f"E{i}", [128, 512], BF16) for i in range(3)]
            acc = [sb(f"acc{i}", [128, 4, 129], F32) for i in range(2)]
            rec = sb("rec", [128, 8], F32)
            yb16 = [sb(f"yb16{i}", [128, 128], BF16) for i in range(2)]
            ybo = [sb(f"ybo{i}", [128, 512], BF16) for i in range(2)]
            P.dma("sp", vbias[:].rearrange("p a b -> p (a b)"), self.vbias, writes=["vbias"])
            P.dma("pool", tri16[:], self.triu, writes=["tri16"])
            P.dma("pool", ident[:], self.ident, writes=["ident"])

            def load_head(hd):
                s_ = hd % 2
                P.dma("sp", q16[s_][:], self.qT16[hd], writes=[("q16", s_)])
                P.dma("sp", k16[s_][:], self.kT16[hd], writes=[("k16", s_)])
                P.dma("sp", v1[s_][:], V1v[:, :, hd, :], writes=[("v1", s_)])
                P.dma("sp", q32[s_][:], self.qT32[hd], writes=[("q32", s_)])
                P.dma("sp", gsh[s_][:], gsv[:, :, hd * 128:(hd + 1) * 128], writes=[("gsh", s_)])
                P.dma("sp", km[s_][:], self.kmean[:, hd, :], writes=[("km", s_)])

            load_head(0)
            cS = cE = cY = 0
            nheads = getattr(self, "c_heads", 16)
            for hd in range(nheads):
                s_ = hd % 2
                if hd + 1 < nheads:
                    load_head(hd + 1)
                for qt in range(NT):
                    P.op("pe", lambda h: h.matmul(ps_gate[:, qt * NCH:(qt + 1) * NCH], q32[s_][:, qt * 128:(qt + 1) * 128],
                                                  km[s_][:], start=True, stop=True),
                         reads=[("q32", s_), ("km", s_)], writes=["ps_gate"], sig=(qt == NT - 1))
                P.op("dve", lambda h: h.tensor_tensor(gm[:].rearrange("p a b -> p (a b)"), ps_gate[:, :],
                                                      vbias[:].rearrange("p a b -> p (a b)"), ALU.add),
                     reads=["ps_gate", "vbias"], writes=["gm"])
                for qt in range(NT):
                    P.op("dve", lambda h: h.max(top8[:, qt, :], gm[:, qt, :]), reads=["gm"], writes=["top8"])
                P.op("dve", lambda h: h.tensor_tensor(mask[:], gm[:], top8[:, :, 2:3].broadcast_to([128, NT, NCH]), ALU.is_ge),
                     reads=["gm", "top8"], writes=["mask"])
                for j in range(NG):
                    ab = j % 2
                    first_acc = [True] * 4
                    steps = []
                    for n in range(2 * j + 2):
                        for kt in range(2):
                            if n < 2 * j:
                                first, diag = 0, False
                            elif n == 2 * j:
                                first, diag = kt, True
                            else:
                                first, diag = 2 + kt, True
                            sbk = cS % 2
                            cS += 1
                            eb = cE % 3
                            cE += 1
                            steps.append(dict(n=n, kt=kt, first=first, diag=diag, sbk=sbk, eb=eb, K=2 * n + kt,
                                              N=(4 - first) * 128, q0=(4 * j + first) * 128))

                    def emit_qk(sp_):
                        sbk, N, q0, K_ = sp_["sbk"], sp_["N"], sp_["q0"], sp_["K"]
                        P.op("pe", lambda h: h.matmul(psS[sbk][:, 0:N], k16[s_][:, K_ * 128:(K_ + 1) * 128],
                                                      q16[s_][:, q0:q0 + N], start=True, stop=True),
                             reads=[("k16", s_), ("q16", s_)], writes=[("psS", sbk)])

                    def emit_exp(sp_):
                        sbk, N, eb = sp_["sbk"], sp_["N"], sp_["eb"]
                        P.op("act", lambda h: h.activation(E[eb][:, 0:N], psS[sbk][:, 0:N], AF.Exp, scale=SC),
                             reads=[("psS", sbk)], writes=[("E", eb)])
                        if sp_["diag"]:
                            P.op("pool", lambda h: h.tensor_tensor(E[eb][:, 0:128], E[eb][:, 0:128], tri16[:], ALU.mult),
                                 reads=[("E", eb), "tri16"], writes=[("E", eb)])

                    blk_state = {}

                    def emit_pv(sp_):
                        n, kt, first, eb, K_ = sp_["n"], sp_["kt"], sp_["first"], sp_["eb"], sp_["K"]
                        nb = n % 2
                        stt = blk_state.setdefault(n, {"X": False, "Y": False, "vis": set()})
                        for t in range(first, 4):
                            x = "X" if t < 2 else "Y"
                            ob = psO[nb][0 if t < 2 else 1]
                            last_kt = (kt == 1) or (n == 2 * j and t == 0) or (n == 2 * j + 1 and t == 2)
                            st = not stt[x]
                            stt[x] = True
                            stt["vis"].add(t)
                            P.op("pe", lambda h: h.matmul(ob[:, (t % 2) * 256:(t % 2) * 256 + 129],
                                                          E[eb][:, (t - first) * 128:(t - first + 1) * 128],
                                                          v1[s_][:, K_, :], start=st, stop=last_kt),
                                 reads=[("E", eb), ("v1", s_)], writes=[("psO", nb, x)], sig=(t == 3))

                    def emit_acc(n):
                        nb = n % 2
                        for t in sorted(blk_state[n]["vis"]):
                            x = "X" if t < 2 else "Y"
                            ob = psO[nb][0 if t < 2 else 1][:, (t % 2) * 256:(t % 2) * 256 + 129]
                            own = (n == 2 * j + t // 2)
                            mcol = mask[:, 4 * j + t, n:n + 1]
                            a_t = acc[ab][:, t, :]
                            if first_acc[t]:
                                first_acc[t] = False
                                if own:
                                    P.op("dve", lambda h: h.tensor_copy(a_t, ob), reads=[("psO", nb, x)], writes=[("acc", ab, t)])
                                else:
                                    P.op("dve", lambda h: h.tensor_scalar(a_t, ob, mcol, None, ALU.mult),
                                         reads=[("psO", nb, x)], writes=[("acc", ab, t)], strict=["mask"])
                            elif own:
                                P.op("dve", lambda h: h.tensor_tensor(a_t, ob, a_t, ALU.add),
                                     reads=[("psO", nb, x), ("acc", ab, t)], writes=[("acc", ab, t)])
                            else:
                                P.op("dve", lambda h: h.scalar_tensor_tensor(a_t, ob, mcol, a_t, ALU.mult, ALU.add),
                                     reads=[("psO", nb, x), ("acc", ab, t)], writes=[("acc", ab, t)], strict=["mask"])

                    emit_qk(steps[0])
                    for i_, sp_ in enumerate(steps):
                        if i_ + 1 < len(steps):
                            emit_qk(steps[i_ + 1])
                        emit_exp(sp_)
                        emit_pv(sp_)
                        if sp_["kt"] == 1:
                            emit_acc(sp_["n"])
                    yo = cY % 2
                    cY += 1
                    for t in range(4):
                        yb_ = t % 2
                        P.op("dve", lambda h: h.reciprocal(rec[:, t:t + 1], acc[ab][:, t, 128:129]),
                             reads=[("acc", ab, t)], writes=[("rec", t)])
                        P.op("dve", lambda h: h.scalar_tensor_tensor(yb16[yb_][:], acc[ab][:, t, 0:128], rec[:, t:t + 1],
                                                                     gsh[s_][:, 4 * j + t, :], ALU.mult, ALU.mult),
                             reads=[("acc", ab, t), ("gsh", s_)], writes=[("yb16", yb_)], strict=[("rec", t)])
                        P.op("pe", lambda h: h.transpose(ps_tr[:, t * 128:(t + 1) * 128], yb16[yb_][:], ident[:]),
                             reads=[("yb16", yb_), "ident"], writes=["ps_tr"])
                    P.op("act", lambda h: h.activation(ybo[yo][:], ps_tr[:, 0:512], AF.Copy), reads=["ps_tr"], writes=[("ybo", yo)])
                    P.dma("sp", self.ybT[hd * 128:(hd + 1) * 128, j * 512:(j + 1) * 512], ybo[yo][:],
                          reads=[("ybo", yo)], writes=[("ybT", hd, j)])

    def outproj_ln(self, parts, w_dram, nk_total, resid, layer, out_dram, xT_out=None, rstd_dram=None):
        P, nc = self.P, self.nc
        wv = w_dram.rearrange("(k p) c -> p k c", p=128)
        with self.stage():
            sb = self.sb
            W = sb("W", [128, nk_total, D], BF16)
            for k in range(nk_total):
                P.dma("pool", W[:, k, :], wv[:, k, :], writes=[("W", k)])
            npart = len(parts)
            psP = [[self.ps(f"psP{i}{a}", [128, 512]) for a in range(npart)] for i in range(2)]
            ps_tr = [self.ps(f"ps_tr{i}", [128, 1024], BF16) for i in range(2)] if xT_out is not None else None
            lt = [[sb(f"lt{i}{a}", [128, parts[a][2], 128], BF16) for a in range(npart)] for i in range(2)]
            xt = [sb(f"xt{i}", [128, D], F32) for i in range(2)]
            v = sb("v", [128, D], F32)
            junk = sb("junk", [128, D], BF16)
            gbc = sb("gbc", [128, D], F32)
            bbc = sb("bbc", [128, D], F32)
            st = sb("st", [128, 8], F32)
            P.dma("sp", gbc[:], self.ln_g[layer:layer + 1, :].partition_broadcast(128), writes=["gbc"])
            P.dma("sp", bbc[:], self.ln_b[layer:layer + 1, :].partition_broadcast(128), writes=["bbc"])
            if rstd_dram is not None:
                rs = sb("rs", [128, NT], F32)
                P.dma("sp", rs[:], rstd_dram, writes=["rs"])
            if xT_out is not None:
                ident = sb("ident", [128, 128], BF16)
                P.dma("pool", ident[:], self.ident, writes=["ident"])
                x1b = sb("x1b", [128, D], BF16)
                xTt = sb("xTt", [128, 16, 128], BF16)
                xTv = xT_out.rearrange("(k p) t -> p k t", p=128)
            fv = [pt[0].rearrange("(k p) t -> p k t", p=128) for pt in parts]

            def load_tile(tt):
                s_ = tt % 2
                for a in range(npart):
                    P.dma("sp", lt[s_][a][:], fv[a][:, :, tt * 128:(tt + 1) * 128], writes=[("lt", s_, a)])
                P.dma("sp", xt[s_][:], resid[tt * 128:(tt + 1) * 128, :], writes=[("xt", s_)])

            load_tile(0)
            cP = 0
            for tt in range(NT):
                s_ = tt % 2
                if tt + 1 < NT:
                    load_tile(tt + 1)
                for cg in range(4):
                    pb = cP % 2
                    cP += 1
                    cs = slice(cg * 512, (cg + 1) * 512)
                    for a, (_, k0, nk, use_rstd) in enumerate(parts):
                        for k in range(nk):
                            P.op("pe", lambda h: h.matmul(psP[pb][a][:, :], lt[s_][a][:, k, :], W[:, k0 + k, cs],
                                                          start=(k == 0), stop=(k == nk - 1)),
                                 reads=[("lt", s_, a), ("W", k0 + k)], writes=[("psP", pb, a)], sig=(k == nk - 1))
                    first = True
                    for a, (_, k0, nk, use_rstd) in enumerate(parts):
                        if use_rstd:
                            continue
                        P.op("dve", lambda h: h.scalar_tensor_tensor(v[:, cs], xt[s_][:, cs], ALPHA, psP[pb][a][:, :],
                                                                     ALU.mult, ALU.add),
                             reads=[("xt", s_), ("psP", pb, a)], writes=[("v", cg)], big=True)
                        first = False
                    for a, (_, k0, nk, use_rstd) in enumerate(parts):
                        if not use_rstd:
                            continue
                        P.op("dve", lambda h: h.scalar_tensor_tensor(v[:, cs], psP[pb][a][:, :], rs[:, tt:tt + 1], v[:, cs],
                                                                     ALU.mult, ALU.add),
                             reads=[("psP", pb, a), ("v", cg)], writes=[("v", cg)], strict=["rs"], big=True)
                vres = [("v", cg) for cg in range(4)]
                P.op("act", lambda h: h.activation(junk[:], v[:], AF.Square), reads=vres, writes=["junk"])
                P.op("dve", lambda h: h.reduce_sum(st[:, 0:1], v[:], AX.X), reads=vres, writes=[("st", 0)])
                P.op("dve", lambda h: h.reduce_sum(st[:, 1:2], junk[:], AX.X), reads=["junk"], writes=[("st", 1)])
                P.op("dve", lambda h: h.tensor_scalar(st[:, 2:3], st[:, 0:1], 1.0 / D, None, ALU.mult),
                     reads=[("st", 0)], writes=[("st", 2)])
                P.op("dve", lambda h: h.tensor_tensor(st[:, 3:4], st[:, 2:3], st[:, 2:3], ALU.mult),
                     reads=[("st", 2)], writes=[("st", 3)])
                P.op("dve", lambda h: h.scalar_tensor_tensor(st[:, 4:5], st[:, 1:2], 1.0 / D, st[:, 3:4], ALU.mult, ALU.subtract),
                     reads=[("st", 1), ("st", 3)], writes=[("st", 4)])
                P.op("dve", lambda h: h.tensor_scalar(st[:, 4:5], st[:, 4:5], 1e-5, None, ALU.add),
                     reads=[("st", 4)], writes=[("st", 4)])
                P.op("act", lambda h: h.activation(st[:, 5:6], st[:, 4:5], AF.Ln), reads=[("st", 4)], writes=[("st", 5)])
                P.op("act", lambda h: h.activation(st[:, 5:6], st[:, 5:6], AF.Exp, scale=-0.5), reads=[("st", 5)], writes=[("st", 5)])
                P.op("dve", lambda h: h.scalar_tensor_tensor(st[:, 6:7], st[:, 2:3], -1.0, st[:, 5:6], ALU.mult, ALU.mult),
                     reads=[("st", 2), ("st", 5)], writes=[("st", 6)])
                P.op("act", lambda h: h.activation(v[:], v[:], AF.Identity, bias=st[:, 6:7], scale=st[:, 5:6]),
                     reads=vres, writes=vres, strict=[("st", 5), ("st", 6)])
                P.op("dve", lambda h: h.tensor_tensor(v[:], v[:], gbc[:], ALU.mult), reads=vres + ["gbc"], writes=vres, big=True)
                P.op("dve", lambda h: h.tensor_tensor(v[:], v[:], bbc[:], ALU.add), reads=vres + ["bbc"], writes=vres, big=True)
                P.dma("sp", out_dram[tt * 128:(tt + 1) * 128, :], v[:], reads=vres, writes=[("out", tt)])
                if xT_out is not None:
                    P.op("act", lambda h: h.activation(x1b[:], v[:], AF.Copy), reads=vres, writes=["x1b"])
                    for hb in range(2):
                        for kk in range(8):
                            k = hb * 8 + kk
                            P.op("pe", lambda h: h.transpose(ps_tr[hb][:, kk * 128:(kk + 1) * 128], x1b[:, k * 128:(k + 1) * 128],
                                                             ident[:]),
                                 reads=["x1b", "ident"], writes=[("ps_tr", hb)], sig=(kk == 7))
                        P.op("act" if hb == 0 else "dve",
                             (lambda h: h.activation(xTt[:, 0:8, :].rearrange("p a b -> p (a b)"), ps_tr[0][:, :], AF.Copy)) if hb == 0 else
                             (lambda h: h.tensor_copy(xTt[:, 8:16, :].rearrange("p a b -> p (a b)"), ps_tr[1][:, :])),
                             reads=[("ps_tr", hb)], writes=[("xTt", hb)])
                    P.dma("sp", xTv[:, :, tt * 128:(tt + 1) * 128], xTt[:], reads=[("xTt", 0), ("xTt", 1)], writes=[("xT_out", tt)])

    def stage_D(self):
        self.outproj_ln([(self.ybT, 16, 16, False), (self.yaT, 0, 16, True)], self.w_out0, 32, self.x, 0, self.x1,
                        xT_out=self.x1T, rstd_dram=self.rstd_s)

    def stage_E(self):
        P, nc = self.P, self.nc
        x1Tv = self.x1T.rearrange("(k p) t -> p k t", p=128)
        with self.stage():
            xTb = self.sb("xTb", [128, 16, T], BF16)
            for k in range(16):
                P.dma("sp", xTb[:, k, :], x1Tv[:, k, :], writes=[("xTb", k)])
            wsl = [self.sb(f"wsl{i}", [128, 16, 128], BF16) for i in range(2)]
            stg = [self.sb(f"stg{i}", [128, 16, 128], F32) for i in range(2)]
            psA = [self.ps(f"psA{i}", [128, 512]) for i in range(4)]
            ob = [self.sb(f"ob{i}", [128, 512], BF16) for i in range(3)]
            co = 0
            cg_ = 0
            def issue_w(ib):
                sl = ib % 2
                P.dma("sp", stg[sl][:], self.w_in1[ib], writes=[("stg", sl)])
                P.op("pool", lambda h: h.tensor_copy(wsl[sl][:], stg[sl][:]), reads=[("stg", sl)], writes=[("w", sl)])

            issue_w(0)
            for i in range(32):
                slot = i % 2
                if i + 1 < 32:
                    issue_w(i + 1)
                for g in range(NG):
                    b = cg_ % 4
                    cg_ += 1
                    for k in range(16):
                        P.op("pe", lambda h: h.matmul(psA[b][:, :], wsl[slot][:, k, :], xTb[:, k, g * 512:(g + 1) * 512],
                                                      start=(k == 0), stop=(k == 15)),
                             reads=[("w", slot), ("xTb", k)], writes=[("psA", b)], sig=(k == 15))
                    o = co % 3
                    co += 1
                    P.op("act", lambda h: h.activation(ob[o][:], psA[b][:, :], AF.Copy if i < 16 else AF.Silu),
                         reads=[("psA", b)], writes=[("ob", o)])
                    dst = self.uT if i < 16 else self.sg1T
                    P.dma("sp", dst[(i % 16) * 128:(i % 16 + 1) * 128, g * 512:(g + 1) * 512], ob[o][:],
                          reads=[("ob", o)], writes=[("eo", i, g)])

    def stage_F(self):
        P, nc = self.P, self.nc
        TWO_PI = 2.0 * math.pi
        BL = 1024
        HB = BL // 512
        uTv = self.uT.rearrange("(k p) t -> p k t", p=128)
        with self.stage():
            sb = self.sb
            psV = [[self.ps(f"psV{i}{a}", [128, 512]) for a in "ri"] for i in range(2)]
            psY = [self.ps(f"psY{i}", [128, 512]) for i in range(2)]
            pl = sb("pl", [128, 3, 64], F32)
            cp = sb("cp", [128, 2, 64, 16], F32)
            d1 = sb("d1", [128, 16], F32)
            rmk = sb("rmk", [128, 8], F32)
            iot = sb("iot", [128, BL + 1], F32)
            pi_c = sb("pi_c", [128, 1], F32)
            P.dma("sp", pl[:], self.s5_pl, writes=["pl"])
            P.dma("sp", cp[:], self.s5_cp, writes=["cp"])
            P.dma("sp", d1[:], self.s5_d1, writes=["d1"])
            P.dma("sp", rmk[:], self.rowmask, writes=["rmk"])
            P.dma("sp", iot[:], self.iota513[:, 0:BL + 1].partition_broadcast(128), writes=["iot"])
            P.op("pool", lambda h: h.memset(pi_c[:], math.pi), writes=["pi_c"])
            dtp = sb("dtp", [128, 64], F32)
            rP = sb("rP", [128, 64], F32)
            thP = sb("thP", [128, 64], F32)
            P.op("act", lambda h: h.activation(dtp[:], pl[:, 2, :], AF.Exp), reads=["pl"], writes=["dtp"])
            P.op("dve", lambda h: h.tensor_tensor(rP[:], pl[:, 0, :], dtp[:], ALU.mult), reads=["pl", "dtp"], writes=["rP"])
            P.op("act", lambda h: h.activation(rP[:], rP[:], AF.Exp), reads=["rP"], writes=["rP"])
            P.op("dve", lambda h: h.tensor_tensor(thP[:], pl[:, 1, :], dtp[:], ALU.mult), reads=["pl", "dtp"], writes=["thP"])
            thm = sb("thm", [128, 64], F32)
            P.op("dve", lambda h: h.tensor_scalar(thm[:], thP[:], 0.0, TWO_PI, ALU.is_lt, ALU.mult), reads=["thP"], writes=["thm"])
            P.op("dve", lambda h: h.tensor_tensor(thP[:], thP[:], thm[:], ALU.add), reads=["thP", "thm"], writes=["thP"])

            def sincos(o_sin, o_cos, a_in, shape, tag):
                ki = sb(f"ki_{tag}", shape, mybir.dt.int32)
                kf = sb(f"kf_{tag}", shape, F32)
                rr = sb(f"rr_{tag}", shape, F32)
                mm_ = sb(f"mm_{tag}", shape, F32)
                r_ = [f"sc_{tag}"]
                P.op("dve", lambda h: h.tensor_scalar(ki[:], a_in, 1.0 / TWO_PI, None, ALU.mult), reads=r_, writes=r_)
                P.op("dve", lambda h: h.tensor_copy(kf[:], ki[:]), reads=r_, writes=r_)
                P.op("dve", lambda h: h.scalar_tensor_tensor(rr[:], kf[:], -TWO_PI, a_in, ALU.mult, ALU.add), reads=r_, writes=r_)
                P.op("dve", lambda h: h.tensor_scalar(mm_[:], rr[:], math.pi, -TWO_PI, ALU.is_gt, ALU.mult), reads=r_, writes=r_)
                P.op("dve", lambda h: h.tensor_tensor(mm_[:], mm_[:], rr[:], ALU.add), reads=r_, writes=r_)
                P.op("act", lambda h: h.activation(o_sin, mm_[:], AF.Sin), reads=r_, writes=r_)
                P.op("dve", lambda h: h.tensor_scalar(rr[:], rr[:], 0.5 * math.pi, None, ALU.add), reads=r_, writes=r_)
                P.op("dve", lambda h: h.tensor_scalar(mm_[:], rr[:], math.pi, -TWO_PI, ALU.is_gt, ALU.mult), reads=r_, writes=r_)
                P.op("dve", lambda h: h.tensor_tensor(mm_[:], mm_[:], rr[:], ALU.add), reads=r_, writes=r_)
                P.op("act", lambda h: h.activation(o_cos, mm_[:], AF.Sin), reads=r_, writes=r_)
            W3 = [128, 16, 64]
            bbr = sb("bbr", W3, F32)
            bbi = sb("bbi", W3, F32)
            outer_es = self._es
            inner_es = ExitStack()
            self._es = inner_es
            wl = sb("wl", [128, 5, 16, 64], F32)
            P.dma("sp", wl[:], self.s5_wl, writes=["wl"])
            dtw = sb("dtw", W3, F32)
            aw = sb("aw", W3, F32)
            tw = sb("tw", W3, F32)
            cw_ = sb("cw_", W3, F32)
            sw_ = sb("sw_", W3, F32)
            fre = sb("fre", W3, F32)
            fim = sb("fim", W3, F32)
            t0 = sb("t0", W3, F32)
            t1 = sb("t1", W3, F32)
            lre, lim, bre, bim = wl[:, 0], wl[:, 1], wl[:, 3], wl[:, 4]
            D_ = lambda fn, r, w: P.op("dve", fn, reads=r, writes=w)
            A_ = lambda fn, r, w, **kw: P.op("act", fn, reads=r, writes=w, **kw)
            A_(lambda h: h.activation(dtw[:], wl[:, 2], AF.Exp), ["wl"], ["dtw"])
            D_(lambda h: h.tensor_tensor(aw[:], lre, dtw[:], ALU.mult), ["wl", "dtw"], ["aw"])
            A_(lambda h: h.activation(aw[:], aw[:], AF.Exp), ["aw"], ["aw"])
            D_(lambda h: h.tensor_tensor(tw[:], lim, dtw[:], ALU.mult), ["wl", "dtw"], ["tw"])
            D_(lambda h: h.tensor_scalar(t0[:], tw[:], 0.0, TWO_PI, ALU.is_lt, ALU.mult), ["tw"], ["t0"])
            D_(lambda h: h.tensor_tensor(tw[:], tw[:], t0[:], ALU.add), ["tw", "t0"], ["tw", "sc_w"])
            sincos(sw_[:], cw_[:], tw[:], W3, "w")
            P.op("dve", lambda h: h.tensor_copy(sw_[:], sw_[:]), reads=["sc_w"], writes=["sw_", "cw_"])
            D_(lambda h: h.tensor_tensor(cw_[:], cw_[:], aw[:], ALU.mult), ["cw_", "aw"], ["cw_"])
            D_(lambda h: h.tensor_tensor(sw_[:], sw_[:], aw[:], ALU.mult), ["sw_", "aw"], ["sw_"])
            D_(lambda h: h.tensor_scalar(cw_[:], cw_[:], -1.0, None, ALU.add), ["cw_"], ["cw_"])
            D_(lambda h: h.tensor_tensor(t0[:], lre, lre, ALU.mult), ["wl"], ["t0"])
            D_(lambda h: h.tensor_tensor(t1[:], lim, lim, ALU.mult), ["wl"], ["t1"])
            D_(lambda h: h.tensor_tensor(t0[:], t0[:], t1[:], ALU.add), ["t0", "t1"], ["t0"])
            D_(lambda h: h.reciprocal(t0[:], t0[:]), ["t0"], ["t0"])
            D_(lambda h: h.tensor_tensor(fre[:], cw_[:], lre, ALU.mult), ["cw_", "wl"], ["fre"])
            D_(lambda h: h.tensor_tensor(t1[:], sw_[:], lim, ALU.mult), ["sw_", "wl"], ["t1"])
            D_(lambda h: h.tensor_tensor(fre[:], fre[:], t1[:], ALU.add), ["fre", "t1"], ["fre"])
            D_(lambda h: h.tensor_tensor(fre[:], fre[:], t0[:], ALU.mult), ["fre", "t0"], ["fre"])
            D_(lambda h: h.tensor_tensor(fim[:], sw_[:], lre, ALU.mult), ["sw_", "wl"], ["fim"])
            D_(lambda h: h.tensor_tensor(t1[:], cw_[:], lim, ALU.mult), ["cw_", "wl"], ["t1"])
            D_(lambda h: h.tensor_tensor(fim[:], fim[:], t1[:], ALU.subtract), ["fim", "t1"], ["fim"])
            D_(lambda h: h.tensor_tensor(fim[:], fim[:], t0[:], ALU.mult), ["fim", "t0"], ["fim"])
            D_(lambda h: h.tensor_tensor(bbr[:], fre[:], bre, ALU.mult), ["fre", "wl"], ["bbr"])
            D_(lambda h: h.tensor_tensor(t1[:], fim[:], bim, ALU.mult), ["fim", "wl"], ["t1"])
            D_(lambda h: h.tensor_tensor(bbr[:], bbr[:], t1[:], ALU.subtract), ["bbr", "t1"], ["bbr"])
            D_(lambda h: h.tensor_tensor(bbi[:], fre[:], bim, ALU.mult), ["fre", "wl"], ["bbi"])
            D_(lambda h: h.tensor_tensor(t1[:], fim[:], bre, ALU.mult), ["fim", "wl"], ["t1"])
            D_(lambda h: h.tensor_tensor(bbi[:], bbi[:], t1[:], ALU.add), ["bbi", "t1"], ["bbi"])
            P.barrier()
            inner_es.close()
            self._es = outer_es
            uc = [sb(f"uc{i}", [128, T], BF16) for i in range(2)]
            Lr = [sb(f"Lr{i}", [128, 128], BF16) for i in range(4)]
            Li = [sb(f"Li{i}", [128, 128], BF16) for i in range(4)]
            Cr = [sb(f"Cr{i}", [128, 128], BF16) for i in range(4)]
            nCr = [sb(f"nCr{i}", [128, 128], BF16) for i in range(4)]
            nCi = [sb(f"nCi{i}", [128, 128], BF16) for i in range(4)]
            cosT = [sb(f"cosT{i}", [128, BL + 1], F32) for i in range(4)]
            sinT = [sb(f"sinT{i}", [128, BL + 1], F32) for i in range(4)]
            ang = sb("ang", [128, BL + 1], F32)
            ki_p = sb("ki_p", [128, BL + 1], mybir.dt.int32)
            kf_p = sb("kf_p", [128, BL + 1], F32)
            rr_p = sb("rr_p", [128, BL + 1], F32)
            mm_p = sb("mm_p", [128, BL + 1], F32)

            def sincos_p(o_sin, o_cos):
                r_ = ["sc_p"]
                P.op("dve", lambda h: h.tensor_scalar(ki_p[:], ang[:], 1.0 / TWO_PI, None, ALU.mult), reads=r_, writes=r_)
                P.op("dve", lambda h: h.tensor_copy(kf_p[:], ki_p[:]), reads=r_, writes=r_)
                P.op("dve", lambda h: h.scalar_tensor_tensor(rr_p[:], kf_p[:], -TWO_PI, ang[:], ALU.mult, ALU.add), reads=r_, writes=r_)
                P.op("dve", lambda h: h.tensor_scalar(mm_p[:], rr_p[:], math.pi, -TWO_PI, ALU.is_gt, ALU.mult), reads=r_, writes=r_)
                P.op("dve", lambda h: h.tensor_tensor(mm_p[:], mm_p[:], rr_p[:], ALU.add), reads=r_, writes=r_)
                P.op("act", lambda h: h.activation(o_sin, mm_p[:], AF.Sin), reads=r_, writes=r_)
                P.op("dve", lambda h: h.tensor_scalar(rr_p[:], rr_p[:], 0.5 * math.pi, None, ALU.add), reads=r_, writes=r_)
                P.op("dve", lambda h: h.tensor_scalar(mm_p[:], rr_p[:], math.pi, -TWO_PI, ALU.is_gt, ALU.mult), reads=r_, writes=r_)
                P.op("dve", lambda h: h.tensor_tensor(mm_p[:], mm_p[:], rr_p[:], ALU.add), reads=r_, writes=r_)
                P.op("act", lambda h: h.activation(o_cos, mm_p[:], AF.Sin), reads=r_, writes=r_)
            qst = [sb(f"qst{i}", [128, 2], F32) for i in range(4)]
            qt_ = sb("qt_", [128, 2], F32)
            Vs = [[sb(f"Vs{i}{a}", [128, BL], F32) for a in "ri"] for i in range(2)]
            m1 = [sb(f"m1{i}", [128, BL], F32) for i in range(2)]
            m2 = [sb(f"m2{i}", [128, BL], F32) for i in range(2)]
            Wr = [sb(f"Wr{i}", [128, BL], F32) for i in range(2)]
            Wi = [sb(f"Wi{i}", [128, BL], F32) for i in range(2)]
            Gr = [sb(f"Gr{i}", [128, BL], F32) for i in range(2)]
            Gi = [sb(f"Gi{i}", [128, BL], F32) for i in range(2)]
            Pp = [[sb(f"Pp{i}{a}", [128, BL], BF16) for a in range(4)] for i in range(2)]
            yv = [sb(f"yv{i}", [128, 512], F32) for i in range(2)]
            ge1 = [sb(f"ge1{i}", [128, 512], F32) for i in range(2)]
            ge2 = [sb(f"ge2{i}", [128, 512], F32) for i in range(2)]
            yo = [sb(f"yo{i}", [128, 512], BF16) for i in range(2)]
            for i in range(4):
                for tl, nm in ((Cr, "Cr"), (nCr, "nCr"), (nCi, "nCi")):
                    P.op("pool", lambda h: h.memset(tl[i][:], 0.0), writes=[(nm, i)])
            P.dma("sp", uc[0][:], uTv[:, 0, :], writes=[("uc", 0)])
            cpb = 0
            nchunks = getattr(self, "f_chunks", 16)
            for j in range(nchunks):
                us = j % 2
                if j + 1 < nchunks:
                    P.dma("sp", uc[(j + 1) % 2][:], uTv[:, j + 1, :], writes=[("uc", (j + 1) % 2)])
                for pc in range(4):
                    pr = 4 * j + pc
                    for g2 in range(2):
                        gl = 2 * pc + g2
                        P.op("dve", lambda h: h.tensor_scalar(Lr[pc][:, g2 * 64:(g2 + 1) * 64], bbr[:, j, :], rmk[:, gl:gl + 1], None, ALU.mult),
                             reads=["bbr", "rmk"], writes=[("Lr", pc)])
                        P.op("dve", lambda h: h.tensor_scalar(Li[pc][:, g2 * 64:(g2 + 1) * 64], bbi[:, j, :], rmk[:, gl:gl + 1], None, ALU.mult),
                             reads=["bbi", "rmk"], writes=[("Li", pc)])
                        rs_ = slice(g2 * 64, (g2 + 1) * 64)
                        cs_ = slice(gl * 16, gl * 16 + 16)
                        P.op("act", lambda h: h.activation(Cr[pc][rs_, cs_], cp[rs_, 0, pr, :], AF.Copy), reads=["cp"], writes=[("Cr", pc)])
                        P.op("act", lambda h: h.activation(nCr[pc][rs_, cs_], cp[rs_, 0, pr, :], AF.Copy, scale=-1.0), reads=["cp"], writes=[("nCr", pc)])
                        P.op("act", lambda h: h.activation(nCi[pc][rs_, cs_], cp[rs_, 1, pr, :], AF.Copy, scale=-1.0), reads=["cp"], writes=[("nCi", pc)])
                    P.op("dve", lambda h: h.tensor_scalar(ang[:], iot[:], thP[:, pr:pr + 1], None, ALU.mult),
                         reads=["iot"], writes=["ang", "sc_p", ("sinT", pc), ("cosT", pc)], strict=["thP"])
                    sincos_p(sinT[pc][:], cosT[pc][:])
                    P.op("dve", lambda h: h.tensor_copy(qt_[:, 0:1], qt_[:, 0:1]), reads=["sc_p"], writes=[("sinT", pc), ("cosT", pc)])
                    P.op("pool", lambda h: h.memset(qst[pc][:], 0.0), writes=[("qst", pc)])
                for b in range(T // BL):
                    for pc in range(4):
                        pr = 4 * j + pc
                        vb = cpb % 2
                        cpb += 1
                        c_, s_t = cosT[pc][:, 0:BL], sinT[pc][:, 0:BL]
                        for hb in range(HB):
                            bs = slice(b * BL + hb * 512, b * BL + (hb + 1) * 512)
                            hs_ = slice(hb * 512, (hb + 1) * 512)
                            P.op("pe", lambda h: h.matmul(psV[hb][0][:, :], Lr[pc][:], uc[us][:, bs], start=True, stop=True),
                                 reads=[("Lr", pc), ("uc", us)], writes=[("psV", hb, 0)])
                            P.op("pe", lambda h: h.matmul(psV[hb][1][:, :], Li[pc][:], uc[us][:, bs], start=True, stop=True),
                                 reads=[("Li", pc), ("uc", us)], writes=[("psV", hb, 1)])
                            P.op("act", lambda h: h.activation(Vs[vb][0][:, hs_], psV[hb][0][:, :], AF.Copy), reads=[("psV", hb, 0)], writes=[("Vs", vb, 0)])
                            P.op("act", lambda h: h.activation(Vs[vb][1][:, hs_], psV[hb][1][:, :], AF.Copy), reads=[("psV", hb, 1)], writes=[("Vs", vb, 1)])
                        P.op("dve", lambda h: h.tensor_tensor(m1[vb][:], Vs[vb][0][:], c_, ALU.mult), reads=[("Vs", vb, 0), ("cosT", pc)], writes=[("m1", vb)], big=True)
                        P.op("dve", lambda h: h.tensor_tensor(m2[vb][:], Vs[vb][1][:], s_t, ALU.mult), reads=[("Vs", vb, 1), ("sinT", pc)], writes=[("m2", vb)], big=True)
                        P.op("dve", lambda h: h.tensor_tensor(Wr[vb][:], m1[vb][:], m2[vb][:], ALU.add), reads=[("m1", vb), ("m2", vb)], writes=[("Wr", vb)], big=True)
                        P.op("dve", lambda h: h.tensor_tensor(m1[vb][:], Vs[vb][1][:], c_, ALU.mult), reads=[("Vs", vb, 1), ("cosT", pc)], writes=[("m1", vb)], big=True)
                        P.op("dve", lambda h: h.tensor_tensor(m2[vb][:], Vs[vb][0][:], s_t, ALU.mult), reads=[("Vs", vb, 0), ("sinT", pc)], writes=[("m2", vb)], big=True)
                        P.op("dve", lambda h: h.tensor_tensor(Wi[vb][:], m1[vb][:], m2[vb][:], ALU.subtract), reads=[("m1", vb), ("m2", vb)], writes=[("Wi", vb)], big=True)
                        rbc = rP[:, pr:pr + 1].broadcast_to([128, BL])
                        P.op("dve", lambda h: h.tensor_tensor_scan(Gr[vb][:], rbc, Wr[vb][:], qst[pc][:, 0:1], ALU.mult, ALU.add),
                             reads=[("Wr", vb), "rP"], writes=[("Gr", vb)], strict=[("qst", pc)], big=True)
                        P.op("dve", lambda h: h.tensor_tensor_scan(Gi[vb][:], rbc, Wi[vb][:], qst[pc][:, 1:2], ALU.mult, ALU.add),
                             reads=[("Wi", vb), "rP"], writes=[("Gi", vb)], strict=[("qst", pc)], big=True)
                        C5, S5 = cosT[pc][:, BL:BL + 1], sinT[pc][:, BL:BL + 1]
                        P.op("dve", lambda h: h.tensor_tensor(qt_[:, 0:1], Gi[vb][:, BL - 1:BL], S5, ALU.mult), reads=[("Gi", vb), ("sinT", pc)], writes=["qt_"])
                        P.op("dve", lambda h: h.tensor_tensor(qt_[:, 1:2], Gr[vb][:, BL - 1:BL], S5, ALU.mult), reads=[("Gr", vb), ("sinT", pc)], writes=["qt_"])
                        P.op("dve", lambda h: h.tensor_tensor(qst[pc][:, 0:1], Gr[vb][:, BL - 1:BL], C5, ALU.mult), reads=[("Gr", vb), ("cosT", pc)], writes=[("qst", pc)])
                        P.op("dve", lambda h: h.tensor_tensor(qst[pc][:, 1:2], Gi[vb][:, BL - 1:BL], C5, ALU.mult), reads=[("Gi", vb), ("cosT", pc)], writes=[("qst", pc)])
                        P.op("dve", lambda h: h.tensor_tensor(qst[pc][:, 0:1], qst[pc][:, 0:1], qt_[:, 0:1], ALU.subtract), reads=[("qst", pc), "qt_"], writes=[("qst", pc)])
                        P.op("dve", lambda h: h.tensor_tensor(qst[pc][:, 1:2], qst[pc][:, 1:2], qt_[:, 1:2], ALU.add), reads=[("qst", pc), "qt_"], writes=[("qst", pc)])
                        P.op("dve", lambda h: h.tensor_tensor(Pp[vb][0][:], Gr[vb][:], c_, ALU.mult), reads=[("Gr", vb), ("cosT", pc)], writes=[("Pp", vb, 0)], big=True)
                        P.op("dve", lambda h: h.tensor_tensor(Pp[vb][1][:], Gi[vb][:], c_, ALU.mult), reads=[("Gi", vb), ("cosT", pc)], writes=[("Pp", vb, 1)], big=True)
                        P.op("dve", lambda h: h.tensor_tensor(Pp[vb][2][:], Gi[vb][:], s_t, ALU.mult), reads=[("Gi", vb), ("sinT", pc)], writes=[("Pp", vb, 2)], big=True)
                        P.op("dve", lambda h: h.tensor_tensor(Pp[vb][3][:], Gr[vb][:], s_t, ALU.mult), reads=[("Gr", vb), ("sinT", pc)], writes=[("Pp", vb, 3)], big=True)
                        for hb in range(HB):
                            hs_ = slice(hb * 512, (hb + 1) * 512)
                            for a, (wt, wn) in enumerate(((Cr, "Cr"), (nCi, "nCi"), (nCr, "nCr"), (nCi, "nCi"))):
                                P.op("pe", lambda h: h.matmul(psY[hb][:, :], wt[pc][:], Pp[vb][a][:, hs_], start=(pc == 0 and a == 0),
                                                              stop=(pc == 3 and a == 3)),
                                     reads=[(wn, pc), ("Pp", vb, a)], writes=[("psY", hb)], sig=(a == 3))
                    for yb_ in range(HB):
                        bs = slice(b * BL + yb_ * 512, b * BL + (yb_ + 1) * 512)
                        P.op("dve", lambda h: h.scalar_tensor_tensor(yv[yb_][:], uc[us][:, bs], d1[:, j:j + 1], psY[yb_][:, :], ALU.mult, ALU.add),
                             reads=[("uc", us), "d1", ("psY", yb_)], writes=[("yv", yb_)])
                        P.op("act", lambda h: h.activation(ge1[yb_][:], yv[yb_][:], AF.Square), reads=[("yv", yb_)], writes=[("ge1", yb_)])
                        P.op("pool", lambda h: h.tensor_scalar(ge1[yb_][:], ge1[yb_][:], 0.044715, 1.0, ALU.mult, ALU.add), reads=[("ge1", yb_)], writes=[("ge1", yb_)])
                        P.op("pool", lambda h: h.tensor_tensor(ge2[yb_][:], ge1[yb_][:], yv[yb_][:], ALU.mult), reads=[("ge1", yb_), ("yv", yb_)], writes=[("ge2", yb_)])
                        P.op("act", lambda h: h.activation(ge2[yb_][:], ge2[yb_][:], AF.Sigmoid, scale=1.5957691216057308), reads=[("ge2", yb_)], writes=[("ge2", yb_)])
                        P.op("pool", lambda h: h.tensor_tensor(yo[yb_][:], ge2[yb_][:], yv[yb_][:], ALU.mult), reads=[("ge2", yb_), ("yv", yb_)], writes=[("yo", yb_)])
                        P.dma("sp", self.ygT[j * 128:(j + 1) * 128, bs], yo[yb_][:], reads=[("yo", yb_)], writes=[("ygT", j, b, yb_)])

    def stage_G1(self):
        P, nc = self.P, self.nc
        ygv = self.ygT.rearrange("(k p) t -> p k t", p=128)
        with self.stage():
            xTb = self.sb("xTb", [128, 16, T], BF16)
            for k in range(16):
                P.dma("sp", xTb[:, k, :], ygv[:, k, :], writes=[("xTb", k)])
            wa = [self.sb(f"wa{i}", [128, 16, 128], BF16) for i in range(2)]
            wb = [self.sb(f"wb{i}", [128, 16, 128], BF16) for i in range(2)]
            stga = [self.sb(f"stga{i}", [128, 16, 128], F32) for i in range(2)]
            stgb = [self.sb(f"stgb{i}", [128, 16, 128], F32) for i in range(2)]
            psA = [[self.ps(f"psA{i}{a}", [128, 512]) for a in "ab"] for i in range(2)]
            sgt = [self.sb(f"sgt{i}", [128, 512], BF16) for i in range(2)]
            sgb = [self.sb(f"sgb{i}", [128, 512], F32) for i in range(2)]
            tt_ = [self.sb(f"tt{i}", [128, 512], F32) for i in range(2)]
            ob = [self.sb(f"ob{i}", [128, 512], BF16) for i in range(2)]
            c2 = 0
            def issue_w(ib):
                sl = ib % 2
                P.dma("sp", stga[sl][:], self.w_glu[ib], writes=[("stga", sl)])
                P.op("pool", lambda h: h.tensor_copy(wa[sl][:], stga[sl][:]), reads=[("stga", sl)], writes=[("wa", sl)])
                P.dma("sp", stgb[sl][:], self.w_glu[16 + ib], writes=[("stgb", sl)])
                P.op("pool", lambda h: h.tensor_copy(wb[sl][:], stgb[sl][:]), reads=[("stgb", sl)], writes=[("wb", sl)])

            issue_w(0)
            for i in range(16):
                slot = i % 2
                if i + 1 < 16:
                    issue_w(i + 1)
                for g in range(NG):
                    b = c2 % 2
                    c2 += 1
                    gs_ = slice(g * 512, (g + 1) * 512)
                    P.dma("sp", sgt[b][:], self.sg1T[i * 128:(i + 1) * 128, gs_], writes=[("sgt", b)])
                    for a, wt, wn in ((0, wa, "wa"), (1, wb, "wb")):
                        for k in range(16):
                            P.op("pe", lambda h: h.matmul(psA[b][a][:, :], wt[slot][:, k, :], xTb[:, k, gs_], start=(k == 0), stop=(k == 15)),
                                 reads=[(wn, slot), ("xTb", k)], writes=[("psA", b, a)], sig=(k == 15))
                    P.op("act", lambda h: h.activation(sgb[b][:], psA[b][1][:, :], AF.Sigmoid), reads=[("psA", b, 1)], writes=[("sgb", b)])
                    P.op("dve", lambda h: h.tensor_tensor(tt_[b][:], psA[b][0][:, :], sgb[b][:], ALU.mult), reads=[("psA", b, 0), ("sgb", b)], writes=[("tt", b)])
                    P.op("pool", lambda h: h.tensor_tensor(ob[b][:], tt_[b][:], sgt[b][:], ALU.mult), reads=[("tt", b), ("sgt", b)], writes=[("ob", b)])
                    P.dma("sp", self.y2T[i * 128:(i + 1) * 128, gs_], ob[b][:], reads=[("ob", b)], writes=[("y2T", i, g)])

    def stage_G2(self):
        self.outproj_ln([(self.y2T, 0, 16, False)], self.w_out1, 16, self.x1, 1, self.out)

    def build(self, stages="AaBCDEFGH"):
        self.declare()
        for ch, fn in (("A", self.stage_A), ("a", self.stage_A2), ("B", self.stage_B), ("C", self.stage_C),
                       ("D", self.stage_D), ("E", self.stage_E), ("F", self.stage_F), ("G", self.stage_G1),
                       ("H", self.stage_G2)):
            if ch in stages:
                fn()
        return self.nc


def _rope_tables():
    half = 16
    inv_freq = (500000.0 ** (-(np.arange(half, dtype=np.float32) * 2.0 / 32))).astype(np.float32)
    pos = np.arange(T, dtype=np.float32)
    ang = (pos[None, :] * inv_freq[:, None]).astype(np.float32)
    c = np.cos(ang).astype(np.float32)
    s = np.sin(ang).astype(np.float32)
    return np.ascontiguousarray(np.concatenate([c, c], 0)), np.ascontiguousarray(np.concatenate([-s, s], 0))


def _constants():
    cosT, sinS = _rope_tables()
    pm = np.zeros((32, 32), np.float32)
    for m in range(32):
        pm[(m + 16) % 32, m] = 1.0
    sel = np.zeros((32, 32, 128), np.float32)
    for h in range(32):
        sel[h, h, :] = 1.0
    triu = np.triu(np.ones((128, 128), np.float32))
    maskg = np.ascontiguousarray(np.concatenate([triu, np.ones((128, 128), np.float32), triu], 1))
    vb = np.zeros((128, NT, NCH), np.float32)
    for qt in range(NT):
        vb[:, qt, qt // 2:] = -1e30
    rowmask = np.zeros((128, 8), np.float32)
    for p in range(128):
        rowmask[p, p // 16] = 1.0
    return dict(cosT=cosT, sinS=sinS, pm32=pm, sel=sel, triu=triu, maskg=maskg, ident=np.eye(128, dtype=np.float32),
                vbias=np.ascontiguousarray(vb.reshape(128, NT * NCH)), rowmask=rowmask,
                iota513=np.arange(1025, dtype=np.float32).reshape(1, 1025))


def _shared_inputs(inp):
    f = lambda a: np.ascontiguousarray(np.asarray(a, dtype=np.float32))
    d = {}
    w0 = np.asarray(inp["in0_w"][0], dtype=np.float32)
    def tile_cols(w, c0, m):
        return w[:, c0:c0 + m].reshape(16, 128, m).transpose(1, 0, 2)
    cols_a = [C_XBC + 128 * i for i in range(24)] + [C_Z + 128 * i for i in range(16)] + \
             [C_Q + 128 * i for i in range(16)] + [C_K + 128 * i for i in range(16)]
    d["w0a"] = f(np.stack([tile_cols(w0, c, 128) for c in cols_a], 0))
    d["w0b"] = f(np.stack([tile_cols(w0, C_V + 512 * i, 512) for i in range(4)] +
                          [tile_cols(w0, C_G + 512 * i, 512) for i in range(4)], 0))
    d["w0dt"] = f(tile_cols(w0, C_DT, 32))
    cwv = np.asarray(inp["conv_w"][0])
    d["cw"] = f(cwv.T.reshape(24, 128, 4).transpose(1, 0, 2))
    d["cb"] = f(np.asarray(inp["conv_b"][0]).reshape(24, 128).T)
    d["dtb"] = f(inp["dt_bias"])
    d["alog"] = f(inp["a_log"])
    dsk = np.asarray(inp["ssd_d"][0])
    d["ssd_dp"] = f(np.repeat(dsk.reshape(16, 2), 64, axis=1).T)
    d["normg"] = f(np.asarray(inp["ssd_norm_g"][0]).reshape(16, 128).T)
    d["out0_w"] = f(inp["out0_w"][0])
    d["ln_g"] = f(inp["ln_g"])
    d["ln_b"] = f(inp["ln_b"])
    w1 = np.asarray(inp["in1_w"][0], dtype=np.float32)
    d["w1t"] = f(np.stack([tile_cols(w1, 128 * i, 128) for i in range(32)], 0))
    wg = np.asarray(inp["glu_w"][0], dtype=np.float32)
    d["wgt"] = f(np.stack([tile_cols(wg, 128 * i, 128) for i in range(32)], 0))
    d["out1_w"] = f(inp["out1_w"][0])
    lre, lim = np.asarray(inp["s5_lam_re"][0]), np.asarray(inp["s5_lam_im"][0])
    ldt = np.asarray(inp["s5_log_dt"][0])
    bre, bim = np.asarray(inp["s5_b_re"][0]), np.asarray(inp["s5_b_im"][0])
    cre, cim = np.asarray(inp["s5_c_re"][0]), np.asarray(inp["s5_c_im"][0])
    ldt_n = np.repeat(ldt[:, None], 64, axis=1)
    pl = np.stack([a.reshape(64, 2, 64).transpose(1, 2, 0).reshape(128, 64) for a in (lre, lim, ldt_n)], 1)
    d["s5_pl"] = f(pl)
    def wl_gn(a):
        return np.repeat(a.reshape(16, 8, 1, 64), 16, axis=2).transpose(1, 2, 0, 3).reshape(128, 16, 64)
    def wl_gnm(a):
        return a.reshape(16, 8, 64, 16).transpose(1, 3, 0, 2).reshape(128, 16, 64)
    d["s5_wl"] = f(np.stack([wl_gn(lre), wl_gn(lim), wl_gn(ldt_n), wl_gnm(bre), wl_gnm(bim)], 1))
    def cp_(a):
        return a.reshape(64, 2, 16, 64).transpose(1, 3, 0, 2).reshape(128, 64, 16)
    d["s5_cp"] = f(np.stack([cp_(cre), cp_(cim)], 1))
    d["s5_d1"] = f(np.asarray(inp["s5_d"][0]).reshape(16, 128).T)
    d.update(_constants())
    return d


N_CORES = 4


def kernel(**inputs):
    x = np.asarray(inputs["x"], dtype=np.float32)
    shared = _shared_inputs(inputs)
    nc = bass.Bass("TRN2", target_bir_lowering=False)
    mk = MK(nc)
    mk.build("AaBCDEFGH")
    in_maps = []
    for b in range(N_CORES):
        m = dict(shared)
        m["x"] = np.ascontiguousarray(x[b])
        m["xT"] = np.ascontiguousarray(x[b].T)
        in_maps.append(m)
    res = run_bass_kernel_spmd(nc, in_maps, core_ids=list(range(N_CORES)))
    return np.stack([np.asarray(res.results[b]["out"], dtype=np.float32) for b in range(N_CORES)], 0)
```
